# Optimizing a Trainium2 kernel written in Bass

```python
import jax, jax.numpy as jnp
from jax import lax
import numpy as np

D_MODEL = 2048
BATCH = 4
SEQ = 2048
DEPTH = 1
DEC_BATCH = 128
DEC_SEQ = 8
PAST_LEN = 16384
PAGE_SIZE = 128

C_CONV = D_MODEL // 4
CONV_K = 31
DN_HEAD_DIM = 128
DN_HEADS = D_MODEL // 256
DN_WIDTH = DN_HEADS * DN_HEAD_DIM
DN_CONV_K = 4
DN_CHUNK = 64
MEM_HEADS = 4
MEM_HEAD_DIM = D_MODEL // 16
MEM_WIDTH = MEM_HEADS * MEM_HEAD_DIM
MEM_TOKENS = 256
MIX_WIDTH = C_CONV + DN_WIDTH + MEM_WIDTH
IN_SPLIT_SIZES = (C_CONV, C_CONV, C_CONV, 3 * DN_WIDTH, DN_WIDTH, DN_HEADS, DN_HEADS, MEM_WIDTH, MEM_WIDTH)
N_IN = 3 * C_CONV + 4 * DN_WIDTH + 2 * DN_HEADS + 2 * MEM_WIDTH
DEEPNORM_ALPHA = (2 * DEPTH) ** 0.25
DEEPNORM_BETA = (8 * DEPTH) ** -0.25
LN_EPS = 1e-5
NORM_EPS = 1e-6

kernel_name = "hymba_conformer_gdn_memory_step"


def layer_norm(x, g, b):
    xf = x.astype(jnp.float32)
    mu = jnp.mean(xf, axis=-1, keepdims=True)
    xc = xf - mu
    var = jnp.mean(xc * xc, axis=-1, keepdims=True)
    y = xc * lax.rsqrt(var + LN_EPS) * g.astype(jnp.float32) + b.astype(jnp.float32)
    return y.astype(x.dtype)


def l2_normalize(x):
    xf = x.astype(jnp.float32)
    return xf * lax.rsqrt(jnp.sum(xf * xf, axis=-1, keepdims=True) + NORM_EPS)


def causal_dwconv(x, prev, w):
    xp = jnp.concatenate([prev.astype(x.dtype), x], axis=1)
    y = lax.conv_general_dilated(
        xp, w[:, None, :].astype(x.dtype), window_strides=(1,), padding="VALID",
        dimension_numbers=("NWC", "WIO", "NWC"), feature_group_count=x.shape[-1])
    return y, xp[:, xp.shape[1] - (w.shape[0] - 1):]


def gated_delta_rule(q, k, v, g, beta, s0):
    b, l, h, _ = q.shape
    dv = v.shape[-1]
    c = min(DN_CHUNK, l)
    n = -(-l // c)
    pad = n * c - l

    def blocks(t):
        t = t.astype(jnp.float32)
        t = jnp.pad(t, [(0, 0), (0, pad)] + [(0, 0)] * (t.ndim - 2))
        t = t.reshape((b, n, c) + t.shape[2:])
        return jnp.swapaxes(t, 2, 3)

    q, k, v, g, beta = (blocks(t) for t in (q, k, v, g, beta))
    gcum = jnp.cumsum(g, axis=-1)
    pos = jnp.arange(c)
    incl = pos[:, None] >= pos[None, :]
    strict = pos[:, None] > pos[None, :]
    decay = jnp.exp(jnp.where(incl, gcum[..., :, None] - gcum[..., None, :], -jnp.inf))
    kb = k * beta[..., None]
    m = jnp.where(strict, jnp.einsum("bnhik,bnhjk->bnhij", kb, k) * decay, 0.0)
    eye = jnp.eye(c, dtype=jnp.float32)
    tinv = lax.linalg.triangular_solve(eye + m, jnp.broadcast_to(eye, m.shape),
                                       left_side=True, lower=True, unit_diagonal=True)
    u0 = tinv @ (v * beta[..., None])
    w = tinv @ (kb * jnp.exp(gcum)[..., None])
    a_qk = jnp.einsum("bnhik,bnhjk->bnhij", q, k) * decay
    q_dec = q * jnp.exp(gcum)[..., None]
    k_dec = k * jnp.exp(gcum[..., -1:] - gcum)[..., None]
    g_last = jnp.exp(gcum[..., -1])[..., None, None]

    def step(s, xs):
        u0_c, w_c, aqk_c, qd_c, kd_c, gl_c = xs
        u = u0_c - jnp.einsum("bhck,bhkv->bhcv", w_c, s)
        o = jnp.einsum("bhck,bhkv->bhcv", qd_c, s) + jnp.einsum("bhij,bhjv->bhiv", aqk_c, u)
        s = s * gl_c + jnp.einsum("bhck,bhcv->bhkv", kd_c, u)
        return s, o

    xs = tuple(jnp.moveaxis(t, 1, 0) for t in (u0, w, a_qk, q_dec, k_dec, g_last))
    s_final, o = lax.scan(step, s0.astype(jnp.float32), xs)
    o = jnp.transpose(o, (1, 0, 3, 2, 4)).reshape(b, n * c, h, dv)[:, :l]
    return o, s_final


def hybrid_layer(x, conv_prev, qkv_prev, s0, mem_k, mem_v, w_in, conv_w, conv_b, conv_ln_g, conv_ln_b,
                 qkv_conv_w, a_log, dt_bias, delta_norm_w, w_out, ln_g, ln_b):
    b, l, _ = x.shape
    dt = x.dtype
    f32 = jnp.float32
    proj = jnp.einsum("bld,de->ble", x, w_in)
    splits = np.cumsum(IN_SPLIT_SIZES)[:-1].tolist()
    glu_a, glu_b, conv_gate, qkv, dn_gate, beta_in, decay_in, mem_q, mem_gate = jnp.split(proj, splits, axis=-1)

    u = glu_a * jax.nn.sigmoid(glu_b)
    hc, conv_state = causal_dwconv(u, conv_prev, conv_w)
    hc = layer_norm(hc + conv_b.astype(dt), conv_ln_g, conv_ln_b)
    out_conv = (jax.nn.silu(hc) * jax.nn.silu(conv_gate)).astype(dt)

    qkv_c, qkv_state = causal_dwconv(qkv, qkv_prev, qkv_conv_w)
    qkv_c = jax.nn.silu(qkv_c)
    q, k, v = jnp.split(qkv_c, 3, axis=-1)
    q = l2_normalize(q.reshape(b, l, DN_HEADS, DN_HEAD_DIM)) * (DN_HEAD_DIM ** -0.5)
    k = l2_normalize(k.reshape(b, l, DN_HEADS, DN_HEAD_DIM))
    v = v.reshape(b, l, DN_HEADS, DN_HEAD_DIM)
    beta = jax.nn.sigmoid(beta_in.astype(f32))
    g = -jnp.exp(a_log.astype(f32)) * jax.nn.softplus(decay_in.astype(f32) + dt_bias.astype(f32))
    o, s_final = gated_delta_rule(q, k, v, g, beta, s0)
    o = o * lax.rsqrt(jnp.mean(o * o, axis=-1, keepdims=True) + NORM_EPS) * delta_norm_w.astype(f32)
    o = o * jax.nn.silu(dn_gate.astype(f32).reshape(b, l, DN_HEADS, DN_HEAD_DIM))
    out_dn = o.reshape(b, l, DN_WIDTH).astype(dt)

    mq = mem_q.reshape(b, l, MEM_HEADS, MEM_HEAD_DIM)
    scores = jnp.einsum("blhd,bmhd->bhlm", mq, mem_k.astype(dt)).astype(f32) * (MEM_HEAD_DIM ** -0.5)
    p = jax.nn.softmax(scores, axis=-1).astype(dt)
    om = jnp.einsum("bhlm,bmhd->blhd", p, mem_v.astype(dt)).reshape(b, l, MEM_WIDTH)
    out_mem = (om * jax.nn.silu(mem_gate)).astype(dt)

    mixed = jnp.concatenate([out_conv, out_dn, out_mem], axis=-1)
    h = jnp.einsum("ble,ed->bld", mixed, w_out)
    y = layer_norm(DEEPNORM_ALPHA * x + h, ln_g, ln_b)
    return y, conv_state, qkv_state, s_final


def setup_inputs(seed: int = 0) -> dict:
    key = jax.random.key(seed)
    ks = jax.random.split(key, 22)
    f32 = jnp.float32
    nrm = lambda k, shape, scale: jax.random.normal(k, shape, f32) * scale
    return {
        "x_prompt": nrm(ks[0], (BATCH, SEQ, D_MODEL), 1.0),
        "x_sample": nrm(ks[1], (DEC_BATCH, DEC_SEQ, D_MODEL), 1.0),
        "mem_prompt": nrm(ks[2], (BATCH, MEM_TOKENS, D_MODEL), 1.0),
        "state_conv": nrm(ks[3], (DEPTH, DEC_BATCH, CONV_K - 1, C_CONV), 0.5),
        "state_qkv_conv": nrm(ks[4], (DEPTH, DEC_BATCH, DN_CONV_K - 1, 3 * DN_WIDTH), 1.0),
        "state_delta": nrm(ks[5], (DEPTH, DEC_BATCH, DN_HEADS, DN_HEAD_DIM, DN_HEAD_DIM), DN_HEAD_DIM ** -0.5),
        "cache_mem_k": nrm(ks[6], (DEPTH, DEC_BATCH, MEM_TOKENS, MEM_HEADS, MEM_HEAD_DIM), 1.0),
        "cache_mem_v": nrm(ks[7], (DEPTH, DEC_BATCH, MEM_TOKENS, MEM_HEADS, MEM_HEAD_DIM), 1.0),
        "w_in": nrm(ks[8], (DEPTH, D_MODEL, N_IN), D_MODEL ** -0.5),
        "conv_w": nrm(ks[9], (DEPTH, CONV_K, C_CONV), CONV_K ** -0.5),
        "conv_b": nrm(ks[10], (DEPTH, C_CONV), 0.01),
        "conv_ln_g": 1.0 + nrm(ks[11], (DEPTH, C_CONV), 0.01),
        "conv_ln_b": nrm(ks[12], (DEPTH, C_CONV), 0.01),
        "qkv_conv_w": nrm(ks[13], (DEPTH, DN_CONV_K, 3 * DN_WIDTH), DN_CONV_K ** -0.5),
        "a_log": jnp.log(jax.random.uniform(ks[14], (DEPTH, DN_HEADS), f32, 1.0, 16.0)),
        "dt_bias": nrm(ks[15], (DEPTH, DN_HEADS), 0.1),
        "delta_norm_w": 1.0 + nrm(ks[16], (DEPTH, DN_HEAD_DIM), 0.01),
        "w_mem_k": nrm(ks[17], (DEPTH, D_MODEL, MEM_WIDTH), D_MODEL ** -0.5),
        "w_mem_v": nrm(ks[18], (DEPTH, D_MODEL, MEM_WIDTH), D_MODEL ** -0.5),
        "w_out": nrm(ks[19], (DEPTH, MIX_WIDTH, D_MODEL), (MIX_WIDTH ** -0.5) * DEEPNORM_BETA),
        "ln_g": 1.0 + nrm(ks[20], (DEPTH, D_MODEL), 0.01),
        "ln_b": nrm(ks[21], (DEPTH, D_MODEL), 0.01),
    }


def reference(x_prompt, x_sample, mem_prompt, state_conv, state_qkv_conv, state_delta, cache_mem_k, cache_mem_v,
              w_in, conv_w, conv_b, conv_ln_g, conv_ln_b, qkv_conv_w, a_log, dt_bias, delta_norm_w,
              w_mem_k, w_mem_v, w_out, ln_g, ln_b):
    hp, hs = x_prompt, x_sample
    bp, bs = x_prompt.shape[0], x_sample.shape[0]
    conv_p, qkv_p, delta_p, memk_p, memv_p = [], [], [], [], []
    conv_s, qkv_s, delta_s = [], [], []
    for i in range(DEPTH):
        lw = (w_in[i], conv_w[i], conv_b[i], conv_ln_g[i], conv_ln_b[i], qkv_conv_w[i], a_log[i], dt_bias[i],
              delta_norm_w[i], w_out[i], ln_g[i], ln_b[i])
        mk = jnp.einsum("bmd,de->bme", mem_prompt, w_mem_k[i]).reshape(bp, MEM_TOKENS, MEM_HEADS, MEM_HEAD_DIM)
        mv = jnp.einsum("bmd,de->bme", mem_prompt, w_mem_v[i]).reshape(bp, MEM_TOKENS, MEM_HEADS, MEM_HEAD_DIM)
        hp, c_st, q_st, s_st = hybrid_layer(
            hp, jnp.zeros((bp, CONV_K - 1, C_CONV), hp.dtype), jnp.zeros((bp, DN_CONV_K - 1, 3 * DN_WIDTH), hp.dtype),
            jnp.zeros((bp, DN_HEADS, DN_HEAD_DIM, DN_HEAD_DIM), jnp.float32), mk, mv, *lw)
        conv_p.append(c_st); qkv_p.append(q_st); delta_p.append(s_st); memk_p.append(mk); memv_p.append(mv)
        hs, c_st, q_st, s_st = hybrid_layer(
            hs, state_conv[i], state_qkv_conv[i], state_delta[i], cache_mem_k[i], cache_mem_v[i], *lw)
        conv_s.append(c_st); qkv_s.append(q_st); delta_s.append(s_st)
    return (hp, hs, jnp.stack(conv_p), jnp.stack(qkv_p), jnp.stack(delta_p), jnp.stack(memk_p), jnp.stack(memv_p),
            jnp.stack(conv_s), jnp.stack(qkv_s), jnp.stack(delta_s))
```

```python
import os
import contextlib
from contextlib import ExitStack
import numpy as np
import ml_dtypes
import concourse.bass as bass
import concourse.mybir as mybir
from concourse.bass_utils import run_bass_kernel_spmd

F32 = mybir.dt.float32
BF = mybir.dt.bfloat16
AF = mybir.ActivationFunctionType
ALU = mybir.AluOpType
AX = mybir.AxisListType

SAME_SYNC = True
STAGE = int(os.environ.get('K_STAGE', '9'))
SUB = int(os.environ.get('K_SUB', '99'))
NT = 17
NCOL = NT * 128
NM = 1152
O_GA, O_GB, O_CG, O_Q, O_K, O_V, O_DG, O_BT, O_DC, O_MQ, O_MG = 0, 512, 1024, 1536, 2560, 3584, 4608, 5632, 5640, 5648, 6160
N_IN = 6672
ALPHA = 2.0 ** 0.25
NEG = -30000.0

CF_ID, CF_ONE, CF_TRIP, CF_NEGP, CF_TRIS, CF_BLKS, CF_NEGS, CF_ROWM = 0, 128, 256, 384, 512, 640, 768, 896
NCF = 912
CB_ID, CB_ONE, CB_STRP, CB_STRS, CB_COLM = 0, 128, 256, 384, 512
CB_TRIP, CB_TRIS, CB_BLKS, CB_NEGP, CB_NEGS = 2560, 2688, 2816, 2944, 3072
NCB = 3200
PV_CW, PV_CB, PV_CLG, PV_CLB, PV_QW, PV_NW, PV_AL, PV_DT = 0, 124, 128, 132, 136, 232, 233, 241
NPV = 249


class _Skip(Exception):
    pass


class Tok:
    __slots__ = ("w", "r", "excl")

    def __init__(self, excl=False):
        self.w = None
        self.r = {}
        self.excl = excl


class Sched:
    ENG = ("pe", "act", "dve", "pool", "sp")

    def __init__(self, nc, es, ndma=40):
        self.nc = nc
        self.sem = {e: es.enter_context(nc.semaphore("sem_" + e)) for e in self.ENG}
        self.cnt = {e: 0 for e in self.ENG}
        self.q = {e: [] for e in self.ENG}
        self.seen = {e: {} for e in self.ENG}
        self.dsem = [es.enter_context(nc.semaphore("dsem%d" % i)) for i in range(ndma)]
        self.dcnt = [0] * ndma
        self.dnext = 0

    def _deps(self, reads, writes):
        ev = []
        for t in reads:
            if t.w is not None:
                ev.append(t.w)
            if t.excl:
                ev.extend(t.r.values())
        for t in writes:
            if t.w is not None:
                ev.append(t.w)
            ev.extend(t.r.values())
        return ev

    def _waits(self, eng, evs):
        out = []
        for (key, val) in evs:
            if key == eng and (eng == "pe" or not SAME_SYNC):
                continue
            if self.seen[eng].get(key, 0) >= val:
                continue
            self.seen[eng][key] = val
            out.append((key, val))
        return out

    def _mark(self, ev, reads, writes):
        for t in writes:
            t.w = ev
            t.r = {}
        k = ev[0]
        for t in reads:
            if t.r.get(k, (k, 0))[1] < ev[1]:
                t.r[k] = ev

    def op(self, eng, fn, reads=(), writes=()):
        w = self._waits(eng, self._deps(reads, writes))
        self.cnt[eng] += 1
        ev = (eng, self.cnt[eng])
        self.q[eng].append(("op", fn, w, None))
        self._mark(ev, reads, writes)
        return ev

    def dma(self, q, out, in_, reads=(), writes=(), **kw):
        k = self.dnext
        self.dnext = (self.dnext + 1) % len(self.dsem)
        evs = self._deps(reads, writes)
        if self.dcnt[k] > 0:
            evs.append((("d", k), 16 * self.dcnt[k]))
        w = self._waits(q, evs)
        self.dcnt[k] += 1
        ev = (("d", k), 16 * self.dcnt[k])
        self.q[q].append(("dma", (out, in_, kw), w, k))
        self._mark(ev, reads, writes)
        return ev

    def barrier(self):
        evs = [(e, self.cnt[e]) for e in self.ENG if self.cnt[e] > 0]
        if not os.environ.get("K_NODMABAR"):
            evs += [(("d", k), 16 * c) for k, c in enumerate(self.dcnt) if c > 0]
        for e in self.ENG:
            w = self._waits(e, evs)
            if w:
                self.q[e].append(("wait", None, w, None))

    def semh(self, key):
        return self.sem[key] if isinstance(key, str) else self.dsem[key[1]]

    def replay(self, e, h):
        for kind, fn, waits, k in self.q[e]:
            for (key, val) in waits:
                h.wait_ge(self.semh(key), val)
            if kind == "op":
                fn(h).then_inc(self.sem[e], 1)
            elif kind == "dma":
                out, in_, kw = fn
                h.dma_start(out=out, in_=in_, **kw).then_inc(self.dsem[k], 16)


def build_nc(dbg=False):
    nc = bass.Bass("TRN2", target_bir_lowering=False)

    def din(name, shape, dt=F32):
        return nc.dram_tensor(name, list(shape), dt, kind="ExternalInput").ap()

    def dout(name, shape, dt=F32):
        return nc.dram_tensor(name, list(shape), dt, kind="ExternalOutput").ap()

    xall = din("xall", [NCOL, 2048])
    memp = din("memp", [256, 2048])
    sconv = din("sconv", [16, 30, 512])
    sqkv = din("sqkv", [16, 3, 3072])
    sdelta = din("sdelta", [16, 8, 128, 128])
    ck = din("ck", [16, 256, 4, 128])
    cv = din("cv", [16, 256, 4, 128])
    w_in = din("w_in", [2048, N_IN])
    w_mk = din("w_mk", [2048, 512])
    w_mv = din("w_mv", [2048, 512])
    w_out = din("w_out", [2048, 2048])
    cf_d = din("cf", [128, NCF])
    cb_d = din("cb", [128, NCB], BF)
    pv_d = din("pv", [128, NPV])
    lngb_d = din("lngb", [128, 2, 2048])

    y_d = dout("y", [NM, 2048])
    oconv_p = dout("oconv_p", [30, 512])
    oqkv_p = dout("oqkv_p", [3, 3072])
    odelta_p = dout("odelta_p", [8, 128, 128])
    omk = dout("omk", [256, 512])
    omv = dout("omv", [256, 512])
    oconv_s = dout("oconv_s", [16, 30, 512])
    oqkv_s = dout("oqkv_s", [16, 3, 3072])
    odelta_s = dout("odelta_s", [16, 8, 128, 128])

    with ExitStack() as es:
        S = Sched(nc, es)
        ctr = [0]

        def sb(es_, shape, dt=F32):
            ctr[0] += 1
            t = es_.enter_context(nc.sbuf_tensor("t%d" % ctr[0], list(shape), dt))
            return t, Tok()

        def ACT(out, in_, func, reads, writes, bias=None, scale=None, accum=None):
            kw = {}
            if bias is not None:
                kw["bias"] = bias
            if scale is not None:
                kw["scale"] = scale
            if accum is not None:
                kw["accum_out"] = accum
            return S.op("act", lambda e: e.activation(out=out, in_=in_, func=func, **kw), reads, writes)

        def TS(eng, out, in0, s1, op0, reads, writes, s2=None, op1=None):
            if op1 is None:
                return S.op(eng, lambda e: e.tensor_scalar(out=out, in0=in0, scalar1=s1, scalar2=None, op0=op0), reads, writes)
            return S.op(eng, lambda e: e.tensor_scalar(out=out, in0=in0, scalar1=s1, scalar2=s2, op0=op0, op1=op1), reads, writes)

        def TT(eng, out, in0, in1, op, reads, writes):
            return S.op(eng, lambda e: e.tensor_tensor(out=out, in0=in0, in1=in1, op=op), reads, writes)

        def STT(out, in0, scalar, in1, op0, op1, reads, writes):
            return S.op("dve", lambda e: e.scalar_tensor_tensor(out=out, in0=in0, scalar=scalar, in1=in1, op0=op0, op1=op1), reads, writes)

        def CP(eng, out, in_, reads, writes):
            if eng == "act":
                return S.op("act", lambda e: e.copy(out=out, in_=in_), reads, writes)
            return S.op(eng, lambda e: e.tensor_copy(out=out, in_=in_), reads, writes)

        def split3(src, tsrc, outs, touts, r_f32, t_r):
            CP("dve", outs[0], src, [tsrc], [touts])
            TT("dve", r_f32, src, outs[0], ALU.subtract, [tsrc, touts], [t_r])
            CP("dve", outs[1], r_f32, [t_r], [touts])
            TT("dve", outs[2], r_f32, outs[1], ALU.subtract, [t_r, touts], [touts])

        def MM(out, lhsT, rhs, start, stop, reads, writes):
            return S.op("pe", lambda e: e.matmul(out=out, lhsT=lhsT, rhs=rhs, start=start, stop=stop), reads, writes)

        def TR(out, in_, ident, reads, writes):
            return S.op("pe", lambda e: e.transpose(out=out, in_=in_, identity=ident), reads, writes)

        cf, t_cf = sb(es, [128, NCF])
        cb, t_cb = sb(es, [128, NCB], BF)
        pv, t_pv = sb(es, [128, NPV])
        mixT = es.enter_context(nc.sbuf_tensor("mixT", [128, 16, NM], BF))
        t_mix = [Tok() for _ in range(16)]
        NW = 4
        wsl = [sb(es, [128, 16, 128], BF) for _ in range(NW)]
        bt, t_sm = sb(es, [128, NT, 8])
        nbt, _ = sb(es, [128, NT, 8])
        gtk, _ = sb(es, [128, NT, 8])
        gc_, _ = sb(es, [128, NT, 8])
        ngc, _ = sb(es, [128, NT, 8])
        ee, _ = sb(es, [128, NT, 8])
        nee, _ = sb(es, [128, NT, 8])
        egl, _ = sb(es, [128, NT, 8])
        glb, _ = sb(es, [128, NT, 8])
        glbs, _ = sb(es, [128, 16, 8])
        nega, _ = sb(es, [128, 8])
        g3 = [sb(es, [128, NT, 8], BF)[0] for _ in range(3)]
        mT, t_mT = sb(es, [128, 16, 256], BF)
        es2 = ExitStack()
        xT, t_xT = sb(es2, [128, 16, NCOL], BF)
        psb = [es.enter_context(nc.psum_tensor("ps%d" % i, [128, 512], F32)) for i in range(8)]
        t_ps = [Tok(excl=True) for _ in range(8)]
        pctr = [0]

        def bank():
            i = pctr[0] % 8
            pctr[0] += 1
            return psb[i], t_ps[i]

        ident_f = cf[:, CF_ID:CF_ID + 128]
        ones_f = cf[:, CF_ONE:CF_ONE + 128]
        ident_b = cb[:, CB_ID:CB_ID + 128]
        ones_b = cb[:, CB_ONE:CB_ONE + 128]

        S.dma("sp", cf[:], cf_d[:, :], writes=[t_cf])
        S.dma("sp", cb[:], cb_d[:, :], writes=[t_cb])
        S.dma("sp", pv[:], pv_d[:, :], writes=[t_pv])

        WL = []
        WL.append((w_in, O_BT, 16))
        for c in range(4):
            WL += [(w_in, O_GA + 128 * c, 128), (w_in, O_GB + 128 * c, 128), (w_in, O_CG + 128 * c, 128)]
        for h in range(4):
            WL += [(w_mk, 128 * h, 128), (w_mv, 128 * h, 128), (w_in, O_MQ + 128 * h, 128), (w_in, O_MG + 128 * h, 128)]
        for h in range(8):
            WL += [(w_in, O_K + 128 * h, 128), (w_in, O_V + 128 * h, 128), (w_in, O_Q + 128 * h, 128), (w_in, O_DG + 128 * h, 128)]
        wst = {"issued": 0, "used": 0}

        def w_issue():
            i = wst["issued"]
            if i >= len(WL):
                return
            src, c0, n = WL[i]
            t, tk = wsl[i % NW]
            S.dma("pool", t[:, :, 0:n], src[:, c0:c0 + n].rearrange("(c p) n -> p c n", p=128), writes=[tk])
            wst["issued"] += 1

        def w_next():
            i = wst["used"]
            while wst["issued"] < min(len(WL), i + NW - 1) or wst["issued"] <= i:
                w_issue()
            wst["used"] += 1
            return wsl[i % NW]

        def proj_fm(wt, wtk, c0, n, ncols=128):
            bk, tb = bank()
            for k in range(16):
                MM(bk[0:ncols, 0:n], wt[:, k, 0:ncols], xT[:, k, c0:c0 + n], k == 0, k == 15, [wtk, t_xT], [tb])
            return bk, tb

        with ExitStack() as ph, contextlib.suppress(_Skip):
            if STAGE < 0:
                raise _Skip()
            xs = [sb(ph, [128, 2048], BF) for _ in range(3)]
            for i in range(NT + 2):
                t, tk = xs[i % 3]
                src = xall[128 * i:128 * i + 128, :] if i < NT else memp[128 * (i - NT):128 * (i - NT) + 128, :]
                S.dma("pool", t[:].rearrange("p (a n) -> p a n", n=512), src.rearrange("p (a n) -> p a n", n=512), writes=[tk])
                for half in range(2):
                    bk, tb = bank()
                    bkb = bk[:].bitcast(BF)
                    for c in range(8):
                        TR(bkb[:, 128 * c:128 * c + 128], t[:, 128 * (8 * half + c):128 * (8 * half + c) + 128], ident_b, [tk, t_cb], [tb])
                    src_v = bkb[:, 0:1024].rearrange("p (c n) -> p c n", c=8)
                    if i < NT:
                        dst = xT[:, 8 * half:8 * half + 8, 128 * i:128 * i + 128]
                        CP("act" if half == 0 else "dve", dst, src_v, [tb], [t_xT])
                    else:
                        j = i - NT
                        dst = mT[:, 8 * half:8 * half + 8, 128 * j:128 * j + 128]
                        CP("act" if half == 0 else "dve", dst, src_v, [tb], [t_mT])
            S.barrier()

        with ExitStack() as ph, contextlib.suppress(_Skip):
            if STAGE < 1:
                raise _Skip()
            bd, t_bd = sb(ph, [128, NT, 16])
            tmp8, t_tmp8 = sb(ph, [128, NT, 8])
            gm, t_gm = sb(ph, [128, 16, 8], BF)
            wt, wtk = w_next()
            bk, tb = bank()
            for i in range(NT):
                for k in range(16):
                    MM(bk[:, 16 * i:16 * i + 16], xT[:, k, 128 * i:128 * i + 128], wt[:, k, 0:16], k == 0, k == 15, [wtk, t_xT], [tb])
            CP("dve", bd[:], bk[:, 0:16 * NT].rearrange("p (t c) -> p t c", c=16), [tb], [t_bd])
            if SUB < 1:
                raise _Skip()
            ACT(bt[:], bd[:, :, 0:8], AF.Sigmoid, [t_bd], [t_sm])
            TS("dve", nbt[:], bt[:], -1.0, ALU.mult, [t_sm], [t_sm])
            if SUB < 2:
                raise _Skip()
            ACT(nega[:], pv[:, PV_AL:PV_AL + 8], AF.Exp, [t_pv], [t_sm])
            TS("dve", nega[:], nega[:], -1.0, ALU.mult, [t_sm], [t_sm])
            if SUB < 3:
                raise _Skip()
            TT("dve", tmp8[:], bd[:, :, 8:16], pv[:, PV_DT:PV_DT + 8].unsqueeze(1).broadcast_to([128, NT, 8]), ALU.add, [t_bd, t_pv], [t_tmp8])
            ACT(tmp8[:], tmp8[:], AF.Exp, [t_tmp8], [t_tmp8])
            ACT(tmp8[:], tmp8[:], AF.Ln, [t_tmp8], [t_tmp8], bias=1.0)
            TT("dve", gtk[:], tmp8[:], nega[:].unsqueeze(1).broadcast_to([128, NT, 8]), ALU.mult, [t_tmp8, t_sm], [t_sm])
            if SUB < 4:
                raise _Skip()
            split3(gtk[:], t_sm, [g3[0][:], g3[1][:], g3[2][:]], t_sm, tmp8[:], t_tmp8)
            bk, tb = bank()
            for i in range(NT):
                tri = cb[:, CB_TRIP:CB_TRIP + 128] if i < 16 else cb[:, CB_TRIS:CB_TRIS + 128]
                blk = ones_b if i < 16 else cb[:, CB_BLKS:CB_BLKS + 128]
                for j in range(3):
                    MM(bk[:, 8 * i:8 * i + 8], tri, g3[j][:, i, :], j == 0, j == 2, [t_cb, t_sm], [tb])
                for j in range(3):
                    MM(bk[:, 256 + 8 * i:256 + 8 * i + 8], blk, g3[j][:, i, :], j == 0, j == 2, [t_cb, t_sm], [tb])
            CP("dve", gc_[:], bk[:, 0:8 * NT].rearrange("p (t c) -> p t c", c=8), [tb], [t_sm])
            CP("act", tmp8[:], bk[:, 256:256 + 8 * NT].rearrange("p (t c) -> p t c", c=8), [tb], [t_tmp8])
            if SUB < 5:
                raise _Skip()
            TS("dve", ngc[:], gc_[:], -1.0, ALU.mult, [t_sm], [t_sm])
            ACT(ee[:], gc_[:], AF.Exp, [t_sm], [t_sm])
            TS("dve", nee[:], ee[:], -1.0, ALU.mult, [t_sm], [t_sm])
            ACT(glb[:], tmp8[:], AF.Exp, [t_tmp8], [t_sm])
            TT("dve", tmp8[:], tmp8[:], gc_[:], ALU.subtract, [t_tmp8, t_sm], [t_tmp8])
            ACT(egl[:], tmp8[:], AF.Exp, [t_tmp8], [t_sm])
            if SUB < 6:
                raise _Skip()
            bk, tb = bank()
            for j in range(3):
                TT("dve", gm[:], g3[j][:, 16, :].unsqueeze(1).broadcast_to([128, 16, 8]),
                   cf[:, CF_ROWM:CF_ROWM + 16].unsqueeze(2).broadcast_to([128, 16, 8]), ALU.mult, [t_sm, t_cf], [t_gm])
                MM(bk[:, 0:128], ones_b, gm[:].rearrange("p s h -> p (s h)"), j == 0, j == 2, [t_cb, t_gm], [tb])
            ACT(glbs[:], bk[:, 0:128].rearrange("p (s h) -> p s h", h=8), AF.Exp, [tb], [t_sm])
            if SUB < 9:
                raise _Skip()
            S.barrier()

        if os.environ.get("K_BAR"):
            S.barrier()
        OWN_CH = [(896, 512), (1408, 512), (1920, 256)]
        with ExitStack() as ph, contextlib.suppress(_Skip):
            if STAGE < 2:
                raise _Skip()
            ubuf, t_ub = sb(ph, [128, 1152])
            usamp, t_us = sb(ph, [128, 16, 38])
            hc, t_hc = sb(ph, [128, 4, NM])
            sgate, t_sg = sb(ph, [128, 4, NM], BF)
            sgt, t_sgt = sb(ph, [128, 512])
            stg, t_stg = sb(ph, [128, 512])
            scs, t_scs = sb(ph, [120, 512])
            S.dma("sp", oconv_s[:, 0:22, :], sconv[:, 8:30, :])
            unew_s, t_uns = sb(ph, [128, 4, 128])
            for c in range(4):
                for g4 in range(4):
                    S.dma("sp", scs[:, 0:128], sconv[4 * g4:4 * g4 + 4, :, 128 * c:128 * c + 128].rearrange("s r n -> (s r) n"), writes=[t_scs])
                    bk, tb = bank()
                    TR(bk[:, 0:120], scs[0:120, 0:128], ident_f[0:120, 0:120], [t_scs, t_cf], [tb])
                    CP("act", usamp[:, 4 * g4:4 * g4 + 4, 0:30], bk[:, 0:120].rearrange("p (s r) -> p s r", r=30), [tb], [t_us])
                wa, ta = w_next()
                wb, tbk = w_next()
                for (c0, n) in OWN_CH:
                    bka, tba = proj_fm(wa, ta, c0, n)
                    bkb_, tbb = proj_fm(wb, tbk, c0, n)
                    ACT(sgt[:, 0:n], bkb_[:, 0:n], AF.Sigmoid, [tbb], [t_sgt])
                    if c0 < 1920:
                        TT("dve", ubuf[:, c0 - 896:c0 - 896 + n], bka[:, 0:n], sgt[:, 0:n], ALU.mult, [tba, t_sgt], [t_ub])
                    else:
                        TT("dve", ubuf[:, 1024:1152], bka[:, 0:128], sgt[:, 0:128], ALU.mult, [tba, t_sgt], [t_ub])
                        TT("dve", unew_s[:, c, :], bka[:, 128:256], sgt[:, 128:256], ALU.mult, [tba, t_sgt], [t_uns])
                        CP("act", usamp[:, :, 30:38], unew_s[:, c, :].rearrange("p (s l) -> p s l", l=8), [t_uns], [t_us])
                wg, tg = w_next()
                for (c0, n) in [(1024, 512), (1536, 512), (2048, 128)]:
                    bkg, tbg = proj_fm(wg, tg, c0, n)
                    ACT(sgate[:, c, c0 - 1024:c0 - 1024 + n], bkg[:, 0:n], AF.Silu, [tbg], [t_sg])
                cw = lambda j: pv[:, PV_CW + 31 * c + j:PV_CW + 31 * c + j + 1]
                TS("dve", hc[:, c, 0:1024], ubuf[:, 98:98 + 1024], cw(0), ALU.mult, [t_ub, t_pv], [t_hc],
                   s2=pv[:, PV_CB + c:PV_CB + c + 1], op1=ALU.add)
                for j in range(1, 31):
                    STT(hc[:, c, 0:1024], ubuf[:, 98 + j:98 + j + 1024], cw(j), hc[:, c, 0:1024], ALU.mult, ALU.add, [t_ub, t_pv, t_hc], [t_hc])
                hs = hc[:, c, 1024:1152].rearrange("p (s l) -> p s l", l=8)
                TS("dve", hs, usamp[:, :, 0:8], cw(0), ALU.mult, [t_us, t_pv], [t_hc], s2=pv[:, PV_CB + c:PV_CB + c + 1], op1=ALU.add)
                for j in range(1, 31):
                    STT(hs, usamp[:, :, j:j + 8], cw(j), hs, ALU.mult, ALU.add, [t_us, t_pv, t_hc], [t_hc])
                bk, tb = bank()
                TR(bk[0:30, 0:128], ubuf[:, 1122:1152], ident_f, [t_ub, t_cf], [tb])
                CP("act", stg[0:30, 128 * c:128 * c + 128], bk[0:30, 0:128], [tb], [t_stg])
            S.dma("sp", oconv_p[:, :], stg[0:30, :], reads=[t_stg])
            stg2, t_stg2 = sb(ph, [128, 512])
            for c in range(4):
                bk, tb = bank()
                TR(bk[:, 0:128], unew_s[:, c, :], ident_f, [t_uns, t_cf], [tb])
                CP("act", stg2[:, 128 * c:128 * c + 128], bk[:, 0:128], [tb], [t_stg2])
            for s_ in range(16):
                S.dma("sp", oconv_s[s_, 22:30, :], stg2[8 * s_:8 * s_ + 8, :], reads=[t_stg2])
            sq, t_sq = sb(ph, [128, 512])
            hl = [sb(ph, [128, 512], BF) for _ in range(4)]
            mean, t_mean = sb(ph, [128, 512])
            var, t_var = sb(ph, [128, 512])
            rstd, t_rstd = sb(ph, [128, 512])
            xc, t_xc = sb(ph, [128, 512])
            for (m0, n) in [(0, 512), (512, 512), (1024, 128)]:
                bk1, tb1 = bank()
                bk2, tb2 = bank()
                for c in range(4):
                    CP("act", hl[0][0][:, 0:n], hc[:, c, m0:m0 + n], [t_hc], [hl[0][1]])
                    TT("dve", hl[1][0][:, 0:n], hc[:, c, m0:m0 + n], hl[0][0][:, 0:n], ALU.subtract, [t_hc, hl[0][1]], [hl[1][1]])
                    MM(bk1[:, 0:n], ones_b, hl[0][0][:, 0:n], c == 0, False, [t_cb, hl[0][1]], [tb1])
                    MM(bk1[:, 0:n], ones_b, hl[1][0][:, 0:n], False, c == 3, [t_cb, hl[1][1]], [tb1])
                for c in range(4):
                    ACT(sq[:, 0:n], hc[:, c, m0:m0 + n], AF.Square, [t_hc], [t_sq])
                    CP("act", hl[2][0][:, 0:n], sq[:, 0:n], [t_sq], [hl[2][1]])
                    TT("dve", hl[3][0][:, 0:n], sq[:, 0:n], hl[2][0][:, 0:n], ALU.subtract, [t_sq, hl[2][1]], [hl[3][1]])
                    MM(bk2[:, 0:n], ones_b, hl[2][0][:, 0:n], c == 0, False, [t_cb, hl[2][1]], [tb2])
                    MM(bk2[:, 0:n], ones_b, hl[3][0][:, 0:n], False, c == 3, [t_cb, hl[3][1]], [tb2])
                ACT(mean[:, 0:n], bk1[:, 0:n], AF.Copy, [tb1], [t_mean], scale=1.0 / 512)
                TT("dve", var[:, 0:n], mean[:, 0:n], mean[:, 0:n], ALU.mult, [t_mean], [t_var])
                STT(var[:, 0:n], bk2[:, 0:n], 1.0 / 512, var[:, 0:n], ALU.mult, ALU.subtract, [tb2, t_var], [t_var])
                TS("dve", var[:, 0:n], var[:, 0:n], 1e-5, ALU.add, [t_var], [t_var])
                ACT(var[:, 0:n], var[:, 0:n], AF.Ln, [t_var], [t_var])
                ACT(rstd[:, 0:n], var[:, 0:n], AF.Exp, [t_var], [t_rstd], scale=-0.5)
                for c in range(4):
                    TT("dve", xc[:, 0:n], hc[:, c, m0:m0 + n], mean[:, 0:n], ALU.subtract, [t_hc, t_mean], [t_xc])
                    TT("dve", xc[:, 0:n], xc[:, 0:n], rstd[:, 0:n], ALU.mult, [t_xc, t_rstd], [t_xc])
                    ACT(xc[:, 0:n], xc[:, 0:n], AF.Silu, [t_xc, t_pv], [t_xc],
                        scale=pv[:, PV_CLG + c:PV_CLG + c + 1], bias=pv[:, PV_CLB + c:PV_CLB + c + 1])
                    TT("dve", mixT[:, c, m0:m0 + n], xc[:, 0:n], sgate[:, c, m0:m0 + n], ALU.mult, [t_xc, t_sg], [t_mix[c]])
            S.barrier()

        MEM_CH = [(1024, 512), (1536, 512), (2048, 128)]
        with ExitStack() as ph, contextlib.suppress(_Skip):
            if STAGE < 3:
                raise _Skip()
            mqT, t_mq = sb(ph, [128, NM], BF)
            mgT, t_mg = sb(ph, [128, NM], BF)
            KTp, t_ktp = sb(ph, [128, 256], BF)
            Vp, t_vp = sb(ph, [128, 2, 128], BF)
            kvst, t_kvst = sb(ph, [128, 2, 2, 128])
            kc, t_kc = sb(ph, [128, 16, 2, 128], BF)
            vc, t_vc = sb(ph, [128, 16, 2, 128], BF)
            kcT, t_kct = sb(ph, [128, 16, 256], BF)
            mqm, t_mqm = sb(ph, [128, 16, 128], BF)
            pf, t_pf = sb(ph, [128, 256])
            pn, t_pn = sb(ph, [128, 256], BF)
            pT, t_pT = sb(ph, [128, 2, 128], BF)
            mx, t_mx = sb(ph, [128, 4])
            for h in range(4):
                wk, tk_ = w_next()
                wv, tv_ = w_next()
                bk, tb = bank()
                for k in range(16):
                    MM(bk[:, 0:256], wk[:, k, :], mT[:, k, :], k == 0, k == 15, [tk_, t_mT], [tb])
                CP("act", KTp[:], bk[:, 0:256], [tb], [t_ktp])
                bk, tb = bank()
                for mc in range(2):
                    for k in range(16):
                        MM(bk[:, 128 * mc:128 * mc + 128], mT[:, k, 128 * mc:128 * mc + 128], wk[:, k, :], k == 0, k == 15, [tk_, t_mT], [tb])
                    for k in range(16):
                        MM(bk[:, 256 + 128 * mc:256 + 128 * mc + 128], mT[:, k, 128 * mc:128 * mc + 128], wv[:, k, :], k == 0, k == 15, [tv_, t_mT], [tb])
                CP("dve", kvst[:].rearrange("p a b d -> p (a b d)"), bk[:, 0:512], [tb], [t_kvst])
                CP("act", Vp[:].rearrange("p b d -> p (b d)"), bk[:, 256:512], [tb], [t_vp])
                S.dma("sp", omk[:, 128 * h:128 * h + 128].rearrange("(mc m) d -> m mc d", m=128), kvst[:, 0, :, :], reads=[t_kvst])
                S.dma("sp", omv[:, 128 * h:128 * h + 128].rearrange("(mc m) d -> m mc d", m=128), kvst[:, 1, :, :], reads=[t_kvst])
                S.dma("pool", kc[:], ck[:, :, h, :].rearrange("s (mc m) d -> m s mc d", m=128), writes=[t_kc])
                S.dma("pool", vc[:], cv[:, :, h, :].rearrange("s (mc m) d -> m s mc d", m=128), writes=[t_vc])
                wq, tq_ = w_next()
                wg, tg_ = w_next()
                for (c0, n) in MEM_CH:
                    bk, tb = proj_fm(wq, tq_, c0, n)
                    ACT(mqT[:, c0 - 1024:c0 - 1024 + n], bk[:, 0:n], AF.Copy, [tb], [t_mq], scale=128.0 ** -0.5)
                    bk, tb = proj_fm(wg, tg_, c0, n)
                    ACT(mgT[:, c0 - 1024:c0 - 1024 + n], bk[:, 0:n], AF.Silu, [tb], [t_mg])
                for s in range(16):
                    if s % 4 == 0:
                        bk, tb = bank()
                        bkb = bk[:].bitcast(BF)
                    for mc in range(2):
                        o = (s % 4) * 256 + mc * 128
                        TR(bkb[:, o:o + 128], kc[:, s, mc, :], ident_b, [t_kc, t_cb], [tb])
                    if s % 4 == 3:
                        CP("act", kcT[:, s - 3:s + 1, :], bkb[:, 0:1024].rearrange("p (s m) -> p s m", m=256), [tb], [t_kct])
                TT("dve", mqm[:], mqT[:, 1024:1152].unsqueeze(1).broadcast_to([128, 16, 128]),
                   cb[:, CB_COLM:CB_COLM + 2048].rearrange("p (s i) -> p s i", i=128), ALU.mult, [t_mq, t_cb], [t_mqm])
                for ti in range(9):
                    m0 = 128 * ti
                    bk, tb = bank()
                    if ti < 8:
                        MM(bk[:, 0:256], mqT[:, m0:m0 + 128], KTp[:], True, True, [t_mq, t_ktp], [tb])
                    else:
                        for s in range(16):
                            MM(bk[:, 0:256], mqm[:, s, :], kcT[:, s, :], s == 0, s == 15, [t_mqm, t_kct], [tb])
                    S.op("dve", lambda e, bk=bk: e.reduce_max(out=mx[:, 0:1], in_=bk[:, 0:256], axis=AX.X), [tb], [t_mx])
                    TS("dve", mx[:, 1:2], mx[:, 0:1], -1.0, ALU.mult, [t_mx], [t_mx])
                    ACT(pf[:], bk[:, 0:256], AF.Exp, [tb, t_mx], [t_pf, t_mx], bias=mx[:, 1:2], accum=mx[:, 2:3])
                    S.op("dve", lambda e: e.reciprocal(out=mx[:, 3:4], in_=mx[:, 2:3]), [t_mx], [t_mx])
                    TS("dve", pn[:], pf[:], mx[:, 3:4], ALU.mult, [t_pf, t_mx], [t_pn])
                    bk2, tb2 = bank()
                    bk2b = bk2[:].bitcast(BF)
                    for mc in range(2):
                        TR(bk2b[:, 128 * mc:128 * mc + 128], pn[:, 128 * mc:128 * mc + 128], ident_b, [t_pn, t_cb], [tb2])
                    CP("act", pT[:].rearrange("p a b -> p (a b)"), bk2b[:, 0:256], [tb2], [t_pT])
                    bk3, tb3 = bank()
                    if ti < 8:
                        for mc in range(2):
                            MM(bk3[:, 0:128], Vp[:, mc, :], pT[:, mc, :], mc == 0, mc == 1, [t_vp, t_pT], [tb3])
                    else:
                        for s in range(16):
                            for mc in range(2):
                                MM(bk3[:, 8 * s:8 * s + 8], vc[:, s, mc, :], pT[:, mc, 8 * s:8 * s + 8], mc == 0, mc == 1, [t_vc, t_pT], [tb3])
                    TT("dve", mixT[:, 12 + h, m0:m0 + 128], bk3[:, 0:128], mgT[:, m0:m0 + 128], ALU.mult, [tb3, t_mg], [t_mix[12 + h]])
            S.barrier()

        KV_CH = [(0, 512), (512, 512), (1024, 512), (1536, 512), (2048, 128)]
        with ExitStack() as ph, contextlib.suppress(_Skip):
            if STAGE < 4:
                raise _Skip()
            kT, t_kT = sb(ph, [128, NCOL], BF)
            vT, t_vT = sb(ph, [128, NCOL], BF)
            qT, t_qT = sb(ph, [128, NM], BF)
            gT, t_gT = sb(ph, [128, NM], BF)
            pre = [sb(ph, [128, 515]) for _ in range(2)]
            pres, t_pres = sb(ph, [128, 16, 11])
            acc, t_acc = sb(ph, [128, 512])
            sqb, t_sqb = sb(ph, [128, 512], BF)
            lnb_, t_lnb = sb(ph, [128, 512])
            sq48, t_sq48 = sb(ph, [48, 128])
            qst, t_qst = sb(ph, [128, 3])
            qst_s, t_qsts = sb(ph, [128, 48])
            ost, t_ost = sb(ph, [48, 128])
            Sf, t_Sf = sb(ph, [128, 128])
            Sb, t_Sb = sb(ph, [128, 128], BF)
            Ss, t_Ss = sb(ph, [128, 16, 128])
            Ssb, t_Ssb = sb(ph, [128, 16, 128], BF)
            Sso, t_Sso = Ss, t_Ss
            kTm, t_kTm = sb(ph, [128, 16, 128], BF)
            qTm, t_qTm = sb(ph, [128, 16, 128], BF)
            kdm, t_kdm = sb(ph, [128, 16, 128], BF)
            kd, t_kd = sb(ph, [128, 128], BF)
            vtok, t_vtok = sb(ph, [128, 128])
            gB3 = [sb(ph, [128, 128], BF) for _ in range(3)]
            decT, t_decT = sb(ph, [128, 128])
            decTs, t_decTs = sb(ph, [128, 128])
            P0, t_P0 = sb(ph, [128, 128], BF)
            Qb = [sb(ph, [128, 128], BF) for _ in range(2)]
            PY = [sb(ph, [128, 256], BF) for _ in range(2)]
            aqkT, t_aqk = sb(ph, [128, 128], BF)
            rbf, t_rbf = sb(ph, [128, 128], BF)
            ubf, t_ubf = sb(ph, [128, 128], BF)
            tsb, t_tsb = sb(ph, [128, 128])
            osb, t_osb = sb(ph, [128, 128])
            junk, t_junk = sb(ph, [128, 128])
            onb, t_onb = sb(ph, [128, 128], BF)
            sm4, t_sm4 = sb(ph, [128, 4])


            def conv_stream(h, kind, wt, wtk, chunks, dstT, t_dst, dcol0, norm, qscale):
                fo = kind * 1024 + 128 * h
                fc = fo // 128
                qw = lambda j: pv[:, PV_QW + 4 * fc + j:PV_QW + 4 * fc + j + 1]
                S.dma("sp", sq48[:], sqkv[:, :, fo:fo + 128].rearrange("s r n -> (s r) n"), writes=[t_sq48])
                bk, tb = bank()
                TR(bk[:, 0:48], sq48[0:48, 0:128], ident_f[0:48, 0:48], [t_sq48, t_cf], [tb])
                CP("act", pres[:, :, 0:3], bk[:, 0:48].rearrange("p (s r) -> p s r", r=3), [tb], [t_pres])
                pi = 0
                S.op("pool", lambda e, p=pre[0][0]: e.memset(p[:, 0:3], 0.0), [], [pre[0][1]])
                for (c0, n) in chunks:
                    bk, tb = proj_fm(wt, wtk, c0, n)
                    npr = n if c0 + n <= 2048 else n - 128
                    skip = max(0, dcol0 - c0)
                    if npr > 0:
                        p_, tp_ = pre[pi]
                        ACT(p_[:, 3:3 + npr], bk[:, 0:npr], AF.Copy, [tb], [tp_])
                        TS("dve", acc[:, 0:npr], p_[:, 0:npr], qw(0), ALU.mult, [tp_, t_pv], [t_acc])
                        for j in range(1, 4):
                            STT(acc[:, 0:npr], p_[:, j:j + npr], qw(j), acc[:, 0:npr], ALU.mult, ALU.add, [tp_, t_pv, t_acc], [t_acc])
                        d0 = c0 - dcol0
                        ACT(dstT[:, d0 + skip:d0 + npr], acc[:, skip:npr], AF.Silu, [t_acc], [t_dst])
                        if c0 + npr == 2048:
                            CP("act", qst[:, 0:3], p_[:, npr:npr + 3], [tp_], [t_qst])
                        else:
                            p2, tp2 = pre[1 - pi]
                            CP("act", p2[:, 0:3], p_[:, npr:npr + 3], [tp_], [tp2])
                            pi = 1 - pi
                    if c0 + n > 2048:
                        CP("act", pres[:, :, 3:11], bk[:, npr:npr + 128].rearrange("p (s l) -> p s l", l=8), [tb], [t_pres])
                        av = acc[:, 0:128].rearrange("p (s l) -> p s l", l=8)
                        TS("dve", av, pres[:, :, 0:8], qw(0), ALU.mult, [t_pres, t_pv], [t_acc])
                        for j in range(1, 4):
                            STT(av, pres[:, :, j:j + 8], qw(j), av, ALU.mult, ALU.add, [t_pres, t_pv, t_acc], [t_acc])
                        d0 = 2048 - dcol0
                        ACT(dstT[:, d0:d0 + 128], acc[:, 0:128], AF.Silu, [t_acc], [t_dst])
                        CP("act", qst_s[:].rearrange("p (s r) -> p s r", r=3), pres[:, :, 8:11], [t_pres], [t_qsts])
                bk, tb = bank()
                TR(bk[0:3, 0:128], qst[:, 0:3], ident_f, [t_qst, t_cf], [tb])
                TR(bk[0:48, 128:256], qst_s[:, 0:48], ident_f, [t_qsts, t_cf], [tb])
                CP("act", ost[0:3, :], bk[0:3, 0:128], [tb], [t_ost])
                S.dma("sp", oqkv_p[:, fo:fo + 128], ost[0:3, :], reads=[t_ost])
                CP("act", ost[0:48, :], bk[0:48, 128:256], [tb], [t_ost])
                S.dma("sp", oqkv_s[:, :, fo:fo + 128].rearrange("s r n -> (s r) n"), ost[0:48, :], reads=[t_ost])
                if norm:
                    ntot = dstT.shape[1]
                    for m0 in range(0, ntot, 512):
                        n = min(512, ntot - m0)
                        ACT(sqb[:, 0:n], dstT[:, m0:m0 + n], AF.Square, [t_dst], [t_sqb])
                        bk, tb = bank()
                        MM(bk[:, 0:n], ones_b, sqb[:, 0:n], True, True, [t_cb, t_sqb], [tb])
                        TS("dve", lnb_[:, 0:n], bk[:, 0:n], 1e-6, ALU.add, [tb], [t_lnb])
                        ACT(lnb_[:, 0:n], lnb_[:, 0:n], AF.Ln, [t_lnb], [t_lnb])
                        ACT(lnb_[:, 0:n], lnb_[:, 0:n], AF.Exp, [t_lnb], [t_lnb], scale=-0.5)
                        if qscale != 1.0:
                            TS("dve", lnb_[:, 0:n], lnb_[:, 0:n], qscale, ALU.mult, [t_lnb], [t_lnb])
                        TT("dve", dstT[:, m0:m0 + n], dstT[:, m0:m0 + n], lnb_[:, 0:n], ALU.mult, [t_dst, t_lnb], [t_dst])

            def dn_tile(h, i):
                samp = (i == 16)
                full = (i >= 8)
                c0 = 128 * i
                mc0 = c0 - 1024
                tri = cf[:, CF_TRIS:CF_TRIS + 128] if samp else cf[:, CF_TRIP:CF_TRIP + 128]
                negm = cf[:, CF_NEGS:CF_NEGS + 128] if samp else cf[:, CF_NEGP:CF_NEGP + 128]
                strict = cb[:, CB_STRS:CB_STRS + 128] if samp else cb[:, CB_STRP:CB_STRP + 128]
                sc = lambda arr: arr[:, i, h:h + 1]
                bk, tb = bank()
                bkb = bk[:].bitcast(BF)
                TR(bkb[:, 0:128], kT[:, c0:c0 + 128], ident_b, [t_kT, t_cb], [tb])
                TR(bkb[:, 128:256], vT[:, c0:c0 + 128], ident_b, [t_vT, t_cb], [tb])
                ACT(kd[:], bkb[:, 0:128], AF.Copy, [tb, t_sm], [t_kd], scale=sc(egl))
                CP("dve", vtok[:], bkb[:, 128:256], [tb], [t_vtok])
                trib = cb[:, CB_TRIS:CB_TRIS + 128] if samp else cb[:, CB_TRIP:CB_TRIP + 128]
                negb = cb[:, CB_NEGS:CB_NEGS + 128] if samp else cb[:, CB_NEGP:CB_NEGP + 128]
                bk, tb = bank()
                for j in range(3):
                    gBj, tgBj = gB3[j]
                    TT("pool", gBj[:], ones_b, g3[j][:, i, h:h + 1].broadcast_to([128, 128]), ALU.mult, [t_cb, t_sm], [tgBj])
                    MM(bk[:, 0:128], gBj[:], trib, j == 0, False, [tgBj, t_cb], [tb])
                MM(bk[:, 0:128], ident_b, negb, False, True, [t_cb], [tb])
                ACT(decT[:], bk[:, 0:128], AF.Exp, [tb, t_sm], [t_decT], bias=sc(ngc))
                TT("pool", decTs[:], decT[:], strict, ALU.mult, [t_decT, t_cb], [t_decTs])
                bk, tb = bank()
                MM(bk[:, 0:128], kT[:, c0:c0 + 128], kT[:, c0:c0 + 128], True, True, [t_kT], [tb])
                if full:
                    MM(bk[:, 128:256], kT[:, c0:c0 + 128], qT[:, mc0:mc0 + 128], True, True, [t_kT, t_qT], [tb])
                STT(P0[:], bk[:, 0:128], sc(nbt), decTs[:], ALU.mult, ALU.mult, [tb, t_sm, t_decTs], [t_P0])
                if full:
                    TT("dve", aqkT[:], bk[:, 128:256], decT[:], ALU.mult, [tb, t_decT], [t_aqk])
                py, tpy = PY[0]
                TT("pool", py[:, 128:256], P0[:], ident_b, ALU.add, [t_P0, t_cb], [tpy])
                bk, tb = bank()
                bkb = bk[:].bitcast(BF)
                TR(bkb[:, 0:128], P0[:], ident_b, [t_P0, t_cb], [tb])
                q_, tq = Qb[0]
                CP("act", q_[:], bkb[:, 0:128], [tb], [tq])
                L = 2 if samp else 6
                bk, tb = bank()
                MM(bk[:, 0:128], q_[:], P0[:], True, True, [tq, t_P0], [tb])
                MM(bk[:, 256:384], P0[:], q_[:], True, True, [tq, t_P0], [tb])
                CP("act", py[:, 0:128], bk[:, 0:128], [tb], [tpy])
                q1, tq1 = Qb[1]
                CP("dve", q1[:], bk[:, 256:384], [tb], [tq1])
                cur = 0
                qc = 1
                for k in range(1, L + 1):
                    py, tpy = PY[cur]
                    pyn, tpyn = PY[1 - cur]
                    qk, tqk = Qb[qc]
                    qn, tqn = Qb[1 - qc]
                    bk, tb = bank()
                    if k < L:
                        MM(bk[:, 0:256], qk[:], py[:, 0:256], True, True, [tqk, tpy], [tb])
                        MM(bk[:, 256:384], py[:, 0:128], qk[:], True, True, [tqk, tpy], [tb])
                        CP("act", pyn[:, 0:128], bk[:, 0:128], [tb], [tpyn])
                        CP("act", qn[:], bk[:, 256:384], [tb], [tqn])
                    else:
                        MM(bk[:, 128:256], qk[:], py[:, 128:256], True, True, [tqk, tpy], [tb])
                    TT("dve", pyn[:, 128:256], bk[:, 128:256], py[:, 128:256], ALU.add, [tb, tpy], [tpyn])
                    cur = 1 - cur
                    qc = 1 - qc
                XT = PY[cur][0][:, 128:256]
                t_XT = PY[cur][1]
                bk, tb = bank()
                if not samp:
                    MM(bk[:, 0:128], kT[:, c0:c0 + 128], Sb[:], True, True, [t_kT, t_Sb], [tb])
                    if full:
                        MM(bk[:, 128:256], qT[:, mc0:mc0 + 128], Sb[:], True, True, [t_qT, t_Sb], [tb])
                else:
                    for s in range(16):
                        MM(bk[:, 0:128], kTm[:, s, :], Ssb[:, s, :], s == 0, s == 15, [t_kTm, t_Ssb], [tb])
                    for s in range(16):
                        MM(bk[:, 128:256], qTm[:, s, :], Ssb[:, s, :], s == 0, s == 15, [t_qTm, t_Ssb], [tb])
                STT(rbf[:], bk[:, 0:128], sc(nee), vtok[:], ALU.mult, ALU.add, [tb, t_sm, t_vtok], [t_rbf])
                bku, tbu = bank()
                MM(bku[:, 0:128], XT, rbf[:], True, True, [t_XT, t_rbf], [tbu])
                ACT(ubf[:], bku[:, 0:128], AF.Copy, [tbu, t_sm], [t_ubf], scale=sc(bt))
                if full:
                    MM(bku[:, 128:256], aqkT[:], ubf[:], True, True, [t_aqk, t_ubf], [tbu])
                    ACT(tsb[:], bk[:, 128:256], AF.Copy, [tb, t_sm], [t_tsb], scale=sc(ee))
                    TT("dve", osb[:], bku[:, 128:256], tsb[:], ALU.add, [tbu, t_tsb], [t_osb])
                    ACT(junk[:], osb[:], AF.Square, [t_osb], [t_junk, t_sm4], accum=sm4[:, 0:1])
                    TS("dve", sm4[:, 1:2], sm4[:, 0:1], 1.0 / 128, ALU.mult, [t_sm4], [t_sm4], s2=1e-6, op1=ALU.add)
                    ACT(sm4[:, 2:3], sm4[:, 1:2], AF.Ln, [t_sm4], [t_sm4])
                    ACT(sm4[:, 3:4], sm4[:, 2:3], AF.Exp, [t_sm4], [t_sm4], scale=-0.5)
                    TS("dve", onb[:], osb[:], sm4[:, 3:4], ALU.mult, [t_osb, t_sm4], [t_onb])
                    bko, tbo = bank()
                    bkob = bko[:].bitcast(BF)
                    TR(bkob[:, 0:128], onb[:], ident_b, [t_onb, t_cb], [tbo])
                    STT(mixT[:, 4 + h, mc0:mc0 + 128], bkob[:, 0:128], pv[:, PV_NW:PV_NW + 1], gT[:, mc0:mc0 + 128],
                        ALU.mult, ALU.mult, [tbo, t_pv, t_gT], [t_mix[4 + h]])
                if not samp:
                    bks, tbs = bank()
                    MM(bks[:, 0:128], kd[:], ubf[:], True, True, [t_kd, t_ubf], [tbs])
                    STT(Sf[:], Sf[:], sc(glb), bks[:, 0:128], ALU.mult, ALU.add, [t_Sf, t_sm, tbs], [t_Sf])
                    CP("act", Sb[:], Sf[:], [t_Sf], [t_Sb])
                else:
                    TT("dve", kdm[:], kd[:].unsqueeze(1).broadcast_to([128, 16, 128]),
                       cf[:, CF_ROWM:CF_ROWM + 16].unsqueeze(2).broadcast_to([128, 16, 128]), ALU.mult, [t_kd, t_cf], [t_kdm])
                    for s in range(16):
                        if s % 4 == 0:
                            bks, tbs = bank()
                        MM(bks[:, 128 * (s % 4):128 * (s % 4) + 128], kdm[:, s, :], ubf[:], True, True, [t_kdm, t_ubf], [tbs])
                        if s % 4 == 3:
                            for s2 in range(s - 3, s + 1):
                                STT(Sso[:, s2, :], Ss[:, s2, :], glbs[:, s2, h:h + 1], bks[:, 128 * (s2 % 4):128 * (s2 % 4) + 128],
                                    ALU.mult, ALU.add, [t_Ss, t_sm, tbs], [t_Ss])

            for h in range(8):
                S.dma("sp", Ss[:], sdelta[:, h, :, :].rearrange("s k v -> k s v"), writes=[t_Ss])
                CP("pool", Ssb[:], Ss[:], [t_Ss], [t_Ssb])
                S.op("pool", lambda e: e.memset(Sf[:], 0.0), [], [t_Sf])
                S.op("pool", lambda e: e.memset(Sb[:], 0.0), [], [t_Sb])
                wk, tk_ = w_next()
                conv_stream(h, 1, wk, tk_, KV_CH, kT, t_kT, 0, True, 1.0)
                wv, tv_ = w_next()
                conv_stream(h, 2, wv, tv_, KV_CH, vT, t_vT, 0, False, 1.0)
                wq, tq_ = w_next()
                conv_stream(h, 0, wq, tq_, OWN_CH, qT, t_qT, 1024, True, 128.0 ** -0.5)
                wg, tg_ = w_next()
                for (c0, n) in MEM_CH:
                    bk, tb = proj_fm(wg, tg_, c0, n)
                    ACT(gT[:, c0 - 1024:c0 - 1024 + n], bk[:, 0:n], AF.Silu, [tb], [t_gT])
                colm = cb[:, CB_COLM:CB_COLM + 2048].rearrange("p (s i) -> p s i", i=128)
                TT("dve", kTm[:], kT[:, 2048:2176].unsqueeze(1).broadcast_to([128, 16, 128]), colm, ALU.mult, [t_kT, t_cb], [t_kTm])
                TT("pool", qTm[:], qT[:, 1024:1152].unsqueeze(1).broadcast_to([128, 16, 128]), colm, ALU.mult, [t_qT, t_cb], [t_qTm])
                for i in range(NT):
                    dn_tile(h, i)
                    if i == 15:
                        S.dma("sp", odelta_p[h, :, :], Sf[:], reads=[t_Sf])
                S.dma("sp", odelta_s[:, h, :, :].rearrange("s k v -> k s v"), Sso[:], reads=[t_Sso])
            S.barrier()

        es2.close()
        with ExitStack() as ph, contextlib.suppress(_Skip):
            if STAGE < 5:
                raise _Skip()
            wo, t_wo = sb(ph, [128, 16, 2048], BF)
            lngb, t_lngb = sb(ph, [128, 2, 2048])
            xr = [sb(ph, [128, 2048]) for _ in range(2)]
            yp, t_yp = sb(ph, [128, 2048])
            yo = [sb(ph, [128, 2048]) for _ in range(2)]
            jk, t_jk = sb(ph, [128, 2048])
            st, t_st = sb(ph, [128, 8])
            for k4 in range(16):
                S.dma("pool", wo[:, k4, :].rearrange("p (a n) -> p a n", n=512),
                      w_out[128 * k4:128 * k4 + 128, :].rearrange("p (a n) -> p a n", n=512), writes=[t_wo])
            S.dma("sp", lngb[:], lngb_d[:, :, :], writes=[t_lngb])
            for ti in range(9):
                xr_, txr = xr[ti % 2]
                yo_, tyo = yo[ti % 2]
                S.dma("sp", xr_[:], xall[1024 + 128 * ti:1024 + 128 * ti + 128, :], writes=[txr])
                for nb in range(4):
                    bk, tb = bank()
                    for k in range(16):
                        MM(bk[:, 0:512], mixT[:, k, 128 * ti:128 * ti + 128], wo[:, k, 512 * nb:512 * nb + 512], k == 0, k == 15, [t_mix[k], t_wo], [tb])
                    STT(yp[:, 512 * nb:512 * nb + 512], xr_[:, 512 * nb:512 * nb + 512], ALPHA, bk[:, 0:512], ALU.mult, ALU.add, [txr, tb], [t_yp])
                S.op("dve", lambda e: e.reduce_sum(out=st[:, 0:1], in_=yp[:], axis=AX.X), [t_yp], [t_st])
                ACT(jk[:], yp[:], AF.Square, [t_yp], [t_jk, t_st], accum=st[:, 1:2])
                TS("dve", st[:, 2:3], st[:, 0:1], 1.0 / 2048, ALU.mult, [t_st], [t_st])
                TT("dve", st[:, 3:4], st[:, 2:3], st[:, 2:3], ALU.mult, [t_st], [t_st])
                STT(st[:, 4:5], st[:, 1:2], 1.0 / 2048, st[:, 3:4], ALU.mult, ALU.subtract, [t_st], [t_st])
                TS("dve", st[:, 4:5], st[:, 4:5], 1e-5, ALU.add, [t_st], [t_st])
                ACT(st[:, 5:6], st[:, 4:5], AF.Ln, [t_st], [t_st])
                ACT(st[:, 6:7], st[:, 5:6], AF.Exp, [t_st], [t_st], scale=-0.5)
                STT(st[:, 7:8], st[:, 2:3], -1.0, st[:, 6:7], ALU.mult, ALU.mult, [t_st], [t_st])
                ACT(yp[:], yp[:], AF.Identity, [t_yp, t_st], [t_yp], scale=st[:, 6:7], bias=st[:, 7:8])
                TT("dve", yp[:], yp[:], lngb[:, 0, :], ALU.mult, [t_yp, t_lngb], [t_yp])
                TT("pool", yo_[:], yp[:], lngb[:, 1, :], ALU.add, [t_yp, t_lngb], [tyo])
                S.dma("sp", y_d[128 * ti:128 * ti + 128, :], yo_[:], reads=[tyo])
            S.barrier()

        with nc.Block() as block:
            @block.tensor
            def _(e):
                S.replay("pe", e)

            @block.scalar
            def _(e):
                S.replay("act", e)

            @block.vector
            def _(e):
                S.replay("dve", e)

            @block.gpsimd
            def _(e):
                S.replay("pool", e)

            @block.sync
            def _(e):
                S.replay("sp", e)
        build_nc.stats = dict(S.cnt)
    return nc


def _consts():
    idx = np.arange(128)
    blk = idx // 8
    cf = np.zeros((128, NCF), np.float32)
    cf[:, CF_ID:CF_ID + 128] = np.eye(128)
    cf[:, CF_ONE:CF_ONE + 128] = 1.0
    t, i = idx[:, None], idx[None, :]
    cf[:, CF_TRIP:CF_TRIP + 128] = (t <= i)
    cf[:, CF_NEGP:CF_NEGP + 128] = np.where(i >= t, 0.0, NEG)
    same = (blk[:, None] == blk[None, :])
    cf[:, CF_TRIS:CF_TRIS + 128] = (t <= i) & same
    cf[:, CF_BLKS:CF_BLKS + 128] = same
    cf[:, CF_NEGS:CF_NEGS + 128] = np.where((i >= t) & same, 0.0, NEG)
    cf[:, CF_ROWM:CF_ROWM + 16] = (blk[:, None] == np.arange(16)[None, :])
    cb = np.zeros((128, NCB), np.float32)
    cb[:, CB_ID:CB_ID + 128] = np.eye(128)
    cb[:, CB_ONE:CB_ONE + 128] = 1.0
    cb[:, CB_STRP:CB_STRP + 128] = (i > t)
    cb[:, CB_STRS:CB_STRS + 128] = (i > t) & same
    colm = np.zeros((128, 16, 128), np.float32)
    for s in range(16):
        colm[:, s, 8 * s:8 * s + 8] = 1.0
    cb[:, CB_COLM:CB_COLM + 2048] = colm.reshape(128, 2048)
    cb[:, CB_TRIP:CB_TRIP + 128] = cf[:, CF_TRIP:CF_TRIP + 128]
    cb[:, CB_TRIS:CB_TRIS + 128] = cf[:, CF_TRIS:CF_TRIS + 128]
    cb[:, CB_BLKS:CB_BLKS + 128] = cf[:, CF_BLKS:CF_BLKS + 128]
    cb[:, CB_NEGP:CB_NEGP + 128] = cf[:, CF_NEGP:CF_NEGP + 128]
    cb[:, CB_NEGS:CB_NEGS + 128] = cf[:, CF_NEGS:CF_NEGS + 128]
    return cf, cb.astype(ml_dtypes.bfloat16)


_NC_CACHE = {}


def kernel(x_prompt, x_sample, mem_prompt, state_conv, state_qkv_conv, state_delta, cache_mem_k, cache_mem_v,
           w_in, conv_w, conv_b, conv_ln_g, conv_ln_b, qkv_conv_w, a_log, dt_bias, delta_norm_w,
           w_mem_k, w_mem_v, w_out, ln_g, ln_b):
    f = lambda a: np.ascontiguousarray(np.asarray(a, dtype=np.float32))
    x_prompt, x_sample, mem_prompt = f(x_prompt), f(x_sample), f(mem_prompt)
    cf, cb = _consts()
    pv = np.zeros((128, NPV), np.float32)
    cwT = f(conv_w)[0].T.reshape(4, 128, 31)
    pv[:, PV_CW:PV_CW + 124] = cwT.transpose(1, 0, 2).reshape(128, 124)
    pv[:, PV_CB:PV_CB + 4] = f(conv_b)[0].reshape(4, 128).T
    pv[:, PV_CLG:PV_CLG + 4] = f(conv_ln_g)[0].reshape(4, 128).T
    pv[:, PV_CLB:PV_CLB + 4] = f(conv_ln_b)[0].reshape(4, 128).T
    qwT = f(qkv_conv_w)[0].T.reshape(24, 128, 4)
    pv[:, PV_QW:PV_QW + 96] = qwT.transpose(1, 0, 2).reshape(128, 96)
    pv[:, PV_NW] = f(delta_norm_w)[0]
    pv[:, PV_AL:PV_AL + 8] = f(a_log)[0][None, :]
    pv[:, PV_DT:PV_DT + 8] = f(dt_bias)[0][None, :]
    lngb = np.ascontiguousarray(np.broadcast_to(np.stack([f(ln_g)[0], f(ln_b)[0]])[None], (128, 2, 2048)))
    W_in, W_mk, W_mv, W_out = f(w_in)[0], f(w_mem_k)[0], f(w_mem_v)[0], f(w_out)[0]
    sc, sq, sd, ck, cv = f(state_conv)[0], f(state_qkv_conv)[0], f(state_delta)[0], f(cache_mem_k)[0], f(cache_mem_v)[0]
    in_maps = []
    for c in range(8):
        b, hf = c // 2, c % 2
        xall = np.zeros((NCOL, 2048), np.float32)
        if hf == 1:
            xall[0:1024] = x_prompt[b, 0:1024]
        xall[1024:2048] = x_prompt[b, 1024 * hf:1024 * hf + 1024]
        xall[2048:] = x_sample[16 * c:16 * c + 16].reshape(128, 2048)
        sl = slice(16 * c, 16 * c + 16)
        in_maps.append({
            "xall": xall, "memp": mem_prompt[b], "sconv": sc[sl], "sqkv": sq[sl], "sdelta": sd[sl],
            "ck": ck[sl], "cv": cv[sl], "w_in": W_in, "w_mk": W_mk, "w_mv": W_mv, "w_out": W_out,
            "cf": cf, "cb": cb, "pv": pv, "lngb": lngb,
        })
    if os.environ.get("K_CORES"):
        ncores = int(os.environ["K_CORES"])
        nc = build_nc()
        res = run_bass_kernel_spmd(nc, in_maps[:ncores], core_ids=list(range(ncores)), trace=bool(os.environ.get("K_TRACE")))
        kernel.last = res
        R = list(res.results) + [res.results[0]] * (8 - ncores)
    else:
        if "nc" not in _NC_CACHE:
            _NC_CACHE["nc"] = build_nc()
        nc = _NC_CACHE["nc"]
        res = run_bass_kernel_spmd(nc, in_maps, core_ids=list(range(8)))
        R = res.results
    y_p = np.zeros((4, 2048, 2048), np.float32)
    y_s = np.zeros((128, 8, 2048), np.float32)
    o_conv_p = np.zeros((1, 4, 30, 512), np.float32)
    o_qkv_p = np.zeros((1, 4, 3, 3072), np.float32)
    o_delta_p = np.zeros((1, 4, 8, 128, 128), np.float32)
    o_mk = np.zeros((1, 4, 256, 4, 128), np.float32)
    o_mv = np.zeros((1, 4, 256, 4, 128), np.float32)
    o_conv_s = np.zeros((1, 128, 30, 512), np.float32)
    o_qkv_s = np.zeros((1, 128, 3, 3072), np.float32)
    o_delta_s = np.zeros((1, 128, 8, 128, 128), np.float32)
    for c in range(8):
        b, hf = c // 2, c % 2
        r = R[c]
        y_p[b, 1024 * hf:1024 * hf + 1024] = r["y"][0:1024]
        y_s[16 * c:16 * c + 16] = r["y"][1024:1152].reshape(16, 8, 2048)
        sl = slice(16 * c, 16 * c + 16)
        o_conv_s[0, sl] = r["oconv_s"]
        o_qkv_s[0, sl] = r["oqkv_s"]
        o_delta_s[0, sl] = r["odelta_s"]
        if hf == 1:
            o_conv_p[0, b] = r["oconv_p"]
            o_qkv_p[0, b] = r["oqkv_p"]
            o_delta_p[0, b] = r["odelta_p"]
        else:
            o_mk[0, b] = r["omk"].reshape(256, 4, 128)
            o_mv[0, b] = r["omv"].reshape(256, 4, 128)
    return (y_p, y_s, o_conv_p, o_qkv_p, o_delta_p, o_mk, o_mv, o_conv_s, o_qkv_s, o_delta_s)
```

```python
import os
import contextlib
from contextlib import ExitStack
import numpy as np
import ml_dtypes
import concourse.bass as bass
import concourse.mybir as mybir
from concourse.bass_utils import run_bass_kernel_spmd

F32 = mybir.dt.float32
BF = mybir.dt.bfloat16
AF = mybir.ActivationFunctionType
ALU = mybir.AluOpType
AX = mybir.AxisListType

SAME_SYNC = True
STAGE = int(os.environ.get('K_STAGE', '9'))
SUB = int(os.environ.get('K_SUB', '99'))
NT = 17
NCOL = NT * 128
NM = 1152
O_GA, O_GB, O_CG, O_Q, O_K, O_V, O_DG, O_BT, O_DC, O_MQ, O_MG = 0, 512, 1024, 1536, 2560, 3584, 4608, 5632, 5640, 5648, 6160
N_IN = 6672
ALPHA = 2.0 ** 0.25
NEG = -30000.0

CF_ID, CF_ONE, CF_TRIP, CF_NEGP, CF_TRIS, CF_BLKS, CF_NEGS, CF_ROWM = 0, 128, 256, 384, 512, 640, 768, 896
NCF = 912
CB_ID, CB_ONE, CB_STRP, CB_STRS, CB_COLM = 0, 128, 256, 384, 512
CB_TRIP, CB_TRIS, CB_BLKS, CB_NEGP, CB_NEGS, CB_NTRP, CB_NTRS = 2560, 2688, 2816, 2944, 3072, 3200, 3328
NCB = 3456
PV_CW, PV_CB, PV_CLG, PV_CLB, PV_QW, PV_NW, PV_AL, PV_DT = 0, 124, 128, 132, 136, 232, 233, 241
NPV = 249


class _Skip(Exception):
    pass


class Tok:
    __slots__ = ("w", "r", "excl")

    def __init__(self, excl=False):
        self.w = None
        self.r = {}
        self.excl = excl


class Sched:
    ENG = ("pe", "act", "dve", "pool", "sp")

    def __init__(self, nc, es, ndma=40):
        self.nc = nc
        self.sem = {e: es.enter_context(nc.semaphore("sem_" + e)) for e in self.ENG}
        self.cnt = {e: 0 for e in self.ENG}
        self.q = {e: [] for e in self.ENG}
        self.seen = {e: {} for e in self.ENG}
        self.dsem = [es.enter_context(nc.semaphore("dsem%d" % i)) for i in range(ndma)]
        self.dcnt = [0] * ndma
        self.dnext = 0

    def _deps(self, reads, writes):
        ev = []
        for t in reads:
            if t.w is not None:
                ev.append(t.w)
            if t.excl:
                ev.extend(t.r.values())
        for t in writes:
            if t.w is not None:
                ev.append(t.w)
            ev.extend(t.r.values())
        return ev

    def _waits(self, eng, evs):
        out = []
        for (key, val) in evs:
            if key == eng and (eng == "pe" or not SAME_SYNC):
                continue
            if self.seen[eng].get(key, 0) >= val:
                continue
            self.seen[eng][key] = val
            out.append((key, val))
        return out

    def _mark(self, ev, reads, writes):
        for t in writes:
            t.w = ev
            t.r = {}
        k = ev[0]
        for t in reads:
            if t.r.get(k, (k, 0))[1] < ev[1]:
                t.r[k] = ev

    def op(self, eng, fn, reads=(), writes=()):
        w = self._waits(eng, self._deps(reads, writes))
        self.cnt[eng] += 1
        ev = (eng, self.cnt[eng])
        self.q[eng].append(("op", fn, w, None))
        self._mark(ev, reads, writes)
        return ev

    def dma(self, q, out, in_, reads=(), writes=(), **kw):
        k = self.dnext
        self.dnext = (self.dnext + 1) % len(self.dsem)
        evs = self._deps(reads, writes)
        if self.dcnt[k] > 0:
            evs.append((("d", k), 16 * self.dcnt[k]))
        w = self._waits(q, evs)
        self.dcnt[k] += 1
        ev = (("d", k), 16 * self.dcnt[k])
        self.q[q].append(("dma", (out, in_, kw), w, k))
        self._mark(ev, reads, writes)
        return ev

    def barrier(self):
        evs = [(e, self.cnt[e]) for e in self.ENG if self.cnt[e] > 0]
        if not os.environ.get("K_NODMABAR"):
            evs += [(("d", k), 16 * c) for k, c in enumerate(self.dcnt) if c > 0]
        for e in self.ENG:
            w = self._waits(e, evs)
            if w:
                self.q[e].append(("wait", None, w, None))

    def semh(self, key):
        return self.sem[key] if isinstance(key, str) else self.dsem[key[1]]

    def replay(self, e, h):
        for kind, fn, waits, k in self.q[e]:
            for (key, val) in waits:
                h.wait_ge(self.semh(key), val)
            if kind == "op":
                fn(h).then_inc(self.sem[e], 1)
            elif kind == "dma":
                out, in_, kw = fn
                h.dma_start(out=out, in_=in_, **kw).then_inc(self.dsem[k], 16)


def build_nc(dbg=False):
    nc = bass.Bass("TRN2", target_bir_lowering=False)

    def din(name, shape, dt=F32):
        return nc.dram_tensor(name, list(shape), dt, kind="ExternalInput").ap()

    def dout(name, shape, dt=F32):
        return nc.dram_tensor(name, list(shape), dt, kind="ExternalOutput").ap()

    xall = din("xall", [NCOL, 2048])
    memp = din("memp", [256, 2048])
    sconv = din("sconv", [16, 30, 512])
    sqkv = din("sqkv", [16, 3, 3072])
    sdelta = din("sdelta", [16, 8, 128, 128])
    ck = din("ck", [16, 256, 4, 128])
    cv = din("cv", [16, 256, 4, 128])
    w_in = din("w_in", [2048, N_IN])
    w_mk = din("w_mk", [2048, 512])
    w_mv = din("w_mv", [2048, 512])
    w_out = din("w_out", [2048, 2048])
    cf_d = din("cf", [128, NCF])
    cb_d = din("cb", [128, NCB], BF)
    pv_d = din("pv", [128, NPV])
    lngb_d = din("lngb", [128, 2, 2048])

    y_d = dout("y", [NM, 2048])
    oconv_p = dout("oconv_p", [30, 512])
    oqkv_p = dout("oqkv_p", [3, 3072])
    odelta_p = dout("odelta_p", [8, 128, 128])
    omk = dout("omk", [256, 512])
    omv = dout("omv", [256, 512])
    oconv_s = dout("oconv_s", [16, 30, 512])
    oqkv_s = dout("oqkv_s", [16, 3, 3072])
    odelta_s = dout("odelta_s", [16, 8, 128, 128])

    with ExitStack() as es:
        S = Sched(nc, es)
        ctr = [0]

        def sb(es_, shape, dt=F32):
            ctr[0] += 1
            t = es_.enter_context(nc.sbuf_tensor("t%d" % ctr[0], list(shape), dt))
            return t, Tok()

        def ACT(out, in_, func, reads, writes, bias=None, scale=None, accum=None):
            kw = {}
            if bias is not None:
                kw["bias"] = bias
            if scale is not None:
                kw["scale"] = scale
            if accum is not None:
                kw["accum_out"] = accum
            return S.op("act", lambda e: e.activation(out=out, in_=in_, func=func, **kw), reads, writes)

        def TS(eng, out, in0, s1, op0, reads, writes, s2=None, op1=None):
            if op1 is None:
                return S.op(eng, lambda e: e.tensor_scalar(out=out, in0=in0, scalar1=s1, scalar2=None, op0=op0), reads, writes)
            return S.op(eng, lambda e: e.tensor_scalar(out=out, in0=in0, scalar1=s1, scalar2=s2, op0=op0, op1=op1), reads, writes)

        def TT(eng, out, in0, in1, op, reads, writes):
            return S.op(eng, lambda e: e.tensor_tensor(out=out, in0=in0, in1=in1, op=op), reads, writes)

        def STT(out, in0, scalar, in1, op0, op1, reads, writes):
            return S.op("dve", lambda e: e.scalar_tensor_tensor(out=out, in0=in0, scalar=scalar, in1=in1, op0=op0, op1=op1), reads, writes)

        def CP(eng, out, in_, reads, writes):
            if eng == "act":
                return S.op("act", lambda e: e.copy(out=out, in_=in_), reads, writes)
            return S.op(eng, lambda e: e.tensor_copy(out=out, in_=in_), reads, writes)

        def split3(src, tsrc, outs, touts, r_f32, t_r):
            CP("dve", outs[0], src, [tsrc], [touts])
            TT("dve", r_f32, src, outs[0], ALU.subtract, [tsrc, touts], [t_r])
            CP("dve", outs[1], r_f32, [t_r], [touts])
            TT("dve", outs[2], r_f32, outs[1], ALU.subtract, [t_r, touts], [touts])

        def MM(out, lhsT, rhs, start, stop, reads, writes):
            return S.op("pe", lambda e: e.matmul(out=out, lhsT=lhsT, rhs=rhs, start=start, stop=stop), reads, writes)

        def TR(out, in_, ident, reads, writes):
            return S.op("pe", lambda e: e.transpose(out=out, in_=in_, identity=ident), reads, writes)

        cf, t_cf = sb(es, [128, NCF])
        cb, t_cb = sb(es, [128, NCB], BF)
        pv, t_pv = sb(es, [128, NPV])
        mixT = es.enter_context(nc.sbuf_tensor("mixT", [128, 16, NM], BF))
        t_mix = [Tok() for _ in range(16)]
        NW = 4
        wsl = [sb(es, [128, 16, 128], BF) for _ in range(NW)]
        bt, t_sm = sb(es, [128, NT, 8])
        nbt, _ = sb(es, [128, NT, 8])
        gtk, _ = sb(es, [128, NT, 8])
        gc_, _ = sb(es, [128, NT, 8])
        ngc, _ = sb(es, [128, NT, 8])
        ee, _ = sb(es, [128, NT, 8])
        nee, _ = sb(es, [128, NT, 8])
        egl, _ = sb(es, [128, NT, 8])
        glb, _ = sb(es, [128, NT, 8])
        glbs, _ = sb(es, [128, 16, 8])
        nega, _ = sb(es, [128, 8])
        g3 = [sb(es, [128, NT, 8], BF)[0] for _ in range(3)]
        es2 = ExitStack()
        xT, t_xT = sb(es2, [128, 16, NCOL], BF)
        psb = [es.enter_context(nc.psum_tensor("ps%d" % i, [128, 512], F32)) for i in range(8)]
        t_ps = [Tok(excl=True) for _ in range(8)]
        pctr = [0]

        def bank():
            i = pctr[0] % 8
            pctr[0] += 1
            return psb[i], t_ps[i]

        ident_f = cf[:, CF_ID:CF_ID + 128]
        ones_f = cf[:, CF_ONE:CF_ONE + 128]
        ident_b = cb[:, CB_ID:CB_ID + 128]
        ones_b = cb[:, CB_ONE:CB_ONE + 128]

        S.dma("sp", cf[:], cf_d[:, :], writes=[t_cf])
        S.dma("sp", cb[:], cb_d[:, :], writes=[t_cb])
        S.dma("sp", pv[:], pv_d[:, :], writes=[t_pv])

        WL = []
        WL.append((w_in, O_BT, 16))
        for c in range(4):
            WL += [(w_in, O_GA + 128 * c, 128), (w_in, O_GB + 128 * c, 128), (w_in, O_CG + 128 * c, 128)]
        for h in range(4):
            WL += [(w_mk, 128 * h, 128), (w_mv, 128 * h, 128), (w_in, O_MQ + 128 * h, 128), (w_in, O_MG + 128 * h, 128)]
        for h in range(8):
            WL += [(w_in, O_K + 128 * h, 128), (w_in, O_V + 128 * h, 128), (w_in, O_Q + 128 * h, 128), (w_in, O_DG + 128 * h, 128)]
        wst = {"issued": 0, "used": 0}

        def w_issue():
            i = wst["issued"]
            if i >= len(WL):
                return
            src, c0, n = WL[i]
            t, tk = wsl[i % NW]
            S.dma("pool", t[:, :, 0:n], src[:, c0:c0 + n].rearrange("(c p) n -> p c n", p=128), writes=[tk])
            wst["issued"] += 1

        def w_next():
            i = wst["used"]
            while wst["issued"] < min(len(WL), i + NW - 1) or wst["issued"] <= i:
                w_issue()
            wst["used"] += 1
            return wsl[i % NW]

        def proj_fm(wt, wtk, c0, n, ncols=128):
            bk, tb = bank()
            for k in range(16):
                MM(bk[0:ncols, 0:n], wt[:, k, 0:ncols], xT[:, k, c0:c0 + n], k == 0, k == 15, [wtk, t_xT], [tb])
            return bk, tb

        with ExitStack() as ph, contextlib.suppress(_Skip):
            if STAGE < 0:
                raise _Skip()
            xs = [sb(ph, [128, 2048], BF) for _ in range(3)]
            for i in range(NT):
                t, tk = xs[i % 3]
                src = xall[128 * i:128 * i + 128, :]
                S.dma("pool", t[:].rearrange("p (a n) -> p a n", n=512), src.rearrange("p (a n) -> p a n", n=512), writes=[tk])
                for half in range(2):
                    bk, tb = bank()
                    bkb = bk[:].bitcast(BF)
                    for c in range(8):
                        TR(bkb[:, 128 * c:128 * c + 128], t[:, 128 * (8 * half + c):128 * (8 * half + c) + 128], ident_b, [tk, t_cb], [tb])
                    src_v = bkb[:, 0:1024].rearrange("p (c n) -> p c n", c=8)
                    dst = xT[:, 8 * half:8 * half + 8, 128 * i:128 * i + 128]
                    CP("act" if half == 0 else "dve", dst, src_v, [tb], [t_xT])
            S.barrier()

        with ExitStack() as ph, contextlib.suppress(_Skip):
            if STAGE < 1:
                raise _Skip()
            bd, t_bd = sb(ph, [128, NT, 16])
            tmp8, t_tmp8 = sb(ph, [128, NT, 8])
            gm, t_gm = sb(ph, [128, 16, 8], BF)
            wt, wtk = w_next()
            bk, tb = bank()
            for i in range(NT):
                for k in range(16):
                    MM(bk[:, 16 * i:16 * i + 16], xT[:, k, 128 * i:128 * i + 128], wt[:, k, 0:16], k == 0, k == 15, [wtk, t_xT], [tb])
            CP("dve", bd[:], bk[:, 0:16 * NT].rearrange("p (t c) -> p t c", c=16), [tb], [t_bd])
            if SUB < 1:
                raise _Skip()
            ACT(bt[:], bd[:, :, 0:8], AF.Sigmoid, [t_bd], [t_sm])
            TS("dve", nbt[:], bt[:], -1.0, ALU.mult, [t_sm], [t_sm])
            if SUB < 2:
                raise _Skip()
            ACT(nega[:], pv[:, PV_AL:PV_AL + 8], AF.Exp, [t_pv], [t_sm])
            TS("dve", nega[:], nega[:], -1.0, ALU.mult, [t_sm], [t_sm])
            if SUB < 3:
                raise _Skip()
            TT("dve", tmp8[:], bd[:, :, 8:16], pv[:, PV_DT:PV_DT + 8].unsqueeze(1).broadcast_to([128, NT, 8]), ALU.add, [t_bd, t_pv], [t_tmp8])
            ACT(tmp8[:], tmp8[:], AF.Exp, [t_tmp8], [t_tmp8])
            ACT(tmp8[:], tmp8[:], AF.Ln, [t_tmp8], [t_tmp8], bias=1.0)
            TT("dve", gtk[:], tmp8[:], nega[:].unsqueeze(1).broadcast_to([128, NT, 8]), ALU.mult, [t_tmp8, t_sm], [t_sm])
            if SUB < 4:
                raise _Skip()
            split3(gtk[:], t_sm, [g3[0][:], g3[1][:], g3[2][:]], t_sm, tmp8[:], t_tmp8)
            bk, tb = bank()
            for i in range(NT):
                tri = cb[:, CB_TRIP:CB_TRIP + 128] if i < 16 else cb[:, CB_TRIS:CB_TRIS + 128]
                blk = ones_b if i < 16 else cb[:, CB_BLKS:CB_BLKS + 128]
                for j in range(3):
                    MM(bk[:, 8 * i:8 * i + 8], tri, g3[j][:, i, :], j == 0, j == 2, [t_cb, t_sm], [tb])
                for j in range(3):
                    MM(bk[:, 256 + 8 * i:256 + 8 * i + 8], blk, g3[j][:, i, :], j == 0, j == 2, [t_cb, t_sm], [tb])
            CP("dve", gc_[:], bk[:, 0:8 * NT].rearrange("p (t c) -> p t c", c=8), [tb], [t_sm])
            CP("act", tmp8[:], bk[:, 256:256 + 8 * NT].rearrange("p (t c) -> p t c", c=8), [tb], [t_tmp8])
            if SUB < 5:
                raise _Skip()
            TS("dve", ngc[:], gc_[:], -1.0, ALU.mult, [t_sm], [t_sm])
            ACT(ee[:], gc_[:], AF.Exp, [t_sm], [t_sm])
            TS("dve", nee[:], ee[:], -1.0, ALU.mult, [t_sm], [t_sm])
            ACT(glb[:], tmp8[:], AF.Exp, [t_tmp8], [t_sm])
            TT("dve", tmp8[:], tmp8[:], gc_[:], ALU.subtract, [t_tmp8, t_sm], [t_tmp8])
            ACT(egl[:], tmp8[:], AF.Exp, [t_tmp8], [t_sm])
            if SUB < 6:
                raise _Skip()
            bk, tb = bank()
            for j in range(3):
                TT("dve", gm[:], g3[j][:, 16, :].unsqueeze(1).broadcast_to([128, 16, 8]),
                   cf[:, CF_ROWM:CF_ROWM + 16].unsqueeze(2).broadcast_to([128, 16, 8]), ALU.mult, [t_sm, t_cf], [t_gm])
                MM(bk[:, 0:128], ones_b, gm[:].rearrange("p s h -> p (s h)"), j == 0, j == 2, [t_cb, t_gm], [tb])
            ACT(glbs[:], bk[:, 0:128].rearrange("p (s h) -> p s h", h=8), AF.Exp, [tb], [t_sm])
            if SUB < 9:
                raise _Skip()
            S.barrier()

        if os.environ.get("K_BAR"):
            S.barrier()
        OWN_CH = [(896, 512), (1408, 512), (1920, 256)]
        with ExitStack() as ph, contextlib.suppress(_Skip):
            if STAGE < 2:
                raise _Skip()
            ubuf, t_ub = sb(ph, [128, 1152])
            usamp, t_us = sb(ph, [128, 16, 38])
            hc, t_hc = sb(ph, [128, 4, NM])
            sgate, t_sg = sb(ph, [128, 4, NM], BF)
            sgt, t_sgt = sb(ph, [128, 512])
            stg, t_stg = sb(ph, [128, 512])
            scs, t_scs = sb(ph, [120, 512])
            S.dma("sp", oconv_s[:, 0:22, :], sconv[:, 8:30, :])
            unew_s, t_uns = sb(ph, [128, 4, 128])
            for c in range(4):
                for g4 in range(4):
                    S.dma("sp", scs[:, 0:128], sconv[4 * g4:4 * g4 + 4, :, 128 * c:128 * c + 128].rearrange("s r n -> (s r) n"), writes=[t_scs])
                    bk, tb = bank()
                    TR(bk[:, 0:120], scs[0:120, 0:128], ident_f[0:120, 0:120], [t_scs, t_cf], [tb])
                    CP("act", usamp[:, 4 * g4:4 * g4 + 4, 0:30], bk[:, 0:120].rearrange("p (s r) -> p s r", r=30), [tb], [t_us])
                wa, ta = w_next()
                wb, tbk = w_next()
                for (c0, n) in OWN_CH:
                    bka, tba = proj_fm(wa, ta, c0, n)
                    bkb_, tbb = proj_fm(wb, tbk, c0, n)
                    ACT(sgt[:, 0:n], bkb_[:, 0:n], AF.Sigmoid, [tbb], [t_sgt])
                    if c0 < 1920:
                        TT("dve", ubuf[:, c0 - 896:c0 - 896 + n], bka[:, 0:n], sgt[:, 0:n], ALU.mult, [tba, t_sgt], [t_ub])
                    else:
                        TT("dve", ubuf[:, 1024:1152], bka[:, 0:128], sgt[:, 0:128], ALU.mult, [tba, t_sgt], [t_ub])
                        TT("dve", unew_s[:, c, :], bka[:, 128:256], sgt[:, 128:256], ALU.mult, [tba, t_sgt], [t_uns])
                        CP("act", usamp[:, :, 30:38], unew_s[:, c, :].rearrange("p (s l) -> p s l", l=8), [t_uns], [t_us])
                wg, tg = w_next()
                for (c0, n) in [(1024, 512), (1536, 512), (2048, 128)]:
                    bkg, tbg = proj_fm(wg, tg, c0, n)
                    ACT(sgate[:, c, c0 - 1024:c0 - 1024 + n], bkg[:, 0:n], AF.Silu, [tbg], [t_sg])
                cw = lambda j: pv[:, PV_CW + 31 * c + j:PV_CW + 31 * c + j + 1]
                TS("dve", hc[:, c, 0:1024], ubuf[:, 98:98 + 1024], cw(0), ALU.mult, [t_ub, t_pv], [t_hc],
                   s2=pv[:, PV_CB + c:PV_CB + c + 1], op1=ALU.add)
                for j in range(1, 31):
                    STT(hc[:, c, 0:1024], ubuf[:, 98 + j:98 + j + 1024], cw(j), hc[:, c, 0:1024], ALU.mult, ALU.add, [t_ub, t_pv, t_hc], [t_hc])
                hs = hc[:, c, 1024:1152].rearrange("p (s l) -> p s l", l=8)
                TS("dve", hs, usamp[:, :, 0:8], cw(0), ALU.mult, [t_us, t_pv], [t_hc], s2=pv[:, PV_CB + c:PV_CB + c + 1], op1=ALU.add)
                for j in range(1, 31):
                    STT(hs, usamp[:, :, j:j + 8], cw(j), hs, ALU.mult, ALU.add, [t_us, t_pv, t_hc], [t_hc])
                bk, tb = bank()
                TR(bk[0:30, 0:128], ubuf[:, 1122:1152], ident_f, [t_ub, t_cf], [tb])
                CP("act", stg[0:30, 128 * c:128 * c + 128], bk[0:30, 0:128], [tb], [t_stg])
            S.dma("sp", oconv_p[:, :], stg[0:30, :], reads=[t_stg])
            stg2, t_stg2 = sb(ph, [128, 512])
            for c in range(4):
                bk, tb = bank()
                TR(bk[:, 0:128], unew_s[:, c, :], ident_f, [t_uns, t_cf], [tb])
                CP("act", stg2[:, 128 * c:128 * c + 128], bk[:, 0:128], [tb], [t_stg2])
            for s_ in range(16):
                S.dma("sp", oconv_s[s_, 22:30, :], stg2[8 * s_:8 * s_ + 8, :], reads=[t_stg2])
            sq, t_sq = sb(ph, [128, 512])
            hl = [sb(ph, [128, 512], BF) for _ in range(4)]
            mean, t_mean = sb(ph, [128, 512])
            var, t_var = sb(ph, [128, 512])
            rstd, t_rstd = sb(ph, [128, 512])
            xc, t_xc = sb(ph, [128, 512])
            for (m0, n) in [(0, 512), (512, 512), (1024, 128)]:
                bk1, tb1 = bank()
                bk2, tb2 = bank()
                for c in range(4):
                    CP("act", hl[0][0][:, 0:n], hc[:, c, m0:m0 + n], [t_hc], [hl[0][1]])
                    TT("dve", hl[1][0][:, 0:n], hc[:, c, m0:m0 + n], hl[0][0][:, 0:n], ALU.subtract, [t_hc, hl[0][1]], [hl[1][1]])
                    MM(bk1[:, 0:n], ones_b, hl[0][0][:, 0:n], c == 0, False, [t_cb, hl[0][1]], [tb1])
                    MM(bk1[:, 0:n], ones_b, hl[1][0][:, 0:n], False, c == 3, [t_cb, hl[1][1]], [tb1])
                for c in range(4):
                    ACT(sq[:, 0:n], hc[:, c, m0:m0 + n], AF.Square, [t_hc], [t_sq])
                    CP("act", hl[2][0][:, 0:n], sq[:, 0:n], [t_sq], [hl[2][1]])
                    TT("dve", hl[3][0][:, 0:n], sq[:, 0:n], hl[2][0][:, 0:n], ALU.subtract, [t_sq, hl[2][1]], [hl[3][1]])
                    MM(bk2[:, 0:n], ones_b, hl[2][0][:, 0:n], c == 0, False, [t_cb, hl[2][1]], [tb2])
                    MM(bk2[:, 0:n], ones_b, hl[3][0][:, 0:n], False, c == 3, [t_cb, hl[3][1]], [tb2])
                ACT(mean[:, 0:n], bk1[:, 0:n], AF.Copy, [tb1], [t_mean], scale=1.0 / 512)
                TT("dve", var[:, 0:n], mean[:, 0:n], mean[:, 0:n], ALU.mult, [t_mean], [t_var])
                STT(var[:, 0:n], bk2[:, 0:n], 1.0 / 512, var[:, 0:n], ALU.mult, ALU.subtract, [tb2, t_var], [t_var])
                TS("dve", var[:, 0:n], var[:, 0:n], 1e-5, ALU.add, [t_var], [t_var])
                ACT(var[:, 0:n], var[:, 0:n], AF.Ln, [t_var], [t_var])
                ACT(rstd[:, 0:n], var[:, 0:n], AF.Exp, [t_var], [t_rstd], scale=-0.5)
                for c in range(4):
                    TT("dve", xc[:, 0:n], hc[:, c, m0:m0 + n], mean[:, 0:n], ALU.subtract, [t_hc, t_mean], [t_xc])
                    TT("dve", xc[:, 0:n], xc[:, 0:n], rstd[:, 0:n], ALU.mult, [t_xc, t_rstd], [t_xc])
                    ACT(xc[:, 0:n], xc[:, 0:n], AF.Silu, [t_xc, t_pv], [t_xc],
                        scale=pv[:, PV_CLG + c:PV_CLG + c + 1], bias=pv[:, PV_CLB + c:PV_CLB + c + 1])
                    TT("dve", mixT[:, c, m0:m0 + n], xc[:, 0:n], sgate[:, c, m0:m0 + n], ALU.mult, [t_xc, t_sg], [t_mix[c]])
            S.barrier()

        MEM_CH = [(1024, 512), (1536, 512), (2048, 128)]
        with ExitStack() as ph, contextlib.suppress(_Skip):
            if STAGE < 3:
                raise _Skip()
            mT, t_mT = sb(ph, [128, 16, 256], BF)
            xs3, t_xs3 = sb(ph, [128, 2048], BF)
            for j in range(2):
                S.dma("pool", xs3[:].rearrange("p (a n) -> p a n", n=512), memp[128 * j:128 * j + 128, :].rearrange("p (a n) -> p a n", n=512), writes=[t_xs3])
                for half in range(2):
                    bk, tb = bank()
                    bkb = bk[:].bitcast(BF)
                    for c in range(8):
                        TR(bkb[:, 128 * c:128 * c + 128], xs3[:, 128 * (8 * half + c):128 * (8 * half + c) + 128], ident_b, [t_xs3, t_cb], [tb])
                    CP("act" if half == 0 else "dve", mT[:, 8 * half:8 * half + 8, 128 * j:128 * j + 128],
                       bkb[:, 0:1024].rearrange("p (c n) -> p c n", c=8), [tb], [t_mT])
            mqT, t_mq = sb(ph, [128, NM], BF)
            mgT, t_mg = sb(ph, [128, NM], BF)
            KTp, t_ktp = sb(ph, [128, 256], BF)
            Vp, t_vp = sb(ph, [128, 2, 128], BF)
            kvst, t_kvst = sb(ph, [128, 2, 2, 128])
            kc, t_kc = sb(ph, [128, 16, 2, 128], BF)
            vc, t_vc = sb(ph, [128, 16, 2, 128], BF)
            kcT, t_kct = sb(ph, [128, 16, 256], BF)
            mqm, t_mqm = sb(ph, [128, 16, 128], BF)
            pf, t_pf = sb(ph, [128, 256])
            pn, t_pn = sb(ph, [128, 256], BF)
            pT, t_pT = sb(ph, [128, 2, 128], BF)
            mx, t_mx = sb(ph, [128, 4])
            for h in range(4):
                wk, tk_ = w_next()
                wv, tv_ = w_next()
                bk, tb = bank()
                for k in range(16):
                    MM(bk[:, 0:256], wk[:, k, :], mT[:, k, :], k == 0, k == 15, [tk_, t_mT], [tb])
                CP("act", KTp[:], bk[:, 0:256], [tb], [t_ktp])
                bk, tb = bank()
                for mc in range(2):
                    for k in range(16):
                        MM(bk[:, 128 * mc:128 * mc + 128], mT[:, k, 128 * mc:128 * mc + 128], wk[:, k, :], k == 0, k == 15, [tk_, t_mT], [tb])
                    for k in range(16):
                        MM(bk[:, 256 + 128 * mc:256 + 128 * mc + 128], mT[:, k, 128 * mc:128 * mc + 128], wv[:, k, :], k == 0, k == 15, [tv_, t_mT], [tb])
                CP("dve", kvst[:].rearrange("p a b d -> p (a b d)"), bk[:, 0:512], [tb], [t_kvst])
                CP("act", Vp[:].rearrange("p b d -> p (b d)"), bk[:, 256:512], [tb], [t_vp])
                S.dma("sp", omk[:, 128 * h:128 * h + 128].rearrange("(mc m) d -> m mc d", m=128), kvst[:, 0, :, :], reads=[t_kvst])
                S.dma("sp", omv[:, 128 * h:128 * h + 128].rearrange("(mc m) d -> m mc d", m=128), kvst[:, 1, :, :], reads=[t_kvst])
                S.dma("pool", kc[:], ck[:, :, h, :].rearrange("s (mc m) d -> m s mc d", m=128), writes=[t_kc])
                S.dma("pool", vc[:], cv[:, :, h, :].rearrange("s (mc m) d -> m s mc d", m=128), writes=[t_vc])
                wq, tq_ = w_next()
                wg, tg_ = w_next()
                for (c0, n) in MEM_CH:
                    bk, tb = proj_fm(wq, tq_, c0, n)
                    ACT(mqT[:, c0 - 1024:c0 - 1024 + n], bk[:, 0:n], AF.Copy, [tb], [t_mq], scale=128.0 ** -0.5)
                    bk, tb = proj_fm(wg, tg_, c0, n)
                    ACT(mgT[:, c0 - 1024:c0 - 1024 + n], bk[:, 0:n], AF.Silu, [tb], [t_mg])
                for s in range(16):
                    if s % 4 == 0:
                        bk, tb = bank()
                        bkb = bk[:].bitcast(BF)
                    for mc in range(2):
                        o = (s % 4) * 256 + mc * 128
                        TR(bkb[:, o:o + 128], kc[:, s, mc, :], ident_b, [t_kc, t_cb], [tb])
                    if s % 4 == 3:
                        CP("act", kcT[:, s - 3:s + 1, :], bkb[:, 0:1024].rearrange("p (s m) -> p s m", m=256), [tb], [t_kct])
                TT("dve", mqm[:], mqT[:, 1024:1152].unsqueeze(1).broadcast_to([128, 16, 128]),
                   cb[:, CB_COLM:CB_COLM + 2048].rearrange("p (s i) -> p s i", i=128), ALU.mult, [t_mq, t_cb], [t_mqm])
                for ti in range(9):
                    m0 = 128 * ti
                    bk, tb = bank()
                    if ti < 8:
                        MM(bk[:, 0:256], mqT[:, m0:m0 + 128], KTp[:], True, True, [t_mq, t_ktp], [tb])
                    else:
                        for s in range(16):
                            MM(bk[:, 0:256], mqm[:, s, :], kcT[:, s, :], s == 0, s == 15, [t_mqm, t_kct], [tb])
                    S.op("dve", lambda e, bk=bk: e.reduce_max(out=mx[:, 0:1], in_=bk[:, 0:256], axis=AX.X), [tb], [t_mx])
                    TS("dve", mx[:, 1:2], mx[:, 0:1], -1.0, ALU.mult, [t_mx], [t_mx])
                    ACT(pf[:], bk[:, 0:256], AF.Exp, [tb, t_mx], [t_pf, t_mx], bias=mx[:, 1:2], accum=mx[:, 2:3])
                    S.op("dve", lambda e: e.reciprocal(out=mx[:, 3:4], in_=mx[:, 2:3]), [t_mx], [t_mx])
                    TS("dve", pn[:], pf[:], mx[:, 3:4], ALU.mult, [t_pf, t_mx], [t_pn])
                    bk2, tb2 = bank()
                    bk2b = bk2[:].bitcast(BF)
                    for mc in range(2):
                        TR(bk2b[:, 128 * mc:128 * mc + 128], pn[:, 128 * mc:128 * mc + 128], ident_b, [t_pn, t_cb], [tb2])
                    CP("act", pT[:].rearrange("p a b -> p (a b)"), bk2b[:, 0:256], [tb2], [t_pT])
                    bk3, tb3 = bank()
                    if ti < 8:
                        for mc in range(2):
                            MM(bk3[:, 0:128], Vp[:, mc, :], pT[:, mc, :], mc == 0, mc == 1, [t_vp, t_pT], [tb3])
                    else:
                        for s in range(16):
                            for mc in range(2):
                                MM(bk3[:, 8 * s:8 * s + 8], vc[:, s, mc, :], pT[:, mc, 8 * s:8 * s + 8], mc == 0, mc == 1, [t_vc, t_pT], [tb3])
                    TT("dve", mixT[:, 12 + h, m0:m0 + 128], bk3[:, 0:128], mgT[:, m0:m0 + 128], ALU.mult, [tb3, t_mg], [t_mix[12 + h]])
            S.barrier()

        KV_CH = [(0, 512), (512, 512), (1024, 512), (1536, 512), (2048, 128)]
        with ExitStack() as ph, contextlib.suppress(_Skip):
            if STAGE < 4:
                raise _Skip()
            kT, t_kT = sb(ph, [128, NCOL], BF)
            vT, t_vT = sb(ph, [128, NCOL], BF)
            qT, t_qT = sb(ph, [128, NM], BF)
            gT, t_gT = sb(ph, [128, NM], BF)
            pre = [sb(ph, [128, 515]) for _ in range(2)]
            pres, t_pres = sb(ph, [128, 16, 11])
            sqb, t_sqb = sb(ph, [128, 512], BF)
            acc, t_acc = sb(ph, [128, 512])
            lnb_, t_lnb = acc, t_acc
            sq48, t_sq48 = sb(ph, [48, 128])
            qst, t_qst = sb(ph, [128, 3])
            qst_s, t_qsts = sb(ph, [128, 48])
            ost, t_ost = sb(ph, [48, 128])
            Sf, t_Sf = sb(ph, [128, 128])
            Sb, t_Sb = sb(ph, [128, 128], BF)
            Ss, t_Ss = sb(ph, [128, 16, 128])
            Ssb, t_Ssb = sb(ph, [128, 16, 128], BF)
            Sso, t_Sso = Ss, t_Ss
            kTm, t_kTm = sb(ph, [128, 16, 128], BF)
            qTm, t_qTm = sb(ph, [128, 16, 128], BF)
            kdm, t_kdm = qTm, t_qTm
            GT = 4
            gB3 = [sb(ph, [128, GT, 128], BF) for _ in range(3)]
            decT, t_decT = sb(ph, [128, GT, 128])
            decTs, t_decTs = sb(ph, [128, GT, 128], BF)
            P0, t_P0 = sb(ph, [128, GT, 128], BF)
            Pb = [sb(ph, [128, GT, 128], BF) for _ in range(2)]
            Qb = [sb(ph, [128, GT, 128], BF) for _ in range(2)]
            Yb = [sb(ph, [128, GT, 128], BF) for _ in range(2)]
            XTb = [sb(ph, [128, GT, 128], BF) for _ in range(2)]
            aqkb = [sb(ph, [128, GT, 128], BF) for _ in range(2)]
            kdb = [sb(ph, [128, GT, 128], BF) for _ in range(2)]
            vtb = [sb(ph, [128, GT, 128], BF) for _ in range(2)]
            rbf, t_rbf = sb(ph, [128, 128], BF)
            ubf, t_ubf = sb(ph, [128, 128], BF)
            tsb, t_tsb = sb(ph, [128, 128])
            osb, t_osb = sb(ph, [128, 128])
            junk, t_junk = sb(ph, [128, 128], BF)
            onb, t_onb = sb(ph, [128, 128], BF)
            sm4, t_sm4 = sb(ph, [128, 4])


            def conv_stream(h, kind, wt, wtk, chunks, dstT, t_dst, dcol0, norm, qscale):
                fo = kind * 1024 + 128 * h
                fc = fo // 128
                qw = lambda j: pv[:, PV_QW + 4 * fc + j:PV_QW + 4 * fc + j + 1]
                S.dma("sp", sq48[:], sqkv[:, :, fo:fo + 128].rearrange("s r n -> (s r) n"), writes=[t_sq48])
                bk, tb = bank()
                TR(bk[:, 0:48], sq48[0:48, 0:128], ident_f[0:48, 0:48], [t_sq48, t_cf], [tb])
                CP("act", pres[:, :, 0:3], bk[:, 0:48].rearrange("p (s r) -> p s r", r=3), [tb], [t_pres])
                pi = 0
                S.op("pool", lambda e, p=pre[0][0]: e.memset(p[:, 0:3], 0.0), [], [pre[0][1]])
                for (c0, n) in chunks:
                    bk, tb = proj_fm(wt, wtk, c0, n)
                    npr = n if c0 + n <= 2048 else n - 128
                    skip = max(0, dcol0 - c0)
                    if npr > 0:
                        p_, tp_ = pre[pi]
                        ACT(p_[:, 3:3 + npr], bk[:, 0:npr], AF.Copy, [tb], [tp_])
                        TS("dve", acc[:, 0:npr], p_[:, 0:npr], qw(0), ALU.mult, [tp_, t_pv], [t_acc])
                        for j in range(1, 4):
                            STT(acc[:, 0:npr], p_[:, j:j + npr], qw(j), acc[:, 0:npr], ALU.mult, ALU.add, [tp_, t_pv, t_acc], [t_acc])
                        d0 = c0 - dcol0
                        ACT(dstT[:, d0 + skip:d0 + npr], acc[:, skip:npr], AF.Silu, [t_acc], [t_dst])
                        if c0 + npr == 2048:
                            CP("act", qst[:, 0:3], p_[:, npr:npr + 3], [tp_], [t_qst])
                        else:
                            p2, tp2 = pre[1 - pi]
                            CP("act", p2[:, 0:3], p_[:, npr:npr + 3], [tp_], [tp2])
                            pi = 1 - pi
                    if c0 + n > 2048:
                        CP("act", pres[:, :, 3:11], bk[:, npr:npr + 128].rearrange("p (s l) -> p s l", l=8), [tb], [t_pres])
                        av = acc[:, 0:128].rearrange("p (s l) -> p s l", l=8)
                        TS("dve", av, pres[:, :, 0:8], qw(0), ALU.mult, [t_pres, t_pv], [t_acc])
                        for j in range(1, 4):
                            STT(av, pres[:, :, j:j + 8], qw(j), av, ALU.mult, ALU.add, [t_pres, t_pv, t_acc], [t_acc])
                        d0 = 2048 - dcol0
                        ACT(dstT[:, d0:d0 + 128], acc[:, 0:128], AF.Silu, [t_acc], [t_dst])
                        CP("act", qst_s[:].rearrange("p (s r) -> p s r", r=3), pres[:, :, 8:11], [t_pres], [t_qsts])
                bk, tb = bank()
                TR(bk[0:3, 0:128], qst[:, 0:3], ident_f, [t_qst, t_cf], [tb])
                TR(bk[0:48, 128:256], qst_s[:, 0:48], ident_f, [t_qsts, t_cf], [tb])
                CP("act", ost[0:3, :], bk[0:3, 0:128], [tb], [t_ost])
                S.dma("sp", oqkv_p[:, fo:fo + 128], ost[0:3, :], reads=[t_ost])
                CP("act", ost[0:48, :], bk[0:48, 128:256], [tb], [t_ost])
                S.dma("sp", oqkv_s[:, :, fo:fo + 128].rearrange("s r n -> (s r) n"), ost[0:48, :], reads=[t_ost])
                if norm:
                    ntot = dstT.shape[1]
                    for m0 in range(0, ntot, 512):
                        n = min(512, ntot - m0)
                        ACT(sqb[:, 0:n], dstT[:, m0:m0 + n], AF.Square, [t_dst], [t_sqb])
                        bk, tb = bank()
                        MM(bk[:, 0:n], ones_b, sqb[:, 0:n], True, True, [t_cb, t_sqb], [tb])
                        TS("dve", lnb_[:, 0:n], bk[:, 0:n], 1e-6, ALU.add, [tb], [t_lnb])
                        ACT(lnb_[:, 0:n], lnb_[:, 0:n], AF.Ln, [t_lnb], [t_lnb])
                        ACT(lnb_[:, 0:n], lnb_[:, 0:n], AF.Exp, [t_lnb], [t_lnb], scale=-0.5)
                        if qscale != 1.0:
                            TS("dve", lnb_[:, 0:n], lnb_[:, 0:n], qscale, ALU.mult, [t_lnb], [t_lnb])
                        TT("dve", dstT[:, m0:m0 + n], dstT[:, m0:m0 + n], lnb_[:, 0:n], ALU.mult, [t_dst, t_lnb], [t_dst])

            def groupA(h, tiles, par):
                ng = len(tiles)
                i0 = tiles[0]
                samp = (i0 == 16)
                full = (i0 >= 8)
                L = 2 if samp else 6
                trib = cb[:, CB_TRIS:CB_TRIS + 128] if samp else cb[:, CB_TRIP:CB_TRIP + 128]
                ntrib = cb[:, CB_NTRS:CB_NTRS + 128] if samp else cb[:, CB_NTRP:CB_NTRP + 128]
                negb = cb[:, CB_NEGS:CB_NEGS + 128] if samp else cb[:, CB_NEGP:CB_NEGP + 128]
                strict = cb[:, CB_STRS:CB_STRS + 128] if samp else cb[:, CB_STRP:CB_STRP + 128]
                W = 128 * ng
                kd, t_kd = kdb[par]
                vt, t_vt = vtb[par]
                aq, t_aq = aqkb[par]
                XT, t_XT = XTb[par]
                v3 = lambda ap: ap.rearrange("p (t n) -> p t n", n=128)
                bk, tb = bank()
                bkb = bk[:].bitcast(BF)
                for t in range(ng):
                    c0 = 128 * (i0 + t)
                    TR(bkb[:, 256 * t:256 * t + 128], kT[:, c0:c0 + 128], ident_b, [t_kT, t_cb], [tb])
                    TR(bkb[:, 256 * t + 128:256 * t + 256], vT[:, c0:c0 + 128], ident_b, [t_vT, t_cb], [tb])
                for t in range(ng):
                    ACT(kd[:, t, :], bkb[:, 256 * t:256 * t + 128], AF.Copy, [tb, t_sm], [t_kd], scale=egl[:, i0 + t, h:h + 1])
                CP("dve", vt[:, 0:ng, :], bkb[:, 0:256 * ng].rearrange("p (t two n) -> p t two n", two=2, n=128)[:, :, 1, :], [tb], [t_vt])
                yield
                for j in range(3):
                    TT("pool", gB3[j][0][:, 0:ng, :], ones_b.unsqueeze(1).broadcast_to([128, ng, 128]),
                       g3[j][:, i0:i0 + ng, h:h + 1].broadcast_to([128, ng, 128]), ALU.mult, [t_cb, t_sm], [gB3[j][1]])
                bk, tb = bank()
                for t in range(ng):
                    o = bk[:, 128 * t:128 * t + 128]
                    for j in range(3):
                        MM(o, gB3[j][0][:, t, :], trib, j == 0, False, [gB3[j][1], t_cb], [tb])
                    for j in range(3):
                        MM(o, ntrib, gB3[j][0][:, t, :], False, False, [gB3[j][1], t_cb], [tb])
                    MM(o, ident_b, negb, False, True, [t_cb], [tb])
                ACT(decT[:, 0:ng, :], v3(bk[:, 0:W]), AF.Exp, [tb], [t_decT])
                TT("pool", decTs[:, 0:ng, :], decT[:, 0:ng, :], strict.unsqueeze(1).broadcast_to([128, ng, 128]), ALU.mult, [t_decT, t_cb], [t_decTs])
                yield
                bkG, tbG = bank()
                if full:
                    bkA, tbA = bank()
                for t in range(ng):
                    c0 = 128 * (i0 + t)
                    MM(bkG[:, 128 * t:128 * t + 128], kT[:, c0:c0 + 128], kT[:, c0:c0 + 128], True, True, [t_kT], [tbG])
                    if full:
                        MM(bkA[:, 128 * t:128 * t + 128], kT[:, c0:c0 + 128], qT[:, c0 - 1024:c0 - 1024 + 128], True, True, [t_kT, t_qT], [tbA])
                for t in range(ng):
                    STT(P0[:, t, :], bkG[:, 128 * t:128 * t + 128], nbt[:, i0 + t, h:h + 1], decTs[:, t, :], ALU.mult, ALU.mult,
                        [tbG, t_sm, t_decTs], [t_P0])
                if full:
                    TT("dve", aq[:, 0:ng, :], v3(bkA[:, 0:W]), decT[:, 0:ng, :], ALU.mult, [tbA, t_decT], [t_aq])
                Y0, t_Y0 = Yb[0]
                TT("pool", Y0[:, 0:ng, :], P0[:, 0:ng, :], ident_b.unsqueeze(1).broadcast_to([128, ng, 128]), ALU.add, [t_P0, t_cb], [t_Y0])
                yield
                bk, tb = bank()
                bkb = bk[:].bitcast(BF)
                for t in range(ng):
                    TR(bkb[:, 128 * t:128 * t + 128], P0[:, t, :], ident_b, [t_P0, t_cb], [tb])
                Q0, t_Q0 = Qb[0]
                CP("act", Q0[:, 0:ng, :], v3(bkb[:, 0:W]), [tb], [t_Q0])
                yield
                bkP, tbP = bank()
                bkQ, tbQ = bank()
                for t in range(ng):
                    MM(bkP[:, 128 * t:128 * t + 128], Q0[:, t, :], P0[:, t, :], True, True, [t_Q0, t_P0], [tbP])
                    MM(bkQ[:, 128 * t:128 * t + 128], P0[:, t, :], Q0[:, t, :], True, True, [t_Q0, t_P0], [tbQ])
                CP("act", Pb[0][0][:, 0:ng, :], v3(bkP[:, 0:W]), [tbP], [Pb[0][1]])
                CP("dve", Qb[1][0][:, 0:ng, :], v3(bkQ[:, 0:W]), [tbQ], [Qb[1][1]])
                yield
                pc, qc, yc = 0, 1, 0
                for k in range(1, L + 1):
                    P, tP = Pb[pc]
                    Q, tQ = Qb[qc]
                    Y, tY = Yb[yc]
                    Pn, tPn = Pb[1 - pc]
                    Qn, tQn = Qb[1 - qc]
                    Yn, tYn = Yb[1 - yc] if k < L else (XT, t_XT)
                    if k < L:
                        bkP, tbP = bank()
                        bkQ, tbQ = bank()
                    bkY, tbY = bank()
                    for t in range(ng):
                        if k < L:
                            MM(bkP[:, 128 * t:128 * t + 128], Q[:, t, :], P[:, t, :], True, True, [tQ, tP], [tbP])
                            MM(bkQ[:, 128 * t:128 * t + 128], P[:, t, :], Q[:, t, :], True, True, [tQ, tP], [tbQ])
                        MM(bkY[:, 128 * t:128 * t + 128], Q[:, t, :], Y[:, t, :], True, True, [tQ, tY], [tbY])
                    if k < L:
                        CP("act", Pn[:, 0:ng, :], v3(bkP[:, 0:W]), [tbP], [tPn])
                        CP("act" if k % 2 == 0 else "dve", Qn[:, 0:ng, :], v3(bkQ[:, 0:W]), [tbQ], [tQn])
                    TT("dve", Yn[:, 0:ng, :], v3(bkY[:, 0:W]), Y[:, 0:ng, :], ALU.add, [tbY, tY], [tYn])
                    pc, qc, yc = 1 - pc, 1 - qc, 1 - yc
                    yield

            def groupB(h, tiles, par):
                kd, t_kd = kdb[par]
                vt, t_vt = vtb[par]
                aq, t_aq = aqkb[par]
                XT, t_XT = XTb[par]
                for t, i in enumerate(tiles):
                    samp = (i == 16)
                    full = (i >= 8)
                    c0 = 128 * i
                    mc0 = c0 - 1024
                    sc = lambda arr: arr[:, i, h:h + 1]
                    bk, tb = bank()
                    if not samp:
                        MM(bk[:, 0:128], kT[:, c0:c0 + 128], Sb[:], True, True, [t_kT, t_Sb], [tb])
                        if full:
                            MM(bk[:, 128:256], qT[:, mc0:mc0 + 128], Sb[:], True, True, [t_qT, t_Sb], [tb])
                    else:
                        for s_ in range(16):
                            MM(bk[:, 0:128], kTm[:, s_, :], Ssb[:, s_, :], s_ == 0, s_ == 15, [t_kTm, t_Ssb], [tb])
                        for s_ in range(16):
                            MM(bk[:, 128:256], qTm[:, s_, :], Ssb[:, s_, :], s_ == 0, s_ == 15, [t_qTm, t_Ssb], [tb])
                    STT(rbf[:], bk[:, 0:128], sc(nee), vt[:, t, :], ALU.mult, ALU.add, [tb, t_sm, t_vt], [t_rbf])
                    if full:
                        ACT(tsb[:], bk[:, 128:256], AF.Copy, [tb, t_sm], [t_tsb], scale=sc(ee))
                    yield
                    bku, tbu = bank()
                    MM(bku[:, 0:128], XT[:, t, :], rbf[:], True, True, [t_XT, t_rbf], [tbu])
                    ACT(ubf[:], bku[:, 0:128], AF.Copy, [tbu, t_sm], [t_ubf], scale=sc(bt))
                    yield
                    if not samp:
                        bks, tbs = bank()
                        MM(bks[:, 0:128], kd[:, t, :], ubf[:], True, True, [t_kd, t_ubf], [tbs])
                        STT(Sf[:], Sf[:], sc(glb), bks[:, 0:128], ALU.mult, ALU.add, [t_Sf, t_sm, tbs], [t_Sf])
                        CP("act", Sb[:], Sf[:], [t_Sf], [t_Sb])
                        yield
                    else:
                        TT("dve", kdm[:], kd[:, t, :].unsqueeze(1).broadcast_to([128, 16, 128]),
                           cf[:, CF_ROWM:CF_ROWM + 16].unsqueeze(2).broadcast_to([128, 16, 128]), ALU.mult, [t_kd, t_cf], [t_kdm])
                        for s_ in range(16):
                            if s_ % 4 == 0:
                                bks, tbs = bank()
                            MM(bks[:, 128 * (s_ % 4):128 * (s_ % 4) + 128], kdm[:, s_, :], ubf[:], True, True, [t_kdm, t_ubf], [tbs])
                            if s_ % 4 == 3:
                                for s2 in range(s_ - 3, s_ + 1):
                                    STT(Sso[:, s2, :], Ss[:, s2, :], glbs[:, s2, h:h + 1], bks[:, 128 * (s2 % 4):128 * (s2 % 4) + 128],
                                        ALU.mult, ALU.add, [t_Ss, t_sm, tbs], [t_Ss])
                                yield
                    if full:
                        bk4, tb4 = bank()
                        MM(bk4[:, 0:128], aq[:, t, :], ubf[:], True, True, [t_aq, t_ubf], [tb4])
                        TT("dve", osb[:], bk4[:, 0:128], tsb[:], ALU.add, [tb4, t_tsb], [t_osb])
                        ACT(junk[:], osb[:], AF.Square, [t_osb], [t_junk, t_sm4], accum=sm4[:, 0:1])
                        yield
                        TS("dve", sm4[:, 1:2], sm4[:, 0:1], 1.0 / 128, ALU.mult, [t_sm4], [t_sm4], s2=1e-6, op1=ALU.add)
                        ACT(sm4[:, 2:3], sm4[:, 1:2], AF.Ln, [t_sm4], [t_sm4])
                        ACT(sm4[:, 3:4], sm4[:, 2:3], AF.Exp, [t_sm4], [t_sm4], scale=-0.5)
                        TS("dve", onb[:], osb[:], sm4[:, 3:4], ALU.mult, [t_osb, t_sm4], [t_onb])
                        yield
                        bko, tbo = bank()
                        bkob = bko[:].bitcast(BF)
                        TR(bkob[:, 0:128], onb[:], ident_b, [t_onb, t_cb], [tbo])
                        STT(mixT[:, 4 + h, mc0:mc0 + 128], bkob[:, 0:128], pv[:, PV_NW:PV_NW + 1], gT[:, mc0:mc0 + 128],
                            ALU.mult, ALU.mult, [tbo, t_pv, t_gT], [t_mix[4 + h]])
                        yield
                    if i == 15:
                        S.dma("sp", odelta_p[h, :, :], Sf[:], reads=[t_Sf])

            def run_il(gens, weights):
                gens = [g for g in gens if g is not None]
                alive = [True] * len(gens)
                while any(alive):
                    for gi, g in enumerate(gens):
                        if not alive[gi]:
                            continue
                        for _ in range(weights[gi]):
                            try:
                                next(g)
                            except StopIteration:
                                alive[gi] = False
                                break

            GROUPS = [[0, 1, 2, 3], [4, 5, 6, 7], [8, 9, 10, 11], [12, 13, 14, 15], [16]]

            for h in range(8):
                S.dma("sp", Ss[:], sdelta[:, h, :, :].rearrange("s k v -> k s v"), writes=[t_Ss])
                CP("pool", Ssb[:], Ss[:], [t_Ss], [t_Ssb])
                S.op("pool", lambda e: e.memset(Sf[:], 0.0), [], [t_Sf])
                S.op("pool", lambda e: e.memset(Sb[:], 0.0), [], [t_Sb])
                wk, tk_ = w_next()
                conv_stream(h, 1, wk, tk_, KV_CH, kT, t_kT, 0, True, 1.0)
                wv, tv_ = w_next()
                conv_stream(h, 2, wv, tv_, KV_CH, vT, t_vT, 0, False, 1.0)
                wq, tq_ = w_next()
                conv_stream(h, 0, wq, tq_, OWN_CH, qT, t_qT, 1024, True, 128.0 ** -0.5)
                wg, tg_ = w_next()
                for (c0, n) in MEM_CH:
                    bk, tb = proj_fm(wg, tg_, c0, n)
                    ACT(gT[:, c0 - 1024:c0 - 1024 + n], bk[:, 0:n], AF.Silu, [tb], [t_gT])
                colm = cb[:, CB_COLM:CB_COLM + 2048].rearrange("p (s i) -> p s i", i=128)
                TT("dve", kTm[:], kT[:, 2048:2176].unsqueeze(1).broadcast_to([128, 16, 128]), colm, ALU.mult, [t_kT, t_cb], [t_kTm])
                TT("pool", qTm[:], qT[:, 1024:1152].unsqueeze(1).broadcast_to([128, 16, 128]), colm, ALU.mult, [t_qT, t_cb], [t_qTm])
                prevB = None
                for gi, tiles in enumerate(GROUPS):
                    ga = groupA(h, tiles, gi % 2)
                    if prevB is None:
                        run_il([ga], [1])
                    else:
                        run_il([ga, prevB], [1, 2])
                    prevB = groupB(h, tiles, gi % 2)
                run_il([prevB], [1])
                S.dma("sp", odelta_s[:, h, :, :].rearrange("s k v -> k s v"), Sso[:], reads=[t_Sso])
            S.barrier()

        es2.close()
        with ExitStack() as ph, contextlib.suppress(_Skip):
            if STAGE < 5:
                raise _Skip()
            wo, t_wo = sb(ph, [128, 16, 2048], BF)
            lngb, t_lngb = sb(ph, [128, 2, 2048])
            xr = [sb(ph, [128, 2048]) for _ in range(2)]
            yp, t_yp = sb(ph, [128, 2048])
            yo = [sb(ph, [128, 2048]) for _ in range(2)]
            jk, t_jk = sb(ph, [128, 2048])
            st, t_st = sb(ph, [128, 8])
            for k4 in range(16):
                S.dma("pool", wo[:, k4, :].rearrange("p (a n) -> p a n", n=512),
                      w_out[128 * k4:128 * k4 + 128, :].rearrange("p (a n) -> p a n", n=512), writes=[t_wo])
            S.dma("sp", lngb[:], lngb_d[:, :, :], writes=[t_lngb])
            for ti in range(9):
                xr_, txr = xr[ti % 2]
                yo_, tyo = yo[ti % 2]
                S.dma("sp", xr_[:], xall[1024 + 128 * ti:1024 + 128 * ti + 128, :], writes=[txr])
                for nb in range(4):
                    bk, tb = bank()
                    for k in range(16):
                        MM(bk[:, 0:512], mixT[:, k, 128 * ti:128 * ti + 128], wo[:, k, 512 * nb:512 * nb + 512], k == 0, k == 15, [t_mix[k], t_wo], [tb])
                    STT(yp[:, 512 * nb:512 * nb + 512], xr_[:, 512 * nb:512 * nb + 512], ALPHA, bk[:, 0:512], ALU.mult, ALU.add, [txr, tb], [t_yp])
                S.op("dve", lambda e: e.reduce_sum(out=st[:, 0:1], in_=yp[:], axis=AX.X), [t_yp], [t_st])
                ACT(jk[:], yp[:], AF.Square, [t_yp], [t_jk, t_st], accum=st[:, 1:2])
                TS("dve", st[:, 2:3], st[:, 0:1], 1.0 / 2048, ALU.mult, [t_st], [t_st])
                TT("dve", st[:, 3:4], st[:, 2:3], st[:, 2:3], ALU.mult, [t_st], [t_st])
                STT(st[:, 4:5], st[:, 1:2], 1.0 / 2048, st[:, 3:4], ALU.mult, ALU.subtract, [t_st], [t_st])
                TS("dve", st[:, 4:5], st[:, 4:5], 1e-5, ALU.add, [t_st], [t_st])
                ACT(st[:, 5:6], st[:, 4:5], AF.Ln, [t_st], [t_st])
                ACT(st[:, 6:7], st[:, 5:6], AF.Exp, [t_st], [t_st], scale=-0.5)
                STT(st[:, 7:8], st[:, 2:3], -1.0, st[:, 6:7], ALU.mult, ALU.mult, [t_st], [t_st])
                ACT(yp[:], yp[:], AF.Identity, [t_yp, t_st], [t_yp], scale=st[:, 6:7], bias=st[:, 7:8])
                TT("dve", yp[:], yp[:], lngb[:, 0, :], ALU.mult, [t_yp, t_lngb], [t_yp])
                TT("pool", yo_[:], yp[:], lngb[:, 1, :], ALU.add, [t_yp, t_lngb], [tyo])
                S.dma("sp", y_d[128 * ti:128 * ti + 128, :], yo_[:], reads=[tyo])
            S.barrier()

        with nc.Block() as block:
            @block.tensor
            def _(e):
                S.replay("pe", e)

            @block.scalar
            def _(e):
                S.replay("act", e)

            @block.vector
            def _(e):
                S.replay("dve", e)

            @block.gpsimd
            def _(e):
                S.replay("pool", e)

            @block.sync
            def _(e):
                S.replay("sp", e)
        build_nc.stats = dict(S.cnt)
    return nc


def _consts():
    idx = np.arange(128)
    blk = idx // 8
    cf = np.zeros((128, NCF), np.float32)
    cf[:, CF_ID:CF_ID + 128] = np.eye(128)
    cf[:, CF_ONE:CF_ONE + 128] = 1.0
    t, i = idx[:, None], idx[None, :]
    cf[:, CF_TRIP:CF_TRIP + 128] = (t <= i)
    cf[:, CF_NEGP:CF_NEGP + 128] = np.where(i >= t, 0.0, NEG)
    same = (blk[:, None] == blk[None, :])
    cf[:, CF_TRIS:CF_TRIS + 128] = (t <= i) & same
    cf[:, CF_BLKS:CF_BLKS + 128] = same
    cf[:, CF_NEGS:CF_NEGS + 128] = np.where((i >= t) & same, 0.0, NEG)
    cf[:, CF_ROWM:CF_ROWM + 16] = (blk[:, None] == np.arange(16)[None, :])
    cb = np.zeros((128, NCB), np.float32)
    cb[:, CB_ID:CB_ID + 128] = np.eye(128)
    cb[:, CB_ONE:CB_ONE + 128] = 1.0
    cb[:, CB_STRP:CB_STRP + 128] = (i > t)
    cb[:, CB_STRS:CB_STRS + 128] = (i > t) & same
    colm = np.zeros((128, 16, 128), np.float32)
    for s in range(16):
        colm[:, s, 8 * s:8 * s + 8] = 1.0
    cb[:, CB_COLM:CB_COLM + 2048] = colm.reshape(128, 2048)
    cb[:, CB_TRIP:CB_TRIP + 128] = cf[:, CF_TRIP:CF_TRIP + 128]
    cb[:, CB_TRIS:CB_TRIS + 128] = cf[:, CF_TRIS:CF_TRIS + 128]
    cb[:, CB_BLKS:CB_BLKS + 128] = cf[:, CF_BLKS:CF_BLKS + 128]
    cb[:, CB_NEGP:CB_NEGP + 128] = cf[:, CF_NEGP:CF_NEGP + 128]
    cb[:, CB_NEGS:CB_NEGS + 128] = cf[:, CF_NEGS:CF_NEGS + 128]
    cb[:, CB_NTRP:CB_NTRP + 128] = -cf[:, CF_TRIP:CF_TRIP + 128]
    cb[:, CB_NTRS:CB_NTRS + 128] = -cf[:, CF_TRIS:CF_TRIS + 128]
    return cf, cb.astype(ml_dtypes.bfloat16)


_NC_CACHE = {}


def kernel(x_prompt, x_sample, mem_prompt, state_conv, state_qkv_conv, state_delta, cache_mem_k, cache_mem_v,
           w_in, conv_w, conv_b, conv_ln_g, conv_ln_b, qkv_conv_w, a_log, dt_bias, delta_norm_w,
           w_mem_k, w_mem_v, w_out, ln_g, ln_b):
    f = lambda a: np.ascontiguousarray(np.asarray(a, dtype=np.float32))
    x_prompt, x_sample, mem_prompt = f(x_prompt), f(x_sample), f(mem_prompt)
    cf, cb = _consts()
    pv = np.zeros((128, NPV), np.float32)
    cwT = f(conv_w)[0].T.reshape(4, 128, 31)
    pv[:, PV_CW:PV_CW + 124] = cwT.transpose(1, 0, 2).reshape(128, 124)
    pv[:, PV_CB:PV_CB + 4] = f(conv_b)[0].reshape(4, 128).T
    pv[:, PV_CLG:PV_CLG + 4] = f(conv_ln_g)[0].reshape(4, 128).T
    pv[:, PV_CLB:PV_CLB + 4] = f(conv_ln_b)[0].reshape(4, 128).T
    qwT = f(qkv_conv_w)[0].T.reshape(24, 128, 4)
    pv[:, PV_QW:PV_QW + 96] = qwT.transpose(1, 0, 2).reshape(128, 96)
    pv[:, PV_NW] = f(delta_norm_w)[0]
    pv[:, PV_AL:PV_AL + 8] = f(a_log)[0][None, :]
    pv[:, PV_DT:PV_DT + 8] = f(dt_bias)[0][None, :]
    lngb = np.ascontiguousarray(np.broadcast_to(np.stack([f(ln_g)[0], f(ln_b)[0]])[None], (128, 2, 2048)))
    W_in, W_mk, W_mv, W_out = f(w_in)[0], f(w_mem_k)[0], f(w_mem_v)[0], f(w_out)[0]
    sc, sq, sd, ck, cv = f(state_conv)[0], f(state_qkv_conv)[0], f(state_delta)[0], f(cache_mem_k)[0], f(cache_mem_v)[0]
    in_maps = []
    for c in range(8):
        b, hf = c // 2, c % 2
        xall = np.zeros((NCOL, 2048), np.float32)
        if hf == 1:
            xall[0:1024] = x_prompt[b, 0:1024]
        xall[1024:2048] = x_prompt[b, 1024 * hf:1024 * hf + 1024]
        xall[2048:] = x_sample[16 * c:16 * c + 16].reshape(128, 2048)
        sl = slice(16 * c, 16 * c + 16)
        in_maps.append({
            "xall": xall, "memp": mem_prompt[b], "sconv": sc[sl], "sqkv": sq[sl], "sdelta": sd[sl],
            "ck": ck[sl], "cv": cv[sl], "w_in": W_in, "w_mk": W_mk, "w_mv": W_mv, "w_out": W_out,
            "cf": cf, "cb": cb, "pv": pv, "lngb": lngb,
        })
    if os.environ.get("K_CORES"):
        ncores = int(os.environ["K_CORES"])
        nc = build_nc()
        res = run_bass_kernel_spmd(nc, in_maps[:ncores], core_ids=list(range(ncores)), trace=bool(os.environ.get("K_TRACE")))
        kernel.last = res
        R = list(res.results) + [res.results[0]] * (8 - ncores)
    else:
        if "nc" not in _NC_CACHE:
            _NC_CACHE["nc"] = build_nc()
        nc = _NC_CACHE["nc"]
        res = run_bass_kernel_spmd(nc, in_maps, core_ids=list(range(8)))
        R = res.results
    y_p = np.zeros((4, 2048, 2048), np.float32)
    y_s = np.zeros((128, 8, 2048), np.float32)
    o_conv_p = np.zeros((1, 4, 30, 512), np.float32)
    o_qkv_p = np.zeros((1, 4, 3, 3072), np.float32)
    o_delta_p = np.zeros((1, 4, 8, 128, 128), np.float32)
    o_mk = np.zeros((1, 4, 256, 4, 128), np.float32)
    o_mv = np.zeros((1, 4, 256, 4, 128), np.float32)
    o_conv_s = np.zeros((1, 128, 30, 512), np.float32)
    o_qkv_s = np.zeros((1, 128, 3, 3072), np.float32)
    o_delta_s = np.zeros((1, 128, 8, 128, 128), np.float32)
    for c in range(8):
        b, hf = c // 2, c % 2
        r = R[c]
        y_p[b, 1024 * hf:1024 * hf + 1024] = r["y"][0:1024]
        y_s[16 * c:16 * c + 16] = r["y"][1024:1152].reshape(16, 8, 2048)
        sl = slice(16 * c, 16 * c + 16)
        o_conv_s[0, sl] = r["oconv_s"]
        o_qkv_s[0, sl] = r["oqkv_s"]
        o_delta_s[0, sl] = r["odelta_s"]
        if hf == 1:
            o_conv_p[0, b] = r["oconv_p"]
            o_qkv_p[0, b] = r["oqkv_p"]
            o_delta_p[0, b] = r["odelta_p"]
        else:
            o_mk[0, b] = r["omk"].reshape(256, 4, 128)
            o_mv[0, b] = r["omv"].reshape(256, 4, 128)
    return (y_p, y_s, o_conv_p, o_qkv_p, o_delta_p, o_mk, o_mv, o_conv_s, o_qkv_s, o_delta_s)
```

```python
import os
import contextlib
from contextlib import ExitStack
import numpy as np
import ml_dtypes
import concourse.bass as bass
import concourse.mybir as mybir
from concourse.bass_utils import run_bass_kernel_spmd

F32 = mybir.dt.float32
BF = mybir.dt.bfloat16
AF = mybir.ActivationFunctionType
ALU = mybir.AluOpType
AX = mybir.AxisListType

SAME_SYNC = not os.environ.get('K_NOSAME')
STAGE = int(os.environ.get('K_STAGE', '9'))
SUB = int(os.environ.get('K_SUB', '99'))
NT = 17
NCOL = NT * 128
NM = 1152
O_GA, O_GB, O_CG, O_Q, O_K, O_V, O_DG, O_BT, O_DC, O_MQ, O_MG = 0, 512, 1024, 1536, 2560, 3584, 4608, 5632, 5640, 5648, 6160
N_IN = 6672
ALPHA = 2.0 ** 0.25
NEG = -30000.0

CF_ID, CF_ONE, CF_TRIP, CF_NEGP, CF_TRIS, CF_BLKS, CF_NEGS, CF_ROWM = 0, 128, 256, 384, 512, 640, 768, 896
NCF = 912
CB_ID, CB_ONE, CB_STRP, CB_STRS, CB_COLM = 0, 128, 256, 384, 512
CB_TRIP, CB_TRIS, CB_BLKS, CB_NEGP, CB_NEGS, CB_NTRP, CB_NTRS = 2560, 2688, 2816, 2944, 3072, 3200, 3328
NCB = 3456
PV_CW, PV_CB, PV_CLG, PV_CLB, PV_QW, PV_NW, PV_AL, PV_DT = 0, 124, 128, 132, 136, 232, 233, 241
NPV = 249


class _Skip(Exception):
    pass


class Tok:
    __slots__ = ("w", "r", "excl")

    def __init__(self, excl=False):
        self.w = None
        self.r = {}
        self.excl = excl


class Sched:
    ENG = ("pe", "act", "dve", "pool", "sp")

    def __init__(self, nc, es, ndma=40):
        self.nc = nc
        self.sem = {e: es.enter_context(nc.semaphore("sem_" + e)) for e in self.ENG}
        self.cnt = {e: 0 for e in self.ENG}
        self.q = {e: [] for e in self.ENG}
        self.seen = {e: {} for e in self.ENG}
        self.dsem = [es.enter_context(nc.semaphore("dsem%d" % i)) for i in range(ndma)]
        self.dcnt = [0] * ndma
        self.dnext = 0

    def _deps(self, reads, writes):
        ev = []
        for t in reads:
            if t.w is not None:
                ev.append(t.w)
            if t.excl:
                ev.extend(t.r.values())
        for t in writes:
            if t.w is not None:
                ev.append(t.w)
            ev.extend(t.r.values())
        return ev

    def _waits(self, eng, evs):
        out = []
        for (key, val) in evs:
            if key == eng and (eng == "pe" or not SAME_SYNC):
                continue
            if self.seen[eng].get(key, 0) >= val:
                continue
            self.seen[eng][key] = val
            out.append((key, val))
        return out

    def _mark(self, ev, reads, writes):
        for t in writes:
            t.w = ev
            t.r = {}
        k = ev[0]
        for t in reads:
            if t.r.get(k, (k, 0))[1] < ev[1]:
                t.r[k] = ev

    def op(self, eng, fn, reads=(), writes=()):
        w = self._waits(eng, self._deps(reads, writes))
        self.cnt[eng] += 1
        ev = (eng, self.cnt[eng])
        self.q[eng].append(("op", fn, w, None))
        self._mark(ev, reads, writes)
        return ev

    def dma(self, q, out, in_, reads=(), writes=(), **kw):
        k = self.dnext
        self.dnext = (self.dnext + 1) % len(self.dsem)
        evs = self._deps(reads, writes)
        if self.dcnt[k] > 0:
            evs.append((("d", k), 16 * self.dcnt[k]))
        w = self._waits(q, evs)
        self.dcnt[k] += 1
        ev = (("d", k), 16 * self.dcnt[k])
        self.q[q].append(("dma", (out, in_, kw), w, k))
        self._mark(ev, reads, writes)
        return ev

    def barrier(self):
        evs = [(e, self.cnt[e]) for e in self.ENG if self.cnt[e] > 0]
        if not os.environ.get("K_NODMABAR"):
            evs += [(("d", k), 16 * c) for k, c in enumerate(self.dcnt) if c > 0]
        for e in self.ENG:
            w = self._waits(e, evs)
            if w:
                self.q[e].append(("wait", None, w, None))

    def semh(self, key):
        return self.sem[key] if isinstance(key, str) else self.dsem[key[1]]

    def replay(self, e, h):
        for kind, fn, waits, k in self.q[e]:
            for (key, val) in waits:
                h.wait_ge(self.semh(key), val)
            if kind == "op":
                fn(h).then_inc(self.sem[e], 1)
            elif kind == "dma":
                out, in_, kw = fn
                h.dma_start(out=out, in_=in_, **kw).then_inc(self.dsem[k], 16)


def build_nc(dbg=False):
    nc = bass.Bass("TRN2", target_bir_lowering=False)

    def din(name, shape, dt=F32):
        return nc.dram_tensor(name, list(shape), dt, kind="ExternalInput").ap()

    def dout(name, shape, dt=F32):
        return nc.dram_tensor(name, list(shape), dt, kind="ExternalOutput").ap()

    xall = din("xall", [NCOL, 2048])
    memp = din("memp", [256, 2048])
    sconv = din("sconv", [16, 30, 512])
    sqkv = din("sqkv", [16, 3, 3072])
    sdelta = din("sdelta", [16, 8, 128, 128])
    ck = din("ck", [16, 256, 4, 128])
    cv = din("cv", [16, 256, 4, 128])
    w_in = din("w_in", [2048, N_IN])
    w_mk = din("w_mk", [2048, 512])
    w_mv = din("w_mv", [2048, 512])
    w_out = din("w_out", [2048, 2048])
    cf_d = din("cf", [128, NCF])
    cb_d = din("cb", [128, NCB], BF)
    pv_d = din("pv", [128, NPV])
    lngb_d = din("lngb", [128, 2, 2048])

    y_d = dout("y", [NM, 2048])
    oconv_p = dout("oconv_p", [30, 512])
    oqkv_p = dout("oqkv_p", [3, 3072])
    odelta_p = dout("odelta_p", [8, 128, 128])
    omk = dout("omk", [256, 512])
    omv = dout("omv", [256, 512])
    oconv_s = dout("oconv_s", [16, 30, 512])
    oqkv_s = dout("oqkv_s", [16, 3, 3072])
    odelta_s = dout("odelta_s", [16, 8, 128, 128])

    with ExitStack() as es:
        S = Sched(nc, es)
        ctr = [0]

        def sb(es_, shape, dt=F32):
            ctr[0] += 1
            t = es_.enter_context(nc.sbuf_tensor("t%d" % ctr[0], list(shape), dt))
            return t, Tok()

        def ACT(out, in_, func, reads, writes, bias=None, scale=None, accum=None):
            kw = {}
            if bias is not None:
                kw["bias"] = bias
            if scale is not None:
                kw["scale"] = scale
            if accum is not None:
                kw["accum_out"] = accum
            return S.op("act", lambda e: e.activation(out=out, in_=in_, func=func, **kw), reads, writes)

        def TS(eng, out, in0, s1, op0, reads, writes, s2=None, op1=None):
            if op1 is None:
                return S.op(eng, lambda e: e.tensor_scalar(out=out, in0=in0, scalar1=s1, scalar2=None, op0=op0), reads, writes)
            return S.op(eng, lambda e: e.tensor_scalar(out=out, in0=in0, scalar1=s1, scalar2=s2, op0=op0, op1=op1), reads, writes)

        def TT(eng, out, in0, in1, op, reads, writes):
            return S.op(eng, lambda e: e.tensor_tensor(out=out, in0=in0, in1=in1, op=op), reads, writes)

        def STT(out, in0, scalar, in1, op0, op1, reads, writes):
            return S.op("dve", lambda e: e.scalar_tensor_tensor(out=out, in0=in0, scalar=scalar, in1=in1, op0=op0, op1=op1), reads, writes)

        def CP(eng, out, in_, reads, writes):
            if eng == "act":
                return S.op("act", lambda e: e.copy(out=out, in_=in_), reads, writes)
            return S.op(eng, lambda e: e.tensor_copy(out=out, in_=in_), reads, writes)

        def split3(src, tsrc, outs, touts, r_f32, t_r):
            CP("dve", outs[0], src, [tsrc], [touts])
            TT("dve", r_f32, src, outs[0], ALU.subtract, [tsrc, touts], [t_r])
            CP("dve", outs[1], r_f32, [t_r], [touts])
            TT("dve", outs[2], r_f32, outs[1], ALU.subtract, [t_r, touts], [touts])

        def MM(out, lhsT, rhs, start, stop, reads, writes):
            return S.op("pe", lambda e: e.matmul(out=out, lhsT=lhsT, rhs=rhs, start=start, stop=stop), reads, writes)

        def TR(out, in_, ident, reads, writes):
            return S.op("pe", lambda e: e.transpose(out=out, in_=in_, identity=ident), reads, writes)

        cf, t_cf = sb(es, [128, NCF])
        cb, t_cb = sb(es, [128, NCB], BF)
        pv, t_pv = sb(es, [128, NPV])
        mixT = es.enter_context(nc.sbuf_tensor("mixT", [128, 16, NM], BF))
        t_mix = [Tok() for _ in range(16)]
        NW = 4
        wsl = [sb(es, [128, 16, 128], BF) for _ in range(NW)]
        bt, t_sm = sb(es, [128, NT, 8])
        nbt, _ = sb(es, [128, NT, 8])
        gtk, _ = sb(es, [128, NT, 8])
        gc_, _ = sb(es, [128, NT, 8])
        ngc, _ = sb(es, [128, NT, 8])
        ee, _ = sb(es, [128, NT, 8])
        nee, _ = sb(es, [128, NT, 8])
        egl, _ = sb(es, [128, NT, 8])
        glb, _ = sb(es, [128, NT, 8])
        glbs, _ = sb(es, [128, 16, 8])
        nega, _ = sb(es, [128, 8])
        g3 = [sb(es, [128, NT, 8], BF)[0] for _ in range(3)]
        es2 = ExitStack()
        xT, t_xT = sb(es2, [128, 16, NCOL], BF)
        psb = [es.enter_context(nc.psum_tensor("ps%d" % i, [128, 512], F32)) for i in range(8)]
        t_ps = [Tok(excl=True) for _ in range(8)]
        pctr = [0]

        def bank():
            i = pctr[0] % 8
            pctr[0] += 1
            return psb[i], t_ps[i]

        ident_f = cf[:, CF_ID:CF_ID + 128]
        ones_f = cf[:, CF_ONE:CF_ONE + 128]
        ident_b = cb[:, CB_ID:CB_ID + 128]
        ones_b = cb[:, CB_ONE:CB_ONE + 128]

        S.dma("sp", cf[:], cf_d[:, :], writes=[t_cf])
        S.dma("sp", cb[:], cb_d[:, :], writes=[t_cb])
        S.dma("sp", pv[:], pv_d[:, :], writes=[t_pv])

        WL = []
        WL.append((w_in, O_BT, 16))
        for c in range(4):
            WL += [(w_in, O_GA + 128 * c, 128), (w_in, O_GB + 128 * c, 128), (w_in, O_CG + 128 * c, 128)]
        for h in range(4):
            WL += [(w_mk, 128 * h, 128), (w_mv, 128 * h, 128), (w_in, O_MQ + 128 * h, 128), (w_in, O_MG + 128 * h, 128)]
        for h in range(8):
            WL += [(w_in, O_K + 128 * h, 128), (w_in, O_V + 128 * h, 128), (w_in, O_Q + 128 * h, 128), (w_in, O_DG + 128 * h, 128)]
        wst = {"issued": 0, "used": 0}

        def w_issue():
            i = wst["issued"]
            if i >= len(WL):
                return
            src, c0, n = WL[i]
            t, tk = wsl[i % NW]
            S.dma("pool", t[:, :, 0:n], src[:, c0:c0 + n].rearrange("(c p) n -> p c n", p=128), writes=[tk])
            wst["issued"] += 1

        def w_next():
            i = wst["used"]
            while wst["issued"] < min(len(WL), i + NW - 1) or wst["issued"] <= i:
                w_issue()
            wst["used"] += 1
            return wsl[i % NW]

        def proj_fm(wt, wtk, c0, n, ncols=128):
            bk, tb = bank()
            for k in range(16):
                MM(bk[0:ncols, 0:n], wt[:, k, 0:ncols], xT[:, k, c0:c0 + n], k == 0, k == 15, [wtk, t_xT], [tb])
            return bk, tb

        with ExitStack() as ph, contextlib.suppress(_Skip):
            if STAGE < 0:
                raise _Skip()
            xs = [sb(ph, [128, 2048], BF) for _ in range(3)]
            for i in range(NT):
                t, tk = xs[i % 3]
                src = xall[128 * i:128 * i + 128, :]
                S.dma("pool", t[:].rearrange("p (a n) -> p a n", n=512), src.rearrange("p (a n) -> p a n", n=512), writes=[tk])
                for half in range(2):
                    bk, tb = bank()
                    bkb = bk[:].bitcast(BF)
                    for c in range(8):
                        TR(bkb[:, 128 * c:128 * c + 128], t[:, 128 * (8 * half + c):128 * (8 * half + c) + 128], ident_b, [tk, t_cb], [tb])
                    src_v = bkb[:, 0:1024].rearrange("p (c n) -> p c n", c=8)
                    dst = xT[:, 8 * half:8 * half + 8, 128 * i:128 * i + 128]
                    CP("act" if half == 0 else "dve", dst, src_v, [tb], [t_xT])
            S.barrier()

        with ExitStack() as ph, contextlib.suppress(_Skip):
            if STAGE < 1:
                raise _Skip()
            bd, t_bd = sb(ph, [128, NT, 16])
            tmp8, t_tmp8 = sb(ph, [128, NT, 8])
            gm, t_gm = sb(ph, [128, 16, 8], BF)
            wt, wtk = w_next()
            bk, tb = bank()
            for i in range(NT):
                for k in range(16):
                    MM(bk[:, 16 * i:16 * i + 16], xT[:, k, 128 * i:128 * i + 128], wt[:, k, 0:16], k == 0, k == 15, [wtk, t_xT], [tb])
            CP("dve", bd[:], bk[:, 0:16 * NT].rearrange("p (t c) -> p t c", c=16), [tb], [t_bd])
            if SUB < 1:
                raise _Skip()
            ACT(bt[:], bd[:, :, 0:8], AF.Sigmoid, [t_bd], [t_sm])
            TS("dve", nbt[:], bt[:], -1.0, ALU.mult, [t_sm], [t_sm])
            if SUB < 2:
                raise _Skip()
            ACT(nega[:], pv[:, PV_AL:PV_AL + 8], AF.Exp, [t_pv], [t_sm])
            TS("dve", nega[:], nega[:], -1.0, ALU.mult, [t_sm], [t_sm])
            if SUB < 3:
                raise _Skip()
            TT("dve", tmp8[:], bd[:, :, 8:16], pv[:, PV_DT:PV_DT + 8].unsqueeze(1).broadcast_to([128, NT, 8]), ALU.add, [t_bd, t_pv], [t_tmp8])
            ACT(tmp8[:], tmp8[:], AF.Exp, [t_tmp8], [t_tmp8])
            ACT(tmp8[:], tmp8[:], AF.Ln, [t_tmp8], [t_tmp8], bias=1.0)
            TT("dve", gtk[:], tmp8[:], nega[:].unsqueeze(1).broadcast_to([128, NT, 8]), ALU.mult, [t_tmp8, t_sm], [t_sm])
            if SUB < 4:
                raise _Skip()
            split3(gtk[:], t_sm, [g3[0][:], g3[1][:], g3[2][:]], t_sm, tmp8[:], t_tmp8)
            bk, tb = bank()
            for i in range(NT):
                tri = cb[:, CB_TRIP:CB_TRIP + 128] if i < 16 else cb[:, CB_TRIS:CB_TRIS + 128]
                blk = ones_b if i < 16 else cb[:, CB_BLKS:CB_BLKS + 128]
                for j in range(3):
                    MM(bk[:, 8 * i:8 * i + 8], tri, g3[j][:, i, :], j == 0, j == 2, [t_cb, t_sm], [tb])
                for j in range(3):
                    MM(bk[:, 256 + 8 * i:256 + 8 * i + 8], blk, g3[j][:, i, :], j == 0, j == 2, [t_cb, t_sm], [tb])
            CP("dve", gc_[:], bk[:, 0:8 * NT].rearrange("p (t c) -> p t c", c=8), [tb], [t_sm])
            CP("act", tmp8[:], bk[:, 256:256 + 8 * NT].rearrange("p (t c) -> p t c", c=8), [tb], [t_tmp8])
            if SUB < 5:
                raise _Skip()
            TS("dve", ngc[:], gc_[:], -1.0, ALU.mult, [t_sm], [t_sm])
            ACT(ee[:], gc_[:], AF.Exp, [t_sm], [t_sm])
            TS("dve", nee[:], ee[:], -1.0, ALU.mult, [t_sm], [t_sm])
            ACT(glb[:], tmp8[:], AF.Exp, [t_tmp8], [t_sm])
            TT("dve", tmp8[:], tmp8[:], gc_[:], ALU.subtract, [t_tmp8, t_sm], [t_tmp8])
            ACT(egl[:], tmp8[:], AF.Exp, [t_tmp8], [t_sm])
            if SUB < 6:
                raise _Skip()
            bk, tb = bank()
            for j in range(3):
                TT("dve", gm[:], g3[j][:, 16, :].unsqueeze(1).broadcast_to([128, 16, 8]),
                   cf[:, CF_ROWM:CF_ROWM + 16].unsqueeze(2).broadcast_to([128, 16, 8]), ALU.mult, [t_sm, t_cf], [t_gm])
                MM(bk[:, 0:128], ones_b, gm[:].rearrange("p s h -> p (s h)"), j == 0, j == 2, [t_cb, t_gm], [tb])
            ACT(glbs[:], bk[:, 0:128].rearrange("p (s h) -> p s h", h=8), AF.Exp, [tb], [t_sm])
            if SUB < 9:
                raise _Skip()
            S.barrier()

        if os.environ.get("K_BAR"):
            S.barrier()
        OWN_CH = [(896, 512), (1408, 512), (1920, 256)]
        with ExitStack() as ph, contextlib.suppress(_Skip):
            if STAGE < 2:
                raise _Skip()
            ubuf, t_ub = sb(ph, [128, 1152])
            usamp, t_us = sb(ph, [128, 16, 38])
            hc, t_hc = sb(ph, [128, 4, NM])
            sgate, t_sg = sb(ph, [128, 4, NM], BF)
            sgt, t_sgt = sb(ph, [128, 512])
            stg, t_stg = sb(ph, [128, 512])
            scs, t_scs = sb(ph, [120, 512])
            S.dma("sp", oconv_s[:, 0:22, :], sconv[:, 8:30, :])
            unew_s, t_uns = sb(ph, [128, 4, 128])
            for c in range(4):
                for g4 in range(4):
                    S.dma("sp", scs[:, 0:128], sconv[4 * g4:4 * g4 + 4, :, 128 * c:128 * c + 128].rearrange("s r n -> (s r) n"), writes=[t_scs])
                    bk, tb = bank()
                    TR(bk[:, 0:120], scs[0:120, 0:128], ident_f[0:120, 0:120], [t_scs, t_cf], [tb])
                    CP("act", usamp[:, 4 * g4:4 * g4 + 4, 0:30], bk[:, 0:120].rearrange("p (s r) -> p s r", r=30), [tb], [t_us])
                wa, ta = w_next()
                wb, tbk = w_next()
                for (c0, n) in OWN_CH:
                    bka, tba = proj_fm(wa, ta, c0, n)
                    bkb_, tbb = proj_fm(wb, tbk, c0, n)
                    ACT(sgt[:, 0:n], bkb_[:, 0:n], AF.Sigmoid, [tbb], [t_sgt])
                    if c0 < 1920:
                        TT("dve", ubuf[:, c0 - 896:c0 - 896 + n], bka[:, 0:n], sgt[:, 0:n], ALU.mult, [tba, t_sgt], [t_ub])
                    else:
                        TT("dve", ubuf[:, 1024:1152], bka[:, 0:128], sgt[:, 0:128], ALU.mult, [tba, t_sgt], [t_ub])
                        TT("dve", unew_s[:, c, :], bka[:, 128:256], sgt[:, 128:256], ALU.mult, [tba, t_sgt], [t_uns])
                        CP("act", usamp[:, :, 30:38], unew_s[:, c, :].rearrange("p (s l) -> p s l", l=8), [t_uns], [t_us])
                wg, tg = w_next()
                for (c0, n) in [(1024, 512), (1536, 512), (2048, 128)]:
                    bkg, tbg = proj_fm(wg, tg, c0, n)
                    ACT(sgate[:, c, c0 - 1024:c0 - 1024 + n], bkg[:, 0:n], AF.Silu, [tbg], [t_sg])
                cw = lambda j: pv[:, PV_CW + 31 * c + j:PV_CW + 31 * c + j + 1]
                TS("dve", hc[:, c, 0:1024], ubuf[:, 98:98 + 1024], cw(0), ALU.mult, [t_ub, t_pv], [t_hc],
                   s2=pv[:, PV_CB + c:PV_CB + c + 1], op1=ALU.add)
                for j in range(1, 31):
                    STT(hc[:, c, 0:1024], ubuf[:, 98 + j:98 + j + 1024], cw(j), hc[:, c, 0:1024], ALU.mult, ALU.add, [t_ub, t_pv, t_hc], [t_hc])
                hs = hc[:, c, 1024:1152].rearrange("p (s l) -> p s l", l=8)
                TS("dve", hs, usamp[:, :, 0:8], cw(0), ALU.mult, [t_us, t_pv], [t_hc], s2=pv[:, PV_CB + c:PV_CB + c + 1], op1=ALU.add)
                for j in range(1, 31):
                    STT(hs, usamp[:, :, j:j + 8], cw(j), hs, ALU.mult, ALU.add, [t_us, t_pv, t_hc], [t_hc])
                bk, tb = bank()
                TR(bk[0:30, 0:128], ubuf[:, 1122:1152], ident_f, [t_ub, t_cf], [tb])
                CP("act", stg[0:30, 128 * c:128 * c + 128], bk[0:30, 0:128], [tb], [t_stg])
            S.dma("sp", oconv_p[:, :], stg[0:30, :], reads=[t_stg])
            stg2, t_stg2 = sb(ph, [128, 512])
            for c in range(4):
                bk, tb = bank()
                TR(bk[:, 0:128], unew_s[:, c, :], ident_f, [t_uns, t_cf], [tb])
                CP("act", stg2[:, 128 * c:128 * c + 128], bk[:, 0:128], [tb], [t_stg2])
            for s_ in range(16):
                S.dma("sp", oconv_s[s_, 22:30, :], stg2[8 * s_:8 * s_ + 8, :], reads=[t_stg2])
            sq, t_sq = sb(ph, [128, 512])
            hl = [sb(ph, [128, 512], BF) for _ in range(4)]
            mean, t_mean = sb(ph, [128, 512])
            var, t_var = sb(ph, [128, 512])
            rstd, t_rstd = sb(ph, [128, 512])
            xc, t_xc = sb(ph, [128, 512])
            for (m0, n) in [(0, 512), (512, 512), (1024, 128)]:
                bk1, tb1 = bank()
                bk2, tb2 = bank()
                for c in range(4):
                    CP("act", hl[0][0][:, 0:n], hc[:, c, m0:m0 + n], [t_hc], [hl[0][1]])
                    TT("dve", hl[1][0][:, 0:n], hc[:, c, m0:m0 + n], hl[0][0][:, 0:n], ALU.subtract, [t_hc, hl[0][1]], [hl[1][1]])
                    MM(bk1[:, 0:n], ones_b, hl[0][0][:, 0:n], c == 0, False, [t_cb, hl[0][1]], [tb1])
                    MM(bk1[:, 0:n], ones_b, hl[1][0][:, 0:n], False, c == 3, [t_cb, hl[1][1]], [tb1])
                for c in range(4):
                    ACT(sq[:, 0:n], hc[:, c, m0:m0 + n], AF.Square, [t_hc], [t_sq])
                    CP("act", hl[2][0][:, 0:n], sq[:, 0:n], [t_sq], [hl[2][1]])
                    TT("dve", hl[3][0][:, 0:n], sq[:, 0:n], hl[2][0][:, 0:n], ALU.subtract, [t_sq, hl[2][1]], [hl[3][1]])
                    MM(bk2[:, 0:n], ones_b, hl[2][0][:, 0:n], c == 0, False, [t_cb, hl[2][1]], [tb2])
                    MM(bk2[:, 0:n], ones_b, hl[3][0][:, 0:n], False, c == 3, [t_cb, hl[3][1]], [tb2])
                ACT(mean[:, 0:n], bk1[:, 0:n], AF.Copy, [tb1], [t_mean], scale=1.0 / 512)
                TT("dve", var[:, 0:n], mean[:, 0:n], mean[:, 0:n], ALU.mult, [t_mean], [t_var])
                STT(var[:, 0:n], bk2[:, 0:n], 1.0 / 512, var[:, 0:n], ALU.mult, ALU.subtract, [tb2, t_var], [t_var])
                TS("dve", var[:, 0:n], var[:, 0:n], 1e-5, ALU.add, [t_var], [t_var])
                ACT(var[:, 0:n], var[:, 0:n], AF.Ln, [t_var], [t_var])
                ACT(rstd[:, 0:n], var[:, 0:n], AF.Exp, [t_var], [t_rstd], scale=-0.5)
                for c in range(4):
                    TT("dve", xc[:, 0:n], hc[:, c, m0:m0 + n], mean[:, 0:n], ALU.subtract, [t_hc, t_mean], [t_xc])
                    TT("dve", xc[:, 0:n], xc[:, 0:n], rstd[:, 0:n], ALU.mult, [t_xc, t_rstd], [t_xc])
                    ACT(xc[:, 0:n], xc[:, 0:n], AF.Silu, [t_xc, t_pv], [t_xc],
                        scale=pv[:, PV_CLG + c:PV_CLG + c + 1], bias=pv[:, PV_CLB + c:PV_CLB + c + 1])
                    TT("dve", mixT[:, c, m0:m0 + n], xc[:, 0:n], sgate[:, c, m0:m0 + n], ALU.mult, [t_xc, t_sg], [t_mix[c]])
            S.barrier()

        MEM_CH = [(1024, 512), (1536, 512), (2048, 128)]
        with ExitStack() as ph, contextlib.suppress(_Skip):
            if STAGE < 3:
                raise _Skip()
            mT, t_mT = sb(ph, [128, 16, 256], BF)
            xs3, t_xs3 = sb(ph, [128, 2048], BF)
            for j in range(2):
                S.dma("pool", xs3[:].rearrange("p (a n) -> p a n", n=512), memp[128 * j:128 * j + 128, :].rearrange("p (a n) -> p a n", n=512), writes=[t_xs3])
                for half in range(2):
                    bk, tb = bank()
                    bkb = bk[:].bitcast(BF)
                    for c in range(8):
                        TR(bkb[:, 128 * c:128 * c + 128], xs3[:, 128 * (8 * half + c):128 * (8 * half + c) + 128], ident_b, [t_xs3, t_cb], [tb])
                    CP("act" if half == 0 else "dve", mT[:, 8 * half:8 * half + 8, 128 * j:128 * j + 128],
                       bkb[:, 0:1024].rearrange("p (c n) -> p c n", c=8), [tb], [t_mT])
            mqT, t_mq = sb(ph, [128, NM], BF)
            mgT, t_mg = sb(ph, [128, NM], BF)
            KTp, t_ktp = sb(ph, [128, 256], BF)
            Vp, t_vp = sb(ph, [128, 2, 128], BF)
            kvst, t_kvst = sb(ph, [128, 2, 2, 128])
            kc, t_kc = sb(ph, [128, 16, 2, 128], BF)
            vc, t_vc = sb(ph, [128, 16, 2, 128], BF)
            kcT, t_kct = sb(ph, [128, 16, 256], BF)
            mqm, t_mqm = sb(ph, [128, 16, 128], BF)
            pf, t_pf = sb(ph, [128, 256])
            pn, t_pn = sb(ph, [128, 256], BF)
            pT, t_pT = sb(ph, [128, 2, 128], BF)
            mx, t_mx = sb(ph, [128, 4])
            for h in range(4):
                wk, tk_ = w_next()
                wv, tv_ = w_next()
                bk, tb = bank()
                for k in range(16):
                    MM(bk[:, 0:256], wk[:, k, :], mT[:, k, :], k == 0, k == 15, [tk_, t_mT], [tb])
                CP("act", KTp[:], bk[:, 0:256], [tb], [t_ktp])
                bk, tb = bank()
                for mc in range(2):
                    for k in range(16):
                        MM(bk[:, 128 * mc:128 * mc + 128], mT[:, k, 128 * mc:128 * mc + 128], wk[:, k, :], k == 0, k == 15, [tk_, t_mT], [tb])
                    for k in range(16):
                        MM(bk[:, 256 + 128 * mc:256 + 128 * mc + 128], mT[:, k, 128 * mc:128 * mc + 128], wv[:, k, :], k == 0, k == 15, [tv_, t_mT], [tb])
                CP("dve", kvst[:].rearrange("p a b d -> p (a b d)"), bk[:, 0:512], [tb], [t_kvst])
                CP("act", Vp[:].rearrange("p b d -> p (b d)"), bk[:, 256:512], [tb], [t_vp])
                S.dma("sp", omk[:, 128 * h:128 * h + 128].rearrange("(mc m) d -> m mc d", m=128), kvst[:, 0, :, :], reads=[t_kvst])
                S.dma("sp", omv[:, 128 * h:128 * h + 128].rearrange("(mc m) d -> m mc d", m=128), kvst[:, 1, :, :], reads=[t_kvst])
                S.dma("pool", kc[:], ck[:, :, h, :].rearrange("s (mc m) d -> m s mc d", m=128), writes=[t_kc])
                S.dma("pool", vc[:], cv[:, :, h, :].rearrange("s (mc m) d -> m s mc d", m=128), writes=[t_vc])
                wq, tq_ = w_next()
                wg, tg_ = w_next()
                for (c0, n) in MEM_CH:
                    bk, tb = proj_fm(wq, tq_, c0, n)
                    ACT(mqT[:, c0 - 1024:c0 - 1024 + n], bk[:, 0:n], AF.Copy, [tb], [t_mq], scale=128.0 ** -0.5)
                    bk, tb = proj_fm(wg, tg_, c0, n)
                    ACT(mgT[:, c0 - 1024:c0 - 1024 + n], bk[:, 0:n], AF.Silu, [tb], [t_mg])
                for s in range(16):
                    if s % 4 == 0:
                        bk, tb = bank()
                        bkb = bk[:].bitcast(BF)
                    for mc in range(2):
                        o = (s % 4) * 256 + mc * 128
                        TR(bkb[:, o:o + 128], kc[:, s, mc, :], ident_b, [t_kc, t_cb], [tb])
                    if s % 4 == 3:
                        CP("act", kcT[:, s - 3:s + 1, :], bkb[:, 0:1024].rearrange("p (s m) -> p s m", m=256), [tb], [t_kct])
                TT("dve", mqm[:], mqT[:, 1024:1152].unsqueeze(1).broadcast_to([128, 16, 128]),
                   cb[:, CB_COLM:CB_COLM + 2048].rearrange("p (s i) -> p s i", i=128), ALU.mult, [t_mq, t_cb], [t_mqm])
                for ti in range(9):
                    m0 = 128 * ti
                    bk, tb = bank()
                    if ti < 8:
                        MM(bk[:, 0:256], mqT[:, m0:m0 + 128], KTp[:], True, True, [t_mq, t_ktp], [tb])
                    else:
                        for s in range(16):
                            MM(bk[:, 0:256], mqm[:, s, :], kcT[:, s, :], s == 0, s == 15, [t_mqm, t_kct], [tb])
                    S.op("dve", lambda e, bk=bk: e.reduce_max(out=mx[:, 0:1], in_=bk[:, 0:256], axis=AX.X), [tb], [t_mx])
                    TS("dve", mx[:, 1:2], mx[:, 0:1], -1.0, ALU.mult, [t_mx], [t_mx])
                    ACT(pf[:], bk[:, 0:256], AF.Exp, [tb, t_mx], [t_pf, t_mx], bias=mx[:, 1:2], accum=mx[:, 2:3])
                    S.op("dve", lambda e: e.reciprocal(out=mx[:, 3:4], in_=mx[:, 2:3]), [t_mx], [t_mx])
                    TS("dve", pn[:], pf[:], mx[:, 3:4], ALU.mult, [t_pf, t_mx], [t_pn])
                    bk2, tb2 = bank()
                    bk2b = bk2[:].bitcast(BF)
                    for mc in range(2):
                        TR(bk2b[:, 128 * mc:128 * mc + 128], pn[:, 128 * mc:128 * mc + 128], ident_b, [t_pn, t_cb], [tb2])
                    CP("act", pT[:].rearrange("p a b -> p (a b)"), bk2b[:, 0:256], [tb2], [t_pT])
                    bk3, tb3 = bank()
                    if ti < 8:
                        for mc in range(2):
                            MM(bk3[:, 0:128], Vp[:, mc, :], pT[:, mc, :], mc == 0, mc == 1, [t_vp, t_pT], [tb3])
                    else:
                        for s in range(16):
                            for mc in range(2):
                                MM(bk3[:, 8 * s:8 * s + 8], vc[:, s, mc, :], pT[:, mc, 8 * s:8 * s + 8], mc == 0, mc == 1, [t_vc, t_pT], [tb3])
                    TT("dve", mixT[:, 12 + h, m0:m0 + 128], bk3[:, 0:128], mgT[:, m0:m0 + 128], ALU.mult, [tb3, t_mg], [t_mix[12 + h]])
            S.barrier()

        KV_CH = [(0, 512), (512, 512), (1024, 512), (1536, 512), (2048, 128)]
        with ExitStack() as ph, contextlib.suppress(_Skip):
            if STAGE < 4:
                raise _Skip()
            kT, t_kT = sb(ph, [128, NCOL], BF)
            vT, t_vT = sb(ph, [128, NCOL], BF)
            qT, t_qT = sb(ph, [128, NM], BF)
            gT, t_gT = sb(ph, [128, NM], BF)
            pre = [sb(ph, [128, 515]) for _ in range(2)]
            pres, t_pres = sb(ph, [128, 16, 11])
            sqb, t_sqb = sb(ph, [128, 512], BF)
            acc, t_acc = sb(ph, [128, 512])
            lnb_, t_lnb = acc, t_acc
            sq48, t_sq48 = sb(ph, [48, 128])
            qst, t_qst = sb(ph, [128, 3])
            qst_s, t_qsts = sb(ph, [128, 48])
            ost, t_ost = sb(ph, [48, 128])
            Sf, t_Sf = sb(ph, [128, 128])
            Sb, t_Sb = sb(ph, [128, 128], BF)
            Ss, t_Ss = sb(ph, [128, 16, 128])
            Ssb, t_Ssb = sb(ph, [128, 16, 128], BF)
            Sso, t_Sso = Ss, t_Ss
            kTm, t_kTm = sb(ph, [128, 16, 128], BF)
            qTm, t_qTm = sb(ph, [128, 16, 128], BF)
            kdm, t_kdm = qTm, t_qTm
            GT = 4
            gB3 = [sb(ph, [128, GT, 128], BF) for _ in range(3)]
            decT, t_decT = sb(ph, [128, GT, 128])
            decTs, t_decTs = sb(ph, [128, GT, 128], BF)
            P0, t_P0 = sb(ph, [128, GT, 128], BF)
            Pb = [sb(ph, [128, GT, 128], BF) for _ in range(2)]
            Qb = [sb(ph, [128, GT, 128], BF) for _ in range(2)]
            Yb = [sb(ph, [128, GT, 128], BF) for _ in range(2)]
            XTb = [sb(ph, [128, GT, 128], BF) for _ in range(2)]
            PbH = [[Tok(), Tok()] for _ in range(2)]
            QbH = [[Tok(), Tok()] for _ in range(2)]
            YbH = [[Tok(), Tok()] for _ in range(2)]
            XTH = [[Tok(), Tok()] for _ in range(2)]
            aqkb = [sb(ph, [128, GT, 128], BF) for _ in range(2)]
            kdb = [sb(ph, [128, GT, 128], BF) for _ in range(2)]
            vtb = [sb(ph, [128, GT, 128], BF) for _ in range(2)]
            rbf, t_rbf = sb(ph, [128, 128], BF)
            ubf, t_ubf = sb(ph, [128, 128], BF)
            tsb, t_tsb = sb(ph, [128, 128])
            osb, t_osb = sb(ph, [128, 128])
            junk, t_junk = sb(ph, [128, 128], BF)
            onb, t_onb = sb(ph, [128, 128], BF)
            sm4, t_sm4 = sb(ph, [128, 4])


            def conv_stream(h, kind, wt, wtk, chunks, dstT, t_dst, dcol0, norm, qscale):
                fo = kind * 1024 + 128 * h
                fc = fo // 128
                qw = lambda j: pv[:, PV_QW + 4 * fc + j:PV_QW + 4 * fc + j + 1]
                S.dma("sp", sq48[:], sqkv[:, :, fo:fo + 128].rearrange("s r n -> (s r) n"), writes=[t_sq48])
                bk, tb = bank()
                TR(bk[:, 0:48], sq48[0:48, 0:128], ident_f[0:48, 0:48], [t_sq48, t_cf], [tb])
                CP("act", pres[:, :, 0:3], bk[:, 0:48].rearrange("p (s r) -> p s r", r=3), [tb], [t_pres])
                pi = 0
                S.op("pool", lambda e, p=pre[0][0]: e.memset(p[:, 0:3], 0.0), [], [pre[0][1]])
                for (c0, n) in chunks:
                    bk, tb = proj_fm(wt, wtk, c0, n)
                    npr = n if c0 + n <= 2048 else n - 128
                    skip = max(0, dcol0 - c0)
                    if npr > 0:
                        p_, tp_ = pre[pi]
                        ACT(p_[:, 3:3 + npr], bk[:, 0:npr], AF.Copy, [tb], [tp_])
                        TS("dve", acc[:, 0:npr], p_[:, 0:npr], qw(0), ALU.mult, [tp_, t_pv], [t_acc])
                        for j in range(1, 4):
                            STT(acc[:, 0:npr], p_[:, j:j + npr], qw(j), acc[:, 0:npr], ALU.mult, ALU.add, [tp_, t_pv, t_acc], [t_acc])
                        d0 = c0 - dcol0
                        ACT(dstT[:, d0 + skip:d0 + npr], acc[:, skip:npr], AF.Silu, [t_acc], [t_dst])
                        if c0 + npr == 2048:
                            CP("act", qst[:, 0:3], p_[:, npr:npr + 3], [tp_], [t_qst])
                        else:
                            p2, tp2 = pre[1 - pi]
                            CP("act", p2[:, 0:3], p_[:, npr:npr + 3], [tp_], [tp2])
                            pi = 1 - pi
                    if c0 + n > 2048:
                        CP("act", pres[:, :, 3:11], bk[:, npr:npr + 128].rearrange("p (s l) -> p s l", l=8), [tb], [t_pres])
                        av = acc[:, 0:128].rearrange("p (s l) -> p s l", l=8)
                        TS("dve", av, pres[:, :, 0:8], qw(0), ALU.mult, [t_pres, t_pv], [t_acc])
                        for j in range(1, 4):
                            STT(av, pres[:, :, j:j + 8], qw(j), av, ALU.mult, ALU.add, [t_pres, t_pv, t_acc], [t_acc])
                        d0 = 2048 - dcol0
                        ACT(dstT[:, d0:d0 + 128], acc[:, 0:128], AF.Silu, [t_acc], [t_dst])
                        CP("act", qst_s[:].rearrange("p (s r) -> p s r", r=3), pres[:, :, 8:11], [t_pres], [t_qsts])
                bk, tb = bank()
                TR(bk[0:3, 0:128], qst[:, 0:3], ident_f, [t_qst, t_cf], [tb])
                TR(bk[0:48, 128:256], qst_s[:, 0:48], ident_f, [t_qsts, t_cf], [tb])
                CP("act", ost[0:3, :], bk[0:3, 0:128], [tb], [t_ost])
                S.dma("sp", oqkv_p[:, fo:fo + 128], ost[0:3, :], reads=[t_ost])
                CP("act", ost[0:48, :], bk[0:48, 128:256], [tb], [t_ost])
                S.dma("sp", oqkv_s[:, :, fo:fo + 128].rearrange("s r n -> (s r) n"), ost[0:48, :], reads=[t_ost])
                if norm:
                    ntot = dstT.shape[1]
                    for m0 in range(0, ntot, 512):
                        n = min(512, ntot - m0)
                        ACT(sqb[:, 0:n], dstT[:, m0:m0 + n], AF.Square, [t_dst], [t_sqb])
                        bk, tb = bank()
                        MM(bk[:, 0:n], ones_b, sqb[:, 0:n], True, True, [t_cb, t_sqb], [tb])
                        TS("dve", lnb_[:, 0:n], bk[:, 0:n], 1e-6, ALU.add, [tb], [t_lnb])
                        ACT(lnb_[:, 0:n], lnb_[:, 0:n], AF.Ln, [t_lnb], [t_lnb])
                        ACT(lnb_[:, 0:n], lnb_[:, 0:n], AF.Exp, [t_lnb], [t_lnb], scale=-0.5)
                        if qscale != 1.0:
                            TS("dve", lnb_[:, 0:n], lnb_[:, 0:n], qscale, ALU.mult, [t_lnb], [t_lnb])
                        TT("dve", dstT[:, m0:m0 + n], dstT[:, m0:m0 + n], lnb_[:, 0:n], ALU.mult, [t_dst, t_lnb], [t_dst])

            def groupA(h, tiles, par):
                ng = len(tiles)
                i0 = tiles[0]
                samp = (i0 == 16)
                full = (i0 >= 8)
                L = 2 if samp else 6
                trib = cb[:, CB_TRIS:CB_TRIS + 128] if samp else cb[:, CB_TRIP:CB_TRIP + 128]
                ntrib = cb[:, CB_NTRS:CB_NTRS + 128] if samp else cb[:, CB_NTRP:CB_NTRP + 128]
                negb = cb[:, CB_NEGS:CB_NEGS + 128] if samp else cb[:, CB_NEGP:CB_NEGP + 128]
                strict = cb[:, CB_STRS:CB_STRS + 128] if samp else cb[:, CB_STRP:CB_STRP + 128]
                W = 128 * ng
                kd, t_kd = kdb[par]
                vt, t_vt = vtb[par]
                aq, t_aq = aqkb[par]
                XT, t_XT = XTb[par]
                v3 = lambda ap: ap.rearrange("p (t n) -> p t n", n=128)
                bk, tb = bank()
                bkb = bk[:].bitcast(BF)
                for t in range(ng):
                    c0 = 128 * (i0 + t)
                    TR(bkb[:, 256 * t:256 * t + 128], kT[:, c0:c0 + 128], ident_b, [t_kT, t_cb], [tb])
                    TR(bkb[:, 256 * t + 128:256 * t + 256], vT[:, c0:c0 + 128], ident_b, [t_vT, t_cb], [tb])
                for t in range(ng):
                    ACT(kd[:, t, :], bkb[:, 256 * t:256 * t + 128], AF.Copy, [tb, t_sm], [t_kd], scale=egl[:, i0 + t, h:h + 1])
                CP("dve", vt[:, 0:ng, :], bkb[:, 0:256 * ng].rearrange("p (t two n) -> p t two n", two=2, n=128)[:, :, 1, :], [tb], [t_vt])
                yield
                for j in range(3):
                    TT("pool", gB3[j][0][:, 0:ng, :], ones_b.unsqueeze(1).broadcast_to([128, ng, 128]),
                       g3[j][:, i0:i0 + ng, h:h + 1].broadcast_to([128, ng, 128]), ALU.mult, [t_cb, t_sm], [gB3[j][1]])
                bk, tb = bank()
                for t in range(ng):
                    o = bk[:, 128 * t:128 * t + 128]
                    for j in range(3):
                        MM(o, gB3[j][0][:, t, :], trib, j == 0, False, [gB3[j][1], t_cb], [tb])
                    for j in range(3):
                        MM(o, ntrib, gB3[j][0][:, t, :], False, False, [gB3[j][1], t_cb], [tb])
                    MM(o, ident_b, negb, False, True, [t_cb], [tb])
                ACT(decT[:, 0:ng, :], v3(bk[:, 0:W]), AF.Exp, [tb], [t_decT])
                TT("pool", decTs[:, 0:ng, :], decT[:, 0:ng, :], strict.unsqueeze(1).broadcast_to([128, ng, 128]), ALU.mult, [t_decT, t_cb], [t_decTs])
                yield
                bkG, tbG = bank()
                if full:
                    bkA, tbA = bank()
                for t in range(ng):
                    c0 = 128 * (i0 + t)
                    MM(bkG[:, 128 * t:128 * t + 128], kT[:, c0:c0 + 128], kT[:, c0:c0 + 128], True, True, [t_kT], [tbG])
                    if full:
                        MM(bkA[:, 128 * t:128 * t + 128], kT[:, c0:c0 + 128], qT[:, c0 - 1024:c0 - 1024 + 128], True, True, [t_kT, t_qT], [tbA])
                for t in range(ng):
                    STT(P0[:, t, :], bkG[:, 128 * t:128 * t + 128], nbt[:, i0 + t, h:h + 1], decTs[:, t, :], ALU.mult, ALU.mult,
                        [tbG, t_sm, t_decTs], [t_P0])
                if full:
                    TT("dve", aq[:, 0:ng, :], v3(bkA[:, 0:W]), decT[:, 0:ng, :], ALU.mult, [tbA, t_decT], [t_aq])
                Y0 = Yb[0][0]
                TT("pool", Y0[:, 0:ng, :], P0[:, 0:ng, :], ident_b.unsqueeze(1).broadcast_to([128, ng, 128]), ALU.add, [t_P0, t_cb], YbH[0])
                yield
                bk, tb = bank()
                bkb = bk[:].bitcast(BF)
                for t in range(ng):
                    TR(bkb[:, 128 * t:128 * t + 128], P0[:, t, :], ident_b, [t_P0, t_cb], [tb])
                Q0 = Qb[0][0]
                CP("act", Q0[:, 0:ng, :], v3(bkb[:, 0:W]), [tb], QbH[0])
                yield
                halves = [(0, 2), (2, 4)] if ng == 4 else [(0, ng)]
                v3h = lambda ap: ap.rearrange("p (t n) -> p t n", n=128)
                mms = []
                for hi, (a, b) in enumerate(halves):
                    bkP, tbP = bank()
                    bkQ, tbQ = bank()
                    for t in range(a, b):
                        o = 128 * (t - a)
                        MM(bkP[:, o:o + 128], Q0[:, t, :], P0[:, t, :], True, True, [QbH[0][hi], t_P0], [tbP])
                        MM(bkQ[:, o:o + 128], P0[:, t, :], Q0[:, t, :], True, True, [QbH[0][hi], t_P0], [tbQ])
                    mms.append((bkP, tbP, bkQ, tbQ))
                for hi, (a, b) in enumerate(halves):
                    bkP, tbP, bkQ, tbQ = mms[hi]
                    wd = 128 * (b - a)
                    CP("act", Pb[0][0][:, a:b, :], v3h(bkP[:, 0:wd]), [tbP], [PbH[0][hi]])
                    CP("dve", Qb[1][0][:, a:b, :], v3h(bkQ[:, 0:wd]), [tbQ], [QbH[1][hi]])
                yield
                pc, qc, yc = 0, 1, 0
                for k in range(1, L + 1):
                    P, Q, Y = Pb[pc][0], Qb[qc][0], Yb[yc][0]
                    Pn, Qn = Pb[1 - pc][0], Qb[1 - qc][0]
                    Yn = Yb[1 - yc][0] if k < L else XT
                    tYn = YbH[1 - yc] if k < L else XTH[par]
                    mms = []
                    for hi, (a, b) in enumerate(halves):
                        if k < L:
                            bkP, tbP = bank()
                            bkQ, tbQ = bank()
                        else:
                            bkP = tbP = bkQ = tbQ = None
                        bkY, tbY = bank()
                        for t in range(a, b):
                            o = 128 * (t - a)
                            if k < L:
                                MM(bkP[:, o:o + 128], Q[:, t, :], P[:, t, :], True, True, [QbH[qc][hi], PbH[pc][hi]], [tbP])
                                MM(bkQ[:, o:o + 128], P[:, t, :], Q[:, t, :], True, True, [QbH[qc][hi], PbH[pc][hi]], [tbQ])
                            MM(bkY[:, o:o + 128], Q[:, t, :], Y[:, t, :], True, True, [QbH[qc][hi], YbH[yc][hi]], [tbY])
                        mms.append((bkP, tbP, bkQ, tbQ, bkY, tbY))
                    for hi, (a, b) in enumerate(halves):
                        bkP, tbP, bkQ, tbQ, bkY, tbY = mms[hi]
                        wd = 128 * (b - a)
                        if k < L:
                            CP("act", Pn[:, a:b, :], v3h(bkP[:, 0:wd]), [tbP], [PbH[1 - pc][hi]])
                            CP("act" if k % 2 == 0 else "dve", Qn[:, a:b, :], v3h(bkQ[:, 0:wd]), [tbQ], [QbH[1 - qc][hi]])
                        TT("dve", Yn[:, a:b, :], v3h(bkY[:, 0:wd]), Y[:, a:b, :], ALU.add, [tbY, YbH[yc][hi]], [tYn[hi]])
                    pc, qc, yc = 1 - pc, 1 - qc, 1 - yc
                    yield

            def groupB(h, tiles, par):
                kd, t_kd = kdb[par]
                vt, t_vt = vtb[par]
                aq, t_aq = aqkb[par]
                XT, t_XT = XTb[par]
                for t, i in enumerate(tiles):
                    samp = (i == 16)
                    full = (i >= 8)
                    c0 = 128 * i
                    mc0 = c0 - 1024
                    sc = lambda arr: arr[:, i, h:h + 1]
                    bk, tb = bank()
                    if not samp:
                        MM(bk[:, 0:128], kT[:, c0:c0 + 128], Sb[:], True, True, [t_kT, t_Sb], [tb])
                        if full:
                            MM(bk[:, 128:256], qT[:, mc0:mc0 + 128], Sb[:], True, True, [t_qT, t_Sb], [tb])
                    else:
                        for s_ in range(16):
                            MM(bk[:, 0:128], kTm[:, s_, :], Ssb[:, s_, :], s_ == 0, s_ == 15, [t_kTm, t_Ssb], [tb])
                        for s_ in range(16):
                            MM(bk[:, 128:256], qTm[:, s_, :], Ssb[:, s_, :], s_ == 0, s_ == 15, [t_qTm, t_Ssb], [tb])
                    STT(rbf[:], bk[:, 0:128], sc(nee), vt[:, t, :], ALU.mult, ALU.add, [tb, t_sm, t_vt], [t_rbf])
                    if full:
                        ACT(tsb[:], bk[:, 128:256], AF.Copy, [tb, t_sm], [t_tsb], scale=sc(ee))
                    yield
                    bku, tbu = bank()
                    MM(bku[:, 0:128], XT[:, t, :], rbf[:], True, True, [XTH[par][(t // 2) if len(tiles) == 4 else 0], t_rbf], [tbu])
                    ACT(ubf[:], bku[:, 0:128], AF.Copy, [tbu, t_sm], [t_ubf], scale=sc(bt))
                    yield
                    if not samp:
                        bks, tbs = bank()
                        MM(bks[:, 0:128], kd[:, t, :], ubf[:], True, True, [t_kd, t_ubf], [tbs])
                        STT(Sb[:], Sf[:], sc(glb), bks[:, 0:128], ALU.mult, ALU.add, [t_Sf, t_sm, tbs], [t_Sb])
                        STT(Sf[:], Sf[:], sc(glb), bks[:, 0:128], ALU.mult, ALU.add, [t_Sf, t_sm, tbs], [t_Sf])
                        yield
                    else:
                        TT("dve", kdm[:], kd[:, t, :].unsqueeze(1).broadcast_to([128, 16, 128]),
                           cf[:, CF_ROWM:CF_ROWM + 16].unsqueeze(2).broadcast_to([128, 16, 128]), ALU.mult, [t_kd, t_cf], [t_kdm])
                        for s_ in range(16):
                            if s_ % 4 == 0:
                                bks, tbs = bank()
                            MM(bks[:, 128 * (s_ % 4):128 * (s_ % 4) + 128], kdm[:, s_, :], ubf[:], True, True, [t_kdm, t_ubf], [tbs])
                            if s_ % 4 == 3:
                                for s2 in range(s_ - 3, s_ + 1):
                                    STT(Sso[:, s2, :], Ss[:, s2, :], glbs[:, s2, h:h + 1], bks[:, 128 * (s2 % 4):128 * (s2 % 4) + 128],
                                        ALU.mult, ALU.add, [t_Ss, t_sm, tbs], [t_Ss])
                                yield
                    if full:
                        bk4, tb4 = bank()
                        MM(bk4[:, 0:128], aq[:, t, :], ubf[:], True, True, [t_aq, t_ubf], [tb4])
                        TT("dve", osb[:], bk4[:, 0:128], tsb[:], ALU.add, [tb4, t_tsb], [t_osb])
                        ACT(junk[:], osb[:], AF.Square, [t_osb], [t_junk, t_sm4], accum=sm4[:, 0:1])
                        yield
                        TS("dve", sm4[:, 1:2], sm4[:, 0:1], 1.0 / 128, ALU.mult, [t_sm4], [t_sm4], s2=1e-6, op1=ALU.add)
                        ACT(sm4[:, 2:3], sm4[:, 1:2], AF.Ln, [t_sm4], [t_sm4])
                        ACT(sm4[:, 3:4], sm4[:, 2:3], AF.Exp, [t_sm4], [t_sm4], scale=-0.5)
                        TS("dve", onb[:], osb[:], sm4[:, 3:4], ALU.mult, [t_osb, t_sm4], [t_onb])
                        yield
                        bko, tbo = bank()
                        bkob = bko[:].bitcast(BF)
                        TR(bkob[:, 0:128], onb[:], ident_b, [t_onb, t_cb], [tbo])
                        STT(mixT[:, 4 + h, mc0:mc0 + 128], bkob[:, 0:128], pv[:, PV_NW:PV_NW + 1], gT[:, mc0:mc0 + 128],
                            ALU.mult, ALU.mult, [tbo, t_pv, t_gT], [t_mix[4 + h]])
                        yield
                    if i == 15:
                        S.dma("sp", odelta_p[h, :, :], Sf[:], reads=[t_Sf])

            def run_il(gens, weights):
                gens = [g for g in gens if g is not None]
                alive = [True] * len(gens)
                while any(alive):
                    for gi, g in enumerate(gens):
                        if not alive[gi]:
                            continue
                        for _ in range(weights[gi]):
                            try:
                                next(g)
                            except StopIteration:
                                alive[gi] = False
                                break

            GROUPS = [[0, 1, 2, 3], [4, 5, 6, 7], [8, 9, 10, 11], [12, 13, 14, 15], [16]]

            for h in range(8):
                S.dma("sp", Ss[:], sdelta[:, h, :, :].rearrange("s k v -> k s v"), writes=[t_Ss])
                CP("pool", Ssb[:], Ss[:], [t_Ss], [t_Ssb])
                S.op("pool", lambda e: e.memset(Sf[:], 0.0), [], [t_Sf])
                S.op("pool", lambda e: e.memset(Sb[:], 0.0), [], [t_Sb])
                wk, tk_ = w_next()
                conv_stream(h, 1, wk, tk_, KV_CH, kT, t_kT, 0, True, 1.0)
                wv, tv_ = w_next()
                conv_stream(h, 2, wv, tv_, KV_CH, vT, t_vT, 0, False, 1.0)
                wq, tq_ = w_next()
                conv_stream(h, 0, wq, tq_, OWN_CH, qT, t_qT, 1024, True, 128.0 ** -0.5)
                wg, tg_ = w_next()
                for (c0, n) in MEM_CH:
                    bk, tb = proj_fm(wg, tg_, c0, n)
                    ACT(gT[:, c0 - 1024:c0 - 1024 + n], bk[:, 0:n], AF.Silu, [tb], [t_gT])
                colm = cb[:, CB_COLM:CB_COLM + 2048].rearrange("p (s i) -> p s i", i=128)
                TT("dve", kTm[:], kT[:, 2048:2176].unsqueeze(1).broadcast_to([128, 16, 128]), colm, ALU.mult, [t_kT, t_cb], [t_kTm])
                TT("pool", qTm[:], qT[:, 1024:1152].unsqueeze(1).broadcast_to([128, 16, 128]), colm, ALU.mult, [t_qT, t_cb], [t_qTm])
                prevB = None
                for gi, tiles in enumerate(GROUPS):
                    ga = groupA(h, tiles, gi % 2)
                    if prevB is None:
                        run_il([ga], [1])
                    else:
                        run_il([ga, prevB], [int(os.environ.get('K_WA', '1')), int(os.environ.get('K_WB', '2'))])
                    prevB = groupB(h, tiles, gi % 2)
                run_il([prevB], [1])
                S.dma("sp", odelta_s[:, h, :, :].rearrange("s k v -> k s v"), Sso[:], reads=[t_Sso])
            S.barrier()

        es2.close()
        with ExitStack() as ph, contextlib.suppress(_Skip):
            if STAGE < 5:
                raise _Skip()
            wo, t_wo = sb(ph, [128, 16, 2048], BF)
            lngb, t_lngb = sb(ph, [128, 2, 2048])
            xr = [sb(ph, [128, 2048]) for _ in range(2)]
            ypb = [sb(ph, [128, 2048]) for _ in range(2)]
            yo = [sb(ph, [128, 2048]) for _ in range(2)]
            jkb = [sb(ph, [128, 2048], BF) for _ in range(2)]
            stb = [sb(ph, [128, 8]) for _ in range(2)]
            for k4 in range(16):
                S.dma("pool", wo[:, k4, :].rearrange("p (a n) -> p a n", n=512),
                      w_out[128 * k4:128 * k4 + 128, :].rearrange("p (a n) -> p a n", n=512), writes=[t_wo])
            S.dma("sp", lngb[:], lngb_d[:, :, :], writes=[t_lngb])
            for ti in range(9):
                xr_, txr = xr[ti % 2]
                yo_, tyo = yo[ti % 2]
                yp, t_yp = ypb[ti % 2]
                jk, t_jk = jkb[ti % 2]
                st, t_st = stb[ti % 2]
                S.dma("sp", xr_[:], xall[1024 + 128 * ti:1024 + 128 * ti + 128, :], writes=[txr])
                for nb in range(4):
                    bk, tb = bank()
                    for k in range(16):
                        MM(bk[:, 0:512], mixT[:, k, 128 * ti:128 * ti + 128], wo[:, k, 512 * nb:512 * nb + 512], k == 0, k == 15, [t_mix[k], t_wo], [tb])
                    STT(yp[:, 512 * nb:512 * nb + 512], xr_[:, 512 * nb:512 * nb + 512], ALPHA, bk[:, 0:512], ALU.mult, ALU.add, [txr, tb], [t_yp])
                S.op("dve", lambda e, st=st, yp=yp: e.reduce_sum(out=st[:, 0:1], in_=yp[:], axis=AX.X), [t_yp], [t_st])
                ACT(jk[:], yp[:], AF.Square, [t_yp], [t_jk, t_st], accum=st[:, 1:2])
                TS("dve", st[:, 2:3], st[:, 0:1], 1.0 / 2048, ALU.mult, [t_st], [t_st])
                TT("dve", st[:, 3:4], st[:, 2:3], st[:, 2:3], ALU.mult, [t_st], [t_st])
                STT(st[:, 4:5], st[:, 1:2], 1.0 / 2048, st[:, 3:4], ALU.mult, ALU.subtract, [t_st], [t_st])
                TS("dve", st[:, 4:5], st[:, 4:5], 1e-5, ALU.add, [t_st], [t_st])
                ACT(st[:, 5:6], st[:, 4:5], AF.Ln, [t_st], [t_st])
                ACT(st[:, 6:7], st[:, 5:6], AF.Exp, [t_st], [t_st], scale=-0.5)
                STT(st[:, 7:8], st[:, 2:3], -1.0, st[:, 6:7], ALU.mult, ALU.mult, [t_st], [t_st])
                ACT(yp[:], yp[:], AF.Identity, [t_yp, t_st], [t_yp], scale=st[:, 6:7], bias=st[:, 7:8])
                TT("dve", yp[:], yp[:], lngb[:, 0, :], ALU.mult, [t_yp, t_lngb], [t_yp])
                TT("pool", yo_[:], yp[:], lngb[:, 1, :], ALU.add, [t_yp, t_lngb], [tyo])
                S.dma("sp", y_d[128 * ti:128 * ti + 128, :], yo_[:], reads=[tyo])
            S.barrier()

        with nc.Block() as block:
            @block.tensor
            def _(e):
                S.replay("pe", e)

            @block.scalar
            def _(e):
                S.replay("act", e)

            @block.vector
            def _(e):
                S.replay("dve", e)

            @block.gpsimd
            def _(e):
                S.replay("pool", e)

            @block.sync
            def _(e):
                S.replay("sp", e)
        build_nc.stats = dict(S.cnt)
    return nc


def _consts():
    idx = np.arange(128)
    blk = idx // 8
    cf = np.zeros((128, NCF), np.float32)
    cf[:, CF_ID:CF_ID + 128] = np.eye(128)
    cf[:, CF_ONE:CF_ONE + 128] = 1.0
    t, i = idx[:, None], idx[None, :]
    cf[:, CF_TRIP:CF_TRIP + 128] = (t <= i)
    cf[:, CF_NEGP:CF_NEGP + 128] = np.where(i >= t, 0.0, NEG)
    same = (blk[:, None] == blk[None, :])
    cf[:, CF_TRIS:CF_TRIS + 128] = (t <= i) & same
    cf[:, CF_BLKS:CF_BLKS + 128] = same
    cf[:, CF_NEGS:CF_NEGS + 128] = np.where((i >= t) & same, 0.0, NEG)
    cf[:, CF_ROWM:CF_ROWM + 16] = (blk[:, None] == np.arange(16)[None, :])
    cb = np.zeros((128, NCB), np.float32)
    cb[:, CB_ID:CB_ID + 128] = np.eye(128)
    cb[:, CB_ONE:CB_ONE + 128] = 1.0
    cb[:, CB_STRP:CB_STRP + 128] = (i > t)
    cb[:, CB_STRS:CB_STRS + 128] = (i > t) & same
    colm = np.zeros((128, 16, 128), np.float32)
    for s in range(16):
        colm[:, s, 8 * s:8 * s + 8] = 1.0
    cb[:, CB_COLM:CB_COLM + 2048] = colm.reshape(128, 2048)
    cb[:, CB_TRIP:CB_TRIP + 128] = cf[:, CF_TRIP:CF_TRIP + 128]
    cb[:, CB_TRIS:CB_TRIS + 128] = cf[:, CF_TRIS:CF_TRIS + 128]
    cb[:, CB_BLKS:CB_BLKS + 128] = cf[:, CF_BLKS:CF_BLKS + 128]
    cb[:, CB_NEGP:CB_NEGP + 128] = cf[:, CF_NEGP:CF_NEGP + 128]
    cb[:, CB_NEGS:CB_NEGS + 128] = cf[:, CF_NEGS:CF_NEGS + 128]
    cb[:, CB_NTRP:CB_NTRP + 128] = -cf[:, CF_TRIP:CF_TRIP + 128]
    cb[:, CB_NTRS:CB_NTRS + 128] = -cf[:, CF_TRIS:CF_TRIS + 128]
    return cf, cb.astype(ml_dtypes.bfloat16)


_NC_CACHE = {}


def kernel(x_prompt, x_sample, mem_prompt, state_conv, state_qkv_conv, state_delta, cache_mem_k, cache_mem_v,
           w_in, conv_w, conv_b, conv_ln_g, conv_ln_b, qkv_conv_w, a_log, dt_bias, delta_norm_w,
           w_mem_k, w_mem_v, w_out, ln_g, ln_b):
    f = lambda a: np.ascontiguousarray(np.asarray(a, dtype=np.float32))
    x_prompt, x_sample, mem_prompt = f(x_prompt), f(x_sample), f(mem_prompt)
    cf, cb = _consts()
    pv = np.zeros((128, NPV), np.float32)
    cwT = f(conv_w)[0].T.reshape(4, 128, 31)
    pv[:, PV_CW:PV_CW + 124] = cwT.transpose(1, 0, 2).reshape(128, 124)
    pv[:, PV_CB:PV_CB + 4] = f(conv_b)[0].reshape(4, 128).T
    pv[:, PV_CLG:PV_CLG + 4] = f(conv_ln_g)[0].reshape(4, 128).T
    pv[:, PV_CLB:PV_CLB + 4] = f(conv_ln_b)[0].reshape(4, 128).T
    qwT = f(qkv_conv_w)[0].T.reshape(24, 128, 4)
    pv[:, PV_QW:PV_QW + 96] = qwT.transpose(1, 0, 2).reshape(128, 96)
    pv[:, PV_NW] = f(delta_norm_w)[0]
    pv[:, PV_AL:PV_AL + 8] = f(a_log)[0][None, :]
    pv[:, PV_DT:PV_DT + 8] = f(dt_bias)[0][None, :]
    lngb = np.ascontiguousarray(np.broadcast_to(np.stack([f(ln_g)[0], f(ln_b)[0]])[None], (128, 2, 2048)))
    W_in, W_mk, W_mv, W_out = f(w_in)[0], f(w_mem_k)[0], f(w_mem_v)[0], f(w_out)[0]
    sc, sq, sd, ck, cv = f(state_conv)[0], f(state_qkv_conv)[0], f(state_delta)[0], f(cache_mem_k)[0], f(cache_mem_v)[0]
    in_maps = []
    for c in range(8):
        b, hf = c // 2, c % 2
        xall = np.zeros((NCOL, 2048), np.float32)
        if hf == 1:
            xall[0:1024] = x_prompt[b, 0:1024]
        xall[1024:2048] = x_prompt[b, 1024 * hf:1024 * hf + 1024]
        xall[2048:] = x_sample[16 * c:16 * c + 16].reshape(128, 2048)
        sl = slice(16 * c, 16 * c + 16)
        in_maps.append({
            "xall": xall, "memp": mem_prompt[b], "sconv": sc[sl], "sqkv": sq[sl], "sdelta": sd[sl],
            "ck": ck[sl], "cv": cv[sl], "w_in": W_in, "w_mk": W_mk, "w_mv": W_mv, "w_out": W_out,
            "cf": cf, "cb": cb, "pv": pv, "lngb": lngb,
        })
    if os.environ.get("K_CORES"):
        ncores = int(os.environ["K_CORES"])
        nc = build_nc()
        res = run_bass_kernel_spmd(nc, in_maps[:ncores], core_ids=list(range(ncores)), trace=bool(os.environ.get("K_TRACE")))
        kernel.last = res
        R = list(res.results) + [res.results[0]] * (8 - ncores)
    else:
        if "nc" not in _NC_CACHE:
            _NC_CACHE["nc"] = build_nc()
        nc = _NC_CACHE["nc"]
        res = run_bass_kernel_spmd(nc, in_maps, core_ids=list(range(8)))
        R = res.results
    y_p = np.zeros((4, 2048, 2048), np.float32)
    y_s = np.zeros((128, 8, 2048), np.float32)
    o_conv_p = np.zeros((1, 4, 30, 512), np.float32)
    o_qkv_p = np.zeros((1, 4, 3, 3072), np.float32)
    o_delta_p = np.zeros((1, 4, 8, 128, 128), np.float32)
    o_mk = np.zeros((1, 4, 256, 4, 128), np.float32)
    o_mv = np.zeros((1, 4, 256, 4, 128), np.float32)
    o_conv_s = np.zeros((1, 128, 30, 512), np.float32)
    o_qkv_s = np.zeros((1, 128, 3, 3072), np.float32)
    o_delta_s = np.zeros((1, 128, 8, 128, 128), np.float32)
    for c in range(8):
        b, hf = c // 2, c % 2
        r = R[c]
        y_p[b, 1024 * hf:1024 * hf + 1024] = r["y"][0:1024]
        y_s[16 * c:16 * c + 16] = r["y"][1024:1152].reshape(16, 8, 2048)
        sl = slice(16 * c, 16 * c + 16)
        o_conv_s[0, sl] = r["oconv_s"]
        o_qkv_s[0, sl] = r["oqkv_s"]
        o_delta_s[0, sl] = r["odelta_s"]
        if hf == 1:
            o_conv_p[0, b] = r["oconv_p"]
            o_qkv_p[0, b] = r["oqkv_p"]
            o_delta_p[0, b] = r["odelta_p"]
        else:
            o_mk[0, b] = r["omk"].reshape(256, 4, 128)
            o_mv[0, b] = r["omv"].reshape(256, 4, 128)
    return (y_p, y_s, o_conv_p, o_qkv_p, o_delta_p, o_mk, o_mv, o_conv_s, o_qkv_s, o_delta_s)
```

```python
import os
import contextlib
from contextlib import ExitStack
import numpy as np
import ml_dtypes
import concourse.bass as bass
import concourse.mybir as mybir
from concourse.bass_utils import run_bass_kernel_spmd

F32 = mybir.dt.float32
BF = mybir.dt.bfloat16
AF = mybir.ActivationFunctionType
ALU = mybir.AluOpType
AX = mybir.AxisListType

SAME_SYNC = not os.environ.get('K_NOSAME')
STAGE = int(os.environ.get('K_STAGE', '9'))
SUB = int(os.environ.get('K_SUB', '99'))
NT = 17
NCOL = NT * 128
NM = 1152
O_GA, O_GB, O_CG, O_Q, O_K, O_V, O_DG, O_BT, O_DC, O_MQ, O_MG = 0, 512, 1024, 1536, 2560, 3584, 4608, 5632, 5640, 5648, 6160
N_IN = 6672
ALPHA = 2.0 ** 0.25
NEG = -30000.0

CF_ID, CF_ONE, CF_TRIP, CF_NEGP, CF_TRIS, CF_BLKS, CF_NEGS, CF_ROWM = 0, 128, 256, 384, 512, 640, 768, 896
NCF = 912
CB_ID, CB_ONE, CB_STRP, CB_STRS, CB_COLM = 0, 128, 256, 384, 512
CB_TRIP, CB_TRIS, CB_BLKS, CB_NEGP, CB_NEGS, CB_NTRP, CB_NTRS = 2560, 2688, 2816, 2944, 3072, 3200, 3328
NCB = 3456
PV_CW, PV_CB, PV_CLG, PV_CLB, PV_QW, PV_NW, PV_AL, PV_DT = 0, 124, 128, 132, 136, 232, 233, 241
NPV = 249


class _Skip(Exception):
    pass


class Tok:
    __slots__ = ("w", "r", "excl")

    def __init__(self, excl=False):
        self.w = None
        self.r = {}
        self.excl = excl


class Sched:
    ENG = ("pe", "act", "dve", "pool", "sp")

    def __init__(self, nc, es, ndma=40):
        self.nc = nc
        self.sem = {e: es.enter_context(nc.semaphore("sem_" + e)) for e in self.ENG}
        self.cnt = {e: 0 for e in self.ENG}
        self.q = {e: [] for e in self.ENG}
        self.seen = {e: {} for e in self.ENG}
        self.dsem = {q: [es.enter_context(nc.semaphore("dsem_%s%d" % (q, i))) for i in range(ndma // 2)] for q in ("sp", "pool")}
        self.dcnt = {q: [0] * (ndma // 2) for q in ("sp", "pool")}
        self.dnext = {"sp": 0, "pool": 0}

    def _deps(self, reads, writes):
        ev = []
        for t in reads:
            if t.w is not None:
                ev.append(t.w)
            if t.excl:
                ev.extend(t.r.values())
        for t in writes:
            if t.w is not None:
                ev.append(t.w)
            ev.extend(t.r.values())
        return ev

    def _waits(self, eng, evs):
        out = []
        for (key, val) in evs:
            if key == eng and (eng == "pe" or not SAME_SYNC):
                continue
            if self.seen[eng].get(key, 0) >= val:
                continue
            self.seen[eng][key] = val
            out.append((key, val))
        return out

    def _mark(self, ev, reads, writes):
        for t in writes:
            t.w = ev
            t.r = {}
        k = ev[0]
        for t in reads:
            if t.r.get(k, (k, 0))[1] < ev[1]:
                t.r[k] = ev

    def op(self, eng, fn, reads=(), writes=()):
        w = self._waits(eng, self._deps(reads, writes))
        self.cnt[eng] += 1
        ev = (eng, self.cnt[eng])
        self.q[eng].append(("op", fn, w, None))
        self._mark(ev, reads, writes)
        return ev

    def dma(self, q, out, in_, reads=(), writes=(), **kw):
        k = self.dnext[q]
        self.dnext[q] = (k + 1) % len(self.dsem[q])
        evs = self._deps(reads, writes)
        if self.dcnt[q][k] > 0:
            evs.append((("d", q, k), 16 * self.dcnt[q][k]))
        w = self._waits(q, evs)
        self.dcnt[q][k] += 1
        ev = (("d", q, k), 16 * self.dcnt[q][k])
        self.q[q].append(("dma", (out, in_, kw), w, k))
        self._mark(ev, reads, writes)
        return ev

    def barrier(self):
        evs = [(e, self.cnt[e]) for e in self.ENG if self.cnt[e] > 0]
        if not os.environ.get("K_NODMABAR"):
            evs += [(("d", q, k), 16 * c) for q in ("sp", "pool") for k, c in enumerate(self.dcnt[q]) if c > 0]
        for e in self.ENG:
            w = self._waits(e, evs)
            if w:
                self.q[e].append(("wait", None, w, None))

    def semh(self, key):
        return self.sem[key] if isinstance(key, str) else self.dsem[key[1]][key[2]]

    def replay(self, e, h):
        for kind, fn, waits, k in self.q[e]:
            for (key, val) in waits:
                h.wait_ge(self.semh(key), val)
            if kind == "op":
                fn(h).then_inc(self.sem[e], 1)
            elif kind == "dma":
                out, in_, kw = fn
                h.dma_start(out=out, in_=in_, **kw).then_inc(self.dsem[e][k], 16)


def build_nc(dbg=False):
    nc = bass.Bass("TRN2", target_bir_lowering=False)

    def din(name, shape, dt=F32):
        return nc.dram_tensor(name, list(shape), dt, kind="ExternalInput").ap()

    def dout(name, shape, dt=F32):
        return nc.dram_tensor(name, list(shape), dt, kind="ExternalOutput").ap()

    xall = din("xall", [NCOL, 2048])
    memp = din("memp", [256, 2048])
    sconv = din("sconv", [16, 30, 512])
    sqkv = din("sqkv", [16, 3, 3072])
    sdelta = din("sdelta", [16, 8, 128, 128])
    ck = din("ck", [16, 256, 4, 128])
    cv = din("cv", [16, 256, 4, 128])
    w_in = din("w_in", [2048, N_IN])
    w_mk = din("w_mk", [2048, 512])
    w_mv = din("w_mv", [2048, 512])
    w_out = din("w_out", [2048, 2048])
    cf_d = din("cf", [128, NCF])
    cb_d = din("cb", [128, NCB], BF)
    pv_d = din("pv", [128, NPV])
    lngb_d = din("lngb", [128, 2, 2048])

    y_d = dout("y", [NM, 2048])
    oconv_p = dout("oconv_p", [30, 512])
    oqkv_p = dout("oqkv_p", [3, 3072])
    odelta_p = dout("odelta_p", [8, 128, 128])
    omk = dout("omk", [256, 512])
    omv = dout("omv", [256, 512])
    oconv_s = dout("oconv_s", [16, 30, 512])
    oqkv_s = dout("oqkv_s", [16, 3, 3072])
    odelta_s = dout("odelta_s", [16, 8, 128, 128])

    with ExitStack() as es:
        S = Sched(nc, es)
        ctr = [0]

        def sb(es_, shape, dt=F32):
            ctr[0] += 1
            t = es_.enter_context(nc.sbuf_tensor("t%d" % ctr[0], list(shape), dt))
            return t, Tok()

        def ACT(out, in_, func, reads, writes, bias=None, scale=None, accum=None):
            kw = {}
            if bias is not None:
                kw["bias"] = bias
            if scale is not None:
                kw["scale"] = scale
            if accum is not None:
                kw["accum_out"] = accum
            return S.op("act", lambda e: e.activation(out=out, in_=in_, func=func, **kw), reads, writes)

        def TS(eng, out, in0, s1, op0, reads, writes, s2=None, op1=None):
            if op1 is None:
                return S.op(eng, lambda e: e.tensor_scalar(out=out, in0=in0, scalar1=s1, scalar2=None, op0=op0), reads, writes)
            return S.op(eng, lambda e: e.tensor_scalar(out=out, in0=in0, scalar1=s1, scalar2=s2, op0=op0, op1=op1), reads, writes)

        def TT(eng, out, in0, in1, op, reads, writes):
            return S.op(eng, lambda e: e.tensor_tensor(out=out, in0=in0, in1=in1, op=op), reads, writes)

        def STT(out, in0, scalar, in1, op0, op1, reads, writes):
            return S.op("dve", lambda e: e.scalar_tensor_tensor(out=out, in0=in0, scalar=scalar, in1=in1, op0=op0, op1=op1), reads, writes)

        def CP(eng, out, in_, reads, writes):
            if eng == "act":
                return S.op("act", lambda e: e.copy(out=out, in_=in_), reads, writes)
            return S.op(eng, lambda e: e.tensor_copy(out=out, in_=in_), reads, writes)

        def split3(src, tsrc, outs, touts, r_f32, t_r):
            CP("dve", outs[0], src, [tsrc], [touts])
            TT("dve", r_f32, src, outs[0], ALU.subtract, [tsrc, touts], [t_r])
            CP("dve", outs[1], r_f32, [t_r], [touts])
            TT("dve", outs[2], r_f32, outs[1], ALU.subtract, [t_r, touts], [touts])

        def MM(out, lhsT, rhs, start, stop, reads, writes):
            return S.op("pe", lambda e: e.matmul(out=out, lhsT=lhsT, rhs=rhs, start=start, stop=stop), reads, writes)

        def TR(out, in_, ident, reads, writes):
            return S.op("pe", lambda e: e.transpose(out=out, in_=in_, identity=ident), reads, writes)

        cf, t_cf = sb(es, [128, NCF])
        cb, t_cb = sb(es, [128, NCB], BF)
        pv, t_pv = sb(es, [128, NPV])
        mixT = es.enter_context(nc.sbuf_tensor("mixT", [128, 16, NM], BF))
        t_mix = [Tok() for _ in range(16)]
        NW = 4
        wsl = [sb(es, [128, 16, 128], BF) for _ in range(NW)]
        bt, t_sm = sb(es, [128, NT, 8])
        nbt, _ = sb(es, [128, NT, 8])
        gtk, _ = sb(es, [128, NT, 8])
        gc_, _ = sb(es, [128, NT, 8])
        ngc, _ = sb(es, [128, NT, 8])
        ee, _ = sb(es, [128, NT, 8])
        nee, _ = sb(es, [128, NT, 8])
        egl, _ = sb(es, [128, NT, 8])
        glb, _ = sb(es, [128, NT, 8])
        glbs, _ = sb(es, [128, 16, 8])
        nega, _ = sb(es, [128, 8])
        g3 = [sb(es, [128, NT, 8], BF)[0] for _ in range(3)]
        es2 = ExitStack()
        xT, t_xT = sb(es2, [128, 16, NCOL], BF)
        psb = [es.enter_context(nc.psum_tensor("ps%d" % i, [128, 512], F32)) for i in range(8)]
        t_ps = [Tok(excl=True) for _ in range(8)]
        pctr = [0]

        def bank():
            i = pctr[0] % 8
            pctr[0] += 1
            return psb[i], t_ps[i]

        ident_f = cf[:, CF_ID:CF_ID + 128]
        ones_f = cf[:, CF_ONE:CF_ONE + 128]
        ident_b = cb[:, CB_ID:CB_ID + 128]
        ones_b = cb[:, CB_ONE:CB_ONE + 128]

        S.dma("sp", cf[:], cf_d[:, :], writes=[t_cf])
        S.dma("sp", cb[:], cb_d[:, :], writes=[t_cb])
        S.dma("sp", pv[:], pv_d[:, :], writes=[t_pv])

        WL = []
        WL.append((w_in, O_BT, 16))
        for c in range(4):
            WL += [(w_in, O_GA + 128 * c, 128), (w_in, O_GB + 128 * c, 128), (w_in, O_CG + 128 * c, 128)]
        for h in range(4):
            WL += [(w_mk, 128 * h, 128), (w_mv, 128 * h, 128), (w_in, O_MQ + 128 * h, 128), (w_in, O_MG + 128 * h, 128)]
        for h in range(8):
            WL += [(w_in, O_K + 128 * h, 128), (w_in, O_V + 128 * h, 128), (w_in, O_Q + 128 * h, 128), (w_in, O_DG + 128 * h, 128)]
        wst = {"issued": 0, "used": 0}

        def w_issue():
            i = wst["issued"]
            if i >= len(WL):
                return
            src, c0, n = WL[i]
            t, tk = wsl[i % NW]
            S.dma("pool", t[:, :, 0:n], src[:, c0:c0 + n].rearrange("(c p) n -> p c n", p=128), writes=[tk])
            wst["issued"] += 1

        def w_next():
            i = wst["used"]
            while wst["issued"] < min(len(WL), i + NW - 1) or wst["issued"] <= i:
                w_issue()
            wst["used"] += 1
            return wsl[i % NW]

        def proj_fm(wt, wtk, c0, n, ncols=128):
            bk, tb = bank()
            for k in range(16):
                MM(bk[0:ncols, 0:n], wt[:, k, 0:ncols], xT[:, k, c0:c0 + n], k == 0, k == 15, [wtk, t_xT], [tb])
            return bk, tb

        with ExitStack() as ph, contextlib.suppress(_Skip):
            if STAGE < 0:
                raise _Skip()
            xs = [sb(ph, [128, 2048], BF) for _ in range(3)]
            for i in range(NT):
                t, tk = xs[i % 3]
                src = xall[128 * i:128 * i + 128, :]
                S.dma("pool", t[:].rearrange("p (a n) -> p a n", n=512), src.rearrange("p (a n) -> p a n", n=512), writes=[tk])
                for half in range(2):
                    bk, tb = bank()
                    bkb = bk[:].bitcast(BF)
                    for c in range(8):
                        TR(bkb[:, 128 * c:128 * c + 128], t[:, 128 * (8 * half + c):128 * (8 * half + c) + 128], ident_b, [tk, t_cb], [tb])
                    src_v = bkb[:, 0:1024].rearrange("p (c n) -> p c n", c=8)
                    dst = xT[:, 8 * half:8 * half + 8, 128 * i:128 * i + 128]
                    CP("act" if half == 0 else "dve", dst, src_v, [tb], [t_xT])
            S.barrier()

        with ExitStack() as ph, contextlib.suppress(_Skip):
            if STAGE < 1:
                raise _Skip()
            bd, t_bd = sb(ph, [128, NT, 16])
            tmp8, t_tmp8 = sb(ph, [128, NT, 8])
            gm, t_gm = sb(ph, [128, 16, 8], BF)
            wt, wtk = w_next()
            bk, tb = bank()
            for i in range(NT):
                for k in range(16):
                    MM(bk[:, 16 * i:16 * i + 16], xT[:, k, 128 * i:128 * i + 128], wt[:, k, 0:16], k == 0, k == 15, [wtk, t_xT], [tb])
            CP("dve", bd[:], bk[:, 0:16 * NT].rearrange("p (t c) -> p t c", c=16), [tb], [t_bd])
            if SUB < 1:
                raise _Skip()
            ACT(bt[:], bd[:, :, 0:8], AF.Sigmoid, [t_bd], [t_sm])
            TS("dve", nbt[:], bt[:], -1.0, ALU.mult, [t_sm], [t_sm])
            if SUB < 2:
                raise _Skip()
            ACT(nega[:], pv[:, PV_AL:PV_AL + 8], AF.Exp, [t_pv], [t_sm])
            TS("dve", nega[:], nega[:], -1.0, ALU.mult, [t_sm], [t_sm])
            if SUB < 3:
                raise _Skip()
            TT("dve", tmp8[:], bd[:, :, 8:16], pv[:, PV_DT:PV_DT + 8].unsqueeze(1).broadcast_to([128, NT, 8]), ALU.add, [t_bd, t_pv], [t_tmp8])
            ACT(tmp8[:], tmp8[:], AF.Exp, [t_tmp8], [t_tmp8])
            ACT(tmp8[:], tmp8[:], AF.Ln, [t_tmp8], [t_tmp8], bias=1.0)
            TT("dve", gtk[:], tmp8[:], nega[:].unsqueeze(1).broadcast_to([128, NT, 8]), ALU.mult, [t_tmp8, t_sm], [t_sm])
            if SUB < 4:
                raise _Skip()
            split3(gtk[:], t_sm, [g3[0][:], g3[1][:], g3[2][:]], t_sm, tmp8[:], t_tmp8)
            bk, tb = bank()
            for i in range(NT):
                tri = cb[:, CB_TRIP:CB_TRIP + 128] if i < 16 else cb[:, CB_TRIS:CB_TRIS + 128]
                blk = ones_b if i < 16 else cb[:, CB_BLKS:CB_BLKS + 128]
                for j in range(3):
                    MM(bk[:, 8 * i:8 * i + 8], tri, g3[j][:, i, :], j == 0, j == 2, [t_cb, t_sm], [tb])
                for j in range(3):
                    MM(bk[:, 256 + 8 * i:256 + 8 * i + 8], blk, g3[j][:, i, :], j == 0, j == 2, [t_cb, t_sm], [tb])
            CP("dve", gc_[:], bk[:, 0:8 * NT].rearrange("p (t c) -> p t c", c=8), [tb], [t_sm])
            CP("act", tmp8[:], bk[:, 256:256 + 8 * NT].rearrange("p (t c) -> p t c", c=8), [tb], [t_tmp8])
            if SUB < 5:
                raise _Skip()
            TS("dve", ngc[:], gc_[:], -1.0, ALU.mult, [t_sm], [t_sm])
            ACT(ee[:], gc_[:], AF.Exp, [t_sm], [t_sm])
            TS("dve", nee[:], ee[:], -1.0, ALU.mult, [t_sm], [t_sm])
            ACT(glb[:], tmp8[:], AF.Exp, [t_tmp8], [t_sm])
            TT("dve", tmp8[:], tmp8[:], gc_[:], ALU.subtract, [t_tmp8, t_sm], [t_tmp8])
            ACT(egl[:], tmp8[:], AF.Exp, [t_tmp8], [t_sm])
            if SUB < 6:
                raise _Skip()
            bk, tb = bank()
            for j in range(3):
                TT("dve", gm[:], g3[j][:, 16, :].unsqueeze(1).broadcast_to([128, 16, 8]),
                   cf[:, CF_ROWM:CF_ROWM + 16].unsqueeze(2).broadcast_to([128, 16, 8]), ALU.mult, [t_sm, t_cf], [t_gm])
                MM(bk[:, 0:128], ones_b, gm[:].rearrange("p s h -> p (s h)"), j == 0, j == 2, [t_cb, t_gm], [tb])
            ACT(glbs[:], bk[:, 0:128].rearrange("p (s h) -> p s h", h=8), AF.Exp, [tb], [t_sm])
            if SUB < 9:
                raise _Skip()
            S.barrier()

        if os.environ.get("K_BAR"):
            S.barrier()
        OWN_CH = [(896, 512), (1408, 512), (1920, 256)]
        with ExitStack() as ph, contextlib.suppress(_Skip):
            if STAGE < 2:
                raise _Skip()
            ubuf, t_ub = sb(ph, [128, 1152])
            usamp, t_us = sb(ph, [128, 16, 38])
            hc, t_hc = sb(ph, [128, 4, NM])
            sgate, t_sg = sb(ph, [128, 4, NM], BF)
            sgt, t_sgt = sb(ph, [128, 512])
            stg, t_stg = sb(ph, [128, 512])
            scs, t_scs = sb(ph, [120, 512])
            S.dma("sp", oconv_s[:, 0:22, :], sconv[:, 8:30, :])
            unew_s, t_uns = sb(ph, [128, 4, 128])
            for c in range(4):
                for g4 in range(4):
                    S.dma("sp", scs[:, 0:128], sconv[4 * g4:4 * g4 + 4, :, 128 * c:128 * c + 128].rearrange("s r n -> (s r) n"), writes=[t_scs])
                    bk, tb = bank()
                    TR(bk[:, 0:120], scs[0:120, 0:128], ident_f[0:120, 0:120], [t_scs, t_cf], [tb])
                    CP("act", usamp[:, 4 * g4:4 * g4 + 4, 0:30], bk[:, 0:120].rearrange("p (s r) -> p s r", r=30), [tb], [t_us])
                wa, ta = w_next()
                wb, tbk = w_next()
                for (c0, n) in OWN_CH:
                    bka, tba = proj_fm(wa, ta, c0, n)
                    bkb_, tbb = proj_fm(wb, tbk, c0, n)
                    ACT(sgt[:, 0:n], bkb_[:, 0:n], AF.Sigmoid, [tbb], [t_sgt])
                    if c0 < 1920:
                        TT("dve", ubuf[:, c0 - 896:c0 - 896 + n], bka[:, 0:n], sgt[:, 0:n], ALU.mult, [tba, t_sgt], [t_ub])
                    else:
                        TT("dve", ubuf[:, 1024:1152], bka[:, 0:128], sgt[:, 0:128], ALU.mult, [tba, t_sgt], [t_ub])
                        TT("dve", unew_s[:, c, :], bka[:, 128:256], sgt[:, 128:256], ALU.mult, [tba, t_sgt], [t_uns])
                        CP("act", usamp[:, :, 30:38], unew_s[:, c, :].rearrange("p (s l) -> p s l", l=8), [t_uns], [t_us])
                wg, tg = w_next()
                for (c0, n) in [(1024, 512), (1536, 512), (2048, 128)]:
                    bkg, tbg = proj_fm(wg, tg, c0, n)
                    ACT(sgate[:, c, c0 - 1024:c0 - 1024 + n], bkg[:, 0:n], AF.Silu, [tbg], [t_sg])
                cw = lambda j: pv[:, PV_CW + 31 * c + j:PV_CW + 31 * c + j + 1]
                TS("dve", hc[:, c, 0:1024], ubuf[:, 98:98 + 1024], cw(0), ALU.mult, [t_ub, t_pv], [t_hc],
                   s2=pv[:, PV_CB + c:PV_CB + c + 1], op1=ALU.add)
                for j in range(1, 31):
                    STT(hc[:, c, 0:1024], ubuf[:, 98 + j:98 + j + 1024], cw(j), hc[:, c, 0:1024], ALU.mult, ALU.add, [t_ub, t_pv, t_hc], [t_hc])
                hs = hc[:, c, 1024:1152].rearrange("p (s l) -> p s l", l=8)
                TS("dve", hs, usamp[:, :, 0:8], cw(0), ALU.mult, [t_us, t_pv], [t_hc], s2=pv[:, PV_CB + c:PV_CB + c + 1], op1=ALU.add)
                for j in range(1, 31):
                    STT(hs, usamp[:, :, j:j + 8], cw(j), hs, ALU.mult, ALU.add, [t_us, t_pv, t_hc], [t_hc])
                bk, tb = bank()
                TR(bk[0:30, 0:128], ubuf[:, 1122:1152], ident_f, [t_ub, t_cf], [tb])
                CP("act", stg[0:30, 128 * c:128 * c + 128], bk[0:30, 0:128], [tb], [t_stg])
            S.dma("sp", oconv_p[:, :], stg[0:30, :], reads=[t_stg])
            stg2, t_stg2 = sb(ph, [128, 512])
            for c in range(4):
                bk, tb = bank()
                TR(bk[:, 0:128], unew_s[:, c, :], ident_f, [t_uns, t_cf], [tb])
                CP("act", stg2[:, 128 * c:128 * c + 128], bk[:, 0:128], [tb], [t_stg2])
            for s_ in range(16):
                S.dma("sp", oconv_s[s_, 22:30, :], stg2[8 * s_:8 * s_ + 8, :], reads=[t_stg2])
            sq, t_sq = sb(ph, [128, 512])
            hl = [sb(ph, [128, 512], BF) for _ in range(4)]
            mean, t_mean = sb(ph, [128, 512])
            var, t_var = sb(ph, [128, 512])
            rstd, t_rstd = sb(ph, [128, 512])
            xc, t_xc = sb(ph, [128, 512])
            for (m0, n) in [(0, 512), (512, 512), (1024, 128)]:
                bk1, tb1 = bank()
                bk2, tb2 = bank()
                for c in range(4):
                    CP("act", hl[0][0][:, 0:n], hc[:, c, m0:m0 + n], [t_hc], [hl[0][1]])
                    TT("dve", hl[1][0][:, 0:n], hc[:, c, m0:m0 + n], hl[0][0][:, 0:n], ALU.subtract, [t_hc, hl[0][1]], [hl[1][1]])
                    MM(bk1[:, 0:n], ones_b, hl[0][0][:, 0:n], c == 0, False, [t_cb, hl[0][1]], [tb1])
                    MM(bk1[:, 0:n], ones_b, hl[1][0][:, 0:n], False, c == 3, [t_cb, hl[1][1]], [tb1])
                for c in range(4):
                    ACT(sq[:, 0:n], hc[:, c, m0:m0 + n], AF.Square, [t_hc], [t_sq])
                    CP("act", hl[2][0][:, 0:n], sq[:, 0:n], [t_sq], [hl[2][1]])
                    TT("dve", hl[3][0][:, 0:n], sq[:, 0:n], hl[2][0][:, 0:n], ALU.subtract, [t_sq, hl[2][1]], [hl[3][1]])
                    MM(bk2[:, 0:n], ones_b, hl[2][0][:, 0:n], c == 0, False, [t_cb, hl[2][1]], [tb2])
                    MM(bk2[:, 0:n], ones_b, hl[3][0][:, 0:n], False, c == 3, [t_cb, hl[3][1]], [tb2])
                ACT(mean[:, 0:n], bk1[:, 0:n], AF.Copy, [tb1], [t_mean], scale=1.0 / 512)
                TT("dve", var[:, 0:n], mean[:, 0:n], mean[:, 0:n], ALU.mult, [t_mean], [t_var])
                STT(var[:, 0:n], bk2[:, 0:n], 1.0 / 512, var[:, 0:n], ALU.mult, ALU.subtract, [tb2, t_var], [t_var])
                TS("dve", var[:, 0:n], var[:, 0:n], 1e-5, ALU.add, [t_var], [t_var])
                ACT(var[:, 0:n], var[:, 0:n], AF.Ln, [t_var], [t_var])
                ACT(rstd[:, 0:n], var[:, 0:n], AF.Exp, [t_var], [t_rstd], scale=-0.5)
                for c in range(4):
                    TT("dve", xc[:, 0:n], hc[:, c, m0:m0 + n], mean[:, 0:n], ALU.subtract, [t_hc, t_mean], [t_xc])
                    TT("dve", xc[:, 0:n], xc[:, 0:n], rstd[:, 0:n], ALU.mult, [t_xc, t_rstd], [t_xc])
                    ACT(xc[:, 0:n], xc[:, 0:n], AF.Silu, [t_xc, t_pv], [t_xc],
                        scale=pv[:, PV_CLG + c:PV_CLG + c + 1], bias=pv[:, PV_CLB + c:PV_CLB + c + 1])
                    TT("dve", mixT[:, c, m0:m0 + n], xc[:, 0:n], sgate[:, c, m0:m0 + n], ALU.mult, [t_xc, t_sg], [t_mix[c]])
            S.barrier()

        MEM_CH = [(1024, 512), (1536, 512), (2048, 128)]
        with ExitStack() as ph, contextlib.suppress(_Skip):
            if STAGE < 3:
                raise _Skip()
            mT, t_mT = sb(ph, [128, 16, 256], BF)
            xs3, t_xs3 = sb(ph, [128, 2048], BF)
            for j in range(2):
                S.dma("pool", xs3[:].rearrange("p (a n) -> p a n", n=512), memp[128 * j:128 * j + 128, :].rearrange("p (a n) -> p a n", n=512), writes=[t_xs3])
                for half in range(2):
                    bk, tb = bank()
                    bkb = bk[:].bitcast(BF)
                    for c in range(8):
                        TR(bkb[:, 128 * c:128 * c + 128], xs3[:, 128 * (8 * half + c):128 * (8 * half + c) + 128], ident_b, [t_xs3, t_cb], [tb])
                    CP("act" if half == 0 else "dve", mT[:, 8 * half:8 * half + 8, 128 * j:128 * j + 128],
                       bkb[:, 0:1024].rearrange("p (c n) -> p c n", c=8), [tb], [t_mT])
            mqT, t_mq = sb(ph, [128, NM], BF)
            mgT, t_mg = sb(ph, [128, NM], BF)
            KTp, t_ktp = sb(ph, [128, 256], BF)
            Vp, t_vp = sb(ph, [128, 2, 128], BF)
            kvst, t_kvst = sb(ph, [128, 2, 2, 128])
            kc, t_kc = sb(ph, [128, 16, 2, 128], BF)
            vc, t_vc = sb(ph, [128, 16, 2, 128], BF)
            kcT, t_kct = sb(ph, [128, 16, 256], BF)
            mqm, t_mqm = sb(ph, [128, 16, 128], BF)
            pf, t_pf = sb(ph, [128, 256])
            pn, t_pn = sb(ph, [128, 256], BF)
            pT, t_pT = sb(ph, [128, 2, 128], BF)
            mx, t_mx = sb(ph, [128, 4])
            for h in range(4):
                wk, tk_ = w_next()
                wv, tv_ = w_next()
                bk, tb = bank()
                for k in range(16):
                    MM(bk[:, 0:256], wk[:, k, :], mT[:, k, :], k == 0, k == 15, [tk_, t_mT], [tb])
                CP("act", KTp[:], bk[:, 0:256], [tb], [t_ktp])
                bk, tb = bank()
                for mc in range(2):
                    for k in range(16):
                        MM(bk[:, 128 * mc:128 * mc + 128], mT[:, k, 128 * mc:128 * mc + 128], wk[:, k, :], k == 0, k == 15, [tk_, t_mT], [tb])
                    for k in range(16):
                        MM(bk[:, 256 + 128 * mc:256 + 128 * mc + 128], mT[:, k, 128 * mc:128 * mc + 128], wv[:, k, :], k == 0, k == 15, [tv_, t_mT], [tb])
                CP("dve", kvst[:].rearrange("p a b d -> p (a b d)"), bk[:, 0:512], [tb], [t_kvst])
                CP("act", Vp[:].rearrange("p b d -> p (b d)"), bk[:, 256:512], [tb], [t_vp])
                S.dma("sp", omk[:, 128 * h:128 * h + 128].rearrange("(mc m) d -> m mc d", m=128), kvst[:, 0, :, :], reads=[t_kvst])
                S.dma("sp", omv[:, 128 * h:128 * h + 128].rearrange("(mc m) d -> m mc d", m=128), kvst[:, 1, :, :], reads=[t_kvst])
                S.dma("pool", kc[:], ck[:, :, h, :].rearrange("s (mc m) d -> m s mc d", m=128), writes=[t_kc])
                S.dma("pool", vc[:], cv[:, :, h, :].rearrange("s (mc m) d -> m s mc d", m=128), writes=[t_vc])
                wq, tq_ = w_next()
                wg, tg_ = w_next()
                for (c0, n) in MEM_CH:
                    bk, tb = proj_fm(wq, tq_, c0, n)
                    ACT(mqT[:, c0 - 1024:c0 - 1024 + n], bk[:, 0:n], AF.Copy, [tb], [t_mq], scale=128.0 ** -0.5)
                    bk, tb = proj_fm(wg, tg_, c0, n)
                    ACT(mgT[:, c0 - 1024:c0 - 1024 + n], bk[:, 0:n], AF.Silu, [tb], [t_mg])
                for s in range(16):
                    if s % 4 == 0:
                        bk, tb = bank()
                        bkb = bk[:].bitcast(BF)
                    for mc in range(2):
                        o = (s % 4) * 256 + mc * 128
                        TR(bkb[:, o:o + 128], kc[:, s, mc, :], ident_b, [t_kc, t_cb], [tb])
                    if s % 4 == 3:
                        CP("act", kcT[:, s - 3:s + 1, :], bkb[:, 0:1024].rearrange("p (s m) -> p s m", m=256), [tb], [t_kct])
                TT("dve", mqm[:], mqT[:, 1024:1152].unsqueeze(1).broadcast_to([128, 16, 128]),
                   cb[:, CB_COLM:CB_COLM + 2048].rearrange("p (s i) -> p s i", i=128), ALU.mult, [t_mq, t_cb], [t_mqm])
                for ti in range(9):
                    m0 = 128 * ti
                    bk, tb = bank()
                    if ti < 8:
                        MM(bk[:, 0:256], mqT[:, m0:m0 + 128], KTp[:], True, True, [t_mq, t_ktp], [tb])
                    else:
                        for s in range(16):
                            MM(bk[:, 0:256], mqm[:, s, :], kcT[:, s, :], s == 0, s == 15, [t_mqm, t_kct], [tb])
                    S.op("dve", lambda e, bk=bk: e.reduce_max(out=mx[:, 0:1], in_=bk[:, 0:256], axis=AX.X), [tb], [t_mx])
                    TS("dve", mx[:, 1:2], mx[:, 0:1], -1.0, ALU.mult, [t_mx], [t_mx])
                    ACT(pf[:], bk[:, 0:256], AF.Exp, [tb, t_mx], [t_pf, t_mx], bias=mx[:, 1:2], accum=mx[:, 2:3])
                    S.op("dve", lambda e: e.reciprocal(out=mx[:, 3:4], in_=mx[:, 2:3]), [t_mx], [t_mx])
                    TS("dve", pn[:], pf[:], mx[:, 3:4], ALU.mult, [t_pf, t_mx], [t_pn])
                    bk2, tb2 = bank()
                    bk2b = bk2[:].bitcast(BF)
                    for mc in range(2):
                        TR(bk2b[:, 128 * mc:128 * mc + 128], pn[:, 128 * mc:128 * mc + 128], ident_b, [t_pn, t_cb], [tb2])
                    CP("act", pT[:].rearrange("p a b -> p (a b)"), bk2b[:, 0:256], [tb2], [t_pT])
                    bk3, tb3 = bank()
                    if ti < 8:
                        for mc in range(2):
                            MM(bk3[:, 0:128], Vp[:, mc, :], pT[:, mc, :], mc == 0, mc == 1, [t_vp, t_pT], [tb3])
                    else:
                        for s in range(16):
                            for mc in range(2):
                                MM(bk3[:, 8 * s:8 * s + 8], vc[:, s, mc, :], pT[:, mc, 8 * s:8 * s + 8], mc == 0, mc == 1, [t_vc, t_pT], [tb3])
                    TT("dve", mixT[:, 12 + h, m0:m0 + 128], bk3[:, 0:128], mgT[:, m0:m0 + 128], ALU.mult, [tb3, t_mg], [t_mix[12 + h]])
            S.barrier()

        KV_CH = [(0, 512), (512, 512), (1024, 512), (1536, 512), (2048, 128)]
        with ExitStack() as ph, contextlib.suppress(_Skip):
            if STAGE < 4:
                raise _Skip()
            kT, t_kT = sb(ph, [128, NCOL], BF)
            vT, t_vT = sb(ph, [128, NCOL], BF)
            qT, t_qT = sb(ph, [128, NM], BF)
            gT, t_gT = sb(ph, [128, NM], BF)
            pre = [sb(ph, [128, 515]) for _ in range(2)]
            pres, t_pres = sb(ph, [128, 16, 11])
            sqb, t_sqb = sb(ph, [128, 512], BF)
            acc, t_acc = sb(ph, [128, 512])
            lnb_, t_lnb = acc, t_acc
            sq48, t_sq48 = sb(ph, [48, 128])
            qst, t_qst = sb(ph, [128, 3])
            qst_s, t_qsts = sb(ph, [128, 48])
            ost, t_ost = sb(ph, [48, 128])
            Sf, t_Sf = sb(ph, [128, 128])
            Sb, t_Sb = sb(ph, [128, 128], BF)
            Ss, t_Ss = sb(ph, [128, 16, 128])
            Ssb, t_Ssb = sb(ph, [128, 16, 128], BF)
            Sso, t_Sso = Ss, t_Ss
            kTm, t_kTm = sb(ph, [128, 16, 128], BF)
            qTm, t_qTm = sb(ph, [128, 16, 128], BF)
            kdm, t_kdm = qTm, t_qTm
            GT = 4
            gB3 = [sb(ph, [128, GT, 128], BF) for _ in range(3)]
            decT, t_decT = sb(ph, [128, GT, 128])
            decTs, t_decTs = sb(ph, [128, GT, 128], BF)
            P0, t_P0 = sb(ph, [128, GT, 128], BF)
            Pb = [sb(ph, [128, GT, 128], BF) for _ in range(2)]
            Qb = [sb(ph, [128, GT, 128], BF) for _ in range(2)]
            Yb = [sb(ph, [128, GT, 128], BF) for _ in range(2)]
            XTb = [sb(ph, [128, GT, 128], BF) for _ in range(2)]
            PbH = [[Tok(), Tok()] for _ in range(2)]
            QbH = [[Tok(), Tok()] for _ in range(2)]
            YbH = [[Tok(), Tok()] for _ in range(2)]
            XTH = [[Tok(), Tok()] for _ in range(2)]
            aqkb = [sb(ph, [128, GT, 128], BF) for _ in range(2)]
            kdb = [sb(ph, [128, GT, 128], BF) for _ in range(2)]
            vtb = [sb(ph, [128, GT, 128], BF) for _ in range(2)]
            rbf, t_rbf = sb(ph, [128, 128], BF)
            ubf, t_ubf = sb(ph, [128, 128], BF)
            tsb, t_tsb = sb(ph, [128, 128])
            osb, t_osb = sb(ph, [128, 128])
            junk, t_junk = sb(ph, [128, 128], BF)
            onb, t_onb = sb(ph, [128, 128], BF)
            sm4, t_sm4 = sb(ph, [128, 4])


            def conv_stream(h, kind, wt, wtk, chunks, dstT, t_dst, dcol0, norm, qscale):
                fo = kind * 1024 + 128 * h
                fc = fo // 128
                qw = lambda j: pv[:, PV_QW + 4 * fc + j:PV_QW + 4 * fc + j + 1]
                S.dma("sp", sq48[:], sqkv[:, :, fo:fo + 128].rearrange("s r n -> (s r) n"), writes=[t_sq48])
                bk, tb = bank()
                TR(bk[:, 0:48], sq48[0:48, 0:128], ident_f[0:48, 0:48], [t_sq48, t_cf], [tb])
                CP("act", pres[:, :, 0:3], bk[:, 0:48].rearrange("p (s r) -> p s r", r=3), [tb], [t_pres])
                pi = 0
                S.op("pool", lambda e, p=pre[0][0]: e.memset(p[:, 0:3], 0.0), [], [pre[0][1]])
                for (c0, n) in chunks:
                    bk, tb = proj_fm(wt, wtk, c0, n)
                    npr = n if c0 + n <= 2048 else n - 128
                    skip = max(0, dcol0 - c0)
                    if npr > 0:
                        p_, tp_ = pre[pi]
                        ACT(p_[:, 3:3 + npr], bk[:, 0:npr], AF.Copy, [tb], [tp_])
                        TS("dve", acc[:, 0:npr], p_[:, 0:npr], qw(0), ALU.mult, [tp_, t_pv], [t_acc])
                        for j in range(1, 4):
                            STT(acc[:, 0:npr], p_[:, j:j + npr], qw(j), acc[:, 0:npr], ALU.mult, ALU.add, [tp_, t_pv, t_acc], [t_acc])
                        d0 = c0 - dcol0
                        ACT(dstT[:, d0 + skip:d0 + npr], acc[:, skip:npr], AF.Silu, [t_acc], [t_dst])
                        if c0 + npr == 2048:
                            CP("act", qst[:, 0:3], p_[:, npr:npr + 3], [tp_], [t_qst])
                        else:
                            p2, tp2 = pre[1 - pi]
                            CP("act", p2[:, 0:3], p_[:, npr:npr + 3], [tp_], [tp2])
                            pi = 1 - pi
                    if c0 + n > 2048:
                        CP("act", pres[:, :, 3:11], bk[:, npr:npr + 128].rearrange("p (s l) -> p s l", l=8), [tb], [t_pres])
                        av = acc[:, 0:128].rearrange("p (s l) -> p s l", l=8)
                        TS("dve", av, pres[:, :, 0:8], qw(0), ALU.mult, [t_pres, t_pv], [t_acc])
                        for j in range(1, 4):
                            STT(av, pres[:, :, j:j + 8], qw(j), av, ALU.mult, ALU.add, [t_pres, t_pv, t_acc], [t_acc])
                        d0 = 2048 - dcol0
                        ACT(dstT[:, d0:d0 + 128], acc[:, 0:128], AF.Silu, [t_acc], [t_dst])
                        CP("act", qst_s[:].rearrange("p (s r) -> p s r", r=3), pres[:, :, 8:11], [t_pres], [t_qsts])
                bk, tb = bank()
                TR(bk[0:3, 0:128], qst[:, 0:3], ident_f, [t_qst, t_cf], [tb])
                TR(bk[0:48, 128:256], qst_s[:, 0:48], ident_f, [t_qsts, t_cf], [tb])
                CP("act", ost[0:3, :], bk[0:3, 0:128], [tb], [t_ost])
                S.dma("sp", oqkv_p[:, fo:fo + 128], ost[0:3, :], reads=[t_ost])
                CP("act", ost[0:48, :], bk[0:48, 128:256], [tb], [t_ost])
                S.dma("sp", oqkv_s[:, :, fo:fo + 128].rearrange("s r n -> (s r) n"), ost[0:48, :], reads=[t_ost])
                if norm:
                    ntot = dstT.shape[1]
                    for m0 in range(0, ntot, 512):
                        n = min(512, ntot - m0)
                        ACT(sqb[:, 0:n], dstT[:, m0:m0 + n], AF.Square, [t_dst], [t_sqb])
                        bk, tb = bank()
                        MM(bk[:, 0:n], ones_b, sqb[:, 0:n], True, True, [t_cb, t_sqb], [tb])
                        TS("dve", lnb_[:, 0:n], bk[:, 0:n], 1e-6, ALU.add, [tb], [t_lnb])
                        ACT(lnb_[:, 0:n], lnb_[:, 0:n], AF.Ln, [t_lnb], [t_lnb])
                        ACT(lnb_[:, 0:n], lnb_[:, 0:n], AF.Exp, [t_lnb], [t_lnb], scale=-0.5)
                        if qscale != 1.0:
                            TS("dve", lnb_[:, 0:n], lnb_[:, 0:n], qscale, ALU.mult, [t_lnb], [t_lnb])
                        TT("dve", dstT[:, m0:m0 + n], dstT[:, m0:m0 + n], lnb_[:, 0:n], ALU.mult, [t_dst, t_lnb], [t_dst])

            def groupA(h, tiles, par):
                ng = len(tiles)
                i0 = tiles[0]
                samp = (i0 == 16)
                full = (i0 >= 8)
                L = 2 if samp else 6
                trib = cb[:, CB_TRIS:CB_TRIS + 128] if samp else cb[:, CB_TRIP:CB_TRIP + 128]
                ntrib = cb[:, CB_NTRS:CB_NTRS + 128] if samp else cb[:, CB_NTRP:CB_NTRP + 128]
                negb = cb[:, CB_NEGS:CB_NEGS + 128] if samp else cb[:, CB_NEGP:CB_NEGP + 128]
                strict = cb[:, CB_STRS:CB_STRS + 128] if samp else cb[:, CB_STRP:CB_STRP + 128]
                W = 128 * ng
                kd, t_kd = kdb[par]
                vt, t_vt = vtb[par]
                aq, t_aq = aqkb[par]
                XT, t_XT = XTb[par]
                v3 = lambda ap: ap.rearrange("p (t n) -> p t n", n=128)
                bk, tb = bank()
                bkb = bk[:].bitcast(BF)
                for t in range(ng):
                    c0 = 128 * (i0 + t)
                    TR(bkb[:, 256 * t:256 * t + 128], kT[:, c0:c0 + 128], ident_b, [t_kT, t_cb], [tb])
                    TR(bkb[:, 256 * t + 128:256 * t + 256], vT[:, c0:c0 + 128], ident_b, [t_vT, t_cb], [tb])
                for t in range(ng):
                    ACT(kd[:, t, :], bkb[:, 256 * t:256 * t + 128], AF.Copy, [tb, t_sm], [t_kd], scale=egl[:, i0 + t, h:h + 1])
                CP("dve", vt[:, 0:ng, :], bkb[:, 0:256 * ng].rearrange("p (t two n) -> p t two n", two=2, n=128)[:, :, 1, :], [tb], [t_vt])
                yield
                for j in range(3):
                    TT("pool", gB3[j][0][:, 0:ng, :], ones_b.unsqueeze(1).broadcast_to([128, ng, 128]),
                       g3[j][:, i0:i0 + ng, h:h + 1].broadcast_to([128, ng, 128]), ALU.mult, [t_cb, t_sm], [gB3[j][1]])
                bk, tb = bank()
                for t in range(ng):
                    o = bk[:, 128 * t:128 * t + 128]
                    for j in range(3):
                        MM(o, gB3[j][0][:, t, :], trib, j == 0, False, [gB3[j][1], t_cb], [tb])
                    for j in range(3):
                        MM(o, ntrib, gB3[j][0][:, t, :], False, False, [gB3[j][1], t_cb], [tb])
                    MM(o, ident_b, negb, False, True, [t_cb], [tb])
                ACT(decT[:, 0:ng, :], v3(bk[:, 0:W]), AF.Exp, [tb], [t_decT])
                TT("pool", decTs[:, 0:ng, :], decT[:, 0:ng, :], strict.unsqueeze(1).broadcast_to([128, ng, 128]), ALU.mult, [t_decT, t_cb], [t_decTs])
                yield
                bkG, tbG = bank()
                if full:
                    bkA, tbA = bank()
                for t in range(ng):
                    c0 = 128 * (i0 + t)
                    MM(bkG[:, 128 * t:128 * t + 128], kT[:, c0:c0 + 128], kT[:, c0:c0 + 128], True, True, [t_kT], [tbG])
                    if full:
                        MM(bkA[:, 128 * t:128 * t + 128], kT[:, c0:c0 + 128], qT[:, c0 - 1024:c0 - 1024 + 128], True, True, [t_kT, t_qT], [tbA])
                for t in range(ng):
                    STT(P0[:, t, :], bkG[:, 128 * t:128 * t + 128], nbt[:, i0 + t, h:h + 1], decTs[:, t, :], ALU.mult, ALU.mult,
                        [tbG, t_sm, t_decTs], [t_P0])
                if full:
                    TT("dve", aq[:, 0:ng, :], v3(bkA[:, 0:W]), decT[:, 0:ng, :], ALU.mult, [tbA, t_decT], [t_aq])
                Y0 = Yb[0][0]
                TT("pool", Y0[:, 0:ng, :], P0[:, 0:ng, :], ident_b.unsqueeze(1).broadcast_to([128, ng, 128]), ALU.add, [t_P0, t_cb], YbH[0])
                yield
                bk, tb = bank()
                bkb = bk[:].bitcast(BF)
                for t in range(ng):
                    TR(bkb[:, 128 * t:128 * t + 128], P0[:, t, :], ident_b, [t_P0, t_cb], [tb])
                Q0 = Qb[0][0]
                CP("act", Q0[:, 0:ng, :], v3(bkb[:, 0:W]), [tb], QbH[0])
                yield
                halves = [(0, 2), (2, 4)] if ng == 4 else [(0, ng)]
                v3h = lambda ap: ap.rearrange("p (t n) -> p t n", n=128)
                mms = []
                for hi, (a, b) in enumerate(halves):
                    bkP, tbP = bank()
                    bkQ, tbQ = bank()
                    for t in range(a, b):
                        o = 128 * (t - a)
                        MM(bkP[:, o:o + 128], Q0[:, t, :], P0[:, t, :], True, True, [QbH[0][hi], t_P0], [tbP])
                        MM(bkQ[:, o:o + 128], P0[:, t, :], Q0[:, t, :], True, True, [QbH[0][hi], t_P0], [tbQ])
                    mms.append((bkP, tbP, bkQ, tbQ))
                for hi, (a, b) in enumerate(halves):
                    bkP, tbP, bkQ, tbQ = mms[hi]
                    wd = 128 * (b - a)
                    CP("act", Pb[0][0][:, a:b, :], v3h(bkP[:, 0:wd]), [tbP], [PbH[0][hi]])
                    CP("dve", Qb[1][0][:, a:b, :], v3h(bkQ[:, 0:wd]), [tbQ], [QbH[1][hi]])
                yield
                pc, qc, yc = 0, 1, 0
                for k in range(1, L + 1):
                    P, Q, Y = Pb[pc][0], Qb[qc][0], Yb[yc][0]
                    Pn, Qn = Pb[1 - pc][0], Qb[1 - qc][0]
                    Yn = Yb[1 - yc][0] if k < L else XT
                    tYn = YbH[1 - yc] if k < L else XTH[par]
                    mms = []
                    for hi, (a, b) in enumerate(halves):
                        if k < L:
                            bkP, tbP = bank()
                            bkQ, tbQ = bank()
                        else:
                            bkP = tbP = bkQ = tbQ = None
                        bkY, tbY = bank()
                        for t in range(a, b):
                            o = 128 * (t - a)
                            if k < L:
                                MM(bkP[:, o:o + 128], Q[:, t, :], P[:, t, :], True, True, [QbH[qc][hi], PbH[pc][hi]], [tbP])
                                MM(bkQ[:, o:o + 128], P[:, t, :], Q[:, t, :], True, True, [QbH[qc][hi], PbH[pc][hi]], [tbQ])
                            MM(bkY[:, o:o + 128], Q[:, t, :], Y[:, t, :], True, True, [QbH[qc][hi], YbH[yc][hi]], [tbY])
                        mms.append((bkP, tbP, bkQ, tbQ, bkY, tbY))
                    for hi, (a, b) in enumerate(halves):
                        bkP, tbP, bkQ, tbQ, bkY, tbY = mms[hi]
                        wd = 128 * (b - a)
                        if k < L:
                            CP("act", Pn[:, a:b, :], v3h(bkP[:, 0:wd]), [tbP], [PbH[1 - pc][hi]])
                            CP("act" if k % 2 == 0 else "dve", Qn[:, a:b, :], v3h(bkQ[:, 0:wd]), [tbQ], [QbH[1 - qc][hi]])
                        TT("dve", Yn[:, a:b, :], v3h(bkY[:, 0:wd]), Y[:, a:b, :], ALU.add, [tbY, YbH[yc][hi]], [tYn[hi]])
                    pc, qc, yc = 1 - pc, 1 - qc, 1 - yc
                    yield

            def groupB(h, tiles, par):
                kd, t_kd = kdb[par]
                vt, t_vt = vtb[par]
                aq, t_aq = aqkb[par]
                XT, t_XT = XTb[par]
                for t, i in enumerate(tiles):
                    samp = (i == 16)
                    full = (i >= 8)
                    c0 = 128 * i
                    mc0 = c0 - 1024
                    sc = lambda arr: arr[:, i, h:h + 1]
                    bk, tb = bank()
                    if not samp:
                        MM(bk[:, 0:128], kT[:, c0:c0 + 128], Sb[:], True, True, [t_kT, t_Sb], [tb])
                        if full:
                            MM(bk[:, 128:256], qT[:, mc0:mc0 + 128], Sb[:], True, True, [t_qT, t_Sb], [tb])
                    else:
                        for s_ in range(16):
                            MM(bk[:, 0:128], kTm[:, s_, :], Ssb[:, s_, :], s_ == 0, s_ == 15, [t_kTm, t_Ssb], [tb])
                        for s_ in range(16):
                            MM(bk[:, 128:256], qTm[:, s_, :], Ssb[:, s_, :], s_ == 0, s_ == 15, [t_qTm, t_Ssb], [tb])
                    STT(rbf[:], bk[:, 0:128], sc(nee), vt[:, t, :], ALU.mult, ALU.add, [tb, t_sm, t_vt], [t_rbf])
                    if full:
                        ACT(tsb[:], bk[:, 128:256], AF.Copy, [tb, t_sm], [t_tsb], scale=sc(ee))
                    yield
                    bku, tbu = bank()
                    MM(bku[:, 0:128], XT[:, t, :], rbf[:], True, True, [XTH[par][(t // 2) if len(tiles) == 4 else 0], t_rbf], [tbu])
                    ACT(ubf[:], bku[:, 0:128], AF.Copy, [tbu, t_sm], [t_ubf], scale=sc(bt))
                    yield
                    if not samp:
                        bks, tbs = bank()
                        MM(bks[:, 0:128], kd[:, t, :], ubf[:], True, True, [t_kd, t_ubf], [tbs])
                        STT(Sb[:], Sf[:], sc(glb), bks[:, 0:128], ALU.mult, ALU.add, [t_Sf, t_sm, tbs], [t_Sb])
                        STT(Sf[:], Sf[:], sc(glb), bks[:, 0:128], ALU.mult, ALU.add, [t_Sf, t_sm, tbs], [t_Sf])
                        yield
                    else:
                        TT("dve", kdm[:], kd[:, t, :].unsqueeze(1).broadcast_to([128, 16, 128]),
                           cf[:, CF_ROWM:CF_ROWM + 16].unsqueeze(2).broadcast_to([128, 16, 128]), ALU.mult, [t_kd, t_cf], [t_kdm])
                        for s_ in range(16):
                            if s_ % 4 == 0:
                                bks, tbs = bank()
                            MM(bks[:, 128 * (s_ % 4):128 * (s_ % 4) + 128], kdm[:, s_, :], ubf[:], True, True, [t_kdm, t_ubf], [tbs])
                            if s_ % 4 == 3:
                                for s2 in range(s_ - 3, s_ + 1):
                                    STT(Sso[:, s2, :], Ss[:, s2, :], glbs[:, s2, h:h + 1], bks[:, 128 * (s2 % 4):128 * (s2 % 4) + 128],
                                        ALU.mult, ALU.add, [t_Ss, t_sm, tbs], [t_Ss])
                                yield
                    if full:
                        bk4, tb4 = bank()
                        MM(bk4[:, 0:128], aq[:, t, :], ubf[:], True, True, [t_aq, t_ubf], [tb4])
                        TT("dve", osb[:], bk4[:, 0:128], tsb[:], ALU.add, [tb4, t_tsb], [t_osb])
                        ACT(junk[:], osb[:], AF.Square, [t_osb], [t_junk, t_sm4], accum=sm4[:, 0:1])
                        yield
                        TS("dve", sm4[:, 1:2], sm4[:, 0:1], 1.0 / 128, ALU.mult, [t_sm4], [t_sm4], s2=1e-6, op1=ALU.add)
                        ACT(sm4[:, 2:3], sm4[:, 1:2], AF.Ln, [t_sm4], [t_sm4])
                        ACT(sm4[:, 3:4], sm4[:, 2:3], AF.Exp, [t_sm4], [t_sm4], scale=-0.5)
                        TS("dve", onb[:], osb[:], sm4[:, 3:4], ALU.mult, [t_osb, t_sm4], [t_onb])
                        yield
                        bko, tbo = bank()
                        bkob = bko[:].bitcast(BF)
                        TR(bkob[:, 0:128], onb[:], ident_b, [t_onb, t_cb], [tbo])
                        STT(mixT[:, 4 + h, mc0:mc0 + 128], bkob[:, 0:128], pv[:, PV_NW:PV_NW + 1], gT[:, mc0:mc0 + 128],
                            ALU.mult, ALU.mult, [tbo, t_pv, t_gT], [t_mix[4 + h]])
                        yield
                    if i == 15:
                        S.dma("sp", odelta_p[h, :, :], Sf[:], reads=[t_Sf])

            def run_il(gens, weights):
                gens = [g for g in gens if g is not None]
                alive = [True] * len(gens)
                while any(alive):
                    for gi, g in enumerate(gens):
                        if not alive[gi]:
                            continue
                        for _ in range(weights[gi]):
                            try:
                                next(g)
                            except StopIteration:
                                alive[gi] = False
                                break

            GROUPS = [[0, 1, 2, 3], [4, 5, 6, 7], [8, 9, 10, 11], [12, 13, 14, 15], [16]]

            for h in range(8):
                S.dma("sp", Ss[:], sdelta[:, h, :, :].rearrange("s k v -> k s v"), writes=[t_Ss])
                CP("pool", Ssb[:], Ss[:], [t_Ss], [t_Ssb])
                S.op("pool", lambda e: e.memset(Sf[:], 0.0), [], [t_Sf])
                S.op("pool", lambda e: e.memset(Sb[:], 0.0), [], [t_Sb])
                wk, tk_ = w_next()
                conv_stream(h, 1, wk, tk_, KV_CH, kT, t_kT, 0, True, 1.0)
                wv, tv_ = w_next()
                conv_stream(h, 2, wv, tv_, KV_CH, vT, t_vT, 0, False, 1.0)
                wq, tq_ = w_next()
                conv_stream(h, 0, wq, tq_, OWN_CH, qT, t_qT, 1024, True, 128.0 ** -0.5)
                wg, tg_ = w_next()
                for (c0, n) in MEM_CH:
                    bk, tb = proj_fm(wg, tg_, c0, n)
                    ACT(gT[:, c0 - 1024:c0 - 1024 + n], bk[:, 0:n], AF.Silu, [tb], [t_gT])
                colm = cb[:, CB_COLM:CB_COLM + 2048].rearrange("p (s i) -> p s i", i=128)
                TT("dve", kTm[:], kT[:, 2048:2176].unsqueeze(1).broadcast_to([128, 16, 128]), colm, ALU.mult, [t_kT, t_cb], [t_kTm])
                TT("pool", qTm[:], qT[:, 1024:1152].unsqueeze(1).broadcast_to([128, 16, 128]), colm, ALU.mult, [t_qT, t_cb], [t_qTm])
                prevB = None
                for gi, tiles in enumerate(GROUPS):
                    ga = groupA(h, tiles, gi % 2)
                    if prevB is None:
                        run_il([ga], [1])
                    else:
                        run_il([ga, prevB], [int(os.environ.get('K_WA', '1')), int(os.environ.get('K_WB', '2'))])
                    prevB = groupB(h, tiles, gi % 2)
                run_il([prevB], [1])
                S.dma("sp", odelta_s[:, h, :, :].rearrange("s k v -> k s v"), Sso[:], reads=[t_Sso])
            S.barrier()

        es2.close()
        with ExitStack() as ph, contextlib.suppress(_Skip):
            if STAGE < 5:
                raise _Skip()
            wo, t_wo = sb(ph, [128, 16, 2048], BF)
            lngb, t_lngb = sb(ph, [128, 2, 2048])
            xr = [sb(ph, [128, 2048]) for _ in range(2)]
            ypb = [sb(ph, [128, 2048]) for _ in range(2)]
            yo = [sb(ph, [128, 2048]) for _ in range(2)]
            jkb = [sb(ph, [128, 2048], BF) for _ in range(2)]
            stb = [sb(ph, [128, 8]) for _ in range(2)]
            for k4 in range(16):
                S.dma("pool", wo[:, k4, :].rearrange("p (a n) -> p a n", n=512),
                      w_out[128 * k4:128 * k4 + 128, :].rearrange("p (a n) -> p a n", n=512), writes=[t_wo])
            S.dma("sp", lngb[:], lngb_d[:, :, :], writes=[t_lngb])
            for ti in range(9):
                xr_, txr = xr[ti % 2]
                yo_, tyo = yo[ti % 2]
                yp, t_yp = ypb[ti % 2]
                jk, t_jk = jkb[ti % 2]
                st, t_st = stb[ti % 2]
                S.dma("sp", xr_[:], xall[1024 + 128 * ti:1024 + 128 * ti + 128, :], writes=[txr])
                for nb in range(4):
                    bk, tb = bank()
                    for k in range(16):
                        MM(bk[:, 0:512], mixT[:, k, 128 * ti:128 * ti + 128], wo[:, k, 512 * nb:512 * nb + 512], k == 0, k == 15, [t_mix[k], t_wo], [tb])
                    STT(yp[:, 512 * nb:512 * nb + 512], xr_[:, 512 * nb:512 * nb + 512], ALPHA, bk[:, 0:512], ALU.mult, ALU.add, [txr, tb], [t_yp])
                S.op("dve", lambda e, st=st, yp=yp: e.reduce_sum(out=st[:, 0:1], in_=yp[:], axis=AX.X), [t_yp], [t_st])
                ACT(jk[:], yp[:], AF.Square, [t_yp], [t_jk, t_st], accum=st[:, 1:2])
                TS("dve", st[:, 2:3], st[:, 0:1], 1.0 / 2048, ALU.mult, [t_st], [t_st])
                TT("dve", st[:, 3:4], st[:, 2:3], st[:, 2:3], ALU.mult, [t_st], [t_st])
                STT(st[:, 4:5], st[:, 1:2], 1.0 / 2048, st[:, 3:4], ALU.mult, ALU.subtract, [t_st], [t_st])
                TS("dve", st[:, 4:5], st[:, 4:5], 1e-5, ALU.add, [t_st], [t_st])
                ACT(st[:, 5:6], st[:, 4:5], AF.Ln, [t_st], [t_st])
                ACT(st[:, 6:7], st[:, 5:6], AF.Exp, [t_st], [t_st], scale=-0.5)
                STT(st[:, 7:8], st[:, 2:3], -1.0, st[:, 6:7], ALU.mult, ALU.mult, [t_st], [t_st])
                ACT(yp[:], yp[:], AF.Identity, [t_yp, t_st], [t_yp], scale=st[:, 6:7], bias=st[:, 7:8])
                TT("dve", yp[:], yp[:], lngb[:, 0, :], ALU.mult, [t_yp, t_lngb], [t_yp])
                TT("pool", yo_[:], yp[:], lngb[:, 1, :], ALU.add, [t_yp, t_lngb], [tyo])
                S.dma("sp", y_d[128 * ti:128 * ti + 128, :], yo_[:], reads=[tyo])
            S.barrier()

        with nc.Block() as block:
            @block.tensor
            def _(e):
                S.replay("pe", e)

            @block.scalar
            def _(e):
                S.replay("act", e)

            @block.vector
            def _(e):
                S.replay("dve", e)

            @block.gpsimd
            def _(e):
                S.replay("pool", e)

            @block.sync
            def _(e):
                S.replay("sp", e)
        build_nc.stats = dict(S.cnt)
    return nc


def _consts():
    idx = np.arange(128)
    blk = idx // 8
    cf = np.zeros((128, NCF), np.float32)
    cf[:, CF_ID:CF_ID + 128] = np.eye(128)
    cf[:, CF_ONE:CF_ONE + 128] = 1.0
    t, i = idx[:, None], idx[None, :]
    cf[:, CF_TRIP:CF_TRIP + 128] = (t <= i)
    cf[:, CF_NEGP:CF_NEGP + 128] = np.where(i >= t, 0.0, NEG)
    same = (blk[:, None] == blk[None, :])
    cf[:, CF_TRIS:CF_TRIS + 128] = (t <= i) & same
    cf[:, CF_BLKS:CF_BLKS + 128] = same
    cf[:, CF_NEGS:CF_NEGS + 128] = np.where((i >= t) & same, 0.0, NEG)
    cf[:, CF_ROWM:CF_ROWM + 16] = (blk[:, None] == np.arange(16)[None, :])
    cb = np.zeros((128, NCB), np.float32)
    cb[:, CB_ID:CB_ID + 128] = np.eye(128)
    cb[:, CB_ONE:CB_ONE + 128] = 1.0
    cb[:, CB_STRP:CB_STRP + 128] = (i > t)
    cb[:, CB_STRS:CB_STRS + 128] = (i > t) & same
    colm = np.zeros((128, 16, 128), np.float32)
    for s in range(16):
        colm[:, s, 8 * s:8 * s + 8] = 1.0
    cb[:, CB_COLM:CB_COLM + 2048] = colm.reshape(128, 2048)
    cb[:, CB_TRIP:CB_TRIP + 128] = cf[:, CF_TRIP:CF_TRIP + 128]
    cb[:, CB_TRIS:CB_TRIS + 128] = cf[:, CF_TRIS:CF_TRIS + 128]
    cb[:, CB_BLKS:CB_BLKS + 128] = cf[:, CF_BLKS:CF_BLKS + 128]
    cb[:, CB_NEGP:CB_NEGP + 128] = cf[:, CF_NEGP:CF_NEGP + 128]
    cb[:, CB_NEGS:CB_NEGS + 128] = cf[:, CF_NEGS:CF_NEGS + 128]
    cb[:, CB_NTRP:CB_NTRP + 128] = -cf[:, CF_TRIP:CF_TRIP + 128]
    cb[:, CB_NTRS:CB_NTRS + 128] = -cf[:, CF_TRIS:CF_TRIS + 128]
    return cf, cb.astype(ml_dtypes.bfloat16)


_NC_CACHE = {}


def kernel(x_prompt, x_sample, mem_prompt, state_conv, state_qkv_conv, state_delta, cache_mem_k, cache_mem_v,
           w_in, conv_w, conv_b, conv_ln_g, conv_ln_b, qkv_conv_w, a_log, dt_bias, delta_norm_w,
           w_mem_k, w_mem_v, w_out, ln_g, ln_b):
    f = lambda a: np.ascontiguousarray(np.asarray(a, dtype=np.float32))
    x_prompt, x_sample, mem_prompt = f(x_prompt), f(x_sample), f(mem_prompt)
    cf, cb = _consts()
    pv = np.zeros((128, NPV), np.float32)
    cwT = f(conv_w)[0].T.reshape(4, 128, 31)
    pv[:, PV_CW:PV_CW + 124] = cwT.transpose(1, 0, 2).reshape(128, 124)
    pv[:, PV_CB:PV_CB + 4] = f(conv_b)[0].reshape(4, 128).T
    pv[:, PV_CLG:PV_CLG + 4] = f(conv_ln_g)[0].reshape(4, 128).T
    pv[:, PV_CLB:PV_CLB + 4] = f(conv_ln_b)[0].reshape(4, 128).T
    qwT = f(qkv_conv_w)[0].T.reshape(24, 128, 4)
    pv[:, PV_QW:PV_QW + 96] = qwT.transpose(1, 0, 2).reshape(128, 96)
    pv[:, PV_NW] = f(delta_norm_w)[0]
    pv[:, PV_AL:PV_AL + 8] = f(a_log)[0][None, :]
    pv[:, PV_DT:PV_DT + 8] = f(dt_bias)[0][None, :]
    lngb = np.ascontiguousarray(np.broadcast_to(np.stack([f(ln_g)[0], f(ln_b)[0]])[None], (128, 2, 2048)))
    W_in, W_mk, W_mv, W_out = f(w_in)[0], f(w_mem_k)[0], f(w_mem_v)[0], f(w_out)[0]
    sc, sq, sd, ck, cv = f(state_conv)[0], f(state_qkv_conv)[0], f(state_delta)[0], f(cache_mem_k)[0], f(cache_mem_v)[0]
    in_maps = []
    for c in range(8):
        b, hf = c // 2, c % 2
        xall = np.zeros((NCOL, 2048), np.float32)
        if hf == 1:
            xall[0:1024] = x_prompt[b, 0:1024]
        xall[1024:2048] = x_prompt[b, 1024 * hf:1024 * hf + 1024]
        xall[2048:] = x_sample[16 * c:16 * c + 16].reshape(128, 2048)
        sl = slice(16 * c, 16 * c + 16)
        in_maps.append({
            "xall": xall, "memp": mem_prompt[b], "sconv": sc[sl], "sqkv": sq[sl], "sdelta": sd[sl],
            "ck": ck[sl], "cv": cv[sl], "w_in": W_in, "w_mk": W_mk, "w_mv": W_mv, "w_out": W_out,
            "cf": cf, "cb": cb, "pv": pv, "lngb": lngb,
        })
    if os.environ.get("K_CORES"):
        ncores = int(os.environ["K_CORES"])
        nc = build_nc()
        res = run_bass_kernel_spmd(nc, in_maps[:ncores], core_ids=list(range(ncores)), trace=bool(os.environ.get("K_TRACE")))
        kernel.last = res
        R = list(res.results) + [res.results[0]] * (8 - ncores)
    else:
        if "nc" not in _NC_CACHE:
            _NC_CACHE["nc"] = build_nc()
        nc = _NC_CACHE["nc"]
        res = run_bass_kernel_spmd(nc, in_maps, core_ids=list(range(8)))
        R = res.results
    y_p = np.zeros((4, 2048, 2048), np.float32)
    y_s = np.zeros((128, 8, 2048), np.float32)
    o_conv_p = np.zeros((1, 4, 30, 512), np.float32)
    o_qkv_p = np.zeros((1, 4, 3, 3072), np.float32)
    o_delta_p = np.zeros((1, 4, 8, 128, 128), np.float32)
    o_mk = np.zeros((1, 4, 256, 4, 128), np.float32)
    o_mv = np.zeros((1, 4, 256, 4, 128), np.float32)
    o_conv_s = np.zeros((1, 128, 30, 512), np.float32)
    o_qkv_s = np.zeros((1, 128, 3, 3072), np.float32)
    o_delta_s = np.zeros((1, 128, 8, 128, 128), np.float32)
    for c in range(8):
        b, hf = c // 2, c % 2
        r = R[c]
        y_p[b, 1024 * hf:1024 * hf + 1024] = r["y"][0:1024]
        y_s[16 * c:16 * c + 16] = r["y"][1024:1152].reshape(16, 8, 2048)
        sl = slice(16 * c, 16 * c + 16)
        o_conv_s[0, sl] = r["oconv_s"]
        o_qkv_s[0, sl] = r["oqkv_s"]
        o_delta_s[0, sl] = r["odelta_s"]
        if hf == 1:
            o_conv_p[0, b] = r["oconv_p"]
            o_qkv_p[0, b] = r["oqkv_p"]
            o_delta_p[0, b] = r["odelta_p"]
        else:
            o_mk[0, b] = r["omk"].reshape(256, 4, 128)
            o_mv[0, b] = r["omv"].reshape(256, 4, 128)
    return (y_p, y_s, o_conv_p, o_qkv_p, o_delta_p, o_mk, o_mv, o_conv_s, o_qkv_s, o_delta_s)
```

```python
import os
import contextlib
from contextlib import ExitStack
import numpy as np
import ml_dtypes
import concourse.bass as bass
import concourse.mybir as mybir
from concourse.bass_utils import run_bass_kernel_spmd

F32 = mybir.dt.float32
BF = mybir.dt.bfloat16
AF = mybir.ActivationFunctionType
ALU = mybir.AluOpType
AX = mybir.AxisListType

SAME_SYNC = not os.environ.get('K_NOSAME')
STAGE = int(os.environ.get('K_STAGE', '9'))
SUB = int(os.environ.get('K_SUB', '99'))
NT = 17
NCOL = NT * 128
NM = 1152
O_GA, O_GB, O_CG, O_Q, O_K, O_V, O_DG, O_BT, O_DC, O_MQ, O_MG = 0, 512, 1024, 1536, 2560, 3584, 4608, 5632, 5640, 5648, 6160
N_IN = 6672
ALPHA = 2.0 ** 0.25
NEG = -30000.0

CF_ID, CF_ROWM = 0, 128
NCF = 144
CB_ID, CB_ONE, CB_STRP, CB_STRS, CB_COLM = 0, 128, 256, 384, 512
CB_TRIP, CB_TRIS, CB_BLKS, CB_NEGP, CB_NEGS, CB_NTRP, CB_NTRS = 2560, 2688, 2816, 2944, 3072, 3200, 3328
NCB = 3456
PV_CW, PV_CB, PV_CLG, PV_CLB, PV_QW, PV_NW, PV_AL, PV_DT = 0, 124, 128, 132, 136, 232, 233, 241
NPV = 249


class _Skip(Exception):
    pass


class Tok:
    __slots__ = ("w", "r", "excl")

    def __init__(self, excl=False):
        self.w = None
        self.r = {}
        self.excl = excl


class Sched:
    ENG = ("pe", "act", "dve", "pool", "sp")

    def __init__(self, nc, es, ndma=40):
        self.nc = nc
        self.sem = {e: es.enter_context(nc.semaphore("sem_" + e)) for e in self.ENG}
        self.cnt = {e: 0 for e in self.ENG}
        self.q = {e: [] for e in self.ENG}
        self.seen = {e: {} for e in self.ENG}
        self.dsem = {q: [es.enter_context(nc.semaphore("dsem_%s%d" % (q, i))) for i in range(ndma // 2)] for q in ("sp", "pool")}
        self.dcnt = {q: [0] * (ndma // 2) for q in ("sp", "pool")}
        self.dnext = {"sp": 0, "pool": 0}

    def _deps(self, reads, writes):
        ev = []
        for t in reads:
            if t.w is not None:
                ev.append(t.w)
            if t.excl:
                ev.extend(t.r.values())
        for t in writes:
            if t.w is not None:
                ev.append(t.w)
            ev.extend(t.r.values())
        return ev

    def _waits(self, eng, evs):
        out = []
        for (key, val) in evs:
            if key == eng and (eng == "pe" or not SAME_SYNC):
                continue
            if self.seen[eng].get(key, 0) >= val:
                continue
            self.seen[eng][key] = val
            out.append((key, val))
        return out

    def _mark(self, ev, reads, writes):
        for t in writes:
            t.w = ev
            t.r = {}
        k = ev[0]
        for t in reads:
            if t.r.get(k, (k, 0))[1] < ev[1]:
                t.r[k] = ev

    def op(self, eng, fn, reads=(), writes=()):
        w = self._waits(eng, self._deps(reads, writes))
        self.cnt[eng] += 1
        ev = (eng, self.cnt[eng])
        self.q[eng].append(("op", fn, w, None))
        self._mark(ev, reads, writes)
        return ev

    def dma(self, q, out, in_, reads=(), writes=(), **kw):
        k = self.dnext[q]
        self.dnext[q] = (k + 1) % len(self.dsem[q])
        evs = self._deps(reads, writes)
        if self.dcnt[q][k] > 0:
            evs.append((("d", q, k), 16 * self.dcnt[q][k]))
        w = self._waits(q, evs)
        self.dcnt[q][k] += 1
        ev = (("d", q, k), 16 * self.dcnt[q][k])
        self.q[q].append(("dma", (out, in_, kw), w, k))
        self._mark(ev, reads, writes)
        return ev

    def barrier(self):
        evs = [(e, self.cnt[e]) for e in self.ENG if self.cnt[e] > 0]
        if not os.environ.get("K_NODMABAR"):
            evs += [(("d", q, k), 16 * c) for q in ("sp", "pool") for k, c in enumerate(self.dcnt[q]) if c > 0]
        for e in self.ENG:
            w = self._waits(e, evs)
            if w:
                self.q[e].append(("wait", None, w, None))

    def semh(self, key):
        return self.sem[key] if isinstance(key, str) else self.dsem[key[1]][key[2]]

    def replay(self, e, h):
        for kind, fn, waits, k in self.q[e]:
            for (key, val) in waits:
                h.wait_ge(self.semh(key), val)
            if kind == "op":
                fn(h).then_inc(self.sem[e], 1)
            elif kind == "dma":
                out, in_, kw = fn
                h.dma_start(out=out, in_=in_, **kw).then_inc(self.dsem[e][k], 16)


def build_nc(dbg=False):
    nc = bass.Bass("TRN2", target_bir_lowering=False)

    def din(name, shape, dt=F32):
        return nc.dram_tensor(name, list(shape), dt, kind="ExternalInput").ap()

    def dout(name, shape, dt=F32):
        return nc.dram_tensor(name, list(shape), dt, kind="ExternalOutput").ap()

    xall = din("xall", [NCOL, 2048])
    memp = din("memp", [256, 2048])
    sconv = din("sconv", [16, 30, 512])
    sqkv = din("sqkv", [16, 3, 3072])
    sdelta = din("sdelta", [16, 8, 128, 128])
    ck = din("ck", [16, 256, 4, 128])
    cv = din("cv", [16, 256, 4, 128])
    w_in = din("w_in", [2048, N_IN])
    w_mk = din("w_mk", [2048, 512])
    w_mv = din("w_mv", [2048, 512])
    w_out = din("w_out", [2048, 2048])
    cf_d = din("cf", [128, NCF])
    cb_d = din("cb", [128, NCB], BF)
    pv_d = din("pv", [128, NPV])
    lngb_d = din("lngb", [128, 2, 2048])

    y_d = dout("y", [NM, 2048])
    oconv_p = dout("oconv_p", [30, 512])
    oqkv_p = dout("oqkv_p", [3, 3072])
    odelta_p = dout("odelta_p", [8, 128, 128])
    omk = dout("omk", [256, 512])
    omv = dout("omv", [256, 512])
    oconv_s = dout("oconv_s", [16, 30, 512])
    oqkv_s = dout("oqkv_s", [16, 3, 3072])
    odelta_s = dout("odelta_s", [16, 8, 128, 128])

    with ExitStack() as es:
        S = Sched(nc, es)
        ctr = [0]

        def sb(es_, shape, dt=F32):
            ctr[0] += 1
            t = es_.enter_context(nc.sbuf_tensor("t%d" % ctr[0], list(shape), dt))
            return t, Tok()

        def ACT(out, in_, func, reads, writes, bias=None, scale=None, accum=None):
            kw = {}
            if bias is not None:
                kw["bias"] = bias
            if scale is not None:
                kw["scale"] = scale
            if accum is not None:
                kw["accum_out"] = accum
            return S.op("act", lambda e: e.activation(out=out, in_=in_, func=func, **kw), reads, writes)

        def TS(eng, out, in0, s1, op0, reads, writes, s2=None, op1=None):
            if op1 is None:
                return S.op(eng, lambda e: e.tensor_scalar(out=out, in0=in0, scalar1=s1, scalar2=None, op0=op0), reads, writes)
            return S.op(eng, lambda e: e.tensor_scalar(out=out, in0=in0, scalar1=s1, scalar2=s2, op0=op0, op1=op1), reads, writes)

        def TT(eng, out, in0, in1, op, reads, writes):
            return S.op(eng, lambda e: e.tensor_tensor(out=out, in0=in0, in1=in1, op=op), reads, writes)

        def STT(out, in0, scalar, in1, op0, op1, reads, writes):
            return S.op("dve", lambda e: e.scalar_tensor_tensor(out=out, in0=in0, scalar=scalar, in1=in1, op0=op0, op1=op1), reads, writes)

        def CP(eng, out, in_, reads, writes):
            if eng == "act":
                return S.op("act", lambda e: e.copy(out=out, in_=in_), reads, writes)
            return S.op(eng, lambda e: e.tensor_copy(out=out, in_=in_), reads, writes)

        def split3(src, tsrc, outs, touts, r_f32, t_r):
            CP("dve", outs[0], src, [tsrc], [touts])
            TT("dve", r_f32, src, outs[0], ALU.subtract, [tsrc, touts], [t_r])
            CP("dve", outs[1], r_f32, [t_r], [touts])
            TT("dve", outs[2], r_f32, outs[1], ALU.subtract, [t_r, touts], [touts])

        def MM(out, lhsT, rhs, start, stop, reads, writes):
            return S.op("pe", lambda e: e.matmul(out=out, lhsT=lhsT, rhs=rhs, start=start, stop=stop), reads, writes)

        def TR(out, in_, ident, reads, writes):
            return S.op("pe", lambda e: e.transpose(out=out, in_=in_, identity=ident), reads, writes)

        cf, t_cf = sb(es, [128, NCF])
        cb, t_cb = sb(es, [128, NCB], BF)
        pv, t_pv = sb(es, [128, NPV])
        mixT = es.enter_context(nc.sbuf_tensor("mixT", [128, 16, NM], BF))
        t_mix = [Tok() for _ in range(16)]
        NW = 4
        wsl = [sb(es, [128, 16, 128], BF) for _ in range(NW)]
        bt, t_sm = sb(es, [128, NT, 8])
        nbt, _ = sb(es, [128, NT, 8])
        ee, _ = sb(es, [128, NT, 8])
        nee, _ = sb(es, [128, NT, 8])
        egl, _ = sb(es, [128, NT, 8])
        glb, _ = sb(es, [128, NT, 8])
        glbs, _ = sb(es, [128, 16, 8])
        nega, _ = sb(es, [128, 8])
        g3 = [sb(es, [128, NT, 8], BF)[0] for _ in range(3)]
        es2 = ExitStack()
        xT, t_xT = sb(es2, [128, 16, NCOL], BF)
        psb = [es.enter_context(nc.psum_tensor("ps%d" % i, [128, 512], F32)) for i in range(8)]
        t_ps = [Tok(excl=True) for _ in range(8)]
        pctr = [0]

        def bank():
            i = pctr[0] % 8
            pctr[0] += 1
            return psb[i], t_ps[i]

        ident_f = cf[:, CF_ID:CF_ID + 128]
        ident_b = cb[:, CB_ID:CB_ID + 128]
        ones_b = cb[:, CB_ONE:CB_ONE + 128]

        S.dma("sp", cf[:], cf_d[:, :], writes=[t_cf])
        S.dma("sp", cb[:], cb_d[:, :], writes=[t_cb])
        S.dma("sp", pv[:], pv_d[:, :], writes=[t_pv])

        WL = []
        WL.append((w_in, O_BT, 16))
        for c in range(4):
            WL += [(w_in, O_GA + 128 * c, 128), (w_in, O_GB + 128 * c, 128), (w_in, O_CG + 128 * c, 128)]
        for h in range(4):
            WL += [(w_mk, 128 * h, 128), (w_mv, 128 * h, 128), (w_in, O_MQ + 128 * h, 128), (w_in, O_MG + 128 * h, 128)]
        for h in range(8):
            WL += [(w_in, O_K + 128 * h, 128), (w_in, O_V + 128 * h, 128), (w_in, O_Q + 128 * h, 128), (w_in, O_DG + 128 * h, 128)]
        wst = {"issued": 0, "used": 0}

        def w_issue():
            i = wst["issued"]
            if i >= len(WL):
                return
            src, c0, n = WL[i]
            t, tk = wsl[i % NW]
            S.dma("pool", t[:, :, 0:n], src[:, c0:c0 + n].rearrange("(c p) n -> p c n", p=128), writes=[tk])
            wst["issued"] += 1

        def w_next():
            i = wst["used"]
            while wst["issued"] < min(len(WL), i + NW - 1) or wst["issued"] <= i:
                w_issue()
            wst["used"] += 1
            return wsl[i % NW]

        def proj_fm(wt, wtk, c0, n, ncols=128):
            bk, tb = bank()
            for k in range(16):
                MM(bk[0:ncols, 0:n], wt[:, k, 0:ncols], xT[:, k, c0:c0 + n], k == 0, k == 15, [wtk, t_xT], [tb])
            return bk, tb

        with ExitStack() as ph, contextlib.suppress(_Skip):
            if STAGE < 0:
                raise _Skip()
            xs = [sb(ph, [128, 2048], BF) for _ in range(3)]
            for i in range(NT):
                t, tk = xs[i % 3]
                src = xall[128 * i:128 * i + 128, :]
                S.dma("pool", t[:].rearrange("p (a n) -> p a n", n=512), src.rearrange("p (a n) -> p a n", n=512), writes=[tk])
                for half in range(2):
                    bk, tb = bank()
                    bkb = bk[:].bitcast(BF)
                    for c in range(8):
                        TR(bkb[:, 128 * c:128 * c + 128], t[:, 128 * (8 * half + c):128 * (8 * half + c) + 128], ident_b, [tk, t_cb], [tb])
                    src_v = bkb[:, 0:1024].rearrange("p (c n) -> p c n", c=8)
                    dst = xT[:, 8 * half:8 * half + 8, 128 * i:128 * i + 128]
                    CP("act" if half == 0 else "dve", dst, src_v, [tb], [t_xT])
            S.barrier()

        with ExitStack() as ph, contextlib.suppress(_Skip):
            if STAGE < 1:
                raise _Skip()
            bd, t_bd = sb(ph, [128, NT, 16])
            gtk, _ = sb(ph, [128, NT, 8])
            gc_, _ = sb(ph, [128, NT, 8])
            tmp8, t_tmp8 = sb(ph, [128, NT, 8])
            gm, t_gm = sb(ph, [128, 16, 8], BF)
            wt, wtk = w_next()
            bk, tb = bank()
            for i in range(NT):
                for k in range(16):
                    MM(bk[:, 16 * i:16 * i + 16], xT[:, k, 128 * i:128 * i + 128], wt[:, k, 0:16], k == 0, k == 15, [wtk, t_xT], [tb])
            CP("dve", bd[:], bk[:, 0:16 * NT].rearrange("p (t c) -> p t c", c=16), [tb], [t_bd])
            if SUB < 1:
                raise _Skip()
            ACT(bt[:], bd[:, :, 0:8], AF.Sigmoid, [t_bd], [t_sm])
            TS("dve", nbt[:], bt[:], -1.0, ALU.mult, [t_sm], [t_sm])
            if SUB < 2:
                raise _Skip()
            ACT(nega[:], pv[:, PV_AL:PV_AL + 8], AF.Exp, [t_pv], [t_sm])
            TS("dve", nega[:], nega[:], -1.0, ALU.mult, [t_sm], [t_sm])
            if SUB < 3:
                raise _Skip()
            TT("dve", tmp8[:], bd[:, :, 8:16], pv[:, PV_DT:PV_DT + 8].unsqueeze(1).broadcast_to([128, NT, 8]), ALU.add, [t_bd, t_pv], [t_tmp8])
            ACT(tmp8[:], tmp8[:], AF.Exp, [t_tmp8], [t_tmp8])
            ACT(tmp8[:], tmp8[:], AF.Ln, [t_tmp8], [t_tmp8], bias=1.0)
            TT("dve", gtk[:], tmp8[:], nega[:].unsqueeze(1).broadcast_to([128, NT, 8]), ALU.mult, [t_tmp8, t_sm], [t_sm])
            if SUB < 4:
                raise _Skip()
            split3(gtk[:], t_sm, [g3[0][:], g3[1][:], g3[2][:]], t_sm, tmp8[:], t_tmp8)
            bk, tb = bank()
            for i in range(NT):
                tri = cb[:, CB_TRIP:CB_TRIP + 128] if i < 16 else cb[:, CB_TRIS:CB_TRIS + 128]
                blk = ones_b if i < 16 else cb[:, CB_BLKS:CB_BLKS + 128]
                for j in range(3):
                    MM(bk[:, 8 * i:8 * i + 8], tri, g3[j][:, i, :], j == 0, j == 2, [t_cb, t_sm], [tb])
                for j in range(3):
                    MM(bk[:, 256 + 8 * i:256 + 8 * i + 8], blk, g3[j][:, i, :], j == 0, j == 2, [t_cb, t_sm], [tb])
            CP("dve", gc_[:], bk[:, 0:8 * NT].rearrange("p (t c) -> p t c", c=8), [tb], [t_sm])
            CP("act", tmp8[:], bk[:, 256:256 + 8 * NT].rearrange("p (t c) -> p t c", c=8), [tb], [t_tmp8])
            if SUB < 5:
                raise _Skip()
            ACT(ee[:], gc_[:], AF.Exp, [t_sm], [t_sm])
            TS("dve", nee[:], ee[:], -1.0, ALU.mult, [t_sm], [t_sm])
            ACT(glb[:], tmp8[:], AF.Exp, [t_tmp8], [t_sm])
            TT("dve", tmp8[:], tmp8[:], gc_[:], ALU.subtract, [t_tmp8, t_sm], [t_tmp8])
            ACT(egl[:], tmp8[:], AF.Exp, [t_tmp8], [t_sm])
            if SUB < 6:
                raise _Skip()
            bk, tb = bank()
            for j in range(3):
                TT("dve", gm[:], g3[j][:, 16, :].unsqueeze(1).broadcast_to([128, 16, 8]),
                   cf[:, CF_ROWM:CF_ROWM + 16].unsqueeze(2).broadcast_to([128, 16, 8]), ALU.mult, [t_sm, t_cf], [t_gm])
                MM(bk[:, 0:128], ones_b, gm[:].rearrange("p s h -> p (s h)"), j == 0, j == 2, [t_cb, t_gm], [tb])
            ACT(glbs[:], bk[:, 0:128].rearrange("p (s h) -> p s h", h=8), AF.Exp, [tb], [t_sm])
            if SUB < 9:
                raise _Skip()
            S.barrier()

        if os.environ.get("K_BAR"):
            S.barrier()
        OWN_CH = [(896, 512), (1408, 512), (1920, 256)]
        with ExitStack() as ph, contextlib.suppress(_Skip):
            if STAGE < 2:
                raise _Skip()
            ubuf, t_ub = sb(ph, [128, 1152])
            usamp, t_us = sb(ph, [128, 16, 38])
            hc, t_hc = sb(ph, [128, 4, NM])
            sgate, t_sg = sb(ph, [128, 4, NM], BF)
            sgt, t_sgt = sb(ph, [128, 512])
            stg, t_stg = sb(ph, [128, 512])
            scs4 = [sb(ph, [120, 128]) for _ in range(4)]
            S.dma("sp", oconv_s[:, 0:22, :], sconv[:, 8:30, :])
            unew_s, t_uns = sb(ph, [128, 4, 128])
            for c in range(4):
                for g4 in range(4):
                    S.dma("sp", scs4[g4][0][:, 0:128], sconv[4 * g4:4 * g4 + 4, :, 128 * c:128 * c + 128].rearrange("s r n -> (s r) n"), writes=[scs4[g4][1]])
                for g4 in range(4):
                    scs, t_scs = scs4[g4]
                    bk, tb = bank()
                    TR(bk[:, 0:120], scs[0:120, 0:128], ident_f[0:120, 0:120], [t_scs, t_cf], [tb])
                    CP("act", usamp[:, 4 * g4:4 * g4 + 4, 0:30], bk[:, 0:120].rearrange("p (s r) -> p s r", r=30), [tb], [t_us])
                wa, ta = w_next()
                wb, tbk = w_next()
                for (c0, n) in OWN_CH:
                    bka, tba = proj_fm(wa, ta, c0, n)
                    bkb_, tbb = proj_fm(wb, tbk, c0, n)
                    ACT(sgt[:, 0:n], bkb_[:, 0:n], AF.Sigmoid, [tbb], [t_sgt])
                    if c0 < 1920:
                        TT("dve", ubuf[:, c0 - 896:c0 - 896 + n], bka[:, 0:n], sgt[:, 0:n], ALU.mult, [tba, t_sgt], [t_ub])
                    else:
                        TT("dve", ubuf[:, 1024:1152], bka[:, 0:128], sgt[:, 0:128], ALU.mult, [tba, t_sgt], [t_ub])
                        TT("dve", unew_s[:, c, :], bka[:, 128:256], sgt[:, 128:256], ALU.mult, [tba, t_sgt], [t_uns])
                        CP("act", usamp[:, :, 30:38], unew_s[:, c, :].rearrange("p (s l) -> p s l", l=8), [t_uns], [t_us])
                wg, tg = w_next()
                for (c0, n) in [(1024, 512), (1536, 512), (2048, 128)]:
                    bkg, tbg = proj_fm(wg, tg, c0, n)
                    ACT(sgate[:, c, c0 - 1024:c0 - 1024 + n], bkg[:, 0:n], AF.Silu, [tbg], [t_sg])
                cw = lambda j: pv[:, PV_CW + 31 * c + j:PV_CW + 31 * c + j + 1]
                TS("dve", hc[:, c, 0:1024], ubuf[:, 98:98 + 1024], cw(0), ALU.mult, [t_ub, t_pv], [t_hc],
                   s2=pv[:, PV_CB + c:PV_CB + c + 1], op1=ALU.add)
                for j in range(1, 31):
                    STT(hc[:, c, 0:1024], ubuf[:, 98 + j:98 + j + 1024], cw(j), hc[:, c, 0:1024], ALU.mult, ALU.add, [t_ub, t_pv, t_hc], [t_hc])
                hs = hc[:, c, 1024:1152].rearrange("p (s l) -> p s l", l=8)
                TS("dve", hs, usamp[:, :, 0:8], cw(0), ALU.mult, [t_us, t_pv], [t_hc], s2=pv[:, PV_CB + c:PV_CB + c + 1], op1=ALU.add)
                for j in range(1, 31):
                    STT(hs, usamp[:, :, j:j + 8], cw(j), hs, ALU.mult, ALU.add, [t_us, t_pv, t_hc], [t_hc])
                bk, tb = bank()
                TR(bk[0:30, 0:128], ubuf[:, 1122:1152], ident_f, [t_ub, t_cf], [tb])
                CP("act", stg[0:30, 128 * c:128 * c + 128], bk[0:30, 0:128], [tb], [t_stg])
            S.dma("sp", oconv_p[:, :], stg[0:30, :], reads=[t_stg])
            stg2, t_stg2 = sb(ph, [128, 512])
            for c in range(4):
                bk, tb = bank()
                TR(bk[:, 0:128], unew_s[:, c, :], ident_f, [t_uns, t_cf], [tb])
                CP("act", stg2[:, 128 * c:128 * c + 128], bk[:, 0:128], [tb], [t_stg2])
            for s_ in range(16):
                S.dma("sp", oconv_s[s_, 22:30, :], stg2[8 * s_:8 * s_ + 8, :], reads=[t_stg2])
            sq, t_sq = sb(ph, [128, 512])
            hl = [sb(ph, [128, 512], BF) for _ in range(4)]
            mean, t_mean = sb(ph, [128, 512])
            var, t_var = sb(ph, [128, 512])
            rstd, t_rstd = sb(ph, [128, 512])
            xc, t_xc = sb(ph, [128, 512])
            for (m0, n) in [(0, 512), (512, 512), (1024, 128)]:
                bk1, tb1 = bank()
                bk2, tb2 = bank()
                for c in range(4):
                    CP("act", hl[0][0][:, 0:n], hc[:, c, m0:m0 + n], [t_hc], [hl[0][1]])
                    TT("dve", hl[1][0][:, 0:n], hc[:, c, m0:m0 + n], hl[0][0][:, 0:n], ALU.subtract, [t_hc, hl[0][1]], [hl[1][1]])
                    MM(bk1[:, 0:n], ones_b, hl[0][0][:, 0:n], c == 0, False, [t_cb, hl[0][1]], [tb1])
                    MM(bk1[:, 0:n], ones_b, hl[1][0][:, 0:n], False, c == 3, [t_cb, hl[1][1]], [tb1])
                for c in range(4):
                    ACT(sq[:, 0:n], hc[:, c, m0:m0 + n], AF.Square, [t_hc], [t_sq])
                    CP("act", hl[2][0][:, 0:n], sq[:, 0:n], [t_sq], [hl[2][1]])
                    TT("dve", hl[3][0][:, 0:n], sq[:, 0:n], hl[2][0][:, 0:n], ALU.subtract, [t_sq, hl[2][1]], [hl[3][1]])
                    MM(bk2[:, 0:n], ones_b, hl[2][0][:, 0:n], c == 0, False, [t_cb, hl[2][1]], [tb2])
                    MM(bk2[:, 0:n], ones_b, hl[3][0][:, 0:n], False, c == 3, [t_cb, hl[3][1]], [tb2])
                ACT(mean[:, 0:n], bk1[:, 0:n], AF.Copy, [tb1], [t_mean], scale=1.0 / 512)
                TT("dve", var[:, 0:n], mean[:, 0:n], mean[:, 0:n], ALU.mult, [t_mean], [t_var])
                STT(var[:, 0:n], bk2[:, 0:n], 1.0 / 512, var[:, 0:n], ALU.mult, ALU.subtract, [tb2, t_var], [t_var])
                TS("dve", var[:, 0:n], var[:, 0:n], 1e-5, ALU.add, [t_var], [t_var])
                ACT(var[:, 0:n], var[:, 0:n], AF.Ln, [t_var], [t_var])
                ACT(rstd[:, 0:n], var[:, 0:n], AF.Exp, [t_var], [t_rstd], scale=-0.5)
                for c in range(4):
                    TT("dve", xc[:, 0:n], hc[:, c, m0:m0 + n], mean[:, 0:n], ALU.subtract, [t_hc, t_mean], [t_xc])
                    TT("dve", xc[:, 0:n], xc[:, 0:n], rstd[:, 0:n], ALU.mult, [t_xc, t_rstd], [t_xc])
                    ACT(xc[:, 0:n], xc[:, 0:n], AF.Silu, [t_xc, t_pv], [t_xc],
                        scale=pv[:, PV_CLG + c:PV_CLG + c + 1], bias=pv[:, PV_CLB + c:PV_CLB + c + 1])
                    TT("dve", mixT[:, c, m0:m0 + n], xc[:, 0:n], sgate[:, c, m0:m0 + n], ALU.mult, [t_xc, t_sg], [t_mix[c]])
            S.barrier()

        MEM_CH = [(1024, 512), (1536, 512), (2048, 128)]
        with ExitStack() as ph, contextlib.suppress(_Skip):
            if STAGE < 3:
                raise _Skip()
            mT, t_mT = sb(ph, [128, 16, 256], BF)
            xs3, t_xs3 = sb(ph, [128, 2048], BF)
            for j in range(2):
                S.dma("pool", xs3[:].rearrange("p (a n) -> p a n", n=512), memp[128 * j:128 * j + 128, :].rearrange("p (a n) -> p a n", n=512), writes=[t_xs3])
                for half in range(2):
                    bk, tb = bank()
                    bkb = bk[:].bitcast(BF)
                    for c in range(8):
                        TR(bkb[:, 128 * c:128 * c + 128], xs3[:, 128 * (8 * half + c):128 * (8 * half + c) + 128], ident_b, [t_xs3, t_cb], [tb])
                    CP("act" if half == 0 else "dve", mT[:, 8 * half:8 * half + 8, 128 * j:128 * j + 128],
                       bkb[:, 0:1024].rearrange("p (c n) -> p c n", c=8), [tb], [t_mT])
            mqT, t_mq = sb(ph, [128, NM], BF)
            mgT, t_mg = sb(ph, [128, NM], BF)
            KTp, t_ktp = sb(ph, [128, 256], BF)
            Vp, t_vp = sb(ph, [128, 2, 128], BF)
            kvst, t_kvst = sb(ph, [128, 2, 2, 128])
            kc, t_kc = sb(ph, [128, 16, 2, 128], BF)
            vc, t_vc = sb(ph, [128, 16, 2, 128], BF)
            kcT, t_kct = sb(ph, [128, 16, 256], BF)
            mqm, t_mqm = sb(ph, [128, 16, 128], BF)
            pf, t_pf = sb(ph, [128, 256])
            pn, t_pn = sb(ph, [128, 256], BF)
            pT, t_pT = sb(ph, [128, 2, 128], BF)
            mx, t_mx = sb(ph, [128, 4])
            for h in range(4):
                wk, tk_ = w_next()
                wv, tv_ = w_next()
                bk, tb = bank()
                for k in range(16):
                    MM(bk[:, 0:256], wk[:, k, :], mT[:, k, :], k == 0, k == 15, [tk_, t_mT], [tb])
                CP("act", KTp[:], bk[:, 0:256], [tb], [t_ktp])
                bk, tb = bank()
                for mc in range(2):
                    for k in range(16):
                        MM(bk[:, 128 * mc:128 * mc + 128], mT[:, k, 128 * mc:128 * mc + 128], wk[:, k, :], k == 0, k == 15, [tk_, t_mT], [tb])
                    for k in range(16):
                        MM(bk[:, 256 + 128 * mc:256 + 128 * mc + 128], mT[:, k, 128 * mc:128 * mc + 128], wv[:, k, :], k == 0, k == 15, [tv_, t_mT], [tb])
                CP("dve", kvst[:].rearrange("p a b d -> p (a b d)"), bk[:, 0:512], [tb], [t_kvst])
                CP("act", Vp[:].rearrange("p b d -> p (b d)"), bk[:, 256:512], [tb], [t_vp])
                S.dma("sp", omk[:, 128 * h:128 * h + 128].rearrange("(mc m) d -> m mc d", m=128), kvst[:, 0, :, :], reads=[t_kvst])
                S.dma("sp", omv[:, 128 * h:128 * h + 128].rearrange("(mc m) d -> m mc d", m=128), kvst[:, 1, :, :], reads=[t_kvst])
                S.dma("pool", kc[:], ck[:, :, h, :].rearrange("s (mc m) d -> m s mc d", m=128), writes=[t_kc])
                S.dma("pool", vc[:], cv[:, :, h, :].rearrange("s (mc m) d -> m s mc d", m=128), writes=[t_vc])
                wq, tq_ = w_next()
                wg, tg_ = w_next()
                for (c0, n) in MEM_CH:
                    bk, tb = proj_fm(wq, tq_, c0, n)
                    ACT(mqT[:, c0 - 1024:c0 - 1024 + n], bk[:, 0:n], AF.Copy, [tb], [t_mq], scale=128.0 ** -0.5)
                    bk, tb = proj_fm(wg, tg_, c0, n)
                    ACT(mgT[:, c0 - 1024:c0 - 1024 + n], bk[:, 0:n], AF.Silu, [tb], [t_mg])
                for s in range(16):
                    if s % 4 == 0:
                        bk, tb = bank()
                        bkb = bk[:].bitcast(BF)
                    for mc in range(2):
                        o = (s % 4) * 256 + mc * 128
                        TR(bkb[:, o:o + 128], kc[:, s, mc, :], ident_b, [t_kc, t_cb], [tb])
                    if s % 4 == 3:
                        CP("act", kcT[:, s - 3:s + 1, :], bkb[:, 0:1024].rearrange("p (s m) -> p s m", m=256), [tb], [t_kct])
                TT("dve", mqm[:], mqT[:, 1024:1152].unsqueeze(1).broadcast_to([128, 16, 128]),
                   cb[:, CB_COLM:CB_COLM + 2048].rearrange("p (s i) -> p s i", i=128), ALU.mult, [t_mq, t_cb], [t_mqm])
                for ti in range(9):
                    m0 = 128 * ti
                    bk, tb = bank()
                    if ti < 8:
                        MM(bk[:, 0:256], mqT[:, m0:m0 + 128], KTp[:], True, True, [t_mq, t_ktp], [tb])
                    else:
                        for s in range(16):
                            MM(bk[:, 0:256], mqm[:, s, :], kcT[:, s, :], s == 0, s == 15, [t_mqm, t_kct], [tb])
                    S.op("dve", lambda e, bk=bk: e.reduce_max(out=mx[:, 0:1], in_=bk[:, 0:256], axis=AX.X), [tb], [t_mx])
                    TS("dve", mx[:, 1:2], mx[:, 0:1], -1.0, ALU.mult, [t_mx], [t_mx])
                    ACT(pf[:], bk[:, 0:256], AF.Exp, [tb, t_mx], [t_pf, t_mx], bias=mx[:, 1:2], accum=mx[:, 2:3])
                    S.op("dve", lambda e: e.reciprocal(out=mx[:, 3:4], in_=mx[:, 2:3]), [t_mx], [t_mx])
                    TS("dve", pn[:], pf[:], mx[:, 3:4], ALU.mult, [t_pf, t_mx], [t_pn])
                    bk2, tb2 = bank()
                    bk2b = bk2[:].bitcast(BF)
                    for mc in range(2):
                        TR(bk2b[:, 128 * mc:128 * mc + 128], pn[:, 128 * mc:128 * mc + 128], ident_b, [t_pn, t_cb], [tb2])
                    CP("act", pT[:].rearrange("p a b -> p (a b)"), bk2b[:, 0:256], [tb2], [t_pT])
                    bk3, tb3 = bank()
                    if ti < 8:
                        for mc in range(2):
                            MM(bk3[:, 0:128], Vp[:, mc, :], pT[:, mc, :], mc == 0, mc == 1, [t_vp, t_pT], [tb3])
                    else:
                        for s in range(16):
                            for mc in range(2):
                                MM(bk3[:, 8 * s:8 * s + 8], vc[:, s, mc, :], pT[:, mc, 8 * s:8 * s + 8], mc == 0, mc == 1, [t_vc, t_pT], [tb3])
                    TT("dve", mixT[:, 12 + h, m0:m0 + 128], bk3[:, 0:128], mgT[:, m0:m0 + 128], ALU.mult, [tb3, t_mg], [t_mix[12 + h]])
            S.barrier()

        KV_CH = [(0, 512), (512, 512), (1024, 512), (1536, 512), (2048, 128)]
        with ExitStack() as ph, contextlib.suppress(_Skip):
            if STAGE < 4:
                raise _Skip()
            kT, t_kT = sb(ph, [128, NCOL], BF)
            vT, t_vT = sb(ph, [128, NCOL], BF)
            qT, t_qT = sb(ph, [128, NM], BF)
            gT, t_gT = sb(ph, [128, NM], BF)
            pre = [sb(ph, [128, 515]) for _ in range(2)]
            pres, t_pres = sb(ph, [128, 16, 11])
            sqb2 = [sb(ph, [128, 512], BF) for _ in range(2)]
            acc, t_acc = sb(ph, [128, 512])
            lnb2 = [sb(ph, [128, 512]) for _ in range(2)]
            sq48, t_sq48 = sb(ph, [48, 128])
            qst, t_qst = sb(ph, [128, 3])
            qst_s, t_qsts = sb(ph, [128, 48])
            ost, t_ost = sb(ph, [48, 128])
            Sf, t_Sf = sb(ph, [128, 128])
            Sb, t_Sb = sb(ph, [128, 128], BF)
            Ss, t_Ss = sb(ph, [128, 16, 128])
            Ssb, t_Ssb = sb(ph, [128, 16, 128], BF)
            Sso, t_Sso = Ss, t_Ss
            kTm, t_kTm = sb(ph, [128, 16, 128], BF)
            qTm, t_qTm = sb(ph, [128, 16, 128], BF)
            kdm, t_kdm = qTm, t_qTm
            GT = 4
            gB3 = [sb(ph, [128, GT, 128], BF) for _ in range(3)]
            decT, t_decT = sb(ph, [128, GT, 128])
            decTs, t_decTs = sb(ph, [128, GT, 128], BF)
            P0, t_P0 = sb(ph, [128, GT, 128], BF)
            Pb = [sb(ph, [128, GT, 128], BF) for _ in range(2)]
            Qb = [sb(ph, [128, GT, 128], BF) for _ in range(2)]
            Yb = [sb(ph, [128, GT, 128], BF) for _ in range(2)]
            XTb = [sb(ph, [128, GT, 128], BF) for _ in range(2)]
            PbH = [[Tok(), Tok()] for _ in range(2)]
            QbH = [[Tok(), Tok()] for _ in range(2)]
            YbH = [[Tok(), Tok()] for _ in range(2)]
            XTH = [[Tok(), Tok()] for _ in range(2)]
            aqkb = [sb(ph, [128, GT, 128], BF) for _ in range(2)]
            kdb = [sb(ph, [128, GT, 128], BF) for _ in range(2)]
            vtb = [sb(ph, [128, GT, 128], BF) for _ in range(2)]
            rbf, t_rbf = sb(ph, [128, 128], BF)
            ubf, t_ubf = sb(ph, [128, 128], BF)
            tsb, t_tsb = sb(ph, [128, 128])
            osb, t_osb = sb(ph, [128, 128])
            junk, t_junk = sb(ph, [128, 128], BF)
            onb, t_onb = sb(ph, [128, 128], BF)
            sm4, t_sm4 = sb(ph, [128, 4])


            def conv_stream(h, kind, wt, wtk, chunks, dstT, t_dst, dcol0, norm, qscale):
                fo = kind * 1024 + 128 * h
                fc = fo // 128
                qw = lambda j: pv[:, PV_QW + 4 * fc + j:PV_QW + 4 * fc + j + 1]
                S.dma("sp", sq48[:], sqkv[:, :, fo:fo + 128].rearrange("s r n -> (s r) n"), writes=[t_sq48])
                pi = 0
                S.op("pool", lambda e, p=pre[0][0]: e.memset(p[:, 0:3], 0.0), [], [pre[0][1]])
                for (c0, n) in chunks:
                    bk, tb = proj_fm(wt, wtk, c0, n)
                    npr = n if c0 + n <= 2048 else n - 128
                    skip = max(0, dcol0 - c0)
                    if npr > 0:
                        p_, tp_ = pre[pi]
                        ACT(p_[:, 3:3 + npr], bk[:, 0:npr], AF.Copy, [tb], [tp_])
                        TS("dve", acc[:, 0:npr], p_[:, 0:npr], qw(0), ALU.mult, [tp_, t_pv], [t_acc])
                        for j in range(1, 4):
                            STT(acc[:, 0:npr], p_[:, j:j + npr], qw(j), acc[:, 0:npr], ALU.mult, ALU.add, [tp_, t_pv, t_acc], [t_acc])
                        d0 = c0 - dcol0
                        ACT(dstT[:, d0 + skip:d0 + npr], acc[:, skip:npr], AF.Silu, [t_acc], [t_dst])
                        if c0 + npr == 2048:
                            CP("act", qst[:, 0:3], p_[:, npr:npr + 3], [tp_], [t_qst])
                        else:
                            p2, tp2 = pre[1 - pi]
                            CP("act", p2[:, 0:3], p_[:, npr:npr + 3], [tp_], [tp2])
                            pi = 1 - pi
                    if c0 + n > 2048:
                        bks_, tbs_ = bank()
                        TR(bks_[:, 0:48], sq48[0:48, 0:128], ident_f[0:48, 0:48], [t_sq48, t_cf], [tbs_])
                        CP("act", pres[:, :, 0:3], bks_[:, 0:48].rearrange("p (s r) -> p s r", r=3), [tbs_], [t_pres])
                        CP("act", pres[:, :, 3:11], bk[:, npr:npr + 128].rearrange("p (s l) -> p s l", l=8), [tb], [t_pres])
                        av = acc[:, 0:128].rearrange("p (s l) -> p s l", l=8)
                        TS("dve", av, pres[:, :, 0:8], qw(0), ALU.mult, [t_pres, t_pv], [t_acc])
                        for j in range(1, 4):
                            STT(av, pres[:, :, j:j + 8], qw(j), av, ALU.mult, ALU.add, [t_pres, t_pv, t_acc], [t_acc])
                        d0 = 2048 - dcol0
                        ACT(dstT[:, d0:d0 + 128], acc[:, 0:128], AF.Silu, [t_acc], [t_dst])
                        CP("act", qst_s[:].rearrange("p (s r) -> p s r", r=3), pres[:, :, 8:11], [t_pres], [t_qsts])
                bk, tb = bank()
                TR(bk[0:3, 0:128], qst[:, 0:3], ident_f, [t_qst, t_cf], [tb])
                TR(bk[0:48, 128:256], qst_s[:, 0:48], ident_f, [t_qsts, t_cf], [tb])
                CP("act", ost[0:3, :], bk[0:3, 0:128], [tb], [t_ost])
                S.dma("sp", oqkv_p[:, fo:fo + 128], ost[0:3, :], reads=[t_ost])
                CP("act", ost[0:48, :], bk[0:48, 128:256], [tb], [t_ost])
                S.dma("sp", oqkv_s[:, :, fo:fo + 128].rearrange("s r n -> (s r) n"), ost[0:48, :], reads=[t_ost])
                if norm:
                    ntot = dstT.shape[1]
                    for idx, m0 in enumerate(range(0, ntot, 512)):
                        n = min(512, ntot - m0)
                        sq_, tsq_ = sqb2[idx % 2]
                        ln_, tln_ = lnb2[idx % 2]
                        ACT(sq_[:, 0:n], dstT[:, m0:m0 + n], AF.Square, [t_dst], [tsq_])
                        bk, tb = bank()
                        MM(bk[:, 0:n], ones_b, sq_[:, 0:n], True, True, [t_cb, tsq_], [tb])
                        TS("dve", ln_[:, 0:n], bk[:, 0:n], 1e-6, ALU.add, [tb], [tln_])
                        ACT(ln_[:, 0:n], ln_[:, 0:n], AF.Ln, [tln_], [tln_])
                        ACT(ln_[:, 0:n], ln_[:, 0:n], AF.Exp, [tln_], [tln_], scale=-0.5)
                        if qscale != 1.0:
                            TS("dve", ln_[:, 0:n], ln_[:, 0:n], qscale, ALU.mult, [tln_], [tln_])
                        TT("dve", dstT[:, m0:m0 + n], dstT[:, m0:m0 + n], ln_[:, 0:n], ALU.mult, [t_dst, tln_], [t_dst])

            def groupA(h, tiles, par):
                ng = len(tiles)
                i0 = tiles[0]
                samp = (i0 == 16)
                full = (i0 >= 8)
                L = 2 if samp else 6
                trib = cb[:, CB_TRIS:CB_TRIS + 128] if samp else cb[:, CB_TRIP:CB_TRIP + 128]
                ntrib = cb[:, CB_NTRS:CB_NTRS + 128] if samp else cb[:, CB_NTRP:CB_NTRP + 128]
                negb = cb[:, CB_NEGS:CB_NEGS + 128] if samp else cb[:, CB_NEGP:CB_NEGP + 128]
                strict = cb[:, CB_STRS:CB_STRS + 128] if samp else cb[:, CB_STRP:CB_STRP + 128]
                W = 128 * ng
                kd, t_kd = kdb[par]
                vt, t_vt = vtb[par]
                aq, t_aq = aqkb[par]
                XT, t_XT = XTb[par]
                v3 = lambda ap: ap.rearrange("p (t n) -> p t n", n=128)
                bk, tb = bank()
                bkb = bk[:].bitcast(BF)
                for t in range(ng):
                    c0 = 128 * (i0 + t)
                    TR(bkb[:, 256 * t:256 * t + 128], kT[:, c0:c0 + 128], ident_b, [t_kT, t_cb], [tb])
                    TR(bkb[:, 256 * t + 128:256 * t + 256], vT[:, c0:c0 + 128], ident_b, [t_vT, t_cb], [tb])
                for t in range(ng):
                    ACT(kd[:, t, :], bkb[:, 256 * t:256 * t + 128], AF.Copy, [tb, t_sm], [t_kd], scale=egl[:, i0 + t, h:h + 1])
                CP("dve", vt[:, 0:ng, :], bkb[:, 0:256 * ng].rearrange("p (t two n) -> p t two n", two=2, n=128)[:, :, 1, :], [tb], [t_vt])
                yield
                for j in range(3):
                    TT("pool", gB3[j][0][:, 0:ng, :], ones_b.unsqueeze(1).broadcast_to([128, ng, 128]),
                       g3[j][:, i0:i0 + ng, h:h + 1].broadcast_to([128, ng, 128]), ALU.mult, [t_cb, t_sm], [gB3[j][1]])
                bk, tb = bank()
                for t in range(ng):
                    o = bk[:, 128 * t:128 * t + 128]
                    for j in range(3):
                        MM(o, gB3[j][0][:, t, :], trib, j == 0, False, [gB3[j][1], t_cb], [tb])
                    for j in range(3):
                        MM(o, ntrib, gB3[j][0][:, t, :], False, False, [gB3[j][1], t_cb], [tb])
                    MM(o, ident_b, negb, False, True, [t_cb], [tb])
                ACT(decT[:, 0:ng, :], v3(bk[:, 0:W]), AF.Exp, [tb], [t_decT])
                TT("pool", decTs[:, 0:ng, :], decT[:, 0:ng, :], strict.unsqueeze(1).broadcast_to([128, ng, 128]), ALU.mult, [t_decT, t_cb], [t_decTs])
                yield
                bkG, tbG = bank()
                if full:
                    bkA, tbA = bank()
                for t in range(ng):
                    c0 = 128 * (i0 + t)
                    MM(bkG[:, 128 * t:128 * t + 128], kT[:, c0:c0 + 128], kT[:, c0:c0 + 128], True, True, [t_kT], [tbG])
                    if full:
                        MM(bkA[:, 128 * t:128 * t + 128], kT[:, c0:c0 + 128], qT[:, c0 - 1024:c0 - 1024 + 128], True, True, [t_kT, t_qT], [tbA])
                for t in range(ng):
                    STT(P0[:, t, :], bkG[:, 128 * t:128 * t + 128], nbt[:, i0 + t, h:h + 1], decTs[:, t, :], ALU.mult, ALU.mult,
                        [tbG, t_sm, t_decTs], [t_P0])
                if full:
                    TT("dve", aq[:, 0:ng, :], v3(bkA[:, 0:W]), decT[:, 0:ng, :], ALU.mult, [tbA, t_decT], [t_aq])
                Y0 = Yb[0][0]
                TT("pool", Y0[:, 0:ng, :], P0[:, 0:ng, :], ident_b.unsqueeze(1).broadcast_to([128, ng, 128]), ALU.add, [t_P0, t_cb], YbH[0])
                yield
                bk, tb = bank()
                bkb = bk[:].bitcast(BF)
                for t in range(ng):
                    TR(bkb[:, 128 * t:128 * t + 128], P0[:, t, :], ident_b, [t_P0, t_cb], [tb])
                Q0 = Qb[0][0]
                CP("act", Q0[:, 0:ng, :], v3(bkb[:, 0:W]), [tb], QbH[0])
                yield
                halves = [(0, 2), (2, 4)] if ng == 4 else [(0, ng)]
                v3h = lambda ap: ap.rearrange("p (t n) -> p t n", n=128)
                mms = []
                for hi, (a, b) in enumerate(halves):
                    bkP, tbP = bank()
                    bkQ, tbQ = bank()
                    for t in range(a, b):
                        o = 128 * (t - a)
                        MM(bkP[:, o:o + 128], Q0[:, t, :], P0[:, t, :], True, True, [QbH[0][hi], t_P0], [tbP])
                        MM(bkQ[:, o:o + 128], P0[:, t, :], Q0[:, t, :], True, True, [QbH[0][hi], t_P0], [tbQ])
                    mms.append((bkP, tbP, bkQ, tbQ))
                for hi, (a, b) in enumerate(halves):
                    bkP, tbP, bkQ, tbQ = mms[hi]
                    wd = 128 * (b - a)
                    CP("act", Pb[0][0][:, a:b, :], v3h(bkP[:, 0:wd]), [tbP], [PbH[0][hi]])
                    CP("dve", Qb[1][0][:, a:b, :], v3h(bkQ[:, 0:wd]), [tbQ], [QbH[1][hi]])
                yield
                pc, qc, yc = 0, 1, 0
                for k in range(1, L + 1):
                    P, Q, Y = Pb[pc][0], Qb[qc][0], Yb[yc][0]
                    Pn, Qn = Pb[1 - pc][0], Qb[1 - qc][0]
                    Yn = Yb[1 - yc][0] if k < L else XT
                    tYn = YbH[1 - yc] if k < L else XTH[par]
                    mms = []
                    for hi, (a, b) in enumerate(halves):
                        if k < L:
                            bkP, tbP = bank()
                            bkQ, tbQ = bank()
                        else:
                            bkP = tbP = bkQ = tbQ = None
                        bkY, tbY = bank()
                        for t in range(a, b):
                            o = 128 * (t - a)
                            if k < L:
                                MM(bkP[:, o:o + 128], Q[:, t, :], P[:, t, :], True, True, [QbH[qc][hi], PbH[pc][hi]], [tbP])
                                MM(bkQ[:, o:o + 128], P[:, t, :], Q[:, t, :], True, True, [QbH[qc][hi], PbH[pc][hi]], [tbQ])
                            MM(bkY[:, o:o + 128], Q[:, t, :], Y[:, t, :], True, True, [QbH[qc][hi], YbH[yc][hi]], [tbY])
                        mms.append((bkP, tbP, bkQ, tbQ, bkY, tbY))
                    for hi, (a, b) in enumerate(halves):
                        bkP, tbP, bkQ, tbQ, bkY, tbY = mms[hi]
                        wd = 128 * (b - a)
                        if k < L:
                            CP("act", Pn[:, a:b, :], v3h(bkP[:, 0:wd]), [tbP], [PbH[1 - pc][hi]])
                            CP("act" if k % 2 == 0 else "dve", Qn[:, a:b, :], v3h(bkQ[:, 0:wd]), [tbQ], [QbH[1 - qc][hi]])
                        TT("dve", Yn[:, a:b, :], v3h(bkY[:, 0:wd]), Y[:, a:b, :], ALU.add, [tbY, YbH[yc][hi]], [tYn[hi]])
                    pc, qc, yc = 1 - pc, 1 - qc, 1 - yc
                    yield

            def groupB(h, tiles, par):
                kd, t_kd = kdb[par]
                vt, t_vt = vtb[par]
                aq, t_aq = aqkb[par]
                XT, t_XT = XTb[par]
                for t, i in enumerate(tiles):
                    samp = (i == 16)
                    full = (i >= 8)
                    c0 = 128 * i
                    mc0 = c0 - 1024
                    sc = lambda arr: arr[:, i, h:h + 1]
                    bk, tb = bank()
                    if not samp:
                        MM(bk[:, 0:128], kT[:, c0:c0 + 128], Sb[:], True, True, [t_kT, t_Sb], [tb])
                        if full:
                            MM(bk[:, 128:256], qT[:, mc0:mc0 + 128], Sb[:], True, True, [t_qT, t_Sb], [tb])
                    else:
                        for s_ in range(16):
                            MM(bk[:, 0:128], kTm[:, s_, :], Ssb[:, s_, :], s_ == 0, s_ == 15, [t_kTm, t_Ssb], [tb])
                        for s_ in range(16):
                            MM(bk[:, 128:256], qTm[:, s_, :], Ssb[:, s_, :], s_ == 0, s_ == 15, [t_qTm, t_Ssb], [tb])
                    STT(rbf[:], bk[:, 0:128], sc(nee), vt[:, t, :], ALU.mult, ALU.add, [tb, t_sm, t_vt], [t_rbf])
                    if full:
                        ACT(tsb[:], bk[:, 128:256], AF.Copy, [tb, t_sm], [t_tsb], scale=sc(ee))
                    yield
                    bku, tbu = bank()
                    MM(bku[:, 0:128], XT[:, t, :], rbf[:], True, True, [XTH[par][(t // 2) if len(tiles) == 4 else 0], t_rbf], [tbu])
                    ACT(ubf[:], bku[:, 0:128], AF.Copy, [tbu, t_sm], [t_ubf], scale=sc(bt))
                    yield
                    if not samp:
                        bks, tbs = bank()
                        MM(bks[:, 0:128], kd[:, t, :], ubf[:], True, True, [t_kd, t_ubf], [tbs])
                        STT(Sb[:], Sf[:], sc(glb), bks[:, 0:128], ALU.mult, ALU.add, [t_Sf, t_sm, tbs], [t_Sb])
                        STT(Sf[:], Sf[:], sc(glb), bks[:, 0:128], ALU.mult, ALU.add, [t_Sf, t_sm, tbs], [t_Sf])
                        yield
                    else:
                        TT("dve", kdm[:], kd[:, t, :].unsqueeze(1).broadcast_to([128, 16, 128]),
                           cf[:, CF_ROWM:CF_ROWM + 16].unsqueeze(2).broadcast_to([128, 16, 128]), ALU.mult, [t_kd, t_cf], [t_kdm])
                        for s_ in range(16):
                            if s_ % 4 == 0:
                                bks, tbs = bank()
                            MM(bks[:, 128 * (s_ % 4):128 * (s_ % 4) + 128], kdm[:, s_, :], ubf[:], True, True, [t_kdm, t_ubf], [tbs])
                            if s_ % 4 == 3:
                                for s2 in range(s_ - 3, s_ + 1):
                                    STT(Sso[:, s2, :], Ss[:, s2, :], glbs[:, s2, h:h + 1], bks[:, 128 * (s2 % 4):128 * (s2 % 4) + 128],
                                        ALU.mult, ALU.add, [t_Ss, t_sm, tbs], [t_Ss])
                                yield
                    if full:
                        bk4, tb4 = bank()
                        MM(bk4[:, 0:128], aq[:, t, :], ubf[:], True, True, [t_aq, t_ubf], [tb4])
                        TT("dve", osb[:], bk4[:, 0:128], tsb[:], ALU.add, [tb4, t_tsb], [t_osb])
                        ACT(junk[:], osb[:], AF.Square, [t_osb], [t_junk, t_sm4], accum=sm4[:, 0:1])
                        yield
                        TS("dve", sm4[:, 1:2], sm4[:, 0:1], 1.0 / 128, ALU.mult, [t_sm4], [t_sm4], s2=1e-6, op1=ALU.add)
                        ACT(sm4[:, 2:3], sm4[:, 1:2], AF.Ln, [t_sm4], [t_sm4])
                        ACT(sm4[:, 3:4], sm4[:, 2:3], AF.Exp, [t_sm4], [t_sm4], scale=-0.5)
                        TS("dve", onb[:], osb[:], sm4[:, 3:4], ALU.mult, [t_osb, t_sm4], [t_onb])
                        yield
                        bko, tbo = bank()
                        bkob = bko[:].bitcast(BF)
                        TR(bkob[:, 0:128], onb[:], ident_b, [t_onb, t_cb], [tbo])
                        STT(mixT[:, 4 + h, mc0:mc0 + 128], bkob[:, 0:128], pv[:, PV_NW:PV_NW + 1], gT[:, mc0:mc0 + 128],
                            ALU.mult, ALU.mult, [tbo, t_pv, t_gT], [t_mix[4 + h]])
                        yield
                    if i == 15:
                        S.dma("sp", odelta_p[h, :, :], Sf[:], reads=[t_Sf])

            def run_il(gens, weights):
                gens = [g for g in gens if g is not None]
                alive = [True] * len(gens)
                while any(alive):
                    for gi, g in enumerate(gens):
                        if not alive[gi]:
                            continue
                        for _ in range(weights[gi]):
                            try:
                                next(g)
                            except StopIteration:
                                alive[gi] = False
                                break

            GROUPS = [[0, 1, 2, 3], [4, 5, 6, 7], [8, 9, 10, 11], [12, 13, 14, 15], [16]]

            for h in range(8):
                S.dma("sp", Ss[:], sdelta[:, h, :, :].rearrange("s k v -> k s v"), writes=[t_Ss])
                CP("pool", Ssb[:], Ss[:], [t_Ss], [t_Ssb])
                S.op("pool", lambda e: e.memset(Sf[:], 0.0), [], [t_Sf])
                S.op("pool", lambda e: e.memset(Sb[:], 0.0), [], [t_Sb])
                wk, tk_ = w_next()
                conv_stream(h, 1, wk, tk_, KV_CH, kT, t_kT, 0, True, 1.0)
                wv, tv_ = w_next()
                conv_stream(h, 2, wv, tv_, KV_CH, vT, t_vT, 0, False, 1.0)
                wq, tq_ = w_next()
                conv_stream(h, 0, wq, tq_, OWN_CH, qT, t_qT, 1024, True, 128.0 ** -0.5)
                wg, tg_ = w_next()
                for (c0, n) in MEM_CH:
                    bk, tb = proj_fm(wg, tg_, c0, n)
                    ACT(gT[:, c0 - 1024:c0 - 1024 + n], bk[:, 0:n], AF.Silu, [tb], [t_gT])
                colm = cb[:, CB_COLM:CB_COLM + 2048].rearrange("p (s i) -> p s i", i=128)
                TT("dve", kTm[:], kT[:, 2048:2176].unsqueeze(1).broadcast_to([128, 16, 128]), colm, ALU.mult, [t_kT, t_cb], [t_kTm])
                TT("pool", qTm[:], qT[:, 1024:1152].unsqueeze(1).broadcast_to([128, 16, 128]), colm, ALU.mult, [t_qT, t_cb], [t_qTm])
                prevB = None
                for gi, tiles in enumerate(GROUPS):
                    ga = groupA(h, tiles, gi % 2)
                    if prevB is None:
                        run_il([ga], [1])
                    else:
                        run_il([ga, prevB], [int(os.environ.get('K_WA', '1')), int(os.environ.get('K_WB', '2'))])
                    prevB = groupB(h, tiles, gi % 2)
                run_il([prevB], [1])
                S.dma("sp", odelta_s[:, h, :, :].rearrange("s k v -> k s v"), Sso[:], reads=[t_Sso])
            S.barrier()

        es2.close()
        with ExitStack() as ph, contextlib.suppress(_Skip):
            if STAGE < 5:
                raise _Skip()
            wo, t_wo = sb(ph, [128, 16, 2048], BF)
            lngb, t_lngb = sb(ph, [128, 2, 2048])
            xr = [sb(ph, [128, 2048]) for _ in range(2)]
            ypb = [sb(ph, [128, 2048]) for _ in range(2)]
            yo = [sb(ph, [128, 2048]) for _ in range(2)]
            jkb = [sb(ph, [128, 2048], BF) for _ in range(2)]
            stb = [sb(ph, [128, 8]) for _ in range(2)]
            for k4 in range(16):
                S.dma("pool", wo[:, k4, :].rearrange("p (a n) -> p a n", n=512),
                      w_out[128 * k4:128 * k4 + 128, :].rearrange("p (a n) -> p a n", n=512), writes=[t_wo])
            S.dma("sp", lngb[:], lngb_d[:, :, :], writes=[t_lngb])
            S.dma("sp", xr[0][0][:], xall[1024:1152, :], writes=[xr[0][1]])
            for ti in range(9):
                xr_, txr = xr[ti % 2]
                yo_, tyo = yo[ti % 2]
                yp, t_yp = ypb[ti % 2]
                jk, t_jk = jkb[ti % 2]
                st, t_st = stb[ti % 2]
                if ti + 1 < 9:
                    S.dma("sp", xr[(ti + 1) % 2][0][:], xall[1024 + 128 * (ti + 1):1024 + 128 * (ti + 1) + 128, :], writes=[xr[(ti + 1) % 2][1]])
                for nb in range(4):
                    bk, tb = bank()
                    for k in range(16):
                        MM(bk[:, 0:512], mixT[:, k, 128 * ti:128 * ti + 128], wo[:, k, 512 * nb:512 * nb + 512], k == 0, k == 15, [t_mix[k], t_wo], [tb])
                    STT(yp[:, 512 * nb:512 * nb + 512], xr_[:, 512 * nb:512 * nb + 512], ALPHA, bk[:, 0:512], ALU.mult, ALU.add, [txr, tb], [t_yp])
                S.op("dve", lambda e, st=st, yp=yp: e.reduce_sum(out=st[:, 0:1], in_=yp[:], axis=AX.X), [t_yp], [t_st])
                ACT(jk[:], yp[:], AF.Square, [t_yp], [t_jk, t_st], accum=st[:, 1:2])
                TS("dve", st[:, 2:3], st[:, 0:1], 1.0 / 2048, ALU.mult, [t_st], [t_st])
                TT("dve", st[:, 3:4], st[:, 2:3], st[:, 2:3], ALU.mult, [t_st], [t_st])
                STT(st[:, 4:5], st[:, 1:2], 1.0 / 2048, st[:, 3:4], ALU.mult, ALU.subtract, [t_st], [t_st])
                TS("dve", st[:, 4:5], st[:, 4:5], 1e-5, ALU.add, [t_st], [t_st])
                ACT(st[:, 5:6], st[:, 4:5], AF.Ln, [t_st], [t_st])
                ACT(st[:, 6:7], st[:, 5:6], AF.Exp, [t_st], [t_st], scale=-0.5)
                STT(st[:, 7:8], st[:, 2:3], -1.0, st[:, 6:7], ALU.mult, ALU.mult, [t_st], [t_st])
                ACT(yp[:], yp[:], AF.Identity, [t_yp, t_st], [t_yp], scale=st[:, 6:7], bias=st[:, 7:8])
                TT("dve", yp[:], yp[:], lngb[:, 0, :], ALU.mult, [t_yp, t_lngb], [t_yp])
                TT("pool", yo_[:], yp[:], lngb[:, 1, :], ALU.add, [t_yp, t_lngb], [tyo])
                S.dma("sp", y_d[128 * ti:128 * ti + 128, :], yo_[:], reads=[tyo])
            S.barrier()

        with nc.Block() as block:
            @block.tensor
            def _(e):
                S.replay("pe", e)

            @block.scalar
            def _(e):
                S.replay("act", e)

            @block.vector
            def _(e):
                S.replay("dve", e)

            @block.gpsimd
            def _(e):
                S.replay("pool", e)

            @block.sync
            def _(e):
                S.replay("sp", e)
        build_nc.stats = dict(S.cnt)
    return nc


def _consts():
    idx = np.arange(128)
    blk = idx // 8
    t, i = idx[:, None], idx[None, :]
    same = (blk[:, None] == blk[None, :])
    cf = np.zeros((128, NCF), np.float32)
    cf[:, CF_ID:CF_ID + 128] = np.eye(128)
    cf[:, CF_ROWM:CF_ROWM + 16] = (blk[:, None] == np.arange(16)[None, :])
    cb = np.zeros((128, NCB), np.float32)
    cb[:, CB_ID:CB_ID + 128] = np.eye(128)
    cb[:, CB_ONE:CB_ONE + 128] = 1.0
    cb[:, CB_STRP:CB_STRP + 128] = (i > t)
    cb[:, CB_STRS:CB_STRS + 128] = (i > t) & same
    colm = np.zeros((128, 16, 128), np.float32)
    for s in range(16):
        colm[:, s, 8 * s:8 * s + 8] = 1.0
    cb[:, CB_COLM:CB_COLM + 2048] = colm.reshape(128, 2048)
    cb[:, CB_TRIP:CB_TRIP + 128] = (t <= i)
    cb[:, CB_TRIS:CB_TRIS + 128] = (t <= i) & same
    cb[:, CB_BLKS:CB_BLKS + 128] = same
    cb[:, CB_NEGP:CB_NEGP + 128] = np.where(i >= t, 0.0, NEG)
    cb[:, CB_NEGS:CB_NEGS + 128] = np.where((i >= t) & same, 0.0, NEG)
    cb[:, CB_NTRP:CB_NTRP + 128] = -1.0 * (t <= i)
    cb[:, CB_NTRS:CB_NTRS + 128] = -1.0 * ((t <= i) & same)
    return cf, cb.astype(ml_dtypes.bfloat16)


_NC_CACHE = {}


def kernel(x_prompt, x_sample, mem_prompt, state_conv, state_qkv_conv, state_delta, cache_mem_k, cache_mem_v,
           w_in, conv_w, conv_b, conv_ln_g, conv_ln_b, qkv_conv_w, a_log, dt_bias, delta_norm_w,
           w_mem_k, w_mem_v, w_out, ln_g, ln_b):
    f = lambda a: np.ascontiguousarray(np.asarray(a, dtype=np.float32))
    x_prompt, x_sample, mem_prompt = f(x_prompt), f(x_sample), f(mem_prompt)
    cf, cb = _consts()
    pv = np.zeros((128, NPV), np.float32)
    cwT = f(conv_w)[0].T.reshape(4, 128, 31)
    pv[:, PV_CW:PV_CW + 124] = cwT.transpose(1, 0, 2).reshape(128, 124)
    pv[:, PV_CB:PV_CB + 4] = f(conv_b)[0].reshape(4, 128).T
    pv[:, PV_CLG:PV_CLG + 4] = f(conv_ln_g)[0].reshape(4, 128).T
    pv[:, PV_CLB:PV_CLB + 4] = f(conv_ln_b)[0].reshape(4, 128).T
    qwT = f(qkv_conv_w)[0].T.reshape(24, 128, 4)
    pv[:, PV_QW:PV_QW + 96] = qwT.transpose(1, 0, 2).reshape(128, 96)
    pv[:, PV_NW] = f(delta_norm_w)[0]
    pv[:, PV_AL:PV_AL + 8] = f(a_log)[0][None, :]
    pv[:, PV_DT:PV_DT + 8] = f(dt_bias)[0][None, :]
    lngb = np.ascontiguousarray(np.broadcast_to(np.stack([f(ln_g)[0], f(ln_b)[0]])[None], (128, 2, 2048)))
    W_in, W_mk, W_mv, W_out = f(w_in)[0], f(w_mem_k)[0], f(w_mem_v)[0], f(w_out)[0]
    sc, sq, sd, ck, cv = f(state_conv)[0], f(state_qkv_conv)[0], f(state_delta)[0], f(cache_mem_k)[0], f(cache_mem_v)[0]
    in_maps = []
    for c in range(8):
        b, hf = c // 2, c % 2
        xall = np.zeros((NCOL, 2048), np.float32)
        if hf == 1:
            xall[0:1024] = x_prompt[b, 0:1024]
        xall[1024:2048] = x_prompt[b, 1024 * hf:1024 * hf + 1024]
        xall[2048:] = x_sample[16 * c:16 * c + 16].reshape(128, 2048)
        sl = slice(16 * c, 16 * c + 16)
        in_maps.append({
            "xall": xall, "memp": mem_prompt[b], "sconv": sc[sl], "sqkv": sq[sl], "sdelta": sd[sl],
            "ck": ck[sl], "cv": cv[sl], "w_in": W_in, "w_mk": W_mk, "w_mv": W_mv, "w_out": W_out,
            "cf": cf, "cb": cb, "pv": pv, "lngb": lngb,
        })
    if os.environ.get("K_CORES"):
        ncores = int(os.environ["K_CORES"])
        nc = build_nc()
        res = run_bass_kernel_spmd(nc, in_maps[:ncores], core_ids=list(range(ncores)), trace=bool(os.environ.get("K_TRACE")))
        kernel.last = res
        R = list(res.results) + [res.results[0]] * (8 - ncores)
    else:
        if "nc" not in _NC_CACHE:
            _NC_CACHE["nc"] = build_nc()
        nc = _NC_CACHE["nc"]
        res = run_bass_kernel_spmd(nc, in_maps, core_ids=list(range(8)))
        R = res.results
    y_p = np.zeros((4, 2048, 2048), np.float32)
    y_s = np.zeros((128, 8, 2048), np.float32)
    o_conv_p = np.zeros((1, 4, 30, 512), np.float32)
    o_qkv_p = np.zeros((1, 4, 3, 3072), np.float32)
    o_delta_p = np.zeros((1, 4, 8, 128, 128), np.float32)
    o_mk = np.zeros((1, 4, 256, 4, 128), np.float32)
    o_mv = np.zeros((1, 4, 256, 4, 128), np.float32)
    o_conv_s = np.zeros((1, 128, 30, 512), np.float32)
    o_qkv_s = np.zeros((1, 128, 3, 3072), np.float32)
    o_delta_s = np.zeros((1, 128, 8, 128, 128), np.float32)
    for c in range(8):
        b, hf = c // 2, c % 2
        r = R[c]
        y_p[b, 1024 * hf:1024 * hf + 1024] = r["y"][0:1024]
        y_s[16 * c:16 * c + 16] = r["y"][1024:1152].reshape(16, 8, 2048)
        sl = slice(16 * c, 16 * c + 16)
        o_conv_s[0, sl] = r["oconv_s"]
        o_qkv_s[0, sl] = r["oqkv_s"]
        o_delta_s[0, sl] = r["odelta_s"]
        if hf == 1:
            o_conv_p[0, b] = r["oconv_p"]
            o_qkv_p[0, b] = r["oqkv_p"]
            o_delta_p[0, b] = r["odelta_p"]
        else:
            o_mk[0, b] = r["omk"].reshape(256, 4, 128)
            o_mv[0, b] = r["omv"].reshape(256, 4, 128)
    return (y_p, y_s, o_conv_p, o_qkv_p, o_delta_p, o_mk, o_mv, o_conv_s, o_qkv_s, o_delta_s)
```

```python
import os
import contextlib
from contextlib import ExitStack
import numpy as np
import ml_dtypes
import concourse.bass as bass
import concourse.mybir as mybir
from concourse.bass_utils import run_bass_kernel_spmd

F32 = mybir.dt.float32
BF = mybir.dt.bfloat16
AF = mybir.ActivationFunctionType
ALU = mybir.AluOpType
AX = mybir.AxisListType

SAME_SYNC = not os.environ.get('K_NOSAME')
STAGE = int(os.environ.get('K_STAGE', '9'))
SUB = int(os.environ.get('K_SUB', '99'))
NT = 17
NCOL = NT * 128
NM = 1152
O_GA, O_GB, O_CG, O_Q, O_K, O_V, O_DG, O_BT, O_DC, O_MQ, O_MG = 0, 512, 1024, 1536, 2560, 3584, 4608, 5632, 5640, 5648, 6160
N_IN = 6672
ALPHA = 2.0 ** 0.25
NEG = -30000.0

CF_ID, CF_ROWM = 0, 128
NCF = 144
CB_ID, CB_ONE, CB_STRP, CB_STRS, CB_COLM = 0, 128, 256, 384, 512
CB_TRIP, CB_TRIS, CB_BLKS, CB_NEGP, CB_NEGS, CB_NTRP, CB_NTRS = 2560, 2688, 2816, 2944, 3072, 3200, 3328
NCB = 3456
PV_CW, PV_CB, PV_CLG, PV_CLB, PV_QW, PV_NW, PV_AL, PV_DT = 0, 124, 128, 132, 136, 232, 233, 241
NPV = 249


class _Skip(Exception):
    pass


class Tok:
    __slots__ = ("w", "r", "excl")

    def __init__(self, excl=False):
        self.w = None
        self.r = {}
        self.excl = excl


class Sched:
    ENG = ("pe", "act", "dve", "pool", "sp")

    def __init__(self, nc, es, ndma=40):
        self.nc = nc
        self.sem = {e: es.enter_context(nc.semaphore("sem_" + e)) for e in self.ENG}
        self.cnt = {e: 0 for e in self.ENG}
        self.q = {e: [] for e in self.ENG}
        self.seen = {e: {} for e in self.ENG}
        self.dsem = {q: [es.enter_context(nc.semaphore("dsem_%s%d" % (q, i))) for i in range(ndma // 2)] for q in ("sp", "pool")}
        self.dcnt = {q: [0] * (ndma // 2) for q in ("sp", "pool")}
        self.dnext = {"sp": 0, "pool": 0}

    def _deps(self, reads, writes):
        ev = []
        for t in reads:
            if t.w is not None:
                ev.append(t.w)
            if t.excl:
                ev.extend(t.r.values())
        for t in writes:
            if t.w is not None:
                ev.append(t.w)
            ev.extend(t.r.values())
        return ev

    def _waits(self, eng, evs):
        out = []
        for (key, val) in evs:
            if key == eng and (eng == "pe" or not SAME_SYNC):
                continue
            if self.seen[eng].get(key, 0) >= val:
                continue
            self.seen[eng][key] = val
            out.append((key, val))
        return out

    def _mark(self, ev, reads, writes):
        for t in writes:
            t.w = ev
            t.r = {}
        k = ev[0]
        for t in reads:
            if t.r.get(k, (k, 0))[1] < ev[1]:
                t.r[k] = ev

    def op(self, eng, fn, reads=(), writes=()):
        w = self._waits(eng, self._deps(reads, writes))
        self.cnt[eng] += 1
        ev = (eng, self.cnt[eng])
        self.q[eng].append(("op", fn, w, None))
        self._mark(ev, reads, writes)
        return ev

    def dma(self, q, out, in_, reads=(), writes=(), **kw):
        k = self.dnext[q]
        self.dnext[q] = (k + 1) % len(self.dsem[q])
        evs = self._deps(reads, writes)
        if self.dcnt[q][k] > 0:
            evs.append((("d", q, k), 16 * self.dcnt[q][k]))
        w = self._waits(q, evs)
        self.dcnt[q][k] += 1
        ev = (("d", q, k), 16 * self.dcnt[q][k])
        self.q[q].append(("dma", (out, in_, kw), w, k))
        self._mark(ev, reads, writes)
        return ev

    def barrier(self):
        evs = [(e, self.cnt[e]) for e in self.ENG if self.cnt[e] > 0]
        if not os.environ.get("K_NODMABAR"):
            evs += [(("d", q, k), 16 * c) for q in ("sp", "pool") for k, c in enumerate(self.dcnt[q]) if c > 0]
        for e in self.ENG:
            w = self._waits(e, evs)
            if w:
                self.q[e].append(("wait", None, w, None))

    def semh(self, key):
        return self.sem[key] if isinstance(key, str) else self.dsem[key[1]][key[2]]

    def replay(self, e, h):
        for kind, fn, waits, k in self.q[e]:
            for (key, val) in waits:
                h.wait_ge(self.semh(key), val)
            if kind == "op":
                fn(h).then_inc(self.sem[e], 1)
            elif kind == "dma":
                out, in_, kw = fn
                h.dma_start(out=out, in_=in_, **kw).then_inc(self.dsem[e][k], 16)


def build_nc(dbg=False):
    nc = bass.Bass("TRN2", target_bir_lowering=False)

    def din(name, shape, dt=F32):
        return nc.dram_tensor(name, list(shape), dt, kind="ExternalInput").ap()

    def dout(name, shape, dt=F32):
        return nc.dram_tensor(name, list(shape), dt, kind="ExternalOutput").ap()

    xall = din("xall", [NCOL, 2048])
    memp = din("memp", [256, 2048])
    sconv = din("sconv", [16, 30, 512])
    sqkv = din("sqkv", [16, 3, 3072])
    sdelta = din("sdelta", [16, 8, 128, 128])
    ck = din("ck", [16, 256, 4, 128])
    cv = din("cv", [16, 256, 4, 128])
    w_in = din("w_in", [2048, N_IN])
    w_mk = din("w_mk", [2048, 512])
    w_mv = din("w_mv", [2048, 512])
    w_out = din("w_out", [2048, 2048])
    cf_d = din("cf", [128, NCF])
    cb_d = din("cb", [128, NCB], BF)
    pv_d = din("pv", [128, NPV])
    lngb_d = din("lngb", [128, 2, 2048])

    y_d = dout("y", [NM, 2048])
    oconv_p = dout("oconv_p", [30, 512])
    oqkv_p = dout("oqkv_p", [3, 3072])
    odelta_p = dout("odelta_p", [8, 128, 128])
    omk = dout("omk", [256, 512])
    omv = dout("omv", [256, 512])
    oconv_s = dout("oconv_s", [16, 30, 512])
    oqkv_s = dout("oqkv_s", [16, 3, 3072])
    odelta_s = dout("odelta_s", [16, 8, 128, 128])

    with ExitStack() as es:
        S = Sched(nc, es)
        ctr = [0]

        def sb(es_, shape, dt=F32):
            ctr[0] += 1
            t = es_.enter_context(nc.sbuf_tensor("t%d" % ctr[0], list(shape), dt))
            return t, Tok()

        def ACT(out, in_, func, reads, writes, bias=None, scale=None, accum=None):
            kw = {}
            if bias is not None:
                kw["bias"] = bias
            if scale is not None:
                kw["scale"] = scale
            if accum is not None:
                kw["accum_out"] = accum
            return S.op("act", lambda e: e.activation(out=out, in_=in_, func=func, **kw), reads, writes)

        def TS(eng, out, in0, s1, op0, reads, writes, s2=None, op1=None):
            if op1 is None:
                return S.op(eng, lambda e: e.tensor_scalar(out=out, in0=in0, scalar1=s1, scalar2=None, op0=op0), reads, writes)
            return S.op(eng, lambda e: e.tensor_scalar(out=out, in0=in0, scalar1=s1, scalar2=s2, op0=op0, op1=op1), reads, writes)

        def TT(eng, out, in0, in1, op, reads, writes):
            return S.op(eng, lambda e: e.tensor_tensor(out=out, in0=in0, in1=in1, op=op), reads, writes)

        def STT(out, in0, scalar, in1, op0, op1, reads, writes):
            return S.op("dve", lambda e: e.scalar_tensor_tensor(out=out, in0=in0, scalar=scalar, in1=in1, op0=op0, op1=op1), reads, writes)

        def CP(eng, out, in_, reads, writes):
            if eng == "act":
                return S.op("act", lambda e: e.copy(out=out, in_=in_), reads, writes)
            return S.op(eng, lambda e: e.tensor_copy(out=out, in_=in_), reads, writes)

        def split3(src, tsrc, outs, touts, r_f32, t_r):
            CP("dve", outs[0], src, [tsrc], [touts])
            TT("dve", r_f32, src, outs[0], ALU.subtract, [tsrc, touts], [t_r])
            CP("dve", outs[1], r_f32, [t_r], [touts])
            TT("dve", outs[2], r_f32, outs[1], ALU.subtract, [t_r, touts], [touts])

        def MM(out, lhsT, rhs, start, stop, reads, writes):
            return S.op("pe", lambda e: e.matmul(out=out, lhsT=lhsT, rhs=rhs, start=start, stop=stop), reads, writes)

        def TR(out, in_, ident, reads, writes):
            return S.op("pe", lambda e: e.transpose(out=out, in_=in_, identity=ident), reads, writes)

        cf, t_cf = sb(es, [128, NCF])
        cb, t_cb = sb(es, [128, NCB], BF)
        pv, t_pv = sb(es, [128, NPV])
        mixT = es.enter_context(nc.sbuf_tensor("mixT", [128, 16, NM], BF))
        t_mix = [Tok() for _ in range(16)]
        NW = 4
        wsl = [sb(es, [128, 16, 128], BF) for _ in range(NW)]
        bt, t_sm = sb(es, [128, NT, 8])
        nbt, _ = sb(es, [128, NT, 8])
        ee, _ = sb(es, [128, NT, 8])
        nee, _ = sb(es, [128, NT, 8])
        egl, _ = sb(es, [128, NT, 8])
        glb, _ = sb(es, [128, NT, 8])
        glbs, _ = sb(es, [128, 16, 8])
        nega, _ = sb(es, [128, 8])
        g3 = [sb(es, [128, NT, 8], BF)[0] for _ in range(3)]
        es2 = ExitStack()
        xT, t_xT = sb(es2, [128, 16, NCOL], BF)
        psb = [es.enter_context(nc.psum_tensor("ps%d" % i, [128, 512], F32)) for i in range(8)]
        t_ps = [Tok(excl=True) for _ in range(8)]
        pctr = [0]

        def bank():
            i = pctr[0] % 8
            pctr[0] += 1
            return psb[i], t_ps[i]

        ident_f = cf[:, CF_ID:CF_ID + 128]
        ident_b = cb[:, CB_ID:CB_ID + 128]
        ones_b = cb[:, CB_ONE:CB_ONE + 128]

        S.dma("sp", cf[:], cf_d[:, :], writes=[t_cf])
        S.dma("sp", cb[:], cb_d[:, :], writes=[t_cb])
        S.dma("sp", pv[:], pv_d[:, :], writes=[t_pv])

        WL = []
        WL.append((w_in, O_BT, 16))
        for c in range(4):
            WL += [(w_in, O_GA + 128 * c, 128), (w_in, O_GB + 128 * c, 128), (w_in, O_CG + 128 * c, 128)]
        for h in range(4):
            WL += [(w_mk, 128 * h, 128), (w_mv, 128 * h, 128), (w_in, O_MQ + 128 * h, 128), (w_in, O_MG + 128 * h, 128)]
        for h in range(8):
            WL += [(w_in, O_K + 128 * h, 128), (w_in, O_V + 128 * h, 128), (w_in, O_Q + 128 * h, 128), (w_in, O_DG + 128 * h, 128)]
        wst = {"issued": 0, "used": 0}

        def w_issue():
            i = wst["issued"]
            if i >= len(WL):
                return
            src, c0, n = WL[i]
            t, tk = wsl[i % NW]
            S.dma("pool", t[:, :, 0:n], src[:, c0:c0 + n].rearrange("(c p) n -> p c n", p=128), writes=[tk])
            wst["issued"] += 1

        def w_next():
            i = wst["used"]
            while wst["issued"] < min(len(WL), i + NW - 1) or wst["issued"] <= i:
                w_issue()
            wst["used"] += 1
            return wsl[i % NW]

        def proj_fm(wt, wtk, c0, n, ncols=128):
            bk, tb = bank()
            for k in range(16):
                MM(bk[0:ncols, 0:n], wt[:, k, 0:ncols], xT[:, k, c0:c0 + n], k == 0, k == 15, [wtk, t_xT], [tb])
            return bk, tb

        with ExitStack() as ph, contextlib.suppress(_Skip):
            if STAGE < 0:
                raise _Skip()
            xs = [sb(ph, [128, 2048], BF) for _ in range(3)]
            for i in range(NT):
                t, tk = xs[i % 3]
                src = xall[128 * i:128 * i + 128, :]
                S.dma("pool", t[:].rearrange("p (a n) -> p a n", n=512), src.rearrange("p (a n) -> p a n", n=512), writes=[tk])
                for half in range(2):
                    bk, tb = bank()
                    bkb = bk[:].bitcast(BF)
                    for c in range(8):
                        TR(bkb[:, 128 * c:128 * c + 128], t[:, 128 * (8 * half + c):128 * (8 * half + c) + 128], ident_b, [tk, t_cb], [tb])
                    src_v = bkb[:, 0:1024].rearrange("p (c n) -> p c n", c=8)
                    dst = xT[:, 8 * half:8 * half + 8, 128 * i:128 * i + 128]
                    CP("act" if half == 0 else "dve", dst, src_v, [tb], [t_xT])
            S.barrier()

        with ExitStack() as ph, contextlib.suppress(_Skip):
            if STAGE < 1:
                raise _Skip()
            bd, t_bd = sb(ph, [128, NT, 16])
            gtk, _ = sb(ph, [128, NT, 8])
            gc_, _ = sb(ph, [128, NT, 8])
            tmp8, t_tmp8 = sb(ph, [128, NT, 8])
            gm, t_gm = sb(ph, [128, 16, 8], BF)
            wt, wtk = w_next()
            bk, tb = bank()
            for i in range(NT):
                for k in range(16):
                    MM(bk[:, 16 * i:16 * i + 16], xT[:, k, 128 * i:128 * i + 128], wt[:, k, 0:16], k == 0, k == 15, [wtk, t_xT], [tb])
            CP("dve", bd[:], bk[:, 0:16 * NT].rearrange("p (t c) -> p t c", c=16), [tb], [t_bd])
            if SUB < 1:
                raise _Skip()
            ACT(bt[:], bd[:, :, 0:8], AF.Sigmoid, [t_bd], [t_sm])
            TS("dve", nbt[:], bt[:], -1.0, ALU.mult, [t_sm], [t_sm])
            if SUB < 2:
                raise _Skip()
            ACT(nega[:], pv[:, PV_AL:PV_AL + 8], AF.Exp, [t_pv], [t_sm])
            TS("dve", nega[:], nega[:], -1.0, ALU.mult, [t_sm], [t_sm])
            if SUB < 3:
                raise _Skip()
            TT("dve", tmp8[:], bd[:, :, 8:16], pv[:, PV_DT:PV_DT + 8].unsqueeze(1).broadcast_to([128, NT, 8]), ALU.add, [t_bd, t_pv], [t_tmp8])
            ACT(tmp8[:], tmp8[:], AF.Exp, [t_tmp8], [t_tmp8])
            ACT(tmp8[:], tmp8[:], AF.Ln, [t_tmp8], [t_tmp8], bias=1.0)
            TT("dve", gtk[:], tmp8[:], nega[:].unsqueeze(1).broadcast_to([128, NT, 8]), ALU.mult, [t_tmp8, t_sm], [t_sm])
            if SUB < 4:
                raise _Skip()
            split3(gtk[:], t_sm, [g3[0][:], g3[1][:], g3[2][:]], t_sm, tmp8[:], t_tmp8)
            bk, tb = bank()
            for i in range(NT):
                tri = cb[:, CB_TRIP:CB_TRIP + 128] if i < 16 else cb[:, CB_TRIS:CB_TRIS + 128]
                blk = ones_b if i < 16 else cb[:, CB_BLKS:CB_BLKS + 128]
                for j in range(3):
                    MM(bk[:, 8 * i:8 * i + 8], tri, g3[j][:, i, :], j == 0, j == 2, [t_cb, t_sm], [tb])
                for j in range(3):
                    MM(bk[:, 256 + 8 * i:256 + 8 * i + 8], blk, g3[j][:, i, :], j == 0, j == 2, [t_cb, t_sm], [tb])
            CP("dve", gc_[:], bk[:, 0:8 * NT].rearrange("p (t c) -> p t c", c=8), [tb], [t_sm])
            CP("act", tmp8[:], bk[:, 256:256 + 8 * NT].rearrange("p (t c) -> p t c", c=8), [tb], [t_tmp8])
            if SUB < 5:
                raise _Skip()
            ACT(ee[:], gc_[:], AF.Exp, [t_sm], [t_sm])
            TS("dve", nee[:], ee[:], -1.0, ALU.mult, [t_sm], [t_sm])
            ACT(glb[:], tmp8[:], AF.Exp, [t_tmp8], [t_sm])
            TT("dve", tmp8[:], tmp8[:], gc_[:], ALU.subtract, [t_tmp8, t_sm], [t_tmp8])
            ACT(egl[:], tmp8[:], AF.Exp, [t_tmp8], [t_sm])
            if SUB < 6:
                raise _Skip()
            bk, tb = bank()
            for j in range(3):
                TT("dve", gm[:], g3[j][:, 16, :].unsqueeze(1).broadcast_to([128, 16, 8]),
                   cf[:, CF_ROWM:CF_ROWM + 16].unsqueeze(2).broadcast_to([128, 16, 8]), ALU.mult, [t_sm, t_cf], [t_gm])
                MM(bk[:, 0:128], ones_b, gm[:].rearrange("p s h -> p (s h)"), j == 0, j == 2, [t_cb, t_gm], [tb])
            ACT(glbs[:], bk[:, 0:128].rearrange("p (s h) -> p s h", h=8), AF.Exp, [tb], [t_sm])
            if SUB < 9:
                raise _Skip()
            S.barrier()

        if os.environ.get("K_BAR"):
            S.barrier()
        OWN_CH = [(896, 512), (1408, 512), (1920, 256)]
        with ExitStack() as ph, contextlib.suppress(_Skip):
            if STAGE < 2:
                raise _Skip()
            ubuf, t_ub = sb(ph, [128, 1152])
            usamp, t_us = sb(ph, [128, 16, 38])
            hc, t_hc = sb(ph, [128, 4, NM])
            sgate, t_sg = sb(ph, [128, 4, NM], BF)
            sgt, t_sgt = sb(ph, [128, 512])
            stg, t_stg = sb(ph, [128, 512])
            scs4 = [sb(ph, [120, 128]) for _ in range(4)]
            S.dma("sp", oconv_s[:, 0:22, :], sconv[:, 8:30, :])
            unew_s, t_uns = sb(ph, [128, 4, 128])
            for c in range(4):
                for g4 in range(4):
                    S.dma("sp", scs4[g4][0][:, 0:128], sconv[4 * g4:4 * g4 + 4, :, 128 * c:128 * c + 128].rearrange("s r n -> (s r) n"), writes=[scs4[g4][1]])
                for g4 in range(4):
                    scs, t_scs = scs4[g4]
                    bk, tb = bank()
                    TR(bk[:, 0:120], scs[0:120, 0:128], ident_f[0:120, 0:120], [t_scs, t_cf], [tb])
                    CP("act", usamp[:, 4 * g4:4 * g4 + 4, 0:30], bk[:, 0:120].rearrange("p (s r) -> p s r", r=30), [tb], [t_us])
                wa, ta = w_next()
                wb, tbk = w_next()
                for (c0, n) in OWN_CH:
                    bka, tba = proj_fm(wa, ta, c0, n)
                    bkb_, tbb = proj_fm(wb, tbk, c0, n)
                    ACT(sgt[:, 0:n], bkb_[:, 0:n], AF.Sigmoid, [tbb], [t_sgt])
                    if c0 < 1920:
                        TT("dve", ubuf[:, c0 - 896:c0 - 896 + n], bka[:, 0:n], sgt[:, 0:n], ALU.mult, [tba, t_sgt], [t_ub])
                    else:
                        TT("dve", ubuf[:, 1024:1152], bka[:, 0:128], sgt[:, 0:128], ALU.mult, [tba, t_sgt], [t_ub])
                        TT("dve", unew_s[:, c, :], bka[:, 128:256], sgt[:, 128:256], ALU.mult, [tba, t_sgt], [t_uns])
                        CP("act", usamp[:, :, 30:38], unew_s[:, c, :].rearrange("p (s l) -> p s l", l=8), [t_uns], [t_us])
                wg, tg = w_next()
                for (c0, n) in [(1024, 512), (1536, 512), (2048, 128)]:
                    bkg, tbg = proj_fm(wg, tg, c0, n)
                    ACT(sgate[:, c, c0 - 1024:c0 - 1024 + n], bkg[:, 0:n], AF.Silu, [tbg], [t_sg])
                cw = lambda j: pv[:, PV_CW + 31 * c + j:PV_CW + 31 * c + j + 1]
                TS("dve", hc[:, c, 0:1024], ubuf[:, 98:98 + 1024], cw(0), ALU.mult, [t_ub, t_pv], [t_hc],
                   s2=pv[:, PV_CB + c:PV_CB + c + 1], op1=ALU.add)
                for j in range(1, 31):
                    STT(hc[:, c, 0:1024], ubuf[:, 98 + j:98 + j + 1024], cw(j), hc[:, c, 0:1024], ALU.mult, ALU.add, [t_ub, t_pv, t_hc], [t_hc])
                hs = hc[:, c, 1024:1152].rearrange("p (s l) -> p s l", l=8)
                TS("dve", hs, usamp[:, :, 0:8], cw(0), ALU.mult, [t_us, t_pv], [t_hc], s2=pv[:, PV_CB + c:PV_CB + c + 1], op1=ALU.add)
                for j in range(1, 31):
                    STT(hs, usamp[:, :, j:j + 8], cw(j), hs, ALU.mult, ALU.add, [t_us, t_pv, t_hc], [t_hc])
                bk, tb = bank()
                TR(bk[0:30, 0:128], ubuf[:, 1122:1152], ident_f, [t_ub, t_cf], [tb])
                CP("act", stg[0:30, 128 * c:128 * c + 128], bk[0:30, 0:128], [tb], [t_stg])
            S.dma("sp", oconv_p[:, :], stg[0:30, :], reads=[t_stg])
            stg2, t_stg2 = sb(ph, [128, 512])
            for c in range(4):
                bk, tb = bank()
                TR(bk[:, 0:128], unew_s[:, c, :], ident_f, [t_uns, t_cf], [tb])
                CP("act", stg2[:, 128 * c:128 * c + 128], bk[:, 0:128], [tb], [t_stg2])
            for s_ in range(16):
                S.dma("sp", oconv_s[s_, 22:30, :], stg2[8 * s_:8 * s_ + 8, :], reads=[t_stg2])
            sq, t_sq = sb(ph, [128, 512])
            hl = [sb(ph, [128, 512], BF) for _ in range(4)]
            mean, t_mean = sb(ph, [128, 512])
            var, t_var = sb(ph, [128, 512])
            rstd, t_rstd = sb(ph, [128, 512])
            xc, t_xc = sb(ph, [128, 512])
            for (m0, n) in [(0, 512), (512, 512), (1024, 128)]:
                bk1, tb1 = bank()
                bk2, tb2 = bank()
                for c in range(4):
                    CP("act", hl[0][0][:, 0:n], hc[:, c, m0:m0 + n], [t_hc], [hl[0][1]])
                    TT("dve", hl[1][0][:, 0:n], hc[:, c, m0:m0 + n], hl[0][0][:, 0:n], ALU.subtract, [t_hc, hl[0][1]], [hl[1][1]])
                    MM(bk1[:, 0:n], ones_b, hl[0][0][:, 0:n], c == 0, False, [t_cb, hl[0][1]], [tb1])
                    MM(bk1[:, 0:n], ones_b, hl[1][0][:, 0:n], False, c == 3, [t_cb, hl[1][1]], [tb1])
                for c in range(4):
                    ACT(sq[:, 0:n], hc[:, c, m0:m0 + n], AF.Square, [t_hc], [t_sq])
                    CP("act", hl[2][0][:, 0:n], sq[:, 0:n], [t_sq], [hl[2][1]])
                    TT("dve", hl[3][0][:, 0:n], sq[:, 0:n], hl[2][0][:, 0:n], ALU.subtract, [t_sq, hl[2][1]], [hl[3][1]])
                    MM(bk2[:, 0:n], ones_b, hl[2][0][:, 0:n], c == 0, False, [t_cb, hl[2][1]], [tb2])
                    MM(bk2[:, 0:n], ones_b, hl[3][0][:, 0:n], False, c == 3, [t_cb, hl[3][1]], [tb2])
                ACT(mean[:, 0:n], bk1[:, 0:n], AF.Copy, [tb1], [t_mean], scale=1.0 / 512)
                TT("dve", var[:, 0:n], mean[:, 0:n], mean[:, 0:n], ALU.mult, [t_mean], [t_var])
                STT(var[:, 0:n], bk2[:, 0:n], 1.0 / 512, var[:, 0:n], ALU.mult, ALU.subtract, [tb2, t_var], [t_var])
                TS("dve", var[:, 0:n], var[:, 0:n], 1e-5, ALU.add, [t_var], [t_var])
                ACT(var[:, 0:n], var[:, 0:n], AF.Ln, [t_var], [t_var])
                ACT(rstd[:, 0:n], var[:, 0:n], AF.Exp, [t_var], [t_rstd], scale=-0.5)
                for c in range(4):
                    TT("dve", xc[:, 0:n], hc[:, c, m0:m0 + n], mean[:, 0:n], ALU.subtract, [t_hc, t_mean], [t_xc])
                    TT("dve", xc[:, 0:n], xc[:, 0:n], rstd[:, 0:n], ALU.mult, [t_xc, t_rstd], [t_xc])
                    ACT(xc[:, 0:n], xc[:, 0:n], AF.Silu, [t_xc, t_pv], [t_xc],
                        scale=pv[:, PV_CLG + c:PV_CLG + c + 1], bias=pv[:, PV_CLB + c:PV_CLB + c + 1])
                    TT("dve", mixT[:, c, m0:m0 + n], xc[:, 0:n], sgate[:, c, m0:m0 + n], ALU.mult, [t_xc, t_sg], [t_mix[c]])
            S.barrier()

        MEM_CH = [(1024, 512), (1536, 512), (2048, 128)]
        with ExitStack() as ph, contextlib.suppress(_Skip):
            if STAGE < 3:
                raise _Skip()
            mT, t_mT = sb(ph, [128, 16, 256], BF)
            xs3, t_xs3 = sb(ph, [128, 2048], BF)
            for j in range(2):
                S.dma("pool", xs3[:].rearrange("p (a n) -> p a n", n=512), memp[128 * j:128 * j + 128, :].rearrange("p (a n) -> p a n", n=512), writes=[t_xs3])
                for half in range(2):
                    bk, tb = bank()
                    bkb = bk[:].bitcast(BF)
                    for c in range(8):
                        TR(bkb[:, 128 * c:128 * c + 128], xs3[:, 128 * (8 * half + c):128 * (8 * half + c) + 128], ident_b, [t_xs3, t_cb], [tb])
                    CP("act" if half == 0 else "dve", mT[:, 8 * half:8 * half + 8, 128 * j:128 * j + 128],
                       bkb[:, 0:1024].rearrange("p (c n) -> p c n", c=8), [tb], [t_mT])
            mqT, t_mq = sb(ph, [128, NM], BF)
            mgT, t_mg = sb(ph, [128, NM], BF)
            KTp, t_ktp = sb(ph, [128, 256], BF)
            Vp, t_vp = sb(ph, [128, 2, 128], BF)
            kvst, t_kvst = sb(ph, [128, 2, 2, 128])
            kc, t_kc = sb(ph, [128, 16, 2, 128], BF)
            vc, t_vc = sb(ph, [128, 16, 2, 128], BF)
            kcT, t_kct = sb(ph, [128, 16, 256], BF)
            mqm, t_mqm = sb(ph, [128, 16, 128], BF)
            pf, t_pf = sb(ph, [128, 256])
            pn, t_pn = sb(ph, [128, 256], BF)
            pT, t_pT = sb(ph, [128, 2, 128], BF)
            mx, t_mx = sb(ph, [128, 4])
            for h in range(4):
                wk, tk_ = w_next()
                wv, tv_ = w_next()
                bk, tb = bank()
                for k in range(16):
                    MM(bk[:, 0:256], wk[:, k, :], mT[:, k, :], k == 0, k == 15, [tk_, t_mT], [tb])
                CP("act", KTp[:], bk[:, 0:256], [tb], [t_ktp])
                bk, tb = bank()
                for mc in range(2):
                    for k in range(16):
                        MM(bk[:, 128 * mc:128 * mc + 128], mT[:, k, 128 * mc:128 * mc + 128], wk[:, k, :], k == 0, k == 15, [tk_, t_mT], [tb])
                    for k in range(16):
                        MM(bk[:, 256 + 128 * mc:256 + 128 * mc + 128], mT[:, k, 128 * mc:128 * mc + 128], wv[:, k, :], k == 0, k == 15, [tv_, t_mT], [tb])
                CP("dve", kvst[:].rearrange("p a b d -> p (a b d)"), bk[:, 0:512], [tb], [t_kvst])
                CP("act", Vp[:].rearrange("p b d -> p (b d)"), bk[:, 256:512], [tb], [t_vp])
                S.dma("sp", omk[:, 128 * h:128 * h + 128].rearrange("(mc m) d -> m mc d", m=128), kvst[:, 0, :, :], reads=[t_kvst])
                S.dma("sp", omv[:, 128 * h:128 * h + 128].rearrange("(mc m) d -> m mc d", m=128), kvst[:, 1, :, :], reads=[t_kvst])
                S.dma("pool", kc[:], ck[:, :, h, :].rearrange("s (mc m) d -> m s mc d", m=128), writes=[t_kc])
                S.dma("pool", vc[:], cv[:, :, h, :].rearrange("s (mc m) d -> m s mc d", m=128), writes=[t_vc])
                wq, tq_ = w_next()
                wg, tg_ = w_next()
                for (c0, n) in MEM_CH:
                    bk, tb = proj_fm(wq, tq_, c0, n)
                    ACT(mqT[:, c0 - 1024:c0 - 1024 + n], bk[:, 0:n], AF.Copy, [tb], [t_mq], scale=128.0 ** -0.5)
                    bk, tb = proj_fm(wg, tg_, c0, n)
                    ACT(mgT[:, c0 - 1024:c0 - 1024 + n], bk[:, 0:n], AF.Silu, [tb], [t_mg])
                for s in range(16):
                    if s % 4 == 0:
                        bk, tb = bank()
                        bkb = bk[:].bitcast(BF)
                    for mc in range(2):
                        o = (s % 4) * 256 + mc * 128
                        TR(bkb[:, o:o + 128], kc[:, s, mc, :], ident_b, [t_kc, t_cb], [tb])
                    if s % 4 == 3:
                        CP("act", kcT[:, s - 3:s + 1, :], bkb[:, 0:1024].rearrange("p (s m) -> p s m", m=256), [tb], [t_kct])
                TT("dve", mqm[:], mqT[:, 1024:1152].unsqueeze(1).broadcast_to([128, 16, 128]),
                   cb[:, CB_COLM:CB_COLM + 2048].rearrange("p (s i) -> p s i", i=128), ALU.mult, [t_mq, t_cb], [t_mqm])
                for ti in range(9):
                    m0 = 128 * ti
                    bk, tb = bank()
                    if ti < 8:
                        MM(bk[:, 0:256], mqT[:, m0:m0 + 128], KTp[:], True, True, [t_mq, t_ktp], [tb])
                    else:
                        for s in range(16):
                            MM(bk[:, 0:256], mqm[:, s, :], kcT[:, s, :], s == 0, s == 15, [t_mqm, t_kct], [tb])
                    S.op("dve", lambda e, bk=bk: e.reduce_max(out=mx[:, 0:1], in_=bk[:, 0:256], axis=AX.X), [tb], [t_mx])
                    TS("dve", mx[:, 1:2], mx[:, 0:1], -1.0, ALU.mult, [t_mx], [t_mx])
                    ACT(pf[:], bk[:, 0:256], AF.Exp, [tb, t_mx], [t_pf, t_mx], bias=mx[:, 1:2], accum=mx[:, 2:3])
                    S.op("dve", lambda e: e.reciprocal(out=mx[:, 3:4], in_=mx[:, 2:3]), [t_mx], [t_mx])
                    TS("dve", pn[:], pf[:], mx[:, 3:4], ALU.mult, [t_pf, t_mx], [t_pn])
                    bk2, tb2 = bank()
                    bk2b = bk2[:].bitcast(BF)
                    for mc in range(2):
                        TR(bk2b[:, 128 * mc:128 * mc + 128], pn[:, 128 * mc:128 * mc + 128], ident_b, [t_pn, t_cb], [tb2])
                    CP("act", pT[:].rearrange("p a b -> p (a b)"), bk2b[:, 0:256], [tb2], [t_pT])
                    bk3, tb3 = bank()
                    if ti < 8:
                        for mc in range(2):
                            MM(bk3[:, 0:128], Vp[:, mc, :], pT[:, mc, :], mc == 0, mc == 1, [t_vp, t_pT], [tb3])
                    else:
                        for s in range(16):
                            for mc in range(2):
                                MM(bk3[:, 8 * s:8 * s + 8], vc[:, s, mc, :], pT[:, mc, 8 * s:8 * s + 8], mc == 0, mc == 1, [t_vc, t_pT], [tb3])
                    TT("dve", mixT[:, 12 + h, m0:m0 + 128], bk3[:, 0:128], mgT[:, m0:m0 + 128], ALU.mult, [tb3, t_mg], [t_mix[12 + h]])
            S.barrier()

        KV_CH = [(0, 512), (512, 512), (1024, 512), (1536, 512), (2048, 128)]
        with ExitStack() as ph, contextlib.suppress(_Skip):
            if STAGE < 4:
                raise _Skip()
            kT, t_kT = sb(ph, [128, NCOL], BF)
            vT, t_vT = sb(ph, [128, NCOL], BF)
            qT, t_qT = sb(ph, [128, NM], BF)
            gT, t_gT = sb(ph, [128, NM], BF)
            pre = [sb(ph, [128, 515]) for _ in range(2)]
            pres, t_pres = sb(ph, [128, 16, 11])
            sqb2 = [sb(ph, [128, 512], BF) for _ in range(2)]
            acc, t_acc = sb(ph, [128, 512])
            lnb2 = [sb(ph, [128, 512]) for _ in range(2)]
            sq48, t_sq48 = sb(ph, [48, 128])
            qst, t_qst = sb(ph, [128, 3])
            qst_s, t_qsts = sb(ph, [128, 48])
            ost, t_ost = sb(ph, [48, 128])
            Sf, t_Sf = sb(ph, [128, 128])
            Sb, t_Sb = sb(ph, [128, 128], BF)
            Ss, t_Ss = sb(ph, [128, 16, 128])
            Ssb, t_Ssb = sb(ph, [128, 16, 128], BF)
            Sso, t_Sso = Ss, t_Ss
            kTm, t_kTm = sb(ph, [128, 16, 128], BF)
            qTm, t_qTm = sb(ph, [128, 16, 128], BF)
            kdm, t_kdm = qTm, t_qTm
            GT = 4
            gB3 = [sb(ph, [128, GT, 128], BF) for _ in range(3)]
            decT, t_decT = sb(ph, [128, GT, 128])
            decTs, t_decTs = sb(ph, [128, GT, 128], BF)
            P0, t_P0 = sb(ph, [128, GT, 128], BF)
            Pb = [sb(ph, [128, GT, 128], BF) for _ in range(2)]
            Qb = [sb(ph, [128, GT, 128], BF) for _ in range(2)]
            Yb = [sb(ph, [128, GT, 128], BF) for _ in range(2)]
            XTb = [sb(ph, [128, GT, 128], BF) for _ in range(2)]
            PbH = [[Tok(), Tok()] for _ in range(2)]
            QbH = [[Tok(), Tok()] for _ in range(2)]
            YbH = [[Tok(), Tok()] for _ in range(2)]
            XTH = [[Tok(), Tok()] for _ in range(2)]
            aqkb = [sb(ph, [128, GT, 128], BF) for _ in range(2)]
            kdb = [sb(ph, [128, GT, 128], BF) for _ in range(2)]
            vtb = [sb(ph, [128, GT, 128], BF) for _ in range(2)]
            rbf, t_rbf = sb(ph, [128, 128], BF)
            ubf, t_ubf = sb(ph, [128, 128], BF)
            tsb, t_tsb = sb(ph, [128, 128])
            osb, t_osb = sb(ph, [128, 128])
            junk, t_junk = sb(ph, [128, 128], BF)
            onb, t_onb = sb(ph, [128, 128], BF)
            sm4, t_sm4 = sb(ph, [128, 4])


            def conv_stream(h, kind, wt, wtk, chunks, dstT, t_dst, dcol0, norm, qscale):
                fo = kind * 1024 + 128 * h
                fc = fo // 128
                qw = lambda j: pv[:, PV_QW + 4 * fc + j:PV_QW + 4 * fc + j + 1]
                S.dma("sp", sq48[:], sqkv[:, :, fo:fo + 128].rearrange("s r n -> (s r) n"), writes=[t_sq48])
                pi = 0
                S.op("pool", lambda e, p=pre[0][0]: e.memset(p[:, 0:3], 0.0), [], [pre[0][1]])
                for (c0, n) in chunks:
                    bk, tb = proj_fm(wt, wtk, c0, n)
                    npr = n if c0 + n <= 2048 else n - 128
                    skip = max(0, dcol0 - c0)
                    if npr > 0:
                        p_, tp_ = pre[pi]
                        ACT(p_[:, 3:3 + npr], bk[:, 0:npr], AF.Copy, [tb], [tp_])
                        TS("dve", acc[:, 0:npr], p_[:, 0:npr], qw(0), ALU.mult, [tp_, t_pv], [t_acc])
                        for j in range(1, 4):
                            STT(acc[:, 0:npr], p_[:, j:j + npr], qw(j), acc[:, 0:npr], ALU.mult, ALU.add, [tp_, t_pv, t_acc], [t_acc])
                        d0 = c0 - dcol0
                        ACT(dstT[:, d0 + skip:d0 + npr], acc[:, skip:npr], AF.Silu, [t_acc], [t_dst])
                        if c0 + npr == 2048:
                            CP("act", qst[:, 0:3], p_[:, npr:npr + 3], [tp_], [t_qst])
                        else:
                            p2, tp2 = pre[1 - pi]
                            CP("act", p2[:, 0:3], p_[:, npr:npr + 3], [tp_], [tp2])
                            pi = 1 - pi
                    if c0 + n > 2048:
                        bks_, tbs_ = bank()
                        TR(bks_[:, 0:48], sq48[0:48, 0:128], ident_f[0:48, 0:48], [t_sq48, t_cf], [tbs_])
                        CP("act", pres[:, :, 0:3], bks_[:, 0:48].rearrange("p (s r) -> p s r", r=3), [tbs_], [t_pres])
                        CP("act", pres[:, :, 3:11], bk[:, npr:npr + 128].rearrange("p (s l) -> p s l", l=8), [tb], [t_pres])
                        av = acc[:, 0:128].rearrange("p (s l) -> p s l", l=8)
                        TS("dve", av, pres[:, :, 0:8], qw(0), ALU.mult, [t_pres, t_pv], [t_acc])
                        for j in range(1, 4):
                            STT(av, pres[:, :, j:j + 8], qw(j), av, ALU.mult, ALU.add, [t_pres, t_pv, t_acc], [t_acc])
                        d0 = 2048 - dcol0
                        ACT(dstT[:, d0:d0 + 128], acc[:, 0:128], AF.Silu, [t_acc], [t_dst])
                        CP("act", qst_s[:].rearrange("p (s r) -> p s r", r=3), pres[:, :, 8:11], [t_pres], [t_qsts])
                bk, tb = bank()
                TR(bk[0:3, 0:128], qst[:, 0:3], ident_f, [t_qst, t_cf], [tb])
                TR(bk[0:48, 128:256], qst_s[:, 0:48], ident_f, [t_qsts, t_cf], [tb])
                CP("act", ost[0:3, :], bk[0:3, 0:128], [tb], [t_ost])
                S.dma("sp", oqkv_p[:, fo:fo + 128], ost[0:3, :], reads=[t_ost])
                CP("act", ost[0:48, :], bk[0:48, 128:256], [tb], [t_ost])
                S.dma("sp", oqkv_s[:, :, fo:fo + 128].rearrange("s r n -> (s r) n"), ost[0:48, :], reads=[t_ost])
                if norm:
                    ntot = dstT.shape[1]
                    for idx, m0 in enumerate(range(0, ntot, 512)):
                        n = min(512, ntot - m0)
                        sq_, tsq_ = sqb2[idx % 2]
                        ln_, tln_ = lnb2[idx % 2]
                        ACT(sq_[:, 0:n], dstT[:, m0:m0 + n], AF.Square, [t_dst], [tsq_])
                        bk, tb = bank()
                        MM(bk[:, 0:n], ones_b, sq_[:, 0:n], True, True, [t_cb, tsq_], [tb])
                        TS("dve", ln_[:, 0:n], bk[:, 0:n], 1e-6, ALU.add, [tb], [tln_])
                        ACT(ln_[:, 0:n], ln_[:, 0:n], AF.Ln, [tln_], [tln_])
                        ACT(ln_[:, 0:n], ln_[:, 0:n], AF.Exp, [tln_], [tln_], scale=-0.5)
                        if qscale != 1.0:
                            TS("dve", ln_[:, 0:n], ln_[:, 0:n], qscale, ALU.mult, [tln_], [tln_])
                        TT("dve", dstT[:, m0:m0 + n], dstT[:, m0:m0 + n], ln_[:, 0:n], ALU.mult, [t_dst, tln_], [t_dst])

            def groupA(h, tiles, par):
                ng = len(tiles)
                i0 = tiles[0]
                samp = (i0 == 16)
                full = (i0 >= 8)
                L = 2 if samp else 6
                trib = cb[:, CB_TRIS:CB_TRIS + 128] if samp else cb[:, CB_TRIP:CB_TRIP + 128]
                ntrib = cb[:, CB_NTRS:CB_NTRS + 128] if samp else cb[:, CB_NTRP:CB_NTRP + 128]
                negb = cb[:, CB_NEGS:CB_NEGS + 128] if samp else cb[:, CB_NEGP:CB_NEGP + 128]
                strict = cb[:, CB_STRS:CB_STRS + 128] if samp else cb[:, CB_STRP:CB_STRP + 128]
                W = 128 * ng
                kd, t_kd = kdb[par]
                vt, t_vt = vtb[par]
                aq, t_aq = aqkb[par]
                XT, t_XT = XTb[par]
                v3 = lambda ap: ap.rearrange("p (t n) -> p t n", n=128)
                bk, tb = bank()
                bkb = bk[:].bitcast(BF)
                for t in range(ng):
                    c0 = 128 * (i0 + t)
                    TR(bkb[:, 256 * t:256 * t + 128], kT[:, c0:c0 + 128], ident_b, [t_kT, t_cb], [tb])
                    TR(bkb[:, 256 * t + 128:256 * t + 256], vT[:, c0:c0 + 128], ident_b, [t_vT, t_cb], [tb])
                for t in range(ng):
                    ACT(kd[:, t, :], bkb[:, 256 * t:256 * t + 128], AF.Copy, [tb, t_sm], [t_kd], scale=egl[:, i0 + t, h:h + 1])
                CP("dve", vt[:, 0:ng, :], bkb[:, 0:256 * ng].rearrange("p (t two n) -> p t two n", two=2, n=128)[:, :, 1, :], [tb], [t_vt])
                yield
                for j in range(3):
                    TT("pool", gB3[j][0][:, 0:ng, :], ones_b.unsqueeze(1).broadcast_to([128, ng, 128]),
                       g3[j][:, i0:i0 + ng, h:h + 1].broadcast_to([128, ng, 128]), ALU.mult, [t_cb, t_sm], [gB3[j][1]])
                bk, tb = bank()
                for t in range(ng):
                    o = bk[:, 128 * t:128 * t + 128]
                    for j in range(3):
                        MM(o, gB3[j][0][:, t, :], trib, j == 0, False, [gB3[j][1], t_cb], [tb])
                    for j in range(3):
                        MM(o, ntrib, gB3[j][0][:, t, :], False, False, [gB3[j][1], t_cb], [tb])
                    MM(o, ident_b, negb, False, True, [t_cb], [tb])
                ACT(decT[:, 0:ng, :], v3(bk[:, 0:W]), AF.Exp, [tb], [t_decT])
                TT("pool", decTs[:, 0:ng, :], decT[:, 0:ng, :], strict.unsqueeze(1).broadcast_to([128, ng, 128]), ALU.mult, [t_decT, t_cb], [t_decTs])
                yield
                bkG, tbG = bank()
                if full:
                    bkA, tbA = bank()
                for t in range(ng):
                    c0 = 128 * (i0 + t)
                    MM(bkG[:, 128 * t:128 * t + 128], kT[:, c0:c0 + 128], kT[:, c0:c0 + 128], True, True, [t_kT], [tbG])
                    if full:
                        MM(bkA[:, 128 * t:128 * t + 128], kT[:, c0:c0 + 128], qT[:, c0 - 1024:c0 - 1024 + 128], True, True, [t_kT, t_qT], [tbA])
                for t in range(ng):
                    STT(P0[:, t, :], bkG[:, 128 * t:128 * t + 128], nbt[:, i0 + t, h:h + 1], decTs[:, t, :], ALU.mult, ALU.mult,
                        [tbG, t_sm, t_decTs], [t_P0])
                if full:
                    TT("dve", aq[:, 0:ng, :], v3(bkA[:, 0:W]), decT[:, 0:ng, :], ALU.mult, [tbA, t_decT], [t_aq])
                Y0 = Yb[0][0]
                TT("pool", Y0[:, 0:ng, :], P0[:, 0:ng, :], ident_b.unsqueeze(1).broadcast_to([128, ng, 128]), ALU.add, [t_P0, t_cb], YbH[0])
                yield
                bk, tb = bank()
                bkb = bk[:].bitcast(BF)
                for t in range(ng):
                    TR(bkb[:, 128 * t:128 * t + 128], P0[:, t, :], ident_b, [t_P0, t_cb], [tb])
                Q0 = Qb[0][0]
                CP("act", Q0[:, 0:ng, :], v3(bkb[:, 0:W]), [tb], QbH[0])
                yield
                halves = [(0, 2), (2, 4)] if ng == 4 else [(0, ng)]
                v3h = lambda ap: ap.rearrange("p (t n) -> p t n", n=128)
                mms = []
                for hi, (a, b) in enumerate(halves):
                    bkP, tbP = bank()
                    bkQ, tbQ = bank()
                    for t in range(a, b):
                        o = 128 * (t - a)
                        MM(bkP[:, o:o + 128], Q0[:, t, :], P0[:, t, :], True, True, [QbH[0][hi], t_P0], [tbP])
                        MM(bkQ[:, o:o + 128], P0[:, t, :], Q0[:, t, :], True, True, [QbH[0][hi], t_P0], [tbQ])
                    mms.append((bkP, tbP, bkQ, tbQ))
                for hi, (a, b) in enumerate(halves):
                    bkP, tbP, bkQ, tbQ = mms[hi]
                    wd = 128 * (b - a)
                    CP("act", Pb[0][0][:, a:b, :], v3h(bkP[:, 0:wd]), [tbP], [PbH[0][hi]])
                    CP("dve", Qb[1][0][:, a:b, :], v3h(bkQ[:, 0:wd]), [tbQ], [QbH[1][hi]])
                yield
                pc, qc, yc = 0, 1, 0
                for k in range(1, L + 1):
                    P, Q, Y = Pb[pc][0], Qb[qc][0], Yb[yc][0]
                    Pn, Qn = Pb[1 - pc][0], Qb[1 - qc][0]
                    Yn = Yb[1 - yc][0] if k < L else XT
                    tYn = YbH[1 - yc] if k < L else XTH[par]
                    mms = []
                    for hi, (a, b) in enumerate(halves):
                        if k < L:
                            bkP, tbP = bank()
                            bkQ, tbQ = bank()
                        else:
                            bkP = tbP = bkQ = tbQ = None
                        bkY, tbY = bank()
                        for t in range(a, b):
                            o = 128 * (t - a)
                            if k < L:
                                MM(bkP[:, o:o + 128], Q[:, t, :], P[:, t, :], True, True, [QbH[qc][hi], PbH[pc][hi]], [tbP])
                                MM(bkQ[:, o:o + 128], P[:, t, :], Q[:, t, :], True, True, [QbH[qc][hi], PbH[pc][hi]], [tbQ])
                            MM(bkY[:, o:o + 128], Q[:, t, :], Y[:, t, :], True, True, [QbH[qc][hi], YbH[yc][hi]], [tbY])
                        mms.append((bkP, tbP, bkQ, tbQ, bkY, tbY))
                    for hi, (a, b) in enumerate(halves):
                        bkP, tbP, bkQ, tbQ, bkY, tbY = mms[hi]
                        wd = 128 * (b - a)
                        if k < L:
                            CP("act", Pn[:, a:b, :], v3h(bkP[:, 0:wd]), [tbP], [PbH[1 - pc][hi]])
                            CP("act" if k % 2 == 0 else "dve", Qn[:, a:b, :], v3h(bkQ[:, 0:wd]), [tbQ], [QbH[1 - qc][hi]])
                        TT("dve", Yn[:, a:b, :], v3h(bkY[:, 0:wd]), Y[:, a:b, :], ALU.add, [tbY, YbH[yc][hi]], [tYn[hi]])
                    pc, qc, yc = 1 - pc, 1 - qc, 1 - yc
                    yield

            def groupB(h, tiles, par):
                kd, t_kd = kdb[par]
                vt, t_vt = vtb[par]
                aq, t_aq = aqkb[par]
                XT, t_XT = XTb[par]
                for t, i in enumerate(tiles):
                    samp = (i == 16)
                    full = (i >= 8)
                    c0 = 128 * i
                    mc0 = c0 - 1024
                    sc = lambda arr: arr[:, i, h:h + 1]
                    bk, tb = bank()
                    if not samp:
                        MM(bk[:, 0:128], kT[:, c0:c0 + 128], Sb[:], True, True, [t_kT, t_Sb], [tb])
                        if full:
                            MM(bk[:, 128:256], qT[:, mc0:mc0 + 128], Sb[:], True, True, [t_qT, t_Sb], [tb])
                    else:
                        for s_ in range(16):
                            MM(bk[:, 0:128], kTm[:, s_, :], Ssb[:, s_, :], s_ == 0, s_ == 15, [t_kTm, t_Ssb], [tb])
                        for s_ in range(16):
                            MM(bk[:, 128:256], qTm[:, s_, :], Ssb[:, s_, :], s_ == 0, s_ == 15, [t_qTm, t_Ssb], [tb])
                    STT(rbf[:], bk[:, 0:128], sc(nee), vt[:, t, :], ALU.mult, ALU.add, [tb, t_sm, t_vt], [t_rbf])
                    if full:
                        ACT(tsb[:], bk[:, 128:256], AF.Copy, [tb, t_sm], [t_tsb], scale=sc(ee))
                    yield
                    bku, tbu = bank()
                    MM(bku[:, 0:128], XT[:, t, :], rbf[:], True, True, [XTH[par][(t // 2) if len(tiles) == 4 else 0], t_rbf], [tbu])
                    ACT(ubf[:], bku[:, 0:128], AF.Copy, [tbu, t_sm], [t_ubf], scale=sc(bt))
                    yield
                    if not samp:
                        bks, tbs = bank()
                        MM(bks[:, 0:128], kd[:, t, :], ubf[:], True, True, [t_kd, t_ubf], [tbs])
                        STT(Sb[:], Sf[:], sc(glb), bks[:, 0:128], ALU.mult, ALU.add, [t_Sf, t_sm, tbs], [t_Sb])
                        STT(Sf[:], Sf[:], sc(glb), bks[:, 0:128], ALU.mult, ALU.add, [t_Sf, t_sm, tbs], [t_Sf])
                        yield
                    else:
                        TT("dve", kdm[:], kd[:, t, :].unsqueeze(1).broadcast_to([128, 16, 128]),
                           cf[:, CF_ROWM:CF_ROWM + 16].unsqueeze(2).broadcast_to([128, 16, 128]), ALU.mult, [t_kd, t_cf], [t_kdm])
                        for s_ in range(16):
                            if s_ % 4 == 0:
                                bks, tbs = bank()
                            MM(bks[:, 128 * (s_ % 4):128 * (s_ % 4) + 128], kdm[:, s_, :], ubf[:], True, True, [t_kdm, t_ubf], [tbs])
                            if s_ % 4 == 3:
                                for s2 in range(s_ - 3, s_ + 1):
                                    STT(Sso[:, s2, :], Ss[:, s2, :], glbs[:, s2, h:h + 1], bks[:, 128 * (s2 % 4):128 * (s2 % 4) + 128],
                                        ALU.mult, ALU.add, [t_Ss, t_sm, tbs], [t_Ss])
                                yield
                    if full:
                        bk4, tb4 = bank()
                        MM(bk4[:, 0:128], aq[:, t, :], ubf[:], True, True, [t_aq, t_ubf], [tb4])
                        TT("dve", osb[:], bk4[:, 0:128], tsb[:], ALU.add, [tb4, t_tsb], [t_osb])
                        ACT(junk[:], osb[:], AF.Square, [t_osb], [t_junk, t_sm4], accum=sm4[:, 0:1])
                        yield
                        TS("dve", sm4[:, 1:2], sm4[:, 0:1], 1.0 / 128, ALU.mult, [t_sm4], [t_sm4], s2=1e-6, op1=ALU.add)
                        ACT(sm4[:, 2:3], sm4[:, 1:2], AF.Ln, [t_sm4], [t_sm4])
                        ACT(sm4[:, 3:4], sm4[:, 2:3], AF.Exp, [t_sm4], [t_sm4], scale=-0.5)
                        TS("dve", onb[:], osb[:], sm4[:, 3:4], ALU.mult, [t_osb, t_sm4], [t_onb])
                        yield
                        bko, tbo = bank()
                        bkob = bko[:].bitcast(BF)
                        TR(bkob[:, 0:128], onb[:], ident_b, [t_onb, t_cb], [tbo])
                        STT(mixT[:, 4 + h, mc0:mc0 + 128], bkob[:, 0:128], pv[:, PV_NW:PV_NW + 1], gT[:, mc0:mc0 + 128],
                            ALU.mult, ALU.mult, [tbo, t_pv, t_gT], [t_mix[4 + h]])
                        yield
                    if i == 15:
                        S.dma("sp", odelta_p[h, :, :], Sf[:], reads=[t_Sf])

            def run_il(gens, weights):
                gens = [g for g in gens if g is not None]
                alive = [True] * len(gens)
                while any(alive):
                    for gi, g in enumerate(gens):
                        if not alive[gi]:
                            continue
                        for _ in range(weights[gi]):
                            try:
                                next(g)
                            except StopIteration:
                                alive[gi] = False
                                break

            GROUPS = [[0, 1, 2, 3], [4, 5, 6, 7], [8, 9, 10, 11], [12, 13, 14, 15], [16]]

            for h in range(8):
                S.dma("sp", Ss[:], sdelta[:, h, :, :].rearrange("s k v -> k s v"), writes=[t_Ss])
                CP("pool", Ssb[:], Ss[:], [t_Ss], [t_Ssb])
                S.op("pool", lambda e: e.memset(Sf[:], 0.0), [], [t_Sf])
                S.op("pool", lambda e: e.memset(Sb[:], 0.0), [], [t_Sb])
                wk, tk_ = w_next()
                conv_stream(h, 1, wk, tk_, KV_CH, kT, t_kT, 0, True, 1.0)
                wv, tv_ = w_next()
                conv_stream(h, 2, wv, tv_, KV_CH, vT, t_vT, 0, False, 1.0)
                wq, tq_ = w_next()
                conv_stream(h, 0, wq, tq_, OWN_CH, qT, t_qT, 1024, True, 128.0 ** -0.5)
                wg, tg_ = w_next()
                for (c0, n) in MEM_CH:
                    bk, tb = proj_fm(wg, tg_, c0, n)
                    ACT(gT[:, c0 - 1024:c0 - 1024 + n], bk[:, 0:n], AF.Silu, [tb], [t_gT])
                colm = cb[:, CB_COLM:CB_COLM + 2048].rearrange("p (s i) -> p s i", i=128)
                TT("dve", kTm[:], kT[:, 2048:2176].unsqueeze(1).broadcast_to([128, 16, 128]), colm, ALU.mult, [t_kT, t_cb], [t_kTm])
                TT("pool", qTm[:], qT[:, 1024:1152].unsqueeze(1).broadcast_to([128, 16, 128]), colm, ALU.mult, [t_qT, t_cb], [t_qTm])
                prevB = None
                for gi, tiles in enumerate(GROUPS):
                    ga = groupA(h, tiles, gi % 2)
                    if prevB is None:
                        run_il([ga], [1])
                    else:
                        run_il([ga, prevB], [int(os.environ.get('K_WA', '1')), int(os.environ.get('K_WB', '1'))])
                    prevB = groupB(h, tiles, gi % 2)
                run_il([prevB], [1])
                S.dma("sp", odelta_s[:, h, :, :].rearrange("s k v -> k s v"), Sso[:], reads=[t_Sso])
            S.barrier()

        es2.close()
        with ExitStack() as ph, contextlib.suppress(_Skip):
            if STAGE < 5:
                raise _Skip()
            wo, t_wo = sb(ph, [128, 16, 2048], BF)
            lngb, t_lngb = sb(ph, [128, 2, 2048])
            xr = [sb(ph, [128, 2048]) for _ in range(2)]
            ypb = [sb(ph, [128, 2048]) for _ in range(2)]
            yo = [sb(ph, [128, 2048]) for _ in range(2)]
            jkb = [sb(ph, [128, 2048], BF) for _ in range(2)]
            stb = [sb(ph, [128, 8]) for _ in range(2)]
            for k4 in range(16):
                S.dma("pool", wo[:, k4, :].rearrange("p (a n) -> p a n", n=512),
                      w_out[128 * k4:128 * k4 + 128, :].rearrange("p (a n) -> p a n", n=512), writes=[t_wo])
            S.dma("sp", lngb[:], lngb_d[:, :, :], writes=[t_lngb])
            S.dma("sp", xr[0][0][:], xall[1024:1152, :], writes=[xr[0][1]])
            for ti in range(9):
                xr_, txr = xr[ti % 2]
                yo_, tyo = yo[ti % 2]
                yp, t_yp = ypb[ti % 2]
                jk, t_jk = jkb[ti % 2]
                st, t_st = stb[ti % 2]
                if ti + 1 < 9:
                    S.dma("sp", xr[(ti + 1) % 2][0][:], xall[1024 + 128 * (ti + 1):1024 + 128 * (ti + 1) + 128, :], writes=[xr[(ti + 1) % 2][1]])
                for nb in range(4):
                    bk, tb = bank()
                    for k in range(16):
                        MM(bk[:, 0:512], mixT[:, k, 128 * ti:128 * ti + 128], wo[:, k, 512 * nb:512 * nb + 512], k == 0, k == 15, [t_mix[k], t_wo], [tb])
                    STT(yp[:, 512 * nb:512 * nb + 512], xr_[:, 512 * nb:512 * nb + 512], ALPHA, bk[:, 0:512], ALU.mult, ALU.add, [txr, tb], [t_yp])
                S.op("dve", lambda e, st=st, yp=yp: e.reduce_sum(out=st[:, 0:1], in_=yp[:], axis=AX.X), [t_yp], [t_st])
                ACT(jk[:], yp[:], AF.Square, [t_yp], [t_jk, t_st], accum=st[:, 1:2])
                TS("dve", st[:, 2:3], st[:, 0:1], 1.0 / 2048, ALU.mult, [t_st], [t_st])
                TT("dve", st[:, 3:4], st[:, 2:3], st[:, 2:3], ALU.mult, [t_st], [t_st])
                STT(st[:, 4:5], st[:, 1:2], 1.0 / 2048, st[:, 3:4], ALU.mult, ALU.subtract, [t_st], [t_st])
                TS("dve", st[:, 4:5], st[:, 4:5], 1e-5, ALU.add, [t_st], [t_st])
                ACT(st[:, 5:6], st[:, 4:5], AF.Ln, [t_st], [t_st])
                ACT(st[:, 6:7], st[:, 5:6], AF.Exp, [t_st], [t_st], scale=-0.5)
                STT(st[:, 7:8], st[:, 2:3], -1.0, st[:, 6:7], ALU.mult, ALU.mult, [t_st], [t_st])
                ACT(yp[:], yp[:], AF.Identity, [t_yp, t_st], [t_yp], scale=st[:, 6:7], bias=st[:, 7:8])
                TT("dve", yp[:], yp[:], lngb[:, 0, :], ALU.mult, [t_yp, t_lngb], [t_yp])
                TT("pool", yo_[:], yp[:], lngb[:, 1, :], ALU.add, [t_yp, t_lngb], [tyo])
                S.dma("sp", y_d[128 * ti:128 * ti + 128, :], yo_[:], reads=[tyo])
            S.barrier()

        with nc.Block() as block:
            @block.tensor
            def _(e):
                S.replay("pe", e)

            @block.scalar
            def _(e):
                S.replay("act", e)

            @block.vector
            def _(e):
                S.replay("dve", e)

            @block.gpsimd
            def _(e):
                S.replay("pool", e)

            @block.sync
            def _(e):
                S.replay("sp", e)
        build_nc.stats = dict(S.cnt)
    return nc


def _consts():
    idx = np.arange(128)
    blk = idx // 8
    t, i = idx[:, None], idx[None, :]
    same = (blk[:, None] == blk[None, :])
    cf = np.zeros((128, NCF), np.float32)
    cf[:, CF_ID:CF_ID + 128] = np.eye(128)
    cf[:, CF_ROWM:CF_ROWM + 16] = (blk[:, None] == np.arange(16)[None, :])
    cb = np.zeros((128, NCB), np.float32)
    cb[:, CB_ID:CB_ID + 128] = np.eye(128)
    cb[:, CB_ONE:CB_ONE + 128] = 1.0
    cb[:, CB_STRP:CB_STRP + 128] = (i > t)
    cb[:, CB_STRS:CB_STRS + 128] = (i > t) & same
    colm = np.zeros((128, 16, 128), np.float32)
    for s in range(16):
        colm[:, s, 8 * s:8 * s + 8] = 1.0
    cb[:, CB_COLM:CB_COLM + 2048] = colm.reshape(128, 2048)
    cb[:, CB_TRIP:CB_TRIP + 128] = (t <= i)
    cb[:, CB_TRIS:CB_TRIS + 128] = (t <= i) & same
    cb[:, CB_BLKS:CB_BLKS + 128] = same
    cb[:, CB_NEGP:CB_NEGP + 128] = np.where(i >= t, 0.0, NEG)
    cb[:, CB_NEGS:CB_NEGS + 128] = np.where((i >= t) & same, 0.0, NEG)
    cb[:, CB_NTRP:CB_NTRP + 128] = -1.0 * (t <= i)
    cb[:, CB_NTRS:CB_NTRS + 128] = -1.0 * ((t <= i) & same)
    return cf, cb.astype(ml_dtypes.bfloat16)


_NC_CACHE = {}


def kernel(x_prompt, x_sample, mem_prompt, state_conv, state_qkv_conv, state_delta, cache_mem_k, cache_mem_v,
           w_in, conv_w, conv_b, conv_ln_g, conv_ln_b, qkv_conv_w, a_log, dt_bias, delta_norm_w,
           w_mem_k, w_mem_v, w_out, ln_g, ln_b):
    f = lambda a: np.ascontiguousarray(np.asarray(a, dtype=np.float32))
    x_prompt, x_sample, mem_prompt = f(x_prompt), f(x_sample), f(mem_prompt)
    cf, cb = _consts()
    pv = np.zeros((128, NPV), np.float32)
    cwT = f(conv_w)[0].T.reshape(4, 128, 31)
    pv[:, PV_CW:PV_CW + 124] = cwT.transpose(1, 0, 2).reshape(128, 124)
    pv[:, PV_CB:PV_CB + 4] = f(conv_b)[0].reshape(4, 128).T
    pv[:, PV_CLG:PV_CLG + 4] = f(conv_ln_g)[0].reshape(4, 128).T
    pv[:, PV_CLB:PV_CLB + 4] = f(conv_ln_b)[0].reshape(4, 128).T
    qwT = f(qkv_conv_w)[0].T.reshape(24, 128, 4)
    pv[:, PV_QW:PV_QW + 96] = qwT.transpose(1, 0, 2).reshape(128, 96)
    pv[:, PV_NW] = f(delta_norm_w)[0]
    pv[:, PV_AL:PV_AL + 8] = f(a_log)[0][None, :]
    pv[:, PV_DT:PV_DT + 8] = f(dt_bias)[0][None, :]
    lngb = np.ascontiguousarray(np.broadcast_to(np.stack([f(ln_g)[0], f(ln_b)[0]])[None], (128, 2, 2048)))
    W_in, W_mk, W_mv, W_out = f(w_in)[0], f(w_mem_k)[0], f(w_mem_v)[0], f(w_out)[0]
    sc, sq, sd, ck, cv = f(state_conv)[0], f(state_qkv_conv)[0], f(state_delta)[0], f(cache_mem_k)[0], f(cache_mem_v)[0]
    in_maps = []
    for c in range(8):
        b, hf = c // 2, c % 2
        xall = np.zeros((NCOL, 2048), np.float32)
        if hf == 1:
            xall[0:1024] = x_prompt[b, 0:1024]
        xall[1024:2048] = x_prompt[b, 1024 * hf:1024 * hf + 1024]
        xall[2048:] = x_sample[16 * c:16 * c + 16].reshape(128, 2048)
        sl = slice(16 * c, 16 * c + 16)
        in_maps.append({
            "xall": xall, "memp": mem_prompt[b], "sconv": sc[sl], "sqkv": sq[sl], "sdelta": sd[sl],
            "ck": ck[sl], "cv": cv[sl], "w_in": W_in, "w_mk": W_mk, "w_mv": W_mv, "w_out": W_out,
            "cf": cf, "cb": cb, "pv": pv, "lngb": lngb,
        })
    if os.environ.get("K_CORES"):
        ncores = int(os.environ["K_CORES"])
        nc = build_nc()
        res = run_bass_kernel_spmd(nc, in_maps[:ncores], core_ids=list(range(ncores)), trace=bool(os.environ.get("K_TRACE")))
        kernel.last = res
        R = list(res.results) + [res.results[0]] * (8 - ncores)
    else:
        if "nc" not in _NC_CACHE:
            _NC_CACHE["nc"] = build_nc()
        nc = _NC_CACHE["nc"]
        res = run_bass_kernel_spmd(nc, in_maps, core_ids=list(range(8)))
        R = res.results
    y_p = np.zeros((4, 2048, 2048), np.float32)
    y_s = np.zeros((128, 8, 2048), np.float32)
    o_conv_p = np.zeros((1, 4, 30, 512), np.float32)
    o_qkv_p = np.zeros((1, 4, 3, 3072), np.float32)
    o_delta_p = np.zeros((1, 4, 8, 128, 128), np.float32)
    o_mk = np.zeros((1, 4, 256, 4, 128), np.float32)
    o_mv = np.zeros((1, 4, 256, 4, 128), np.float32)
    o_conv_s = np.zeros((1, 128, 30, 512), np.float32)
    o_qkv_s = np.zeros((1, 128, 3, 3072), np.float32)
    o_delta_s = np.zeros((1, 128, 8, 128, 128), np.float32)
    for c in range(8):
        b, hf = c // 2, c % 2
        r = R[c]
        y_p[b, 1024 * hf:1024 * hf + 1024] = r["y"][0:1024]
        y_s[16 * c:16 * c + 16] = r["y"][1024:1152].reshape(16, 8, 2048)
        sl = slice(16 * c, 16 * c + 16)
        o_conv_s[0, sl] = r["oconv_s"]
        o_qkv_s[0, sl] = r["oqkv_s"]
        o_delta_s[0, sl] = r["odelta_s"]
        if hf == 1:
            o_conv_p[0, b] = r["oconv_p"]
            o_qkv_p[0, b] = r["oqkv_p"]
            o_delta_p[0, b] = r["odelta_p"]
        else:
            o_mk[0, b] = r["omk"].reshape(256, 4, 128)
            o_mv[0, b] = r["omv"].reshape(256, 4, 128)
    return (y_p, y_s, o_conv_p, o_qkv_p, o_delta_p, o_mk, o_mv, o_conv_s, o_qkv_s, o_delta_s)
```
